# Optimizing a Trainium2 kernel written in Bass

```python
import math
import jax, jax.numpy as jnp
from jax import lax
import numpy as np

D_MODEL = 1024
BATCH = 8
SEQ = 4096
DEPTH = 2

GRID_W = 64
CTX_LEN = 256
CHUNK = 64
EPS = 1e-6
A_HEADS = 4
A_DK = 128
A_DV = 128
A_WIDTH = A_HEADS * A_DK
B_HEADS = 4
B_DK = 128
B_DV = 128
B_WIDTH = B_HEADS * B_DV
B_CONV = 5
C_WIDTH = 512
C_GROUP = 16
C_GROUPS = C_WIDTH // C_GROUP
C_STATE = 64
N_BRANCH = 3
D_FF = 4 * D_MODEL
IN_SIZES = (A_WIDTH, A_HEADS * A_DV, 2 * A_WIDTH, A_HEADS * A_DV,
            2 * B_HEADS * B_DK + B_HEADS * B_DV, B_WIDTH, 2 * B_HEADS, 2 * B_HEADS,
            C_WIDTH, N_BRANCH * D_MODEL)
IN_SPLITS = tuple(int(s) for s in np.cumsum(IN_SIZES)[:-1])
D_IN = int(sum(IN_SIZES))

kernel_name = 'hybrid_hgrn2_gdn_s5_prefix_dit'


def _rmsnorm(x, w):
    xf = x.astype(jnp.float32)
    y = xf * lax.rsqrt(jnp.mean(xf * xf, axis=-1, keepdims=True) + EPS)
    return (y * w.astype(jnp.float32)).astype(x.dtype)


def _modulate(h, shift, scale):
    return h * (1.0 + scale) + shift


def _heads(a, n_heads):
    b, n, _ = a.shape
    return a.reshape(b, n, n_heads, -1).transpose(0, 2, 1, 3)


def _merge_heads(a):
    b, h, n, d = a.shape
    return a.transpose(0, 2, 1, 3).reshape(b, n, h * d)


def _l2norm(a):
    return a * lax.rsqrt(jnp.sum(a * a, axis=-1, keepdims=True) + EPS)


def _to_chunks(a):
    b, h, n = a.shape[:3]
    return jnp.moveaxis(a.reshape((b, h, n // CHUNK, CHUNK) + a.shape[3:]), 2, 0)


def _from_chunks(a):
    a = jnp.moveaxis(a, 0, 2)
    return a.reshape(a.shape[:2] + (-1,) + a.shape[4:])


def _gla_chunked(q, k, v, log_f, s0):
    incl = jnp.tril(jnp.ones((CHUNK, CHUNK), dtype=bool))

    def step(s, inp):
        qc, kc, vc, gc = inp
        cum = jnp.cumsum(gc, axis=2)
        diff = cum[:, :, :, None, :] - cum[:, :, None, :, :]
        decay = jnp.where(incl[:, :, None], jnp.exp(jnp.minimum(diff, 0.0)), 0.0)
        att = jnp.einsum('bhtd,bhsd,bhtsd->bhts', qc, kc, decay)
        o = (jnp.einsum('bhts,bhsv->bhtv', att, vc)
             + jnp.einsum('bhtd,bhdv->bhtv', qc * jnp.exp(cum), s))
        last = cum[:, :, -1:, :]
        s = (s * jnp.exp(last[:, :, 0, :, None])
             + jnp.einsum('bhsd,bhsv->bhdv', kc * jnp.exp(last - cum), vc))
        return s, o

    s_fin, o = lax.scan(step, s0, (_to_chunks(q), _to_chunks(k), _to_chunks(v), _to_chunks(log_f)))
    return _from_chunks(o), s_fin


def _delta_chunked(q, k, v, beta, log_a, s0):
    incl = jnp.tril(jnp.ones((CHUNK, CHUNK), dtype=bool))
    strict = jnp.tril(jnp.ones((CHUNK, CHUNK), dtype=bool), k=-1)
    eye = jnp.eye(CHUNK, dtype=jnp.float32)
    dv = v.shape[-1]

    def step(s, inp):
        qc, kc, vc, bc, gc = inp
        cum = jnp.cumsum(gc, axis=-1)
        dmask = jnp.where(incl, jnp.exp(jnp.minimum(cum[..., :, None] - cum[..., None, :], 0.0)), 0.0)
        kb = kc * bc[..., None]
        m = jnp.where(strict, -jnp.einsum('bhtd,bhsd->bhts', kb, kc) * dmask, 0.0)
        rhs = jnp.concatenate([vc * bc[..., None], kb * jnp.exp(cum)[..., None]], axis=-1)
        sol = lax.linalg.triangular_solve(eye - m, rhs, left_side=True, lower=True, unit_diagonal=True)
        u, w = sol[..., :dv], sol[..., dv:]
        v_new = u - jnp.einsum('bhtd,bhdv->bhtv', w, s)
        att = jnp.einsum('bhtd,bhsd->bhts', qc, kc) * dmask
        o = (jnp.einsum('bhtd,bhdv->bhtv', qc * jnp.exp(cum)[..., None], s)
             + jnp.einsum('bhts,bhsv->bhtv', att, v_new))
        last = cum[..., -1:]
        s = (s * jnp.exp(last)[..., None]
             + jnp.einsum('bhsd,bhsv->bhdv', kc * jnp.exp(last - cum)[..., None], v_new))
        return s, o

    s_fin, o = lax.scan(step, s0, (_to_chunks(q), _to_chunks(k), _to_chunks(v),
                                   _to_chunks(beta), _to_chunks(log_a)))
    return _from_chunks(o), s_fin


def _bidir_scan(run, ctx_f, ctx_b, lat_f, lat_b, s0):
    flip = lambda arrs: [jnp.flip(a, axis=2) for a in arrs]
    yc_f, sc_f = run(*ctx_f, s0)
    yc_b, sc_b = run(*flip(ctx_b), s0)
    yl_f, _ = run(*lat_f, sc_f)
    yl_b, _ = run(*flip(lat_b), sc_b)
    return yc_f + jnp.flip(yc_b, axis=2), yl_f + jnp.flip(yl_b, axis=2)


def _hgrn2_prepare(q, i, f2, lb):
    b, n, _ = q.shape
    qh = _heads(jax.nn.silu(q.astype(jnp.float32)), A_HEADS)
    vh = _heads(i.astype(jnp.float32), A_HEADS)
    z = jnp.moveaxis(f2.astype(jnp.float32).reshape(b, n, 2, A_WIDTH), 2, 0)
    lbb = lb[:, None, None, :]
    fval = lbb + (1.0 - lbb) * jax.nn.sigmoid(z)
    args = [(qh, _heads(1.0 - fval[d], A_HEADS), vh, _heads(jnp.log(fval[d]), A_HEADS)) for d in range(2)]
    return args[0], args[1]


def _short_conv(x, w):
    ch = x.shape[-1]
    return lax.conv_general_dilated(x, w[:, None, :].astype(x.dtype), window_strides=(1,),
                                    padding=[(B_CONV // 2, B_CONV // 2)],
                                    dimension_numbers=('NWC', 'WIO', 'NWC'), feature_group_count=ch)


def _gdn_prepare(qkv, beta2, a2, conv_w, a_log, dt_bias):
    b, n, _ = qkv.shape
    f32 = jnp.float32
    qkv = jax.nn.silu(_short_conv(qkv, conv_w)).astype(f32)
    q, k, v = jnp.split(qkv, [B_HEADS * B_DK, 2 * B_HEADS * B_DK], axis=-1)
    q = _l2norm(_heads(q, B_HEADS)) * (B_DK ** -0.5)
    k = _l2norm(_heads(k, B_HEADS))
    v = _heads(v, B_HEADS)
    beta = jax.nn.sigmoid(beta2.astype(f32)).reshape(b, n, 2, B_HEADS).transpose(2, 0, 3, 1)
    a = a2.astype(f32).reshape(b, n, 2, B_HEADS).transpose(2, 0, 3, 1)
    g = -jnp.exp(a_log.astype(f32))[:, None, :, None] * jax.nn.softplus(a + dt_bias.astype(f32)[:, None, :, None])
    return (q, k, v, beta[0], g[0]), (q, k, v, beta[1], g[1])


def _s5_discretize(lam_re, lam_im, log_dt, b_re, b_im):
    f32 = jnp.float32
    lr = jnp.minimum(lam_re.astype(f32), -1e-4)
    li = lam_im.astype(f32)
    dt = jnp.exp(log_dt.astype(f32))[:, None]
    mag = jnp.exp(lr * dt)
    ar, ai = mag * jnp.cos(li * dt), mag * jnp.sin(li * dt)
    den = lr * lr + li * li
    cr = ((ar - 1.0) * lr + ai * li) / den
    ci = (ai * lr - (ar - 1.0) * li) / den
    br, bi = b_re.astype(f32), b_im.astype(f32)
    bbr = cr[..., None] * br - ci[..., None] * bi
    bbi = cr[..., None] * bi + ci[..., None] * br
    return ar, ai, bbr, bbi


def _complex_affine_combine(e1, e2):
    a1r, a1i, b1r, b1i = e1
    a2r, a2i, b2r, b2i = e2
    return (a2r * a1r - a2i * a1i, a2r * a1i + a2i * a1r,
            a2r * b1r - a2i * b1i + b2r, a2r * b1i + a2i * b1r + b2i)


def _s5_scan(u, ar, ai, bbr, bbi, x0r, x0i):
    n = u.shape[1]
    bur = jnp.einsum('gsc,bngc->bngs', bbr, u)
    bui = jnp.einsum('gsc,bngc->bngs', bbi, u)
    bur = bur.at[:, 0].add(ar * x0r - ai * x0i)
    bui = bui.at[:, 0].add(ar * x0i + ai * x0r)
    a_r = jnp.broadcast_to(ar, (1, n) + ar.shape)
    a_i = jnp.broadcast_to(ai, (1, n) + ai.shape)
    _, _, xr, xi = lax.associative_scan(_complex_affine_combine, (a_r, a_i, bur, bui), axis=1)
    return xr, xi


def _s5_readout(xr, xi, c_re, c_im):
    f32 = jnp.float32
    y = (jnp.einsum('gcs,bngs->bngc', c_re.astype(f32), xr)
         - jnp.einsum('gcs,bngs->bngc', c_im.astype(f32), xi))
    return y.reshape(y.shape[0], y.shape[1], C_WIDTH)


def _s5_mixer(u_ctx, u_lat, rows, lam_re, lam_im, log_dt, b_re, b_im, c_re, c_im, d_skip):
    f32 = jnp.float32
    dt = u_lat.dtype
    bsz, n, _ = u_lat.shape
    uc = u_ctx.astype(f32)
    ul = u_lat.astype(f32).reshape(bsz, rows, GRID_W, C_WIDTH).transpose(0, 2, 1, 3).reshape(bsz, n, C_WIDTH)
    gc = uc.reshape(bsz, uc.shape[1], C_GROUPS, C_GROUP)
    gl = ul.reshape(bsz, n, C_GROUPS, C_GROUP)
    x0 = jnp.zeros((bsz, C_GROUPS, C_STATE), f32)
    disc_f = _s5_discretize(lam_re[0], lam_im[0], log_dt[0], b_re, b_im)
    disc_b = _s5_discretize(lam_re[1], lam_im[1], log_dt[1], b_re, b_im)
    cfr, cfi = _s5_scan(gc, *disc_f, x0, x0)
    cbr, cbi = _s5_scan(jnp.flip(gc, axis=1), *disc_b, x0, x0)
    lfr, lfi = _s5_scan(gl, *disc_f, cfr[:, -1], cfi[:, -1])
    lbr, lbi = _s5_scan(jnp.flip(gl, axis=1), *disc_b, cbr[:, -1], cbi[:, -1])
    d = d_skip.astype(f32)
    yc = (_s5_readout(cfr, cfi, c_re, c_im) + jnp.flip(_s5_readout(cbr, cbi, c_re, c_im), axis=1) + d * uc)
    yl = (_s5_readout(lfr, lfi, c_re, c_im) + jnp.flip(_s5_readout(lbr, lbi, c_re, c_im), axis=1) + d * ul)
    yl = yl.reshape(bsz, GRID_W, rows, C_WIDTH).transpose(0, 2, 1, 3).reshape(bsz, n, C_WIDTH)
    return yc.astype(dt), yl.astype(dt)


def _norm_gate(o, w, gate):
    return _merge_heads(_rmsnorm(o, w)) * jax.nn.silu(gate.astype(jnp.float32))


def _branch_merge(oa, ga, ob, zb, yc, gate_pre, hgrn_norm_w, gdn_norm_w, w_glu, w_ba, w_bb, w_bc, w_o):
    dt = gate_pre.dtype
    ya = _norm_gate(oa, hgrn_norm_w, ga).astype(dt) @ w_ba
    yb = _norm_gate(ob, gdn_norm_w, zb).astype(dt) @ w_bb
    z = jax.nn.gelu(yc)
    ycc = (z * jax.nn.sigmoid(z @ w_glu)) @ w_bc
    g_a, g_b, g_c = jnp.split(jax.nn.sigmoid(gate_pre.astype(jnp.float32)).astype(dt), N_BRANCH, axis=-1)
    return (g_a * ya + g_b * yb + g_c * ycc) @ w_o


def _sqrelu_mlp(h, w1, w2):
    return jnp.square(jax.nn.relu(h @ w1)) @ w2


def setup_inputs(seed: int = 0) -> dict:
    key = jax.random.key(seed)
    ks = iter(jax.random.split(key, 40))
    f32 = jnp.float32
    nrm = lambda shape, scale: jax.random.normal(next(ks), shape, f32) * scale
    unif = lambda shape, lo, hi: jax.random.uniform(next(ks), shape, f32, minval=lo, maxval=hi)
    d = D_MODEL
    dt_init = jnp.exp(unif((DEPTH, 2, B_HEADS), math.log(1e-3), math.log(1e-1)))
    return {
        'x': nrm((BATCH, SEQ, d), 1.0),
        'c': nrm((BATCH, d), 1.0),
        'ctx': nrm((BATCH, CTX_LEN, d), 1.0),
        'c_ctx': nrm((d,), 1.0),
        'ada_w': nrm((DEPTH, d, 6 * d), 0.5 * d ** -0.5),
        'ada_b': nrm((DEPTH, 6 * d), 0.02),
        'norm1_w': 1.0 + nrm((DEPTH, d), 0.02),
        'w_in': nrm((DEPTH, d, D_IN), d ** -0.5),
        'hgrn_lb_logits': nrm((DEPTH, 2, A_WIDTH), 0.5),
        'hgrn_norm_w': 1.0 + nrm((DEPTH, A_DV), 0.02),
        'gdn_conv_w': nrm((DEPTH, B_CONV, 2 * B_HEADS * B_DK + B_HEADS * B_DV), B_CONV ** -0.5),
        'gdn_a_log': jnp.log(unif((DEPTH, 2, B_HEADS), 1.0, 16.0)),
        'gdn_dt_bias': dt_init + jnp.log(-jnp.expm1(-dt_init)),
        'gdn_norm_w': 1.0 + nrm((DEPTH, B_DV), 0.02),
        's5_lam_re': -0.5 + nrm((DEPTH, 2, C_GROUPS, C_STATE), 0.01),
        's5_lam_im': jnp.pi * jnp.arange(C_STATE, dtype=f32) + nrm((DEPTH, 2, C_GROUPS, C_STATE), 0.01),
        's5_log_dt': unif((DEPTH, 2, C_GROUPS), math.log(1e-3), math.log(1e-1)),
        's5_b_re': nrm((DEPTH, C_GROUPS, C_STATE, C_GROUP), (2 * C_GROUP) ** -0.5),
        's5_b_im': nrm((DEPTH, C_GROUPS, C_STATE, C_GROUP), (2 * C_GROUP) ** -0.5),
        's5_c_re': nrm((DEPTH, C_GROUPS, C_GROUP, C_STATE), C_STATE ** -0.5),
        's5_c_im': nrm((DEPTH, C_GROUPS, C_GROUP, C_STATE), C_STATE ** -0.5),
        's5_d': nrm((DEPTH, C_WIDTH), 1.0),
        's5_w_glu': nrm((DEPTH, C_WIDTH, C_WIDTH), C_WIDTH ** -0.5),
        'w_branch_a': nrm((DEPTH, A_HEADS * A_DV, d), (A_HEADS * A_DV) ** -0.5),
        'w_branch_b': nrm((DEPTH, B_WIDTH, d), B_WIDTH ** -0.5),
        'w_branch_c': nrm((DEPTH, C_WIDTH, d), C_WIDTH ** -0.5),
        'w_out': nrm((DEPTH, d, d), d ** -0.5),
        'norm2_w': 1.0 + nrm((DEPTH, d), 0.02),
        'w_ff1': nrm((DEPTH, d, D_FF), d ** -0.5),
        'w_ff2': nrm((DEPTH, D_FF, d), D_FF ** -0.5),
        'final_norm_w': 1.0 + nrm((d,), 0.02),
    }


def reference(x, c, ctx, c_ctx, ada_w, ada_b, norm1_w, w_in, hgrn_lb_logits, hgrn_norm_w,
              gdn_conv_w, gdn_a_log, gdn_dt_bias, gdn_norm_w,
              s5_lam_re, s5_lam_im, s5_log_dt, s5_b_re, s5_b_im, s5_c_re, s5_c_im, s5_d, s5_w_glu,
              w_branch_a, w_branch_b, w_branch_c, w_out, norm2_w, w_ff1, w_ff2, final_norm_w):
    bsz, n, _ = x.shape
    rows = n // GRID_W
    lb_all = jnp.cumsum(jax.nn.softmax(hgrn_lb_logits.astype(jnp.float32), axis=0), axis=0)
    lb_all = lb_all - lb_all[0]
    s0_a = jnp.zeros((bsz, A_HEADS, A_DK, A_DV), jnp.float32)
    s0_b = jnp.zeros((bsz, B_HEADS, B_DK, B_DV), jnp.float32)
    xl, xc = x, ctx
    for l in range(DEPTH):
        last = l == DEPTH - 1
        ml = [m[:, None, :] for m in jnp.split(jax.nn.silu(c) @ ada_w[l] + ada_b[l], 6, axis=-1)]
        mc = jnp.split(jax.nn.silu(c_ctx) @ ada_w[l] + ada_b[l], 6, axis=-1)
        pl = jnp.split(_modulate(_rmsnorm(xl, norm1_w[l]), ml[0], ml[1]) @ w_in[l], IN_SPLITS, axis=-1)
        pc = jnp.split(_modulate(_rmsnorm(xc, norm1_w[l]), mc[0], mc[1]) @ w_in[l], IN_SPLITS, axis=-1)
        a_cf, a_cb = _hgrn2_prepare(pc[0], pc[1], pc[2], lb_all[l])
        a_lf, a_lb = _hgrn2_prepare(pl[0], pl[1], pl[2], lb_all[l])
        oa_c, oa_l = _bidir_scan(_gla_chunked, a_cf, a_cb, a_lf, a_lb, s0_a)
        b_cf, b_cb = _gdn_prepare(pc[4], pc[6], pc[7], gdn_conv_w[l], gdn_a_log[l], gdn_dt_bias[l])
        b_lf, b_lb = _gdn_prepare(pl[4], pl[6], pl[7], gdn_conv_w[l], gdn_a_log[l], gdn_dt_bias[l])
        ob_c, ob_l = _bidir_scan(_delta_chunked, b_cf, b_cb, b_lf, b_lb, s0_b)
        yc_c, yc_l = _s5_mixer(pc[8], pl[8], rows, s5_lam_re[l], s5_lam_im[l], s5_log_dt[l],
                               s5_b_re[l], s5_b_im[l], s5_c_re[l], s5_c_im[l], s5_d[l])
        br = (hgrn_norm_w[l], gdn_norm_w[l], s5_w_glu[l], w_branch_a[l], w_branch_b[l], w_branch_c[l], w_out[l])
        xl = xl + ml[2] * _branch_merge(oa_l, pl[3], ob_l, pl[5], yc_l, pl[9], *br)
        xl = xl + ml[5] * _sqrelu_mlp(_modulate(_rmsnorm(xl, norm2_w[l]), ml[3], ml[4]), w_ff1[l], w_ff2[l])
        if not last:
            xc = xc + mc[2] * _branch_merge(oa_c, pc[3], ob_c, pc[5], yc_c, pc[9], *br)
            xc = xc + mc[5] * _sqrelu_mlp(_modulate(_rmsnorm(xc, norm2_w[l]), mc[3], mc[4]), w_ff1[l], w_ff2[l])
    return _rmsnorm(xl, final_norm_w)
```

```python
import numpy as np
from contextlib import ExitStack
import concourse.bass as bass
import concourse.mybir as mybir
from concourse.bass_utils import run_bass_kernel_spmd

F32 = mybir.dt.float32
BF16 = mybir.dt.bfloat16
I32 = mybir.dt.int32
U8 = mybir.dt.uint8
AF = mybir.ActivationFunctionType
ALU = mybir.AluOpType
AX = mybir.AxisListType

ENGS = ("tensor", "vector", "scalar", "gpsimd", "sync")

T = 4352
NCTX = 256
NCH = 68
D = 1024
DIN = 8208
EPS = 1e-6


class Buf:
    def __init__(self, t, name):
        self.t = t
        self.name = name
        self.tr = {}

    def __getitem__(self, idx):
        return self.t[idx]


def _norm(x):
    if isinstance(x, Buf):
        return (x, "*")
    return x


_UID = [0]


_SHARED = {}


def _shared(nc, n_dma_sems=16):
    k = id(nc)
    if k not in _SHARED:
        st = ExitStack()
        sh = dict(stack=st, esem={}, ecount={e: 0 for e in ENGS}, dsems={}, dval={}, dnext={}, waited={e: {} for e in ENGS})
        for e in ENGS:
            sh["esem"][e] = st.enter_context(nc.semaphore("es_" + e))
        for q in ("sync", "gpsimd"):
            sh["dsems"][q] = [st.enter_context(nc.semaphore("ds_%s%d" % (q, i))) for i in range(n_dma_sems)]
            sh["dval"][q] = [0] * n_dma_sems
            sh["dnext"][q] = 0
        _SHARED[k] = sh
    return _SHARED[k]


class Prog:
    def __init__(self, nc, n_dma_sems=16):
        self.nc = nc
        self.stack = ExitStack()
        sh = _shared(nc, n_dma_sems)
        self.sh = sh
        self.ops = {e: [] for e in ENGS}
        self.esem = sh["esem"]
        self.ecount = sh["ecount"]
        self.dsems = sh["dsems"]
        self.dval = sh["dval"]
        self.dnext = sh["dnext"]
        self.waited = sh["waited"]
        self.nops = 0
        self._nm = 0

    def sb(self, name, shape, dt=F32):
        _UID[0] += 1
        return Buf(self.stack.enter_context(self.nc.sbuf_tensor("%s_%d" % (name, _UID[0]), list(shape), dt)), name)

    def ps(self, name, shape, dt=F32):
        _UID[0] += 1
        b = Buf(self.stack.enter_context(self.nc.psum_tensor("%s_%d" % (name, _UID[0]), list(shape), dt)), name)
        b.excl = True
        return b

    def _deps(self, reads, writes):
        deps = {}

        def add(tok):
            if tok is None:
                return
            s, v = tok
            k = id(s)
            if k not in deps or deps[k][1] < v:
                deps[k] = (s, v)

        for b, key in map(_norm, reads):
            keys = list(b.tr.keys()) if key == "*" else [key, "*"]
            for k in keys:
                tr = b.tr.get(k)
                if tr:
                    add(tr[0])
        for b, key in map(_norm, writes):
            keys = list(b.tr.keys()) if key == "*" else [key, "*"]
            for k in keys:
                tr = b.tr.get(k)
                if tr:
                    add(tr[0])
                    for tok in tr[1].values():
                        add(tok)
        return deps

    def _update(self, reads, writes, tok):
        s, v = tok
        for b, key in map(_norm, reads):
            tr = b.tr.setdefault(key, [None, {}])
            tr[1][id(s)] = tok
        for b, key in map(_norm, writes):
            if key == "*":
                b.tr = {"*": [tok, {}]}
            else:
                b.tr[key] = [tok, {}]

    def _waits(self, eng, deps, skip_own=False):
        w = []
        wd = self.waited[eng]
        for k, (s, v) in deps.items():
            if skip_own and s is self.esem[eng]:
                continue
            if wd.get(k, 0) >= v:
                continue
            wd[k] = v
            w.append((s, v))
        return w

    @staticmethod
    def _excl(reads, writes):
        r2, w2 = [], []
        for x in reads:
            b = x if isinstance(x, Buf) else x[0]
            (w2 if getattr(b, "excl", False) else r2).append(b if getattr(b, "excl", False) else x)
        for x in writes:
            b = x if isinstance(x, Buf) else x[0]
            w2.append(b if getattr(b, "excl", False) else x)
        return r2, w2

    def op(self, eng, fn, reads=(), writes=()):
        reads, writes = self._excl(reads, writes)
        deps = self._deps(reads, writes)
        waits = self._waits(eng, deps, skip_own=(eng == "tensor"))
        self.ecount[eng] += 1
        tok = (self.esem[eng], self.ecount[eng])
        self.ops[eng].append((waits, [fn], tok[0], 1))
        self._update(reads, writes, tok)
        self.nops += 1
        return tok

    def I(self, eng, method, reads=(), writes=(), **kw):
        return self.op(eng, lambda e: getattr(e, method)(**kw), reads, writes)

    def mm(self, out, lhsT, rhs, start, stop, reads, writes):
        return self.op("tensor", lambda e: e.matmul(out, lhsT=lhsT, rhs=rhs, start=start, stop=stop), reads, writes)

    def tr(self, out, in_, ident, reads, writes):
        return self.op("tensor", lambda e: e.transpose(out=out, in_=in_, identity=ident), reads, writes)

    def dma(self, q, fns, reads=(), writes=()):
        if not isinstance(fns, (list, tuple)):
            fns = [fns]
        deps = self._deps(reads, writes)
        i = self.dnext[q]
        self.dnext[q] = (i + 1) % len(self.dsems[q])
        s = self.dsems[q][i]
        prev = self.dval[q][i]
        if prev > 0:
            k = id(s)
            if k not in deps or deps[k][1] < prev:
                deps[k] = (s, prev)
        waits = self._waits(q, deps)
        val = prev + 16 * len(fns)
        self.dval[q][i] = val
        tok = (s, val)
        self.ops[q].append((waits, list(fns), s, 16))
        self._update(reads, writes, tok)
        self.nops += 1
        return tok

    def D(self, q, out, in_, reads=(), writes=(), **kw):
        return self.dma(q, lambda e: e.dma_start(out=out, in_=in_, **kw), reads, writes)

    def emit(self):
        nc = self.nc
        for q in self.dsems:
            fin = []
            for s, v in zip(self.dsems[q], self.dval[q]):
                if v > 0 and self.waited[q].get(id(s), 0) < v:
                    fin.append((s, v))
            if fin:
                self.ops[q].append((fin, [], None, 0))
        ops = self.ops
        with nc.Block() as block:
            def mk(ename):
                def body(e):
                    for waits, fns, sem, inc in ops[ename]:
                        for s, v in waits:
                            e.wait_ge(s, v)
                        for fn in fns:
                            fn(e).then_inc(sem, inc)
                return body
            for ename in ENGS:
                if ops[ename]:
                    getattr(block, ename)(mk(ename))
        self.stack.close()


def AP(t, offset, dims):
    tt = t.t if isinstance(t, Buf) else t
    return bass.AP(tt, offset, [list(d) for d in dims])


TTILES = [(0, 256, 1)] + [(256 + 512 * i, 512, 0) for i in range(8)]

FM_ROUTES = [
    (0, 512, "AQ", 0), (1024, 1024, "AF", 0), (2560, 1536, "BQKV", 0), (4624, 512, "CU", 0), (5136, 3072, "MG", 0),
]
TM_ROUTES = [
    (512, 512, "AI"), (2048, 512, "AG"), (4096, 512, "BG"), (4608, 16, "BBA"),
]
SCR = {
    "AQ": ([512, T], BF16), "AF": ([1024, T], F32), "BQKV": ([1536, T], BF16), "CU": ([512, T], BF16),
    "MG": ([3072, T], BF16), "AI": ([T, 512], BF16), "AG": ([T, 512], BF16), "BG": ([T, 512], BF16),
    "BBA": ([T, 16], F32),
    "XA": ([D, T], F32), "XB": ([D, T], F32),
    "NGA": ([512, T], BF16), "NGB": ([512, T], BF16), "YC": ([512, T], BF16),
    "OF": ([T, 512], F32),
    "OBW": ([T, 512], F32), "S5A": ([512, 256], BF16), "S5B": ([512, 256], BF16),
}


class Ctx:
    pass


def tkey(tok):
    return 0 if tok < 256 else 256 + ((tok - 256) // 512) * 512


def dump_sb(nc, C, buf, name, shape):
    P = Prog(nc)
    d = nc.dram_tensor(name, list(shape), F32, kind="ExternalOutput")
    P.D("sync", d.ap(), buf[:], reads=[buf])
    P.emit()


def make_consts(nc, C):
    P = Prog(nc)
    st = C.gstack
    def gsb(name, shape, dt=F32):
        return Buf(st.enter_context(nc.sbuf_tensor(name, list(shape), dt)), name)
    C.ident = gsb("ident", [128, 128], F32)
    C.identb = gsb("identb", [128, 128], BF16)
    C.ones = gsb("ones", [128, 128], F32)
    C.onesb = gsb("onesb", [128, 128], BF16)
    P.I("gpsimd", "memset", writes=[C.ident], ap=C.ident[:], constant=0.0)
    P.I("gpsimd", "affine_select", reads=[C.ident], writes=[C.ident], out=C.ident[:], in_=C.ident[:],
        pattern=[[-1, 128]], compare_op=ALU.not_equal, fill=1.0, base=0, channel_multiplier=1)
    P.I("vector", "tensor_copy", reads=[C.ident], writes=[C.identb], out=C.identb[:], in_=C.ident[:])
    P.I("vector", "memset", writes=[C.ones], ap=C.ones[:], constant=1.0)
    P.I("vector", "memset", writes=[C.onesb], ap=C.onesb[:], constant=1.0)
    make_masks(P, C)
    make_s5_masks(P, C)
    C.mod = [gsb("mod%d" % l, [128, 48, 2], F32) for l in range(2)]
    C.g1 = [gsb("g1_%d" % l, [128, 8, 2], F32) for l in range(2)]
    C.g2 = [gsb("g2_%d" % l, [128, 8, 2], F32) for l in range(2)]
    P.emit()


def phase_adaln(nc, C, l):
    P = Prog(nc)
    cv = P.sb("cv", [128, 8, 2])
    sc = P.sb("sc", [128, 8, 2])
    ab = P.sb("ab", [128, 48])
    nw = P.sb("nw", [128, 16])
    pm = P.ps("pm", [128, 48, 2])
    P.D("sync", cv[:], C.din["cvec"][:], reads=[], writes=[cv])
    P.D("sync", ab[:], C.din["ada_b"][l], writes=[ab])
    P.D("sync", nw[:, 0:8], C.din["norm1_w"][l], writes=[(nw, 0)])
    P.D("sync", nw[:, 8:16], C.din["norm2_w"][l], writes=[(nw, 1)])
    P.I("scalar", "activation", reads=[cv], writes=[sc], out=sc[:], in_=cv[:], func=AF.Silu)
    wbufs = [P.sb("adw%d" % i, [128, 8, 512]) for i in range(2)]
    aw = C.din["ada_w"][l].rearrange("(k p) c -> p k c", p=128)
    for og in range(12):
        wb = wbufs[og % 2]
        P.dma("sync", [lambda e, k=k, wb=wb, og=og: e.dma_start(out=wb[:, k, :], in_=aw[:, k, og * 512:(og + 1) * 512]) for k in range(8)],
              writes=[wb])
        for m in range(4):
            j = og * 4 + m
            for k in range(8):
                P.mm(pm[:, j, :], wb[:, k, m * 128:(m + 1) * 128], sc[:, k, :], k == 0, k == 7, reads=[wb, sc], writes=[(pm, j)])
    mod = C.mod[l]
    abb = AP(ab, 0, [[48, 128], [1, 48], [0, 2]])
    P.I("vector", "tensor_tensor", reads=[pm, ab], writes=[mod], out=mod[:], in0=pm[:], in1=abb, op=ALU.add)
    for (g, soff, noff) in ((C.g1[l], 8, 0), (C.g2[l], 32, 8)):
        nwb = AP(nw, noff, [[16, 128], [1, 8], [0, 2]])
        P.I("vector", "scalar_tensor_tensor", reads=[mod, nw], writes=[g], out=g[:], in0=mod[:, soff:soff + 8, :], scalar=1.0,
            in1=nwb, op0=ALU.add, op1=ALU.mult)
    P.emit()


def phase_proj(nc, C, l, xsrc):
    P = Prog(nc)
    hT = P.sb("hT", [128, 8, T], BF16)
    compute_hT(P, C, xsrc, hT, C.g1[l], C.mod[l], 0)
    win = C.din["w_in"][l].rearrange("(k p) c -> p k c", p=128)
    wbufs = [P.sb("wb%d" % i, [128, 8, 512], BF16) for i in range(2)]
    pss = [P.ps("pp%d" % i, [128, 512]) for i in range(4)]
    stF = [P.sb("stF%d" % i, [128, T], F32) for i in range(2)]
    wi = 0
    pi = 0
    si = 0
    ei = 0
    for (c0, ncols, sname, row0) in FM_ROUTES:
        dst = C.scr[sname]
        dt = SCR[sname][1]
        for g0 in range(0, ncols, 512):
            wb = wbufs[wi % 2]
            wi += 1
            P.D("gpsimd", wb[:], win[:, :, c0 + g0:c0 + g0 + 512], writes=[wb])
            for m in range(4):
                stb = stF[si % 2]
                si += 1
                stv = stb[:] if dt == F32 else stb[:].bitcast(BF16)[:, 0:T]
                for (n0, nsz, w) in TTILES:
                    pp = pss[pi % 4]
                    pi += 1
                    for k in range(8):
                        P.mm(pp[:, :nsz], wb[:, k, m * 128:(m + 1) * 128], hT[:, k, n0:n0 + nsz], k == 0, k == 7,
                             reads=[wb, (hT, n0)], writes=[pp])
                    if ei % 2 == 0:
                        P.I("scalar", "activation", reads=[pp], writes=[(stb, n0)], out=stv[:, n0:n0 + nsz], in_=pp[:, :nsz], func=AF.Copy)
                    else:
                        P.I("vector", "tensor_copy", reads=[pp], writes=[(stb, n0)], out=stv[:, n0:n0 + nsz], in_=pp[:, :nsz])
                    ei += 1
                r0 = row0 + g0 + m * 128
                P.D("sync", dst[r0:r0 + 128, :], stv, reads=[stb], writes=[(dst, r0)])
    stT = [P.sb("stT%d" % i, [128, 512], F32) for i in range(3)]
    for (c0, ncols, sname) in TM_ROUTES:
        dst = C.scr[sname]
        dt = SCR[sname][1]
        wb = wbufs[wi % 2]
        wi += 1
        P.D("gpsimd", wb[:, :, :ncols], win[:, :, c0:c0 + ncols], writes=[wb])
        for tb in range(T // 128):
            pp = pss[pi % 4]
            pi += 1
            for k in range(8):
                P.mm(pp[:, :ncols], hT[:, k, tb * 128:(tb + 1) * 128], wb[:, k, :ncols], k == 0, k == 7,
                     reads=[wb, (hT, tkey(tb * 128))], writes=[pp])
            stb = stT[si % 3]
            si += 1
            stv = stb[:] if dt == F32 else stb[:].bitcast(BF16)[:, 0:512]
            if ei % 2 == 0:
                P.I("scalar", "activation", reads=[pp], writes=[stb], out=stv[:, :ncols], in_=pp[:, :ncols], func=AF.Copy)
            else:
                P.I("vector", "tensor_copy", reads=[pp], writes=[stb], out=stv[:, :ncols], in_=pp[:, :ncols])
            ei += 1
            P.D("sync", dst[tb * 128:(tb + 1) * 128, :], stv[:, :ncols], reads=[stb], writes=[(dst, tb)])
    P.emit()


def compute_hT(P, C, xsrc, hT, g, mod, shoff, tiles=None, hoff=0):
    nc = P.nc
    xv = xsrc.t.ap().rearrange("(k p) t -> p k t", p=128)
    xts = [P.sb("xt%d" % i, [128, 8, 512]) for i in range(2)]
    sq = P.sb("sq", [128, 8, 512])
    rs = P.sb("rs", [128, 512])
    pss = P.ps("pss", [128, 512])
    for ti, (n0, nsz, w) in enumerate(tiles or TTILES):
        xt = xts[ti % 2]
        P.D("sync", xt[:, :, :nsz], xv[:, :, n0:n0 + nsz], reads=[(xsrc, n0)], writes=[xt])
        P.I("scalar", "activation", reads=[xt], writes=[sq], out=sq[:, :, :nsz], in_=xt[:, :, :nsz], func=AF.Square)
        for k in range(8):
            P.mm(pss[:, :nsz], C.ones[:], sq[:, k, :nsz], k == 0, k == 7, reads=[sq, C.ones], writes=[pss])
        P.I("scalar", "activation", reads=[pss], writes=[rs], out=rs[:, :nsz], in_=pss[:, :nsz], func=AF.Sqrt,
            scale=1.0 / D, bias=EPS)
        P.I("vector", "reciprocal", reads=[rs], writes=[rs], out=rs[:, :nsz], in_=rs[:, :nsz])
        rsb = AP(rs, 0, [[512, 128], [0, 8], [1, nsz]])
        P.I("vector", "tensor_tensor", reads=[xt, rs], writes=[sq], out=sq[:, :, :nsz], in0=xt[:, :, :nsz], in1=rsb, op=ALU.mult)
        for k in range(8):
            eng = "gpsimd" if k % 2 == 0 else "vector"
            P.I(eng, "tensor_scalar", reads=[sq, g, mod], writes=[(hT, n0)], out=hT[:, k, n0 - hoff:n0 - hoff + nsz], in0=sq[:, k, :nsz],
                scalar1=g[:, k, w:w + 1], scalar2=mod[:, shoff + k, w:w + 1], op0=ALU.mult, op1=ALU.add)


def build(nlayers=2, stop=None, debug=(), skip=(), heads=range(4), s5stop=None):
    nc = bass.Bass("TRN2", target_bir_lowering=False)
    C = Ctx()
    C.gstack = ExitStack()
    C.din = {}
    C.heads = heads
    C.s5stop = s5stop
    C.skip = skip

    def din(name, shape, dt=F32):
        C.din[name] = nc.dram_tensor(name, list(shape), dt, kind="ExternalInput")

    din("xT", [D, T]); din("cvec", [128, 8, 2]); din("ada_w", [2, D, 6 * D]); din("ada_b", [2, 128, 48])
    din("norm1_w", [2, 128, 8]); din("norm2_w", [2, 128, 8]); din("w_in", [2, D, DIN])
    din("hgrn_lb", [2, 128, 8]); din("hgrn_norm_w", [2, 128])
    din("s5_lam_re", [2, 64, 2, 32]); din("s5_lam_im", [2, 64, 2, 32]); din("s5_log_dt", [2, 2, 32])
    din("s5_b_re", [2, 64, 32, 16]); din("s5_b_im", [2, 64, 32, 16]); din("s5_c_re", [2, 64, 32, 16]); din("s5_c_im", [2, 64, 32, 16]); din("s5_dtab", [2, 128, 32])
    for nm_, shp_ in (("w_branch_a", [2, 512, D]), ("w_branch_b", [2, 512, D]), ("w_branch_c", [2, 512, D]), ("s5_w_glu", [2, 512, 512]), ("w_out", [2, D, D]), ("w_ff1", [2, D, 4 * D]), ("w_ff2", [2, 4 * D, D]), ("final_norm_w", [128, 8])):
        din(nm_, shp_)
    din("gdn_conv_w", [2, 128, 12, 5]); din("gdn_a_log", [2, 8]); din("gdn_dt_bias", [2, 8]); din("gdn_norm_w", [2, 128])
    C.debug = debug
    C.dbg = {}
    if "s5tab" in debug:
        for nm_, shp_ in (("Er", [64, 32, NPOW]), ("Ei", [64, 32, NPOW]), ("bbr", [64, 32, 16]), ("bbi", [64, 32, 16])):
            C.dbg[nm_] = nc.dram_tensor("dbg_" + nm_, shp_, F32, kind="ExternalOutput")
    if "ob" in debug:
        C.dbg["ob"] = nc.dram_tensor("dbg_ob", [T, 512], F32, kind="ExternalOutput")
    if "oa" in debug:
        C.dbg["oa"] = nc.dram_tensor("dbg_oa", [4, T, 128], F32, kind="ExternalOutput")
    C.scr = {}
    for name, (shape, dt) in SCR.items():
        kind = "ExternalOutput" if name in debug else "Internal"
        C.scr[name] = Buf(nc.dram_tensor(name, shape, dt, kind=kind), name)
    C.out = Buf(nc.dram_tensor("outT", [D, 4096], F32, kind="ExternalOutput"), "outT")
    C.xin = Buf(C.din["xT"], "xT")
    with C.gstack:
        make_consts(nc, C)
        for l in range(nlayers):
            phase_adaln(nc, C, l)
            if 'mod' in debug:
                dump_sb(nc, C, C.mod[l], 'dbg_mod%d' % l, [128, 48, 2])
            if stop == "adaln":
                break
            xsrc = C.xin if l == 0 else C.scr["XB"]
            if 'proj' not in skip:
                phase_proj(nc, C, l, xsrc)
            if stop == "proj":
                break
            if 'mixA' not in skip:
                phase_mixA(nc, C, l, heads=C.heads)
            if stop == "mixA":
                break
            if 'mixB' not in skip:
                phase_mixB(nc, C, l)
            if stop == "mixB":
                break
            if 's5' not in skip:
                phase_s5(nc, C, l)
            if stop == "s5":
                break
            last = (l == nlayers - 1) and nlayers == 2
            if 'merge' not in skip:
                phase_merge(nc, C, l, xsrc, C.scr["XA"], tiles=(TTILES[1:] if last else None))
            if stop == "merge":
                break
            if 'ffn' not in skip:
                phase_ffn(nc, C, l, C.scr["XA"], C.scr["XB"], last)
            if stop == "ffn":
                break
    _SHARED[id(nc)]["stack"].close()
    return nc


def host_inputs(inp, b):
    f = np.float32
    m = {}
    xcat = np.concatenate([inp["ctx"][b], inp["x"][b]], axis=0)
    m["xT"] = np.ascontiguousarray(xcat.T)
    cv = np.stack([inp["c"][b].reshape(8, 128).T, inp["c_ctx"].reshape(8, 128).T], axis=-1)
    m["cvec"] = np.ascontiguousarray(cv.astype(f))
    m["ada_w"] = inp["ada_w"]
    m["ada_b"] = np.ascontiguousarray(inp["ada_b"].reshape(2, 48, 128).transpose(0, 2, 1))
    m["norm1_w"] = np.ascontiguousarray(inp["norm1_w"].reshape(2, 8, 128).transpose(0, 2, 1))
    m["norm2_w"] = np.ascontiguousarray(inp["norm2_w"].reshape(2, 8, 128).transpose(0, 2, 1))
    m["w_in"] = inp["w_in"]
    m["hgrn_lb"] = np.ascontiguousarray(inp["hgrn_lb_logits"].reshape(2, 8, 128).transpose(0, 2, 1))
    m["hgrn_norm_w"] = inp["hgrn_norm_w"]
    m["gdn_conv_w"] = np.ascontiguousarray(inp["gdn_conv_w"].reshape(2, 5, 12, 128).transpose(0, 3, 2, 1))
    m["gdn_a_log"] = np.ascontiguousarray(inp["gdn_a_log"].reshape(2, 8))
    m["gdn_dt_bias"] = np.ascontiguousarray(inp["gdn_dt_bias"].reshape(2, 8))
    m["gdn_norm_w"] = inp["gdn_norm_w"]
    for nm_ in ("w_branch_a", "w_branch_b", "w_branch_c", "s5_w_glu", "w_out", "w_ff1", "w_ff2"):
        m[nm_] = inp[nm_]
    m["final_norm_w"] = np.ascontiguousarray(inp["final_norm_w"].reshape(8, 128).T)
    m["s5_lam_re"] = np.ascontiguousarray(inp["s5_lam_re"].transpose(0, 3, 1, 2))
    m["s5_lam_im"] = np.ascontiguousarray(inp["s5_lam_im"].transpose(0, 3, 1, 2))
    m["s5_log_dt"] = inp["s5_log_dt"]
    m["s5_b_re"] = np.ascontiguousarray(inp["s5_b_re"].transpose(0, 2, 1, 3))
    m["s5_b_im"] = np.ascontiguousarray(inp["s5_b_im"].transpose(0, 2, 1, 3))
    m["s5_c_re"] = np.ascontiguousarray(inp["s5_c_re"].transpose(0, 3, 1, 2))
    m["s5_c_im"] = np.ascontiguousarray(inp["s5_c_im"].transpose(0, 3, 1, 2))
    dt_ = inp["s5_d"].reshape(2, 32, 16).transpose(0, 2, 1)
    m["s5_dtab"] = np.ascontiguousarray(np.tile(dt_, (1, 8, 1)))
    return m


_NC = None


def kernel(**inputs):
    global _NC
    inp = {k: np.asarray(v) for k, v in inputs.items()}
    if _NC is None:
        _NC = build()
    in_maps = [host_inputs(inp, b) for b in range(8)]
    res = run_bass_kernel_spmd(_NC, in_maps, core_ids=list(range(8)))
    out = np.stack([np.ascontiguousarray(res.results[b]["outT"].T) for b in range(8)], axis=0)
    return out.astype(np.float32)


CH_FWD = list(range(68))
CH_BWD = [3, 2, 1, 0] + list(range(67, 3, -1))


def make_masks(P, C):
    nc = P.nc
    st = C.gstack
    def gsb(name, shape, dt=F32):
        return Buf(st.enter_context(nc.sbuf_tensor(name, list(shape), dt)), name)
    one8 = gsb("one8", [64, 8, 64])
    P.I("vector", "memset", writes=[one8], ap=one8[:], constant=1.0)
    C.maskf = {}
    C.maski = {}
    for nm, cm, st_, op in (("U", -1, 1, ALU.is_ge), ("L", 1, -1, ALU.is_ge), ("Us", -1, 1, ALU.is_gt), ("Ls", 1, -1, ALU.is_gt)):
        mf = gsb("maskf" + nm, [64, 8, 64])
        P.I("gpsimd", "affine_select", reads=[one8], writes=[mf], out=mf[:], in_=one8[:], pattern=[[0, 8], [st_, 64]],
            compare_op=op, fill=0.0, base=0, channel_multiplier=cm)
        mi = gsb("maski" + nm, [64, 8, 64], I32)
        P.I("vector", "tensor_copy", reads=[mf], writes=[mi], out=mi[:], in_=mf[:])
        C.maskf[nm] = mf
        C.maski[nm] = mi
    C.rmask = gsb("rmask", [128, 512])
    P.I("vector", "memset", writes=[C.rmask], ap=C.rmask[:], constant=1.0)
    P.I("vector", "memset", reads=[C.rmask], writes=[C.rmask], ap=AP(C.rmask, 0, [[512, 128], [64, 8]]), constant=0.0)


def phase_mixA(nc, C, l, heads=range(4)):
    heads = list(heads)
    P = Prog(nc)
    lbl = P.sb("lbl", [128, 2, 8])
    lb = P.sb("lb", [128, 8])
    oml = P.sb("oml", [128, 8])
    if l == 0:
        P.I("vector", "memset", writes=[lb], ap=lb[:], constant=0.0)
        P.I("vector", "memset", writes=[oml], ap=oml[:], constant=1.0)
    else:
        P.D("sync", lbl[:], C.din["hgrn_lb"].ap().rearrange("l p e -> p l e"), writes=[lbl])
        P.I("vector", "tensor_tensor", reads=[lbl], writes=[lb], out=lb[:], in0=lbl[:, 1, :], in1=lbl[:, 0, :], op=ALU.subtract)
        P.I("scalar", "activation", reads=[lb], writes=[lb], out=lb[:], in_=lb[:], func=AF.Sigmoid)
        P.I("vector", "tensor_scalar", reads=[lb], writes=[oml], out=oml[:], in0=lb[:], scalar1=-1.0, scalar2=1.0, op0=ALU.mult, op1=ALU.add)
    nwb = P.sb("nwb", [64, 128])
    P.D("sync", nwb[:], C.din["hgrn_norm_w"][l].partition_broadcast(64), writes=[nwb])
    v_tms = [P.sb("v_tm%d" % i, [64, 68, 128], BF16) for i in range(1)]
    o_acc = P.sb("o_acc", [64, 68, 128])
    ngT = P.sb("ngT", [128, T], BF16)
    sets = []
    for i in range(2):
        sets.append(dict(attT=P.sb("attT%d" % i, [64, 68, 64], BF16), kd_tm=P.sb("kd_tm%d" % i, [64, 68, 128], BF16),
                         qg=P.sb("qg%d" % i, [128, T], BF16), dec=P.sb("dec%d" % i, [128, 68]), last_dir=[None]))
        P.I("gpsimd", "memset", writes=[sets[i]["attT"]], ap=sets[i]["attT"][:], constant=0.0)
    S = P.sb("S", [128, 128])
    Sb = P.sb("Sb", [128, 128], BF16)
    qpre = P.sb("qpre", [128, 512], BF16)
    W = {n: P.sb(n, [128, 512]) for n in ("q32", "F", "G", "Kt", "CUM", "Dd", "E", "E2")}
    qe = P.sb("qe", [128, 512], BF16)
    ke = P.sb("ke", [128, 512], BF16)
    kdT = P.sb("kdT", [128, 512], BF16)
    qeA = P.sb("qeA", [128, 512], BF16)
    keA = P.sb("keA", [128, 512], BF16)
    refA = P.sb("refA", [128, 16])
    sm = {n: P.sb(n, [128, 8]) for n in ("tot", "lastc", "refc")}
    pa = [P.ps("pa%d" % i, [128, 512]) for i in range(2)]
    pt = [P.ps("pt%d" % i, [128, 512]) for i in range(2)]
    po = [P.ps("po%d" % i, [128, 512]) for i in range(2)]
    pd = [P.ps("pd%d" % i, [128, 512]) for i in range(2)]
    for pz in pa:
        P.I("vector", "memset", writes=[pz], ap=pz[:], constant=0.0)
    gate = P.sb("gate", [64, 8, 128], BF16)
    sg = P.sb("sg", [64, 8, 128])
    sq = P.sb("sqo", [64, 8, 128])
    on = P.sb("on", [64, 8, 128])
    onb = P.sb("onb", [64, 8, 128], BF16)
    ssq = P.sb("ssq", [64, 8])
    agv = C.scr["AG"].t.ap().rearrange("(c p) d -> p c d", p=64)
    aiv = C.scr["AI"].t.ap().rearrange("(c p) d -> p c d", p=64)
    cnt = [0]

    def prep(h, dr, st_):
        attT, kd_tm, qg, dec = st_["attT"], st_["kd_tm"], st_["qg"], st_["dec"]
        dh = dr * 4 + h
        if st_["last_dir"][0] is not None and st_["last_dir"][0] != dr:
            P.I("gpsimd", "memset", reads=[attT], writes=[attT], ap=attT[:], constant=0.0)
        st_["last_dir"][0] = dr
        for (n0, n, w) in TTILES:
            nch = n // 64
            c0 = n0 // 64
            q32, Fb, G, Kt, CUM, Dd, E, E2 = (W[k] for k in ("q32", "F", "G", "Kt", "CUM", "Dd", "E", "E2"))
            P.D("sync", qpre[:, :n], C.scr["AQ"][h * 128:(h + 1) * 128, n0:n0 + n], reads=[C.scr["AQ"]], writes=[qpre])
            r0 = dr * 512 + h * 128
            P.D("sync", Fb[:, :n], C.scr["AF"][r0:r0 + 128, n0:n0 + n], reads=[C.scr["AF"]], writes=[Fb])
            P.I("scalar", "activation", reads=[qpre], writes=[q32], out=q32[:, :n], in_=qpre[:, :n], func=AF.Silu)
            P.I("scalar", "activation", reads=[Fb], writes=[Fb], out=Fb[:, :n], in_=Fb[:, :n], func=AF.Sigmoid)
            P.I("vector", "tensor_scalar", reads=[Fb, oml, lb], writes=[Fb], out=Fb[:, :n], in0=Fb[:, :n], scalar1=oml[:, dh:dh + 1],
                scalar2=lb[:, dh:dh + 1], op0=ALU.mult, op1=ALU.add)
            yield
            P.I("scalar", "activation", reads=[Fb], writes=[G], out=G[:, :n], in_=Fb[:, :n], func=AF.Ln)
            P.I("gpsimd", "tensor_scalar", reads=[Fb], writes=[Kt], out=Kt[:, :n], in0=Fb[:, :n], scalar1=-1.0, scalar2=1.0, op0=ALU.mult, op1=ALU.add)
            P.I("vector", "tensor_tensor_scan", reads=[G, C.rmask], writes=[CUM], out=CUM[:, :n], data0=C.rmask[:, :n], data1=G[:, :n],
                initial=0.0, op0=ALU.mult, op1=ALU.add)

            def cview(buf, off):
                return AP(buf, off, [[512, 128], [64, nch]])

            def bview(buf):
                return AP(buf, 0, [[8, 128], [1, nch], [0, 64]])

            def v3(buf):
                return AP(buf, 0, [[512, 128], [64, nch], [1, 64]])
            if dr == 1:
                P.I("vector", "tensor_copy", reads=[CUM], writes=[sm["tot"]], out=sm["tot"][:, :nch], in_=cview(CUM, 63))
                P.I("gpsimd", "tensor_tensor", reads=[G, CUM], writes=[G], out=G[:, :n], in0=G[:, :n], in1=CUM[:, :n], op=ALU.subtract)
                P.I("vector", "tensor_tensor", reads=[G, sm["tot"]], writes=[CUM], out=v3(CUM), in0=v3(G), in1=bview(sm["tot"]), op=ALU.add)
            yield
            P.I("vector", "tensor_copy", reads=[CUM], writes=[sm["lastc"]], out=sm["lastc"][:, :nch], in_=cview(CUM, 63 if dr == 0 else 0))
            P.I("vector", "tensor_copy", reads=[CUM], writes=[sm["refc"]], out=sm["refc"][:, :nch], in_=cview(CUM, 32))
            P.I("scalar", "activation", reads=[sm["lastc"]], writes=[(dec, c0)], out=dec[:, c0:c0 + nch], in_=sm["lastc"][:, :nch], func=AF.Exp)
            P.I("vector", "tensor_tensor", reads=[CUM, sm["refc"]], writes=[Dd], out=v3(Dd), in0=v3(CUM), in1=bview(sm["refc"]), op=ALU.subtract)
            P.I("vector", "tensor_scalar", reads=[Dd], writes=[Dd], out=Dd[:, :n], in0=Dd[:, :n], scalar1=80.0, scalar2=-80.0, op0=ALU.min, op1=ALU.max)
            yield
            P.I("scalar", "activation", reads=[Dd], writes=[E], out=E[:, :n], in_=Dd[:, :n], func=AF.Exp)
            P.I("gpsimd", "tensor_tensor", reads=[q32, E], writes=[qe], out=qe[:, :n], in0=q32[:, :n], in1=E[:, :n], op=ALU.mult)
            P.I("scalar", "activation", reads=[Dd], writes=[E2], out=E2[:, :n], in_=Dd[:, :n], func=AF.Exp, scale=-1.0)
            P.I("vector", "tensor_tensor", reads=[Kt, E2], writes=[ke], out=ke[:, :n], in0=Kt[:, :n], in1=E2[:, :n], op=ALU.mult)
            yield
            nb2 = 2 * nch
            P.I("vector", "tensor_copy", reads=[CUM], writes=[refA], out=refA[:, :nb2], in_=AP(CUM, 16, [[512, 128], [32, nb2]]))
            v3a = lambda buf: AP(buf, 0, [[512, 128], [32, nb2], [1, 32]])
            P.I("vector", "tensor_tensor", reads=[CUM, refA], writes=[Dd], out=v3a(Dd), in0=v3a(CUM), in1=AP(refA, 0, [[16, 128], [1, nb2], [0, 32]]), op=ALU.subtract)
            P.I("vector", "tensor_scalar", reads=[Dd], writes=[Dd], out=Dd[:, :n], in0=Dd[:, :n], scalar1=40.0, scalar2=-40.0, op0=ALU.min, op1=ALU.max)
            P.I("scalar", "activation", reads=[Dd], writes=[E], out=E[:, :n], in_=Dd[:, :n], func=AF.Exp)
            P.I("gpsimd", "tensor_tensor", reads=[q32, E], writes=[qeA], out=qeA[:, :n], in0=q32[:, :n], in1=E[:, :n], op=ALU.mult)
            yield
            P.I("scalar", "activation", reads=[Dd], writes=[E2], out=E2[:, :n], in_=Dd[:, :n], func=AF.Exp, scale=-1.0)
            P.I("vector", "tensor_tensor", reads=[Kt, E2], writes=[keA], out=keA[:, :n], in0=Kt[:, :n], in1=E2[:, :n], op=ALU.mult)
            P.I("scalar", "activation", reads=[CUM], writes=[E], out=E[:, :n], in_=CUM[:, :n], func=AF.Exp)
            P.I("gpsimd", "tensor_tensor", reads=[q32, E], writes=[(qg, n0)], out=qg[:, n0:n0 + n], in0=q32[:, :n], in1=E[:, :n], op=ALU.mult)
            yield
            P.I("vector", "tensor_tensor", reads=[CUM, sm["lastc"]], writes=[Dd], out=v3(Dd), in0=bview(sm["lastc"]), in1=v3(CUM), op=ALU.subtract)
            P.I("scalar", "activation", reads=[Dd], writes=[E2], out=E2[:, :n], in_=Dd[:, :n], func=AF.Exp)
            P.I("gpsimd", "tensor_tensor", reads=[Kt, E2], writes=[kdT], out=kdT[:, :n], in0=Kt[:, :n], in1=E2[:, :n], op=ALU.mult)
            ppa = pa[cnt[0] % 2]
            ppt = pt[cnt[0] % 2]
            cnt[0] += 1
            for j in range(nch):
                b_ = j * 64
                P.mm(ppa[0:32, b_:b_ + 32], keA[:, b_:b_ + 32], qeA[:, b_:b_ + 32], True, True, reads=[keA, qeA], writes=[ppa])
                P.mm(ppa[32:64, b_ + 32:b_ + 64], keA[:, b_ + 32:b_ + 64], qeA[:, b_ + 32:b_ + 64], True, True, reads=[keA, qeA], writes=[ppa])
                if dr == 0:
                    P.mm(ppa[0:32, b_ + 32:b_ + 64], ke[:, b_:b_ + 32], qe[:, b_ + 32:b_ + 64], True, True, reads=[ke, qe], writes=[ppa])
                else:
                    P.mm(ppa[32:64, b_:b_ + 32], ke[:, b_ + 32:b_ + 64], qe[:, b_:b_ + 32], True, True, reads=[ke, qe], writes=[ppa])
            yield
            mk = C.maski["U" if dr == 0 else "L"]
            P.I("vector", "copy_predicated", reads=[ppa, mk, (attT, n0)], writes=[(attT, n0)], out=attT[:, c0:c0 + nch, :],
                mask=mk[:, :nch, :], data=ppa[0:64, 0:nch * 64].rearrange("p (a b) -> p a b", b=64))
            ptv = ppt[0:64, :].bitcast(BF16).rearrange("p (a b) -> p a b", b=128)
            for j in range(nch):
                P.tr(ptv[:, j, :], kdT[:, j * 64:(j + 1) * 64], C.identb[:], reads=[kdT, C.identb], writes=[ppt])
            P.I("scalar", "activation", reads=[ppt], writes=[(kd_tm, n0)], out=kd_tm[:, c0:c0 + nch, :], in_=ptv[:, :nch, :], func=AF.Copy)
            yield

    def chain(h, dr, st_, v_tm):
        attT, kd_tm, qg, dec = st_["attT"], st_["kd_tm"], st_["qg"], st_["dec"]
        if dr == 0:
            P.D("sync", v_tm[:], aiv[:, :, h * 128:(h + 1) * 128], reads=[C.scr["AI"]], writes=[v_tm])
        P.I("vector", "memset", reads=[S], writes=[S], ap=S[:], constant=0.0)
        P.I("gpsimd", "memset", reads=[Sb], writes=[Sb], ap=Sb[:], constant=0.0)
        for ci, c in enumerate(CH_FWD if dr == 0 else CH_BWD):
            key = tkey(c * 64)
            ppo = po[ci % 2]
            ppd = pd[ci % 2]
            P.mm(ppo[0:64, 0:128], attT[:, c, :], v_tm[:, c, :], True, False, reads=[(attT, key), v_tm], writes=[ppo])
            P.mm(ppo[0:64, 0:128], qg[:, c * 64:(c + 1) * 64], Sb[:], False, True, reads=[(qg, key), Sb], writes=[ppo])
            P.mm(ppd[:, 0:128], kd_tm[:, c, :], v_tm[:, c, :], True, True, reads=[(kd_tm, key), v_tm], writes=[ppd])
            yield
            if dr == 0:
                P.I("scalar", "activation", reads=[ppo], writes=[(o_acc, c)], out=o_acc[:, c, :], in_=ppo[0:64, 0:128], func=AF.Copy)
            else:
                P.I("vector", "tensor_tensor", reads=[ppo, (o_acc, c)], writes=[(o_acc, c)], out=o_acc[:, c, :], in0=ppo[0:64, 0:128],
                    in1=o_acc[:, c, :], op=ALU.add)
            P.I("vector", "scalar_tensor_tensor", reads=[S, ppd, (dec, tkey(c * 64) // 64)], writes=[S], out=S[:], in0=S[:], scalar=dec[:, c:c + 1],
                in1=ppd[:, 0:128], op0=ALU.mult, op1=ALU.add)
            P.I("gpsimd", "tensor_copy", reads=[S], writes=[Sb], out=Sb[:], in_=S[:])
            yield
        if dr == 0:
            return
        for (n0, n, w) in TTILES:
            nch = n // 64
            c0 = n0 // 64
            ov = o_acc[:, c0:c0 + nch, :]
            okeys = [(o_acc, c) for c in range(c0, c0 + nch)]
            P.D("sync", gate[:, :nch, :], agv[:, c0:c0 + nch, h * 128:(h + 1) * 128], reads=[C.scr["AG"]], writes=[gate])
            P.I("scalar", "activation", reads=[gate], writes=[sg], out=sg[:, :nch, :], in_=gate[:, :nch, :], func=AF.Silu)
            P.I("gpsimd", "tensor_tensor", reads=okeys, writes=[sq], out=sq[:, :nch, :], in0=ov, in1=ov, op=ALU.mult)
            P.I("vector", "tensor_reduce", reads=[sq], writes=[ssq], out=ssq[:, :nch], in_=sq[:, :nch, :], axis=AX.X, op=ALU.add)
            P.I("scalar", "activation", reads=[ssq], writes=[ssq], out=ssq[:, :nch], in_=ssq[:, :nch], func=AF.Sqrt, scale=1.0 / 128, bias=EPS)
            P.I("vector", "reciprocal", reads=[ssq], writes=[ssq], out=ssq[:, :nch], in_=ssq[:, :nch])
            yield
            P.I("vector", "tensor_tensor", reads=okeys + [ssq], writes=[on], out=on[:, :nch, :], in0=ov, in1=AP(ssq, 0, [[8, 64], [1, nch], [0, 128]]), op=ALU.mult)
            P.I("gpsimd", "tensor_tensor", reads=[on, nwb], writes=[on], out=on[:, :nch, :], in0=on[:, :nch, :], in1=AP(nwb, 0, [[128, 64], [0, nch], [1, 128]]), op=ALU.mult)
            P.I("vector", "tensor_tensor", reads=[on, sg], writes=[onb], out=onb[:, :nch, :], in0=on[:, :nch, :], in1=sg[:, :nch, :], op=ALU.mult)
            ppt = pt[cnt[0] % 2]
            cnt[0] += 1
            ptv = ppt[:, 0:256].bitcast(BF16).rearrange("p (a b) -> p a b", b=64)
            for j in range(nch):
                P.tr(ptv[:, j, :], onb[:, j, :], C.identb[0:64, 0:64], reads=[onb, C.identb], writes=[ppt])
            P.I("scalar", "activation", reads=[ppt], writes=[(ngT, n0)], out=ngT[:, n0:n0 + n], in_=ppt[:, 0:256].bitcast(BF16)[:, 0:n], func=AF.Copy)
            yield
        P.D("sync", C.scr["NGA"][h * 128:(h + 1) * 128, :], ngT[:], reads=[ngT], writes=[(C.scr["NGA"], h)])
        if "oa" in C.debug:
            P.D("sync", C.dbg["oa"].ap().rearrange("h (c p) d -> h p c d", p=64)[h], o_acc[:], reads=[o_acc])
        yield

    units = [(h, dr) for h in heads for dr in (0, 1)]

    def drive(gens):
        alive = list(gens)
        while alive:
            for g_ in list(alive):
                try:
                    next(g_)
                except StopIteration:
                    alive.remove(g_)

    drive([prep(units[0][0], units[0][1], sets[0])])
    for i, (h, dr) in enumerate(units):
        gens = [chain(h, dr, sets[i % 2], v_tms[0])]
        if i + 1 < len(units):
            gens.append(prep(units[i + 1][0], units[i + 1][1], sets[(i + 1) % 2]))
        drive(gens)
    P.emit()


def phase_mixB(nc, C, l):
    with ExitStack() as ost:
        _phase_mixB(nc, C, l, ost)
    _mixB_post(nc, C, l)


def _phase_mixB(nc, C, l, ost):
    def psb(name, shape, dt=F32):
        _UID[0] += 1
        return Buf(ost.enter_context(nc.sbuf_tensor("%s_%d" % (name, _UID[0]), list(shape), dt)), name)
    qnT = [psb("qnT%d" % h, [128, T], BF16) for h in range(4)]
    knT = [psb("knT%d" % h, [128, T], BF16) for h in range(4)]
    vT = [psb("vT%d" % h, [128, T], BF16) for h in range(4)]
    bet = psb("bet", [64, 68, 8])
    nbet = psb("nbet", [64, 68, 8])
    gg = psb("gg", [64, 68, 8])
    nwb = psb("nwbB", [64, 128])
    P = Prog(nc)
    I = P.I
    banks = [P.ps("bk%d" % i, [128, 512]) for i in range(8)]
    b0, b1, b2, b3, b4, b5, b6, b7 = banks
    cw = P.sb("cw", [128, 12, 5])
    P.D("sync", cw[:], C.din["gdn_conv_w"][l], writes=[cw])
    xpads = [P.sb("xpad%d" % i, [128, T + 8], BF16) for i in range(1)]
    for xp in xpads:
        I("gpsimd", "memset", writes=[xp], ap=xp[:], constant=0.0)
    diag = P.sb("diag", [128, 5, 128], BF16)
    xs = P.sb("xs", [128, 512])
    sqb = P.sb("sqb", [128, 512], BF16)
    rs = P.sb("rs", [128, 512])
    bq = C.scr["BQKV"]
    for ch in range(12):
        xp = xpads[0]
        P.D("sync", xp[:, 2:258], bq[ch * 128:(ch + 1) * 128, 0:256], reads=[bq], writes=[(xp, 0)])
        P.D("sync", xp[:, 262:4358], bq[ch * 128:(ch + 1) * 128, 256:T], reads=[bq], writes=[(xp, 1)])
        for j in range(5):
            I("vector", "tensor_scalar", reads=[C.identb, cw], writes=[(diag, j)], out=diag[:, j, :], in0=C.identb[:], scalar1=cw[:, ch, j:j + 1],
              scalar2=None, op0=ALU.mult)
        kind, h = ch // 4, ch % 4
        for ti, (n0, n, w) in enumerate(TTILES):
            po_ = n0 + 2 if n0 == 0 else n0 + 6
            pc = banks[ti % 2]
            for j in range(5):
                P.mm(pc[:, :n], diag[:, j, :], xp[:, po_ + j - 2:po_ + j - 2 + n], j == 0, j == 4, reads=[diag, xp], writes=[pc])
            if kind == 2:
                I("scalar", "activation", reads=[pc], writes=[(vT[h], n0)], out=vT[h][:, n0:n0 + n], in_=pc[:, :n], func=AF.Silu)
                continue
            I("scalar", "activation", reads=[pc], writes=[xs], out=xs[:, :n], in_=pc[:, :n], func=AF.Silu)
            I("gpsimd", "tensor_tensor", reads=[xs], writes=[sqb], out=sqb[:, :n], in0=xs[:, :n], in1=xs[:, :n], op=ALU.mult)
            pq = banks[2 + ti % 2]
            P.mm(pq[:, :n], C.onesb[:], sqb[:, :n], True, True, reads=[sqb, C.onesb], writes=[pq])
            I("scalar", "activation", reads=[pq], writes=[rs], out=rs[:, :n], in_=pq[:, :n], func=AF.Sqrt, bias=EPS)
            I("vector", "reciprocal", reads=[rs], writes=[rs], out=rs[:, :n], in_=rs[:, :n])
            dst = qnT[h] if kind == 0 else knT[h]
            I("vector", "scalar_tensor_tensor", reads=[xs, rs], writes=[(dst, n0)], out=dst[:, n0:n0 + n], in0=xs[:, :n],
              scalar=(128 ** -0.5 if kind == 0 else 1.0), in1=rs[:, :n], op0=ALU.mult, op1=ALU.mult)
    bba = P.sb("bba", [64, 68, 16])
    P.D("sync", bba[:], C.scr["BBA"].t.ap().rearrange("(c p) e -> p c e", p=64), reads=[C.scr["BBA"]], writes=[bba])
    alb = P.sb("alb", [64, 8])
    dtb = P.sb("dtb", [64, 8])
    P.D("sync", alb[:], C.din["gdn_a_log"][l].partition_broadcast(64), writes=[alb])
    P.D("sync", dtb[:], C.din["gdn_dt_bias"][l].partition_broadcast(64), writes=[dtb])
    P.D("sync", nwb[:], C.din["gdn_norm_w"][l].partition_broadcast(64), writes=[nwb])
    xa = P.sb("xa", [64, 68, 8])
    t1 = P.sb("t1", [64, 68, 8])
    I("scalar", "activation", reads=[bba], writes=[bet], out=bet[:], in_=bba[:, :, 0:8], func=AF.Sigmoid)
    I("vector", "tensor_scalar", reads=[bet], writes=[nbet], out=nbet[:], in0=bet[:], scalar1=-1.0, scalar2=None, op0=ALU.mult)
    I("vector", "tensor_tensor", reads=[bba, dtb], writes=[xa], out=xa[:], in0=bba[:, :, 8:16], in1=AP(dtb, 0, [[8, 64], [0, 68], [1, 8]]), op=ALU.add)
    I("vector", "tensor_scalar", reads=[xa], writes=[t1], out=t1[:], in0=xa[:], scalar1=-1.0, scalar2=None, op0=ALU.mult)
    I("vector", "tensor_tensor", reads=[xa, t1], writes=[t1], out=t1[:], in0=xa[:], in1=t1[:], op=ALU.max)
    I("scalar", "activation", reads=[t1], writes=[t1], out=t1[:], in_=t1[:], func=AF.Exp, scale=-1.0)
    I("scalar", "activation", reads=[t1], writes=[t1], out=t1[:], in_=t1[:], func=AF.Ln, bias=1.0)
    I("vector", "tensor_scalar", reads=[xa], writes=[xa], out=xa[:], in0=xa[:], scalar1=0.0, scalar2=None, op0=ALU.max)
    I("vector", "tensor_tensor", reads=[xa, t1], writes=[xa], out=xa[:], in0=xa[:], in1=t1[:], op=ALU.add)
    I("scalar", "activation", reads=[alb], writes=[alb], out=alb[:], in_=alb[:], func=AF.Exp)
    I("vector", "scalar_tensor_tensor", reads=[xa, alb], writes=[gg], out=gg[:], in0=xa[:], scalar=-1.0, in1=AP(alb, 0, [[8, 64], [0, 68], [1, 8]]),
      op0=ALU.mult, op1=ALU.mult)
    P.emit()
    P = Prog(nc)
    I = P.I
    banks = [P.ps("bk%d" % i, [128, 512]) for i in range(8)]
    for bk in banks:
        I("vector", "memset", writes=[bk], ap=bk[:], constant=0.0)
    ident4 = AP(C.ident, 0, [[128, 64], [0, 4], [1, 64]])

    def v464(bank):
        return bank[0:64, 0:256].rearrange("p (h s) -> p h s", s=64)

    def v4128(bank):
        return bank[0:64, 0:512].rearrange("p (h s) -> p h s", s=128)

    def stream(dr, k0, k1, k2, k3):
        sfx = "_%d" % dr
        sb = lambda name, shape, dt=F32: P.sb(name + sfx, shape, dt)
        kv_tm = sb("kv_tm", [64, 8, 128], BF16)
        ct = sb("ct", [128, 8])
        ecum = sb("ecum", [64, 4]); edl = sb("edl", [64, 4]); etot = sb("etot", [128, 4]); be = sb("be", [64, 4])
        G1 = sb("G1", [64, 4, 64]); G2 = sb("G2", [64, 4, 64])
        m1 = sb("m1", [64, 4, 64]); m2 = sb("m2", [64, 4, 64])
        attT = sb("attTb", [64, 4, 64], BF16)
        tmpM = sb("tmpM", [64, 4, 64])
        X = [sb("X%d" % i, [64, 4, 64]) for i in range(2)]
        Y = [sb("Y%d" % i, [64, 4, 64]) for i in range(2)]
        R = [sb("R%d" % i, [64, 4, 64]) for i in range(2)]
        IM2 = sb("IM2", [64, 4, 64])
        TTb = sb("TTb", [64, 4, 64], BF16)
        vb = sb("vb", [64, 4, 128], BF16); kbe = sb("kbe", [64, 4, 128], BF16); kd = sb("kd", [64, 4, 128], BF16)
        u_sb = sb("u_sb", [64, 4, 128]); wTb = sb("wTb", [128, 4, 64])
        v_new = sb("v_new", [64, 4, 128], BF16)
        t2 = sb("t2", [64, 4, 128]); o_sb = [sb("o_sb%d" % i, [64, 4, 128]) for i in range(2)]
        S = sb("Sg", [128, 4, 128]); St = sb("St", [128, 4, 128]); Sb = sb("Sbg", [128, 4, 128], BF16)
        ODST = C.scr["OF"] if dr == 0 else C.scr["OBW"]
        I("vector", "memset", writes=[S], ap=S[:], constant=0.0)
        I("gpsimd", "memset", writes=[Sb], ap=Sb[:], constant=0.0)
        mU = C.maskf["U" if dr == 0 else "L"]
        mLs = C.maskf["Ls" if dr == 0 else "Us"]
        triM = mU[:, 0, :]
        tri4 = AP(mU, 0, [[512, 64], [0, 4], [1, 64]])
        for ci, c in enumerate(CH_FWD if dr == 0 else CH_BWD):
            key = tkey(c * 64)
            cs = slice(c * 64, (c + 1) * 64)
            goff = c * 8 + dr * 4
            g4 = AP(gg, goff, [[544, 64], [1, 4]])
            g4b = AP(gg, goff, [[544, 64], [1, 4], [0, 64]])
            bet4b = AP(bet, goff, [[544, 64], [1, 4], [0, 128]])
            nbet4b = AP(nbet, goff, [[544, 64], [1, 4], [0, 64]])
            bet4 = AP(bet, goff, [[544, 64], [1, 4]])
            k0v = k0[0:64, :].bitcast(BF16).rearrange("p (a b) -> p a b", b=128)
            for h in range(4):
                P.tr(k0v[:, h, :], knT[h][:, cs], C.identb[:], reads=[(knT[h], key), C.identb], writes=[k0])
                P.tr(k0v[:, 4 + h, :], vT[h][:, cs], C.identb[:], reads=[(vT[h], key), C.identb], writes=[k0])
            k1v = k1[0:64, :].rearrange("p (h two s) -> p h two s", h=4, two=2)
            for h in range(4):
                P.mm(k1[0:64, (2 * h) * 64:(2 * h + 1) * 64], knT[h][:, cs], knT[h][:, cs], True, True, reads=[(knT[h], key)], writes=[k1])
                P.mm(k1[0:64, (2 * h + 1) * 64:(2 * h + 2) * 64], knT[h][:, cs], qnT[h][:, cs], True, True, reads=[(knT[h], key), (qnT[h], key)], writes=[k1])
            P.mm(k2[0:64, 256:260], triM, g4, True, True, reads=[mU, gg], writes=[k2])
            P.mm(k2[:, 260:264], C.ones[0:64, :], g4, True, True, reads=[C.ones, gg], writes=[k2])
            yield
            I("scalar", "activation", reads=[k0], writes=[kv_tm], out=kv_tm[:], in_=k0v, func=AF.Copy)
            I("vector", "tensor_copy", reads=[k2], writes=[ct], out=ct[:], in_=k2[:, 256:264])
            I("scalar", "activation", reads=[ct], writes=[ecum], out=ecum[:], in_=ct[0:64, 0:4], func=AF.Exp)
            I("vector", "tensor_tensor", reads=[ct], writes=[edl], out=edl[:], in0=ct[0:64, 4:8], in1=ct[0:64, 0:4], op=ALU.subtract)
            I("scalar", "activation", reads=[edl], writes=[edl], out=edl[:], in_=edl[:], func=AF.Exp)
            I("scalar", "activation", reads=[ct], writes=[etot], out=etot[:], in_=ct[:, 4:8], func=AF.Exp)
            I("vector", "tensor_tensor", reads=[bet, ecum], writes=[be], out=be[:], in0=bet4, in1=ecum[:], op=ALU.mult)
            yield
            cumb = AP(ct, 0, [[8, 64], [1, 4], [0, 64]])
            I("vector", "tensor_tensor", reads=[C.ident, ct], writes=[G1], out=G1[:], in0=ident4, in1=cumb, op=ALU.mult)
            P.mm(k2[0:64, 0:256], C.ones[0:64, 0:64], G1[:].rearrange("p h s -> p (h s)"), True, True, reads=[G1, C.ones], writes=[k2])
            yield
            I("vector", "tensor_tensor", reads=[k2, ct], writes=[G2], out=G2[:], in0=v464(k2), in1=cumb, op=ALU.subtract)
            I("vector", "tensor_scalar", reads=[G2], writes=[m1], out=m1[:], in0=G2[:], scalar1=0.0, scalar2=None, op0=ALU.min)
            I("gpsimd", "tensor_scalar", reads=[G2], writes=[m2], out=m2[:], in0=G2[:], scalar1=0.0, scalar2=-1.0, op0=ALU.max, op1=ALU.mult)
            I("scalar", "activation", reads=[m1], writes=[m1], out=m1[:], in_=m1[:], func=AF.Exp)
            I("scalar", "activation", reads=[m2], writes=[m2], out=m2[:], in_=m2[:], func=AF.Exp)
            I("gpsimd", "tensor_tensor", reads=[m1, mU], writes=[m1], out=m1[:], in0=m1[:], in1=mU[:, 0:4, :], op=ALU.mult)
            I("gpsimd", "tensor_tensor", reads=[m2, mLs], writes=[m2], out=m2[:], in0=m2[:], in1=mLs[:, 0:4, :], op=ALU.mult)
            yield
            I("vector", "tensor_tensor", reads=[k1, m1], writes=[attT], out=attT[:], in0=k1v[:, :, 1, :], in1=m1[:], op=ALU.mult)
            I("vector", "tensor_tensor", reads=[k1, nbet], writes=[tmpM], out=tmpM[:], in0=k1v[:, :, 0, :], in1=nbet4b, op=ALU.mult)
            I("gpsimd", "tensor_tensor", reads=[tmpM, m2], writes=[X[0]], out=X[0][:], in0=tmpM[:], in1=m2[:], op=ALU.mult)
            k0n = k0[0:64, 0:256].rearrange("p (a b) -> p a b", b=64)
            for h in range(4):
                P.tr(k0n[:, h, :], X[0][:, h, :], C.ident[0:64, 0:64], reads=[X[0], C.ident], writes=[k0])
            yield
            I("scalar", "activation", reads=[k0], writes=[Y[0]], out=Y[0][:], in_=k0n, func=AF.Copy)
            I("vector", "tensor_tensor", reads=[k0, C.ident], writes=[R[0]], out=R[0][:], in0=k0n, in1=ident4, op=ALU.add)
            yield
            cur = 0
            for step in range(5):
                last = step == 4
                nxt = 1 - cur
                if not last:
                    for h in range(4):
                        P.mm(k2[0:64, h * 64:(h + 1) * 64], X[cur][:, h, :], Y[cur][:, h, :], True, True, reads=[X[cur], Y[cur]], writes=[k2])
                for h in range(4):
                    P.mm(k3[0:64, h * 64:(h + 1) * 64], Y[cur][:, h, :], X[cur][:, h, :], True, True, reads=[X[cur], Y[cur]], writes=[k3])
                if not last:
                    I("scalar", "activation", reads=[k2], writes=[Y[nxt]], out=Y[nxt][:], in_=v464(k2), func=AF.Copy)
                I("vector", "tensor_tensor", reads=[k3, C.ident], writes=[IM2], out=IM2[:], in0=v464(k3), in1=ident4, op=ALU.add)
                if not last:
                    I("scalar", "activation", reads=[k3], writes=[X[nxt]], out=X[nxt][:], in_=v464(k3), func=AF.Copy)
                yield
                for h in range(4):
                    P.mm(k2[0:64, h * 64:(h + 1) * 64], IM2[:, h, :], R[cur][:, h, :], True, True, reads=[IM2, R[cur]], writes=[k2])
                yield
                if not last:
                    I("vector", "tensor_copy", reads=[k2], writes=[R[nxt]], out=R[nxt][:], in_=v464(k2))
                else:
                    I("vector", "tensor_copy", reads=[k2], writes=[TTb], out=TTb[:], in_=v464(k2))
                cur = nxt
                yield
            TT_ = TTb
            I("gpsimd", "tensor_tensor", reads=[kv_tm, bet], writes=[vb], out=vb[:], in0=kv_tm[:, 4:8, :], in1=bet4b, op=ALU.mult)
            I("vector", "tensor_tensor", reads=[kv_tm, be], writes=[kbe], out=kbe[:], in0=kv_tm[:, 0:4, :], in1=AP(be, 0, [[4, 64], [1, 4], [0, 128]]), op=ALU.mult)
            I("gpsimd", "tensor_tensor", reads=[kv_tm, edl], writes=[kd], out=kd[:], in0=kv_tm[:, 0:4, :], in1=AP(edl, 0, [[4, 64], [1, 4], [0, 128]]), op=ALU.mult)
            for h in range(4):
                P.mm(k1[0:64, h * 128:(h + 1) * 128], TT_[:, h, :], vb[:, h, :], True, True, reads=[TT_, vb], writes=[k1])
            for h in range(4):
                P.mm(k0[:, h * 64:(h + 1) * 64], kbe[:, h, :], TT_[:, h, :], True, True, reads=[kbe, TT_], writes=[k0])
            yield
            I("scalar", "activation", reads=[k1], writes=[u_sb], out=u_sb[:], in_=v4128(k1), func=AF.Copy)
            I("vector", "tensor_copy", reads=[k0], writes=[wTb], out=wTb[:], in_=k0[:, 0:256].rearrange("p (h s) -> p h s", s=64))
            yield
            for h in range(4):
                P.mm(k1[0:64, h * 128:(h + 1) * 128], wTb[:, h, :], S[:, h, :], True, True, reads=[wTb, S], writes=[k1])
            for h in range(4):
                P.mm(k3[0:64, h * 128:(h + 1) * 128], qnT[h][:, cs], Sb[:, h, :], True, True, reads=[(qnT[h], key), Sb], writes=[k3])
            yield
            I("vector", "tensor_tensor", reads=[u_sb, k1], writes=[v_new], out=v_new[:], in0=u_sb[:], in1=v4128(k1), op=ALU.subtract)
            I("vector", "tensor_tensor", reads=[k3, ecum], writes=[t2], out=t2[:], in0=v4128(k3), in1=AP(ecum, 0, [[4, 64], [1, 4], [0, 128]]), op=ALU.mult)
            for h in range(4):
                P.mm(k3[0:64, h * 128:(h + 1) * 128], attT[:, h, :], v_new[:, h, :], True, True, reads=[attT, v_new], writes=[k3])
            for h in range(4):
                P.mm(k1[:, h * 128:(h + 1) * 128], kd[:, h, :], v_new[:, h, :], True, True, reads=[kd, v_new], writes=[k1])
            yield
            osb = o_sb[ci % 2]
            I("vector", "tensor_tensor", reads=[k3, t2], writes=[osb], out=osb[:], in0=v4128(k3), in1=t2[:], op=ALU.add)
            I("gpsimd", "tensor_tensor", reads=[S, etot], writes=[St], out=St[:], in0=S[:], in1=AP(etot, 0, [[4, 128], [1, 4], [0, 128]]), op=ALU.mult)
            I("vector", "tensor_tensor", reads=[St, k1], writes=[S], out=S[:], in0=St[:], in1=k1[:, :].rearrange("p (h s) -> p h s", s=128), op=ALU.add)
            I("scalar", "activation", reads=[S], writes=[Sb], out=Sb[:], in_=S[:], func=AF.Copy)
            P.D("sync", ODST[cs, :], osb[:].rearrange("p h d -> p (h d)"), reads=[osb], writes=[(ODST, c)])
            yield

    gens = [stream(0, *banks[0:4]), stream(1, *banks[4:8])]
    alive = list(gens)
    while alive:
        for g_ in list(alive):
            try:
                next(g_)
            except StopIteration:
                alive.remove(g_)
    P.emit()


def _mixB_post(nc, C, l):
    P = Prog(nc)
    I = P.I
    sb = P.sb
    nwb = sb("nwbB2", [64, 128])
    P.D("sync", nwb[:], C.din["gdn_norm_w"][l].partition_broadcast(64), writes=[nwb])
    pt = [P.ps("ptB%d" % i, [128, 512]) for i in range(2)]
    of = [sb("ofB%d" % i, [64, 8, 512]) for i in range(2)]
    ob = [sb("obB%d" % i, [64, 8, 512]) for i in range(2)]
    gate = [sb("gateB%d" % i, [64, 8, 512], BF16) for i in range(2)]
    sq = sb("sqB", [64, 8, 512]); ssq = sb("ssqB", [64, 32]); sg = sb("sgB", [64, 8, 512])
    onb = sb("onbB", [64, 8, 512], BF16)
    ngT = sb("ngTB", [128, 4, 512], BF16)
    tmv = lambda b_: b_.t.ap().rearrange("(c p) d -> p c d", p=64)
    ngb_v = C.scr["NGB"].t.ap().rearrange("(h p) t -> p h t", p=128)
    for ti, (n0, n, w) in enumerate(TTILES):
        nch = n // 64
        c0 = n0 // 64
        f_, b_, g_ = of[ti % 2], ob[ti % 2], gate[ti % 2]
        P.D("sync", f_[:, :nch, :], tmv(C.scr["OF"])[:, c0:c0 + nch, :], reads=[C.scr["OF"]], writes=[f_])
        P.D("sync", b_[:, :nch, :], tmv(C.scr["OBW"])[:, c0:c0 + nch, :], reads=[C.scr["OBW"]], writes=[b_])
        P.D("sync", g_[:, :nch, :], tmv(C.scr["BG"])[:, c0:c0 + nch, :], reads=[C.scr["BG"]], writes=[g_])
        I("gpsimd", "tensor_tensor", reads=[f_, b_], writes=[f_], out=f_[:, :nch, :], in0=f_[:, :nch, :], in1=b_[:, :nch, :], op=ALU.add)
        if "ob" in C.debug:
            P.D("sync", C.dbg["ob"].ap().rearrange("(c p) d -> p c d", p=64)[:, c0:c0 + nch, :], f_[:, :nch, :], reads=[f_])
        I("scalar", "activation", reads=[g_], writes=[sg], out=sg[:, :nch, :], in_=g_[:, :nch, :], func=AF.Silu)
        I("gpsimd", "tensor_tensor", reads=[f_], writes=[sq], out=sq[:, :nch, :], in0=f_[:, :nch, :], in1=f_[:, :nch, :], op=ALU.mult)
        I("vector", "tensor_reduce", reads=[sq], writes=[ssq], out=ssq[:, :nch * 4], in_=sq[:, :nch, :].rearrange("p c (h d) -> p (c h) d", d=128), axis=AX.X, op=ALU.add)
        I("scalar", "activation", reads=[ssq], writes=[ssq], out=ssq[:, :nch * 4], in_=ssq[:, :nch * 4], func=AF.Sqrt, scale=1.0 / 128, bias=EPS)
        I("vector", "reciprocal", reads=[ssq], writes=[ssq], out=ssq[:, :nch * 4], in_=ssq[:, :nch * 4])
        I("vector", "tensor_tensor", reads=[f_, ssq], writes=[sq], out=sq[:, :nch, :].rearrange("p c (h d) -> p (c h) d", d=128),
          in0=f_[:, :nch, :].rearrange("p c (h d) -> p (c h) d", d=128), in1=AP(ssq, 0, [[32, 64], [1, nch * 4], [0, 128]]), op=ALU.mult)
        I("gpsimd", "tensor_tensor", reads=[sq, nwb], writes=[sq], out=sq[:, :nch, :].rearrange("p c (h d) -> p (c h) d", d=128),
          in0=sq[:, :nch, :].rearrange("p c (h d) -> p (c h) d", d=128), in1=AP(nwb, 0, [[128, 64], [0, nch * 4], [1, 128]]), op=ALU.mult)
        I("vector", "tensor_tensor", reads=[sq, sg], writes=[onb], out=onb[:, :nch, :], in0=sq[:, :nch, :], in1=sg[:, :nch, :], op=ALU.mult)
        for h in range(4):
            ppt = pt[h % 2]
            ptv = ppt[:, 0:256].bitcast(BF16).rearrange("p (a b) -> p a b", b=64)
            for j in range(nch):
                P.tr(ptv[:, j, :], onb[:, j, h * 128:(h + 1) * 128], C.identb[0:64, 0:64], reads=[onb, C.identb], writes=[ppt])
            I("scalar", "activation", reads=[ppt], writes=[(ngT, h)], out=ngT[:, h, :n], in_=ppt[:, 0:256].bitcast(BF16)[:, 0:n], func=AF.Copy)
        P.D("sync", ngb_v[:, :, n0:n0 + n], ngT[:, :, :n], reads=[ngT], writes=[(C.scr["NGB"], n0)])
    P.emit()


TWO_PI = 6.283185307179586
PI = 3.141592653589793
NPOW = 129


def make_s5_masks(P, C):
    nc = P.nc
    st = C.gstack
    def gsb(name, shape, dt=F32):
        return Buf(st.enter_context(nc.sbuf_tensor(name, list(shape), dt)), name)
    A = gsb("s5mA", [8, 128]); Bge = gsb("s5mB1", [8, 128]); Ble = gsb("s5mB2", [8, 128]); on8 = gsb("s5on", [8, 128])
    C.maskZ = [gsb("maskZf", [128, 128]), gsb("maskZb", [128, 128])]
    pm = P.ps("pmask", [128, 512])
    P.I("vector", "memset", writes=[on8], ap=on8[:], constant=1.0)
    P.I("gpsimd", "affine_select", reads=[on8], writes=[A], out=A[:], in_=on8[:], pattern=[[1, 128]], compare_op=ALU.is_ge, fill=0.0, base=0, channel_multiplier=-16)
    P.I("gpsimd", "affine_select", reads=[A], writes=[A], out=A[:], in_=A[:], pattern=[[-1, 128]], compare_op=ALU.is_ge, fill=0.0, base=15, channel_multiplier=16)
    P.I("gpsimd", "affine_select", reads=[on8], writes=[Bge], out=Bge[:], in_=on8[:], pattern=[[1, 8], [0, 16]], compare_op=ALU.is_ge, fill=0.0, base=0, channel_multiplier=-1)
    P.I("gpsimd", "affine_select", reads=[on8], writes=[Ble], out=Ble[:], in_=on8[:], pattern=[[-1, 8], [0, 16]], compare_op=ALU.is_ge, fill=0.0, base=0, channel_multiplier=1)
    P.mm(pm[:, 0:128], A[:], Bge[:], True, True, reads=[A, Bge], writes=[pm])
    P.mm(pm[:, 128:256], A[:], Ble[:], True, True, reads=[A, Ble], writes=[pm])
    P.I("vector", "tensor_copy", reads=[pm], writes=[C.maskZ[0]], out=C.maskZ[0][:], in_=pm[:, 0:128])
    P.I("vector", "tensor_copy", reads=[pm], writes=[C.maskZ[1]], out=C.maskZ[1][:], in_=pm[:, 128:256])


def phase_s5(nc, C, l):
    CU = C.scr["CU"]
    YC = C.scr["YC"]
    cu_g = CU.t.ap().rearrange("(g c) n -> c g n", c=16)
    yc_g = YC.t.ap().rearrange("(g c) n -> c g n", c=16)
    P = Prog(nc)
    S5A = C.scr["S5A"]
    for q in range(4):
        tmpc = P.sb("tmpc%d" % q, [128, 256], BF16)
        ctr = P.sb("ctr%d" % q, [128, 8, 8, 4], BF16)
        P.D("sync", tmpc[:], CU[q * 128:(q + 1) * 128, 0:256], reads=[CU], writes=[tmpc])
        P.I("vector", "tensor_copy", reads=[tmpc], writes=[ctr], out=ctr[:], in_=tmpc[:].rearrange("p (sc blk t) -> p t blk sc", sc=4, blk=8))
        P.D("sync", S5A[q * 128:(q + 1) * 128, :], ctr[:].rearrange("p t b s -> p (t b s)"), reads=[ctr], writes=[(S5A, q)])
    P.emit()
    for hf in range(2):
        s5_half(nc, C, l, hf, cu_g, yc_g)
    P = Prog(nc)
    S5B = C.scr["S5B"]
    for q in range(4):
        tmpc = P.sb("tmpd%d" % q, [128, 256], BF16)
        ctr = P.sb("ctd%d" % q, [128, 8, 8, 4], BF16)
        P.D("sync", ctr[:].rearrange("p t b s -> p (t b s)"), S5B[q * 128:(q + 1) * 128, :], reads=[S5B], writes=[ctr])
        P.I("vector", "tensor_copy", reads=[ctr], writes=[tmpc], out=tmpc[:].rearrange("p (sc blk t) -> p t blk sc", sc=4, blk=8), in_=ctr[:])
        P.D("sync", YC[q * 128:(q + 1) * 128, 0:256], tmpc[:], reads=[tmpc], writes=[(YC, q)])
    P.emit()


def s5_half(nc, C, l, hf, cu_g, yc_g):
    G0 = hf * 16
    hst = ExitStack()
    cntr = [0]

    def pst(name, shape, dt=F32):
        _UID[0] += 1
        return Buf(hst.enter_context(nc.sbuf_tensor("s5p_%s_%d" % (name, _UID[0]), list(shape), dt)), name)

    with hst:
        Er = pst("Er", [64, 32, NPOW]); Ei = pst("Ei", [64, 32, NPOW])
        bbr = pst("bbr", [64, 32, 16]); bbi = pst("bbi", [64, 32, 16])
        cre = pst("cre", [64, 16, 16]); cim = pst("cim", [64, 16, 16]); ncim = pst("ncim", [64, 16, 16]); ncre = pst("ncre", [64, 16, 16])
        dtab = pst("dtab", [128, 16])
        U2 = pst("U2", [128, 16, 8, 68], BF16)
        Sall = pst("Sall", [64, 2, 2, 16, 68])
        XPb = pst("XPb", [64, 2, 2, 16, 68], BF16)
        P = Prog(nc)
        I = P.I
        sb = P.sb
        lre = sb("lre", [64, 2, 16]); lim = sb("lim", [64, 2, 16]); ldt = sb("ldt", [64, 2, 16])
        P.D("sync", lre[:], C.din["s5_lam_re"][l][:, :, G0:G0 + 16], writes=[lre])
        P.D("sync", lim[:], C.din["s5_lam_im"][l][:, :, G0:G0 + 16], writes=[lim])
        P.D("sync", ldt[:], C.din["s5_log_dt"][l].partition_broadcast(64)[:, :, G0:G0 + 16], writes=[ldt])
        br = sb("br", [64, 16, 16]); bi = sb("bi", [64, 16, 16])
        P.D("sync", br[:], C.din["s5_b_re"][l][:, G0:G0 + 16, :], writes=[br])
        P.D("sync", bi[:], C.din["s5_b_im"][l][:, G0:G0 + 16, :], writes=[bi])
        P.D("sync", cre[:], C.din["s5_c_re"][l][:, G0:G0 + 16, :], writes=[cre])
        P.D("sync", cim[:], C.din["s5_c_im"][l][:, G0:G0 + 16, :], writes=[cim])
        I("vector", "tensor_scalar", reads=[cim], writes=[ncim], out=ncim[:], in0=cim[:], scalar1=-1.0, scalar2=None, op0=ALU.mult)
        I("vector", "tensor_scalar", reads=[cre], writes=[ncre], out=ncre[:], in0=cre[:], scalar1=-1.0, scalar2=None, op0=ALU.mult)
        P.D("sync", dtab[:], C.din["s5_dtab"][l][:, G0:G0 + 16], writes=[dtab])
        U2c = sb("U2c", [128, 16, 32], BF16)
        s5a = C.scr["S5A"].t.ap().rearrange("(g c) (t x) -> t c g x", c=16, t=8)
        for t in range(8):
            src = cu_g[:, G0:G0 + 16, 256:T].rearrange("c g (blk t col) -> c g blk t col", blk=8, t=8)
            P.dma("sync", [lambda e, t=t, b_=b_, src=src: e.dma_start(out=U2[16 * t:16 * t + 16, :, b_, 4:68], in_=src[:, :, b_, t, :]) for b_ in range(8)],
                  reads=[C.scr["CU"]], writes=[(U2, t)])
            P.D("sync", U2c[16 * t:16 * t + 16, :, :], s5a[t][:, G0:G0 + 16, :], reads=[C.scr["S5A"]], writes=[(U2c, t)])
        I("vector", "tensor_copy", reads=[U2c, U2], writes=[U2], out=U2[:, :, :, 0:4], in_=U2c[:].rearrange("p g (b s) -> p g b s", s=4))
        dtt = sb("dtt", [64, 32]); xx = sb("xx", [64, 32]); th = sb("th", [64, 32]); lr = sb("lr", [64, 32])
        lre2 = lre[:].rearrange("p d g -> p (d g)"); lim2 = lim[:].rearrange("p d g -> p (d g)"); ldt2 = ldt[:].rearrange("p d g -> p (d g)")
        I("scalar", "activation", reads=[ldt], writes=[dtt], out=dtt[:], in_=ldt2, func=AF.Exp)
        I("vector", "tensor_scalar", reads=[lre], writes=[lr], out=lr[:], in0=lre2, scalar1=-1e-4, scalar2=None, op0=ALU.min)
        I("vector", "tensor_tensor", reads=[lr, dtt], writes=[xx], out=xx[:], in0=lr[:], in1=dtt[:], op=ALU.mult)
        I("vector", "tensor_tensor", reads=[lim, dtt], writes=[th], out=th[:], in0=lim2, in1=dtt[:], op=ALU.mult)
        jt = sb("jt", [64, NPOW])
        I("gpsimd", "iota", writes=[(jt, 0)], out=jt[:, 0:65], pattern=[[1, 65]], base=0, channel_multiplier=0, allow_small_or_imprecise_dtypes=True)
        I("gpsimd", "iota", writes=[(jt, 1)], out=jt[:, 65:129], pattern=[[-1, 64]], base=0, channel_multiplier=0, allow_small_or_imprecise_dtypes=True)
        ph = sb("ph", [64, 32, NPOW]); qf = sb("qf", [64, 32, NPOW]); qi = sb("qi", [64, 32, NPOW], I32); mk = sb("mk", [64, 32, NPOW])
        jb_ = AP(jt, 0, [[NPOW, 64], [0, 32], [1, NPOW]])
        thb = AP(th, 0, [[32, 64], [1, 32], [0, NPOW]])
        xb = AP(xx, 0, [[32, 64], [1, 32], [0, NPOW]])
        I("vector", "tensor_tensor", reads=[th, jt], writes=[ph], out=ph[:], in0=thb, in1=jb_, op=ALU.mult)
        I("vector", "tensor_scalar", reads=[ph], writes=[qf], out=qf[:], in0=ph[:], scalar1=1.0 / TWO_PI, scalar2=None, op0=ALU.mult)
        I("vector", "tensor_copy", reads=[qf], writes=[qi], out=qi[:], in_=qf[:])
        I("vector", "tensor_copy", reads=[qi], writes=[qf], out=qf[:], in_=qi[:])
        I("vector", "scalar_tensor_tensor", reads=[qf, ph], writes=[ph], out=ph[:], in0=qf[:], scalar=-TWO_PI, in1=ph[:], op0=ALU.mult, op1=ALU.add)
        for (cmp_, sgn) in ((ALU.is_gt, -1.0), (ALU.is_lt, 1.0)):
            I("vector", "tensor_scalar", reads=[ph], writes=[mk], out=mk[:], in0=ph[:], scalar1=(PI if sgn < 0 else -PI), scalar2=None, op0=cmp_)
            I("vector", "scalar_tensor_tensor", reads=[mk, ph], writes=[ph], out=ph[:], in0=mk[:], scalar=sgn * TWO_PI, in1=ph[:], op0=ALU.mult, op1=ALU.add)
        I("scalar", "activation", reads=[ph], writes=[Ei], out=Ei[:], in_=ph[:], func=AF.Sin)
        I("vector", "tensor_scalar", reads=[ph], writes=[ph], out=ph[:], in0=ph[:], scalar1=PI / 2, scalar2=None, op0=ALU.add)
        I("vector", "tensor_scalar", reads=[ph], writes=[mk], out=mk[:], in0=ph[:], scalar1=PI, scalar2=None, op0=ALU.is_gt)
        I("vector", "scalar_tensor_tensor", reads=[mk, ph], writes=[ph], out=ph[:], in0=mk[:], scalar=-TWO_PI, in1=ph[:], op0=ALU.mult, op1=ALU.add)
        I("scalar", "activation", reads=[ph], writes=[Er], out=Er[:], in_=ph[:], func=AF.Sin)
        I("vector", "tensor_tensor", reads=[xx, jt], writes=[qf], out=qf[:], in0=xb, in1=jb_, op=ALU.mult)
        I("scalar", "activation", reads=[qf], writes=[qf], out=qf[:], in_=qf[:], func=AF.Exp)
        I("vector", "tensor_tensor", reads=[Er, qf], writes=[Er], out=Er[:], in0=Er[:], in1=qf[:], op=ALU.mult)
        I("gpsimd", "tensor_tensor", reads=[Ei, qf], writes=[Ei], out=Ei[:], in0=Ei[:], in1=qf[:], op=ALU.mult)

        def col(tab, j):
            return AP(tab, j, [[32 * NPOW, 64], [NPOW, 32]])
        den = sb("den", [64, 32]); t0 = sb("t0", [64, 32]); t1 = sb("t1b", [64, 32]); crr = sb("crr", [64, 32]); cii = sb("cii", [64, 32]); am1 = sb("am1", [64, 32])
        I("vector", "tensor_tensor", reads=[lr], writes=[den], out=den[:], in0=lr[:], in1=lr[:], op=ALU.mult)
        I("vector", "tensor_tensor", reads=[lim], writes=[t0], out=t0[:], in0=lim2, in1=lim2, op=ALU.mult)
        I("vector", "tensor_tensor", reads=[den, t0], writes=[den], out=den[:], in0=den[:], in1=t0[:], op=ALU.add)
        I("vector", "reciprocal", reads=[den], writes=[den], out=den[:], in_=den[:])
        I("vector", "tensor_scalar", reads=[Er], writes=[am1], out=am1[:], in0=col(Er, 1), scalar1=-1.0, scalar2=None, op0=ALU.add)
        I("vector", "tensor_tensor", reads=[am1, lr], writes=[t0], out=t0[:], in0=am1[:], in1=lr[:], op=ALU.mult)
        I("vector", "tensor_tensor", reads=[Ei, lim], writes=[t1], out=t1[:], in0=col(Ei, 1), in1=lim2, op=ALU.mult)
        I("vector", "tensor_tensor", reads=[t0, t1], writes=[crr], out=crr[:], in0=t0[:], in1=t1[:], op=ALU.add)
        I("vector", "tensor_tensor", reads=[crr, den], writes=[crr], out=crr[:], in0=crr[:], in1=den[:], op=ALU.mult)
        I("vector", "tensor_tensor", reads=[Ei, lr], writes=[t0], out=t0[:], in0=col(Ei, 1), in1=lr[:], op=ALU.mult)
        I("vector", "tensor_tensor", reads=[am1, lim], writes=[t1], out=t1[:], in0=am1[:], in1=lim2, op=ALU.mult)
        I("vector", "tensor_tensor", reads=[t0, t1], writes=[cii], out=cii[:], in0=t0[:], in1=t1[:], op=ALU.subtract)
        I("vector", "tensor_tensor", reads=[cii, den], writes=[cii], out=cii[:], in0=cii[:], in1=den[:], op=ALU.mult)
        tb = sb("tb", [64, 32, 16])
        crb4 = AP(crr, 0, [[32, 64], [16, 2], [1, 16], [0, 16]]); cib4 = AP(cii, 0, [[32, 64], [16, 2], [1, 16], [0, 16]])
        br4 = AP(br, 0, [[256, 64], [0, 2], [16, 16], [1, 16]]); bi4 = AP(bi, 0, [[256, 64], [0, 2], [16, 16], [1, 16]])
        o4 = lambda t_: t_[:].rearrange("p (d g) c -> p d g c", d=2)
        I("vector", "tensor_tensor", reads=[crr, br], writes=[bbr], out=o4(bbr), in0=crb4, in1=br4, op=ALU.mult)
        I("vector", "tensor_tensor", reads=[cii, bi], writes=[tb], out=o4(tb), in0=cib4, in1=bi4, op=ALU.mult)
        I("vector", "tensor_tensor", reads=[bbr, tb], writes=[bbr], out=bbr[:], in0=bbr[:], in1=tb[:], op=ALU.subtract)
        I("vector", "tensor_tensor", reads=[crr, bi], writes=[bbi], out=o4(bbi), in0=crb4, in1=bi4, op=ALU.mult)
        I("vector", "tensor_tensor", reads=[cii, br], writes=[tb], out=o4(tb), in0=cib4, in1=br4, op=ALU.mult)
        I("vector", "tensor_tensor", reads=[bbi, tb], writes=[bbi], out=bbi[:], in0=bbi[:], in1=tb[:], op=ALU.add)
        if "s5tab" in C.debug and hf == 0:
            P.D("sync", C.dbg["Er"].ap(), Er[:], reads=[Er]); P.D("sync", C.dbg["Ei"].ap(), Ei[:], reads=[Ei])
            P.D("sync", C.dbg["bbr"].ap(), bbr[:], reads=[bbr]); P.D("sync", C.dbg["bbi"].ap(), bbi[:], reads=[bbi])
        P.emit()
        if getattr(C, 's5stop', None) == 'tab':
            return
        s5_main(nc, C, l, hf, locals())


def s5_main(nc, C, l, hf, L):
    Er, Ei, bbr, bbi, cre, cim, ncim, ncre, dtab, U2, Sall, XPb = (L[k] for k in
        ("Er", "Ei", "bbr", "bbi", "cre", "cim", "ncim", "ncre", "dtab", "U2", "Sall", "XPb"))
    G0 = hf * 16
    P = Prog(nc)
    I = P.I
    sb = P.sb
    banks = [P.ps("s5b%d" % i, [128, 512]) for i in range(8)]
    PS = 32 * NPOW

    def Ev(tab, dg, c0, n):
        return AP(tab, dg * NPOW + c0, [[PS, 64], [1, n], [0, 16]])

    def Bv(tab, dg, n):
        return AP(tab, dg * 16, [[512, 64], [0, n], [1, 16]])

    def Cv(tab, gl, n):
        return AP(tab, gl * 16, [[256, 64], [0, n], [1, 16]])

    tA = [sb("tA%d" % i, [64, 65, 16]) for i in range(2)]
    tB = [sb("tB%d" % i, [64, 65, 16]) for i in range(2)]
    cnt = [0]

    def cprod(outre, outim, n, er, ei, xr, xi, xrn=None, sub_im=False):
        k = cnt[0] % 2
        cnt[0] += 1
        a, b = tA[k], tB[k]
        e1, e2 = ("vector", "gpsimd") if k == 0 else ("gpsimd", "vector")
        I(e1, "tensor_tensor", reads=[Er, bbr, cre], writes=[a], out=a[:, :n, :], in0=er, in1=xr, op=ALU.mult)
        I(e2, "tensor_tensor", reads=[Ei, bbi, cim], writes=[b], out=b[:, :n, :], in0=ei, in1=xi, op=ALU.mult)
        I(e1, "tensor_tensor", reads=[a, b], writes=[outre], out=outre[:, :n, :], in0=a[:, :n, :], in1=b[:, :n, :], op=ALU.subtract)
        if not sub_im:
            I(e1, "tensor_tensor", reads=[Er, bbi], writes=[a], out=a[:, :n, :], in0=er, in1=xi, op=ALU.mult)
            I(e2, "tensor_tensor", reads=[Ei, bbr], writes=[b], out=b[:, :n, :], in0=ei, in1=xr, op=ALU.mult)
            I(e2, "tensor_tensor", reads=[a, b], writes=[outim], out=outim[:, :n, :], in0=a[:, :n, :], in1=b[:, :n, :], op=ALU.add)
        else:
            I(e1, "tensor_tensor", reads=[Er, ncim], writes=[a], out=a[:, :n, :], in0=er, in1=xrn, op=ALU.mult)
            I(e2, "tensor_tensor", reads=[Ei, cre], writes=[b], out=b[:, :n, :], in0=ei, in1=xr, op=ALU.mult)
            I(e2, "tensor_tensor", reads=[a, b], writes=[outim], out=outim[:, :n, :], in0=a[:, :n, :], in1=b[:, :n, :], op=ALU.subtract)

    Wre = [sb("Wre%d" % i, [64, 64, 16], BF16) for i in range(2)]
    Wim = [sb("Wim%d" % i, [64, 64, 16], BF16) for i in range(2)]
    PTs = [sb("PTs%d" % i, [128, 8, 128], BF16) for i in range(2)]
    it = 0
    for d in range(2):
        for gl in range(16):
            dg = d * 16 + gl
            wr, wi, pts = Wre[it % 2], Wim[it % 2], PTs[it % 2]
            c0 = 65 if d == 0 else 0
            cprod(wr, wi, 64, Ev(Er, dg, c0, 64), Ev(Ei, dg, c0, 64), Bv(bbr, dg, 64), Bv(bbi, dg, 64))
            pt = banks[it % 2]
            ptv = pt[:, :].bitcast(BF16).rearrange("p (a b) -> p a b", b=128)
            for blk in range(8):
                P.tr(ptv[:, blk, 0:64], wr[:, blk * 8:(blk + 1) * 8, :].rearrange("p m c -> p (m c)"), C.identb[0:64, 0:64], reads=[wr, C.identb], writes=[pt])
                P.tr(ptv[:, blk, 64:128], wi[:, blk * 8:(blk + 1) * 8, :].rearrange("p m c -> p (m c)"), C.identb[0:64, 0:64], reads=[wi, C.identb], writes=[pt])
            I("scalar", "activation", reads=[pt], writes=[pts], out=pts[:], in_=ptv, func=AF.Copy)
            ps = banks[2 + it % 2]
            for blk in range(8):
                P.mm(ps[0:64, 0:68], pts[:, blk, 0:64], U2[:, gl, blk, :], blk == 0, blk == 7, reads=[pts, U2], writes=[ps])
            for blk in range(8):
                P.mm(ps[0:64, 68:136], pts[:, blk, 64:128], U2[:, gl, blk, :], blk == 0, blk == 7, reads=[pts, U2], writes=[ps])
            I("scalar", "activation", reads=[ps], writes=[(Sall, d)], out=Sall[:, :, d, gl, :], in_=ps[0:64, 0:136].rearrange("p (a b) -> p a b", b=68), func=AF.Copy)
            it += 1
    if getattr(C, 's5stop', None) == 'st1':
        P.emit()
        return
    s1 = sb("s1", [64, 16, 68]); s2 = sb("s2", [64, 16, 68]); s3 = sb("s3", [64, 16, 68]); s4 = sb("s4", [64, 16, 68])
    a63r = AP(Er, 63, [[PS, 64], [NPOW, 16], [0, 68]]); a63i = AP(Ei, 63, [[PS, 64], [NPOW, 16], [0, 68]])
    Sr = Sall[:, 0, 0, :, :]; Si = Sall[:, 1, 0, :, :]
    I("vector", "tensor_tensor", reads=[(Sall, 0), Er], writes=[s1], out=s1[:], in0=Sr, in1=a63r, op=ALU.mult)
    I("gpsimd", "tensor_tensor", reads=[(Sall, 0), Ei], writes=[s2], out=s2[:], in0=Si, in1=a63i, op=ALU.mult)
    I("vector", "tensor_tensor", reads=[(Sall, 0), Er], writes=[s3], out=s3[:], in0=Si, in1=a63r, op=ALU.mult)
    I("gpsimd", "tensor_tensor", reads=[(Sall, 0), Ei], writes=[s4], out=s4[:], in0=Sr, in1=a63i, op=ALU.mult)
    I("vector", "tensor_tensor", reads=[s1, s2], writes=[(Sall, 0)], out=Sr, in0=s1[:], in1=s2[:], op=ALU.subtract)
    I("gpsimd", "tensor_tensor", reads=[s3, s4, (Sall, 0)], writes=[(Sall, 0)], out=Si, in0=s3[:], in1=s4[:], op=ALU.add)
    SS = 2 * 2 * 16 * 68
    for d in range(2):
        eng = "vector" if d == 0 else "gpsimd"
        AA = sb("AA%d" % d, [64, 2, 16]); AC = sb("AC%d" % d, [64, 2, 16])
        a64r = AP(Er, d * 16 * NPOW + 64, [[PS, 64], [NPOW, 16]]); a64i = AP(Ei, d * 16 * NPOW + 64, [[PS, 64], [NPOW, 16]])
        I(eng, "tensor_copy", reads=[Er], writes=[AA], out=AA[:, 0, :], in_=a64r)
        I(eng, "tensor_copy", reads=[Er, AA], writes=[AA], out=AA[:, 1, :], in_=a64r)
        I(eng, "tensor_copy", reads=[Ei], writes=[AC], out=AC[:, 0, :], in_=a64i)
        I(eng, "tensor_scalar", reads=[Ei, AC], writes=[AC], out=AC[:, 1, :], in0=a64i, scalar1=-1.0, scalar2=None, op0=ALU.mult)
        X2 = [sb("X2_%d_%d" % (d, i), [64, 2, 16]) for i in range(2)]
        p1 = sb("p1_%d" % d, [64, 2, 16]); p2 = sb("p2_%d" % d, [64, 2, 16]); dec = sb("dec_%d" % d, [64, 2, 16])
        I(eng, "memset", writes=[X2[0]], ap=X2[0][:], constant=0.0)
        for i, sc in enumerate(CH_FWD if d == 0 else CH_BWD):
            xc, xn = X2[i % 2], X2[(i + 1) % 2]
            sv = AP(Sall, d * 16 * 68 + sc, [[SS, 64], [2 * 16 * 68, 2], [68, 16]])
            xv = AP(XPb, d * 16 * 68 + sc, [[SS, 64], [2 * 16 * 68, 2], [68, 16]])
            if d == 0:
                I(eng, "tensor_copy", reads=[xc], writes=[(XPb, d)], out=xv, in_=xc[:])
            I(eng, "tensor_tensor", reads=[xc, AA], writes=[p1], out=p1[:], in0=xc[:], in1=AA[:], op=ALU.mult)
            I(eng, "tensor_tensor", reads=[xc, AC], writes=[p2], out=p2[:], in0=xc[:], in1=AC[:], op=ALU.mult)
            I(eng, "tensor_tensor", reads=[p1, p2], writes=[dec], out=dec[:, 0, :], in0=p1[:, 0, :], in1=p2[:, 1, :], op=ALU.add)
            I(eng, "tensor_tensor", reads=[p1, p2, dec], writes=[dec], out=dec[:, 1, :], in0=p1[:, 1, :], in1=p2[:, 0, :], op=ALU.add)
            if d == 1:
                I(eng, "tensor_copy", reads=[dec], writes=[(XPb, d)], out=xv, in_=dec[:])
            I(eng, "tensor_tensor", reads=[dec, (Sall, d)], writes=[xn], out=xn[:], in0=dec[:], in1=sv, op=ALU.add)
    if getattr(C, 's5stop', None) == 'scan':
        P.emit()
        return
    Wf = [sb("Wfr", [64, 8, 16], BF16), sb("Wfi", [64, 8, 16], BF16)]
    Rf = [sb("Rft", [64, 65, 16], BF16), sb("Rfb", [64, 65, 16], BF16)]
    Wb = [sb("Wbr", [64, 64, 16], BF16), sb("Wbi", [64, 64, 16], BF16)]
    Rb = [sb("Rbt", [64, 64, 16], BF16), sb("Rbb", [64, 64, 16], BF16)]
    Zf = sb("Zf", [128, 64, 16], BF16)
    Zb = sb("Zb", [128, 8, 128], BF16)
    Ysb = sb("Ysb", [128, 16, 8, 68], BF16)
    flat = lambda ap_: ap_.rearrange("p m c -> p (m c)")
    for gl in range(16):
        df, db = gl, 16 + gl
        cprod(Wf[0], Wf[1], 8, Ev(Er, df, 65, 8), Ev(Ei, df, 65, 8), Bv(bbr, df, 8), Bv(bbi, df, 8))
        cprod(Rf[0], Rf[1], 65, Ev(Er, df, 0, 65), Ev(Ei, df, 0, 65), Cv(cre, gl, 65), Cv(cim, gl, 65), xrn=Cv(ncim, gl, 65), sub_im=True)
        cprod(Wb[0], Wb[1], 64, Ev(Er, db, 0, 64), Ev(Ei, db, 0, 64), Bv(bbr, db, 64), Bv(bbi, db, 64))
        cprod(Rb[0], Rb[1], 64, Ev(Er, db, 65, 64), Ev(Ei, db, 65, 64), Cv(cre, gl, 64), Cv(cim, gl, 64), xrn=Cv(ncim, gl, 64), sub_im=True)
        z0, z1, z2, z3 = banks[0], banks[1], banks[2], banks[3]
        for hh, zb in enumerate((z0, z1)):
            P.mm(zb[:, :], flat(Wf[0][:, :, :]), flat(Rf[0][:, hh * 32:(hh + 1) * 32, :]), True, False, reads=[Wf[0], Rf[0]], writes=[zb])
            P.mm(zb[:, :], flat(Wf[1][:, :, :]), flat(Rf[1][:, hh * 32:(hh + 1) * 32, :]), False, True, reads=[Wf[1], Rf[1]], writes=[zb])
        I("vector", "tensor_tensor", reads=[z0, C.maskZ[0]], writes=[(Zf, 0)], out=flat(Zf[:, 0:8, :]), in0=z0[:, 0:128], in1=C.maskZ[0][:], op=ALU.mult)
        I("scalar", "activation", reads=[z0], writes=[(Zf, 1)], out=flat(Zf[:, 8:32, :]), in_=z0[:, 128:512], func=AF.Copy)
        I("scalar", "activation", reads=[z1], writes=[(Zf, 2)], out=flat(Zf[:, 32:64, :]), in_=z1[:, :], func=AF.Copy)
        for dl in range(8):
            zb = z2 if dl < 4 else z3
            o_ = zb[:, (dl % 4) * 128:(dl % 4 + 1) * 128]
            P.mm(o_, flat(Wb[0][:, dl * 8:(dl + 1) * 8, :]), flat(Rb[0][:, 0:8, :]), True, False, reads=[Wb[0], Rb[0]], writes=[zb])
            P.mm(o_, flat(Wb[1][:, dl * 8:(dl + 1) * 8, :]), flat(Rb[1][:, 0:8, :]), False, True, reads=[Wb[1], Rb[1]], writes=[zb])
        I("vector", "tensor_tensor", reads=[z2, C.maskZ[1]], writes=[(Zb, 0)], out=Zb[:, 0, :], in0=z2[:, 0:128], in1=C.maskZ[1][:], op=ALU.mult)
        I("scalar", "activation", reads=[z2], writes=[(Zb, 1)], out=Zb[:, 1:4, :], in_=z2[:, 128:512].rearrange("p (a b) -> p a b", b=128), func=AF.Copy)
        I("scalar", "activation", reads=[z3], writes=[(Zb, 2)], out=Zb[:, 4:8, :], in_=z3[:, :].rearrange("p (a b) -> p a b", b=128), func=AF.Copy)
        if getattr(C, 's5stop', None) in ('z', 'z3'):
            continue
        for hb in range(2):
            yb = banks[4 + (2 * gl + hb) % 4]
            for i in range(4):
                ib = 4 * hb + i
                o_ = yb[:, i * 68:(i + 1) * 68]
                ops_ = []
                for jb in range(0, ib + 1):
                    ops_.append((flat(Zf[:, (ib - jb) * 8:(ib - jb + 1) * 8, :]), U2[:, gl, jb, :], [Zf, U2]))
                for jb in range(ib, 8):
                    ops_.append((Zb[:, jb - ib, :], U2[:, gl, jb, :], [Zb, U2]))
                ops_.append((flat(Rf[0][:, 8 * ib + 1:8 * ib + 9, :]), XPb[:, 0, 0, gl, :], [Rf[0], XPb]))
                ops_.append((flat(Rf[1][:, 8 * ib + 1:8 * ib + 9, :]), XPb[:, 1, 0, gl, :], [Rf[1], XPb]))
                ops_.append((flat(Rb[0][:, 8 * ib:8 * ib + 8, :]), XPb[:, 0, 1, gl, :], [Rb[0], XPb]))
                ops_.append((flat(Rb[1][:, 8 * ib:8 * ib + 8, :]), XPb[:, 1, 1, gl, :], [Rb[1], XPb]))
                for k, (lh, rh, rd) in enumerate(ops_):
                    P.mm(o_, lh, rh, k == 0, k == len(ops_) - 1, reads=rd, writes=[yb])
            I("vector", "scalar_tensor_tensor", reads=[U2, dtab, yb], writes=[(Ysb, gl)], out=Ysb[:, gl, 4 * hb:4 * hb + 4, :], in0=U2[:, gl, 4 * hb:4 * hb + 4, :],
              scalar=dtab[:, gl:gl + 1], in1=yb[:, 0:272].rearrange("p (a b) -> p a b", b=68), op0=ALU.mult, op1=ALU.add)
    if getattr(C, 's5stop', None) in ('st3', 'z3'):
        P.emit()
        return
    Yc2 = sb("Yc2", [128, 16, 32], BF16)
    I("vector", "tensor_copy", reads=[Ysb], writes=[Yc2], out=Yc2[:].rearrange("p g (b s) -> p g b s", s=4), in_=Ysb[:, :, :, 0:4])
    yc_g = C.scr["YC"].t.ap().rearrange("(g c) n -> c g n", c=16)
    s5b = C.scr["S5B"].t.ap().rearrange("(g c) (t x) -> t c g x", c=16, t=8)
    for t in range(8):
        dst = yc_g[:, G0:G0 + 16, 256:T].rearrange("c g (blk t col) -> c g blk t col", blk=8, t=8)
        P.dma("sync", [lambda e, t=t, b_=b_, dst=dst: e.dma_start(out=dst[:, :, b_, t, :], in_=Ysb[16 * t:16 * t + 16, :, b_, 4:68]) for b_ in range(8)],
              reads=[Ysb], writes=[(C.scr["YC"], (hf, t))])
        P.D("sync", s5b[t][:, G0:G0 + 16, :], Yc2[16 * t:16 * t + 16, :, :], reads=[Yc2], writes=[(C.scr["S5B"], (hf, t))])
    P.emit()


def phase_merge(nc, C, l, xsrc, xdst, tiles=None):
    P = Prog(nc)
    I = P.I
    sb = P.sb
    wts = {}
    for nm, src, kc, ncol in (("ba", "w_branch_a", 4, 1024), ("bb", "w_branch_b", 4, 1024), ("bc", "w_branch_c", 4, 1024),
                              ("glu", "s5_w_glu", 4, 512), ("wo", "w_out", 8, 1024)):
        wts[nm] = sb("w_" + nm, [128, kc, ncol], BF16)
        wv = C.din[src][l].rearrange("(k p) c -> p k c", p=128)
        for k in range(kc):
            P.D("gpsimd", wts[nm][:, k, :], wv[:, k, :], writes=[(wts[nm], k)])
    nga = sb("nga", [128, 4, 512], BF16); ngb = sb("ngb", [128, 4, 512], BF16); ycb = sb("ycb", [128, 4, 512], BF16)
    mg = sb("mg", [128, 24, 512], BF16)
    xT = sb("xTm", [128, 8, 512])
    yc = sb("yc32", [128, 4, 512]); x2 = sb("x2", [128, 4, 512]); xh = sb("xh", [128, 4, 512])
    tt = x2
    zb = sb("zb", [128, 4, 512], BF16); zz = sb("zz", [128, 4, 512], BF16)
    zf = yc
    sig = sb("sigg", [128, 512])
    gts = [sb("gt%d" % i, [128, 3, 512]) for i in range(2)]
    m1 = sb("mm1", [128, 512]); m2 = sb("mm2", [128, 512]); m3 = sb("mm3", [128, 512])
    mrg = sb("mrg", [128, 8, 512], BF16)
    xn = sb("xn", [128, 8, 512])
    pg = P.ps("pg", [128, 512]); po = P.ps("po", [128, 512])
    pabc = [[P.ps("pabc%d_%d" % (i, j), [128, 512]) for j in range(3)] for i in range(2)]
    view = lambda b_: b_.t.ap().rearrange("(k p) t -> p k t", p=128)
    xv = view(xsrc); xo = view(xdst)
    for (n0, n, w) in (tiles or TTILES):
        ts_ = slice(n0, n0 + n)
        P.D("sync", nga[:, :, :n], view(C.scr["NGA"])[:, :, ts_], reads=[C.scr["NGA"]], writes=[nga])
        P.D("sync", ngb[:, :, :n], view(C.scr["NGB"])[:, :, ts_], reads=[C.scr["NGB"]], writes=[ngb])
        P.D("sync", ycb[:, :, :n], view(C.scr["YC"])[:, :, ts_], reads=[C.scr["YC"]], writes=[ycb])
        P.D("sync", mg[:, :, :n], view(C.scr["MG"])[:, :, ts_], reads=[C.scr["MG"]], writes=[mg])
        P.D("sync", xT[:, :, :n], xv[:, :, ts_], reads=[(xsrc, n0)], writes=[xT])
        I("vector", "tensor_copy", reads=[ycb], writes=[yc], out=yc[:, :, :n], in_=ycb[:, :, :n])
        I("gpsimd", "tensor_tensor", reads=[yc], writes=[x2], out=x2[:, :, :n], in0=yc[:, :, :n], in1=yc[:, :, :n], op=ALU.mult)
        I("vector", "tensor_scalar", reads=[x2], writes=[x2], out=x2[:, :, :n], in0=x2[:, :, :n], scalar1=0.044715, scalar2=1.0, op0=ALU.mult, op1=ALU.add)
        I("gpsimd", "tensor_tensor", reads=[x2, yc], writes=[x2], out=x2[:, :, :n], in0=x2[:, :, :n], in1=yc[:, :, :n], op=ALU.mult)
        I("scalar", "activation", reads=[x2], writes=[tt], out=tt[:, :, :n], in_=x2[:, :, :n], func=AF.Tanh, scale=0.7978845608028654)
        I("scalar", "mul", reads=[yc], writes=[xh], out=xh[:, :, :n], in_=yc[:, :, :n], mul=0.5)
        I("vector", "scalar_tensor_tensor", reads=[tt, xh], writes=[zf], out=zf[:, :, :n], in0=tt[:, :, :n], scalar=1.0, in1=xh[:, :, :n], op0=ALU.add, op1=ALU.mult)
        I("gpsimd", "tensor_copy", reads=[zf], writes=[zb], out=zb[:, :, :n], in_=zf[:, :, :n])
        for m in range(4):
            for k in range(4):
                P.mm(pg[:, :n], wts["glu"][:, k, m * 128:(m + 1) * 128], zb[:, k, :n], k == 0, k == 3, reads=[wts["glu"], zb], writes=[pg])
            I("scalar", "activation", reads=[pg], writes=[sig], out=sig[:, :n], in_=pg[:, :n], func=AF.Sigmoid)
            I("vector", "tensor_tensor", reads=[zf, sig], writes=[(zz, m)], out=zz[:, m, :n], in0=zf[:, m, :n], in1=sig[:, :n], op=ALU.mult)
        for oc in range(8):
            pa, pb, pc = pabc[oc % 2]
            gt = gts[oc % 2]
            for (pp, wn, src) in ((pa, "ba", nga), (pb, "bb", ngb), (pc, "bc", zz)):
                for k in range(4):
                    P.mm(pp[:, :n], wts[wn][:, k, oc * 128:(oc + 1) * 128], src[:, k, :n], k == 0, k == 3, reads=[wts[wn], src], writes=[pp])
            mgv = AP(mg, oc * 512, [[24 * 512, 128], [8 * 512, 3], [1, n]])
            I("scalar", "activation", reads=[mg], writes=[gt], out=gt[:, :, :n], in_=mgv, func=AF.Sigmoid)
            I("vector", "tensor_tensor", reads=[pa, gt], writes=[m1], out=m1[:, :n], in0=pa[:, :n], in1=gt[:, 0, :n], op=ALU.mult)
            I("vector", "tensor_tensor", reads=[pb, gt], writes=[m2], out=m2[:, :n], in0=pb[:, :n], in1=gt[:, 1, :n], op=ALU.mult)
            I("vector", "tensor_tensor", reads=[pc, gt], writes=[m3], out=m3[:, :n], in0=pc[:, :n], in1=gt[:, 2, :n], op=ALU.mult)
            I("gpsimd", "tensor_tensor", reads=[m1, m2], writes=[m1], out=m1[:, :n], in0=m1[:, :n], in1=m2[:, :n], op=ALU.add)
            I("gpsimd", "tensor_tensor", reads=[m1, m3], writes=[(mrg, oc)], out=mrg[:, oc, :n], in0=m1[:, :n], in1=m3[:, :n], op=ALU.add)
        for oc in range(8):
            for k in range(8):
                P.mm(po[:, :n], wts["wo"][:, k, oc * 128:(oc + 1) * 128], mrg[:, k, :n], k == 0, k == 7, reads=[wts["wo"], mrg], writes=[po])
            I("vector", "scalar_tensor_tensor", reads=[po, xT, C.mod[l]], writes=[(xn, oc)], out=xn[:, oc, :n], in0=po[:, :n],
              scalar=C.mod[l][:, 16 + oc, w:w + 1], in1=xT[:, oc, :n], op0=ALU.mult, op1=ALU.add)
        P.D("sync", xo[:, :, ts_], xn[:, :, :n], reads=[xn], writes=[(xdst, n0)])
    P.emit()


FTILES = [(0, 256, 1)] + [(256 + 1024 * i, 1024, 0) for i in range(4)]


def phase_ffn(nc, C, l, xsrc, xdst, last):
    for (t0, tn, w) in FTILES:
        if last and w == 1:
            continue
        with ExitStack() as ost:
            _UID[0] += 1
            hT = Buf(ost.enter_context(nc.sbuf_tensor("hT2_%d" % _UID[0], [128, 8, 1024], BF16)), "hT2")
            P = Prog(nc)
            sub = [(t0 + i, min(512, tn - i), w) for i in range(0, tn, 512)]
            compute_hT(P, C, xsrc, hT, C.g2[l], C.mod[l], 24, tiles=sub, hoff=t0)
            P.emit()
            ffn_tile(nc, C, l, xsrc, xdst, last, hT, t0, tn, w)


def ffn_tile(nc, C, l, xsrc, xdst, last, hT, t0, tn, w):
    P = Prog(nc)
    I = P.I
    sb = P.sb
    mid = sb("mid", [128, 32, 1024], BF16)
    w1 = [sb("w1_%d" % i, [128, 8, 512], BF16) for i in range(2)]
    w2 = [sb("w2_%d" % i, [128, 32, 128], BF16) for i in range(2)]
    rl = [sb("rl%d" % i, [128, 512]) for i in range(2)]
    xt = [sb("xtf%d" % i, [128, 1024]) for i in range(2)]
    pss = [P.ps("pf%d" % i, [128, 512]) for i in range(4)]
    w1v = C.din["w_ff1"][l].rearrange("(k p) c -> p k c", p=128)
    w2v = C.din["w_ff2"][l].rearrange("(k p) c -> p k c", p=128)
    xv = xsrc.t.ap().rearrange("(k p) t -> p k t", p=128)
    subs = [(i, min(512, tn - i)) for i in range(0, tn, 512)]
    pi = 0
    for g in range(8):
        wb = w1[g % 2]
        P.D("gpsimd", wb[:], w1v[:, :, g * 512:(g + 1) * 512], writes=[wb])
        for m in range(4):
            mc = g * 4 + m
            for (s0, sn) in subs:
                pp = pss[pi % 4]; r_ = rl[pi % 2]; pi += 1
                for k in range(8):
                    P.mm(pp[:, :sn], wb[:, k, m * 128:(m + 1) * 128], hT[:, k, s0:s0 + sn], k == 0, k == 7, reads=[wb, hT], writes=[pp])
                I("scalar", "activation", reads=[pp], writes=[r_], out=r_[:, :sn], in_=pp[:, :sn], func=AF.Relu)
                I("gpsimd" if pi % 2 else "vector", "tensor_tensor", reads=[r_], writes=[(mid, (mc, s0))], out=mid[:, mc, s0:s0 + sn], in0=r_[:, :sn], in1=r_[:, :sn], op=ALU.mult)
    if last:
        xn = sb("xnf", [128, 8, 1024])
    else:
        xns = [sb("xns%d" % i, [128, 1024]) for i in range(2)]
    for oc in range(8):
        wb = w2[oc % 2]
        P.D("gpsimd", wb[:], w2v[:, :, oc * 128:(oc + 1) * 128], writes=[wb])
        x_ = xt[oc % 2]
        P.D("sync", x_[:, :tn], xv[:, oc, t0:t0 + tn], reads=[(xsrc, t0)], writes=[x_])
        for (s0, sn) in subs:
            pp = pss[pi % 4]; pi += 1
            for k in range(32):
                P.mm(pp[:, :sn], wb[:, k, :], mid[:, k, s0:s0 + sn], k == 0, k == 31, reads=[wb, mid], writes=[pp])
            if last:
                I("vector", "scalar_tensor_tensor", reads=[pp, x_, C.mod[l]], writes=[(xn, (oc, s0))], out=xn[:, oc, s0:s0 + sn], in0=pp[:, :sn],
                  scalar=C.mod[l][:, 40 + oc, w:w + 1], in1=x_[:, s0:s0 + sn], op0=ALU.mult, op1=ALU.add)
            else:
                xo_ = xns[oc % 2]
                I("vector", "scalar_tensor_tensor", reads=[pp, x_, C.mod[l]], writes=[(xo_, s0)], out=xo_[:, s0:s0 + sn], in0=pp[:, :sn],
                  scalar=C.mod[l][:, 40 + oc, w:w + 1], in1=x_[:, s0:s0 + sn], op0=ALU.mult, op1=ALU.add)
        if not last:
            P.D("sync", xdst[oc * 128:(oc + 1) * 128, t0:t0 + tn], xns[oc % 2][:, :tn], reads=[xns[oc % 2]], writes=[(xdst, (oc, t0))])
    if last:
        fw = sb("fw", [128, 8])
        P.D("sync", fw[:], C.din["final_norm_w"][:], writes=[fw])
        sq = sb("sqf", [128, 512])
        rs = sb("rsf", [128, 512])
        ov = C.out.t.ap().rearrange("(k p) t -> p k t", p=128)
        for (s0, sn) in subs:
            pp = pss[pi % 4]; pi += 1
            for k in range(8):
                I("gpsimd" if k % 2 else "vector", "tensor_tensor", reads=[xn], writes=[sq], out=sq[:, :sn], in0=xn[:, k, s0:s0 + sn], in1=xn[:, k, s0:s0 + sn], op=ALU.mult)
                P.mm(pp[:, :sn], C.ones[:], sq[:, :sn], k == 0, k == 7, reads=[sq, C.ones], writes=[pp])
            I("scalar", "activation", reads=[pp], writes=[rs], out=rs[:, :sn], in_=pp[:, :sn], func=AF.Sqrt, scale=1.0 / D, bias=EPS)
            I("vector", "reciprocal", reads=[rs], writes=[rs], out=rs[:, :sn], in_=rs[:, :sn])
            for k in range(8):
                I("vector", "scalar_tensor_tensor", reads=[xn, fw, rs], writes=[xn], out=xn[:, k, s0:s0 + sn], in0=xn[:, k, s0:s0 + sn], scalar=fw[:, k:k + 1],
                  in1=rs[:, :sn], op0=ALU.mult, op1=ALU.mult)
        P.D("sync", ov[:, :, t0 - 256:t0 - 256 + tn], xn[:, :, :tn], reads=[xn], writes=[(C.out, t0)])
    P.emit()
```

```python
import numpy as np
from contextlib import ExitStack
import concourse.bass as bass
import concourse.mybir as mybir
from concourse.bass_utils import run_bass_kernel_spmd

F32 = mybir.dt.float32
BF16 = mybir.dt.bfloat16
I32 = mybir.dt.int32
U8 = mybir.dt.uint8
AF = mybir.ActivationFunctionType
ALU = mybir.AluOpType
AX = mybir.AxisListType

ENGS = ("tensor", "vector", "scalar", "gpsimd", "sync")

T = 4352
NCTX = 256
NCH = 68
D = 1024
DIN = 8208
EPS = 1e-6


class Buf:
    def __init__(self, t, name):
        self.t = t
        self.name = name
        self.tr = {}

    def __getitem__(self, idx):
        return self.t[idx]


def _norm(x):
    if isinstance(x, Buf):
        return (x, "*")
    return x


_UID = [0]


_SHARED = {}


def _shared(nc, n_dma_sems=16):
    k = id(nc)
    if k not in _SHARED:
        st = ExitStack()
        sh = dict(stack=st, esem={}, ecount={e: 0 for e in ENGS}, dsems={}, dval={}, dnext={}, waited={e: {} for e in ENGS})
        for e in ENGS:
            sh["esem"][e] = st.enter_context(nc.semaphore("es_" + e))
        for q in ("sync", "gpsimd"):
            sh["dsems"][q] = [st.enter_context(nc.semaphore("ds_%s%d" % (q, i))) for i in range(n_dma_sems)]
            sh["dval"][q] = [0] * n_dma_sems
            sh["dnext"][q] = 0
        _SHARED[k] = sh
    return _SHARED[k]


class Prog:
    def __init__(self, nc, n_dma_sems=16):
        self.nc = nc
        self.stack = ExitStack()
        sh = _shared(nc, n_dma_sems)
        self.sh = sh
        self.ops = {e: [] for e in ENGS}
        self.esem = sh["esem"]
        self.ecount = sh["ecount"]
        self.dsems = sh["dsems"]
        self.dval = sh["dval"]
        self.dnext = sh["dnext"]
        self.waited = sh["waited"]
        self.nops = 0
        self._nm = 0

    def sb(self, name, shape, dt=F32):
        _UID[0] += 1
        return Buf(self.stack.enter_context(self.nc.sbuf_tensor("%s_%d" % (name, _UID[0]), list(shape), dt)), name)

    def ps(self, name, shape, dt=F32):
        _UID[0] += 1
        b = Buf(self.stack.enter_context(self.nc.psum_tensor("%s_%d" % (name, _UID[0]), list(shape), dt)), name)
        b.excl = True
        return b

    def _deps(self, reads, writes):
        deps = {}

        def add(tok):
            if tok is None:
                return
            s, v = tok
            k = id(s)
            if k not in deps or deps[k][1] < v:
                deps[k] = (s, v)

        for b, key in map(_norm, reads):
            keys = list(b.tr.keys()) if key == "*" else [key, "*"]
            for k in keys:
                tr = b.tr.get(k)
                if tr:
                    add(tr[0])
        for b, key in map(_norm, writes):
            keys = list(b.tr.keys()) if key == "*" else [key, "*"]
            for k in keys:
                tr = b.tr.get(k)
                if tr:
                    add(tr[0])
                    for tok in tr[1].values():
                        add(tok)
        return deps

    def _update(self, reads, writes, tok):
        s, v = tok
        for b, key in map(_norm, reads):
            tr = b.tr.setdefault(key, [None, {}])
            tr[1][id(s)] = tok
        for b, key in map(_norm, writes):
            if key == "*":
                b.tr = {"*": [tok, {}]}
            else:
                b.tr[key] = [tok, {}]

    def _waits(self, eng, deps, skip_own=False):
        w = []
        wd = self.waited[eng]
        for k, (s, v) in deps.items():
            if skip_own and s is self.esem[eng]:
                continue
            if wd.get(k, 0) >= v:
                continue
            wd[k] = v
            w.append((s, v))
        return w

    @staticmethod
    def _excl(reads, writes):
        r2, w2 = [], []
        for x in reads:
            b = x if isinstance(x, Buf) else x[0]
            (w2 if getattr(b, "excl", False) else r2).append(b if getattr(b, "excl", False) else x)
        for x in writes:
            b = x if isinstance(x, Buf) else x[0]
            w2.append(b if getattr(b, "excl", False) else x)
        return r2, w2

    def op(self, eng, fn, reads=(), writes=()):
        reads, writes = self._excl(reads, writes)
        deps = self._deps(reads, writes)
        waits = self._waits(eng, deps, skip_own=(eng == "tensor"))
        self.ecount[eng] += 1
        tok = (self.esem[eng], self.ecount[eng])
        self.ops[eng].append((waits, [fn], tok[0], 1))
        self._update(reads, writes, tok)
        self.nops += 1
        return tok

    def I(self, eng, method, reads=(), writes=(), **kw):
        return self.op(eng, lambda e: getattr(e, method)(**kw), reads, writes)

    def mm(self, out, lhsT, rhs, start, stop, reads, writes):
        return self.op("tensor", lambda e: e.matmul(out, lhsT=lhsT, rhs=rhs, start=start, stop=stop), reads, writes)

    def tr(self, out, in_, ident, reads, writes):
        return self.op("tensor", lambda e: e.transpose(out=out, in_=in_, identity=ident), reads, writes)

    def dma(self, q, fns, reads=(), writes=()):
        if not isinstance(fns, (list, tuple)):
            fns = [fns]
        deps = self._deps(reads, writes)
        i = self.dnext[q]
        self.dnext[q] = (i + 1) % len(self.dsems[q])
        s = self.dsems[q][i]
        prev = self.dval[q][i]
        if prev > 0:
            k = id(s)
            if k not in deps or deps[k][1] < prev:
                deps[k] = (s, prev)
        waits = self._waits(q, deps)
        val = prev + 16 * len(fns)
        self.dval[q][i] = val
        tok = (s, val)
        self.ops[q].append((waits, list(fns), s, 16))
        self._update(reads, writes, tok)
        self.nops += 1
        return tok

    def D(self, q, out, in_, reads=(), writes=(), **kw):
        return self.dma(q, lambda e: e.dma_start(out=out, in_=in_, **kw), reads, writes)

    def emit(self):
        nc = self.nc
        for q in self.dsems:
            fin = []
            for s, v in zip(self.dsems[q], self.dval[q]):
                if v > 0 and self.waited[q].get(id(s), 0) < v:
                    fin.append((s, v))
            if fin:
                self.ops[q].append((fin, [], None, 0))
        ops = self.ops
        with nc.Block() as block:
            def mk(ename):
                def body(e):
                    for waits, fns, sem, inc in ops[ename]:
                        for s, v in waits:
                            e.wait_ge(s, v)
                        for fn in fns:
                            fn(e).then_inc(sem, inc)
                return body
            for ename in ENGS:
                if ops[ename]:
                    getattr(block, ename)(mk(ename))
        self.stack.close()


def AP(t, offset, dims):
    tt = t.t if isinstance(t, Buf) else t
    return bass.AP(tt, offset, [list(d) for d in dims])


TTILES = [(0, 256, 1)] + [(256 + 512 * i, 512, 0) for i in range(8)]

FM_ROUTES = [
    (0, 512, "AQ", 0), (1024, 1024, "AF", 0), (2560, 1536, "BQKV", 0), (4624, 512, "CU", 0), (5136, 3072, "MG", 0),
]
TM_ROUTES = [
    (512, 512, "AI"), (2048, 512, "AG"), (4096, 512, "BG"), (4608, 16, "BBA"),
]
SCR = {
    "AQ": ([512, T], BF16), "AF": ([1024, T], F32), "BQKV": ([1536, T], BF16), "CU": ([512, T], BF16),
    "MG": ([3072, T], BF16), "AI": ([T, 512], BF16), "AG": ([T, 512], BF16), "BG": ([T, 512], BF16),
    "BBA": ([T, 16], F32),
    "XA": ([D, T], F32), "XB": ([D, T], F32),
    "NGA": ([512, T], BF16), "NGB": ([512, T], BF16), "YC": ([512, T], BF16),
    "OF": ([T, 512], F32),
    "OBW": ([T, 512], F32), "S5A": ([512, 256], BF16), "S5B": ([512, 256], BF16),
}


class Ctx:
    pass


def tkey(tok):
    return 0 if tok < 256 else 256 + ((tok - 256) // 512) * 512


def dump_sb(nc, C, buf, name, shape):
    P = Prog(nc)
    d = nc.dram_tensor(name, list(shape), F32, kind="ExternalOutput")
    P.D("sync", d.ap(), buf[:], reads=[buf])
    P.emit()


def make_consts(nc, C):
    P = Prog(nc)
    st = C.gstack
    def gsb(name, shape, dt=F32):
        return Buf(st.enter_context(nc.sbuf_tensor(name, list(shape), dt)), name)
    C.ident = gsb("ident", [128, 128], F32)
    C.identb = gsb("identb", [128, 128], BF16)
    C.ones = gsb("ones", [128, 128], F32)
    C.onesb = gsb("onesb", [128, 128], BF16)
    P.I("gpsimd", "memset", writes=[C.ident], ap=C.ident[:], constant=0.0)
    P.I("gpsimd", "affine_select", reads=[C.ident], writes=[C.ident], out=C.ident[:], in_=C.ident[:],
        pattern=[[-1, 128]], compare_op=ALU.not_equal, fill=1.0, base=0, channel_multiplier=1)
    P.I("vector", "tensor_copy", reads=[C.ident], writes=[C.identb], out=C.identb[:], in_=C.ident[:])
    P.I("vector", "memset", writes=[C.ones], ap=C.ones[:], constant=1.0)
    P.I("vector", "memset", writes=[C.onesb], ap=C.onesb[:], constant=1.0)
    make_masks(P, C)
    make_s5_masks(P, C)
    C.mod = [gsb("mod%d" % l, [128, 48, 2], F32) for l in range(2)]
    C.g1 = [gsb("g1_%d" % l, [128, 8, 2], F32) for l in range(2)]
    C.g2 = [gsb("g2_%d" % l, [128, 8, 2], F32) for l in range(2)]
    P.emit()


def phase_adaln(nc, C, l):
    P = Prog(nc)
    cv = P.sb("cv", [128, 8, 2])
    sc = P.sb("sc", [128, 8, 2])
    ab = P.sb("ab", [128, 48])
    nw = P.sb("nw", [128, 16])
    pm = P.ps("pm", [128, 48, 2])
    P.D("sync", cv[:], C.din["cvec"][:], reads=[], writes=[cv])
    P.D("sync", ab[:], C.din["ada_b"][l], writes=[ab])
    P.D("sync", nw[:, 0:8], C.din["norm1_w"][l], writes=[(nw, 0)])
    P.D("sync", nw[:, 8:16], C.din["norm2_w"][l], writes=[(nw, 1)])
    P.I("scalar", "activation", reads=[cv], writes=[sc], out=sc[:], in_=cv[:], func=AF.Silu)
    wbufs = [P.sb("adw%d" % i, [128, 8, 512]) for i in range(2)]
    aw = C.din["ada_w"][l].rearrange("(k p) c -> p k c", p=128)
    for og in range(12):
        wb = wbufs[og % 2]
        P.dma("sync", [lambda e, k=k, wb=wb, og=og: e.dma_start(out=wb[:, k, :], in_=aw[:, k, og * 512:(og + 1) * 512]) for k in range(8)],
              writes=[wb])
        for m in range(4):
            j = og * 4 + m
            for k in range(8):
                P.mm(pm[:, j, :], wb[:, k, m * 128:(m + 1) * 128], sc[:, k, :], k == 0, k == 7, reads=[wb, sc], writes=[(pm, j)])
    mod = C.mod[l]
    abb = AP(ab, 0, [[48, 128], [1, 48], [0, 2]])
    P.I("vector", "tensor_tensor", reads=[pm, ab], writes=[mod], out=mod[:], in0=pm[:], in1=abb, op=ALU.add)
    for (g, soff, noff) in ((C.g1[l], 8, 0), (C.g2[l], 32, 8)):
        nwb = AP(nw, noff, [[16, 128], [1, 8], [0, 2]])
        P.I("vector", "scalar_tensor_tensor", reads=[mod, nw], writes=[g], out=g[:], in0=mod[:, soff:soff + 8, :], scalar=1.0,
            in1=nwb, op0=ALU.add, op1=ALU.mult)
    P.emit()


def phase_proj(nc, C, l, xsrc):
    P = Prog(nc)
    hT = P.sb("hT", [128, 8, T], BF16)
    compute_hT(P, C, xsrc, hT, C.g1[l], C.mod[l], 0)
    win = C.din["w_in"][l].rearrange("(k p) c -> p k c", p=128)
    wbufs = [P.sb("wb%d" % i, [128, 8, 512], BF16) for i in range(2)]
    pss = [P.ps("pp%d" % i, [128, 512]) for i in range(4)]
    stF = [P.sb("stF%d" % i, [128, T], F32) for i in range(2)]
    wi = 0
    pi = 0
    si = 0
    ei = 0
    for (c0, ncols, sname, row0) in FM_ROUTES:
        dst = C.scr[sname]
        dt = SCR[sname][1]
        for g0 in range(0, ncols, 512):
            wb = wbufs[wi % 2]
            wi += 1
            P.D("gpsimd", wb[:], win[:, :, c0 + g0:c0 + g0 + 512], writes=[wb])
            for m in range(4):
                stb = stF[si % 2]
                si += 1
                stv = stb[:] if dt == F32 else stb[:].bitcast(BF16)[:, 0:T]
                for (n0, nsz, w) in TTILES:
                    pp = pss[pi % 4]
                    pi += 1
                    for k in range(8):
                        P.mm(pp[:, :nsz], wb[:, k, m * 128:(m + 1) * 128], hT[:, k, n0:n0 + nsz], k == 0, k == 7,
                             reads=[wb, (hT, n0)], writes=[pp])
                    if ei % 2 == 0:
                        P.I("scalar", "activation", reads=[pp], writes=[(stb, n0)], out=stv[:, n0:n0 + nsz], in_=pp[:, :nsz], func=AF.Copy)
                    else:
                        P.I("vector", "tensor_copy", reads=[pp], writes=[(stb, n0)], out=stv[:, n0:n0 + nsz], in_=pp[:, :nsz])
                    ei += 1
                r0 = row0 + g0 + m * 128
                P.D("sync", dst[r0:r0 + 128, :], stv, reads=[stb], writes=[(dst, r0)])
    stT = [P.sb("stT%d" % i, [128, 512], F32) for i in range(3)]
    for (c0, ncols, sname) in TM_ROUTES:
        dst = C.scr[sname]
        dt = SCR[sname][1]
        wb = wbufs[wi % 2]
        wi += 1
        P.D("gpsimd", wb[:, :, :ncols], win[:, :, c0:c0 + ncols], writes=[wb])
        for tb in range(T // 128):
            pp = pss[pi % 4]
            pi += 1
            for k in range(8):
                P.mm(pp[:, :ncols], hT[:, k, tb * 128:(tb + 1) * 128], wb[:, k, :ncols], k == 0, k == 7,
                     reads=[wb, (hT, tkey(tb * 128))], writes=[pp])
            stb = stT[si % 3]
            si += 1
            stv = stb[:] if dt == F32 else stb[:].bitcast(BF16)[:, 0:512]
            if ei % 2 == 0:
                P.I("scalar", "activation", reads=[pp], writes=[stb], out=stv[:, :ncols], in_=pp[:, :ncols], func=AF.Copy)
            else:
                P.I("vector", "tensor_copy", reads=[pp], writes=[stb], out=stv[:, :ncols], in_=pp[:, :ncols])
            ei += 1
            P.D("sync", dst[tb * 128:(tb + 1) * 128, :], stv[:, :ncols], reads=[stb], writes=[(dst, tb)])
    P.emit()


def compute_hT(P, C, xsrc, hT, g, mod, shoff, tiles=None, hoff=0):
    nc = P.nc
    xv = xsrc.t.ap().rearrange("(k p) t -> p k t", p=128)
    xts = [P.sb("xt%d" % i, [128, 8, 512]) for i in range(2)]
    sq = P.sb("sq", [128, 8, 512])
    rs = P.sb("rs", [128, 512])
    pss = P.ps("pss", [128, 512])
    for ti, (n0, nsz, w) in enumerate(tiles or TTILES):
        xt = xts[ti % 2]
        P.D("sync", xt[:, :, :nsz], xv[:, :, n0:n0 + nsz], reads=[(xsrc, n0)], writes=[xt])
        P.I("scalar", "activation", reads=[xt], writes=[sq], out=sq[:, :, :nsz], in_=xt[:, :, :nsz], func=AF.Square)
        for k in range(8):
            P.mm(pss[:, :nsz], C.ones[:], sq[:, k, :nsz], k == 0, k == 7, reads=[sq, C.ones], writes=[pss])
        P.I("scalar", "activation", reads=[pss], writes=[rs], out=rs[:, :nsz], in_=pss[:, :nsz], func=AF.Sqrt,
            scale=1.0 / D, bias=EPS)
        P.I("vector", "reciprocal", reads=[rs], writes=[rs], out=rs[:, :nsz], in_=rs[:, :nsz])
        rsb = AP(rs, 0, [[512, 128], [0, 8], [1, nsz]])
        P.I("vector", "tensor_tensor", reads=[xt, rs], writes=[sq], out=sq[:, :, :nsz], in0=xt[:, :, :nsz], in1=rsb, op=ALU.mult)
        for k in range(8):
            eng = "gpsimd" if k % 2 == 0 else "vector"
            P.I(eng, "tensor_scalar", reads=[sq, g, mod], writes=[(hT, n0)], out=hT[:, k, n0 - hoff:n0 - hoff + nsz], in0=sq[:, k, :nsz],
                scalar1=g[:, k, w:w + 1], scalar2=mod[:, shoff + k, w:w + 1], op0=ALU.mult, op1=ALU.add)


def build(nlayers=2, stop=None, debug=(), skip=(), heads=range(4), s5stop=None):
    nc = bass.Bass("TRN2", target_bir_lowering=False)
    C = Ctx()
    C.gstack = ExitStack()
    C.din = {}
    C.heads = heads
    C.s5stop = s5stop
    C.skip = skip

    def din(name, shape, dt=F32):
        C.din[name] = nc.dram_tensor(name, list(shape), dt, kind="ExternalInput")

    din("xT", [D, T]); din("cvec", [128, 8, 2]); din("ada_w", [2, D, 6 * D]); din("ada_b", [2, 128, 48])
    din("norm1_w", [2, 128, 8]); din("norm2_w", [2, 128, 8]); din("w_in", [2, D, DIN])
    din("hgrn_lb", [2, 128, 8]); din("hgrn_norm_w", [2, 128])
    din("s5_lam_re", [2, 64, 2, 32]); din("s5_lam_im", [2, 64, 2, 32]); din("s5_log_dt", [2, 2, 32])
    din("s5_b_re", [2, 64, 32, 16]); din("s5_b_im", [2, 64, 32, 16]); din("s5_c_re", [2, 64, 32, 16]); din("s5_c_im", [2, 64, 32, 16]); din("s5_dtab", [2, 128, 32])
    for nm_, shp_ in (("w_branch_a", [2, 512, D]), ("w_branch_b", [2, 512, D]), ("w_branch_c", [2, 512, D]), ("s5_w_glu", [2, 512, 512]), ("w_out", [2, D, D]), ("w_ff1", [2, D, 4 * D]), ("w_ff2", [2, 4 * D, D]), ("final_norm_w", [128, 8])):
        din(nm_, shp_)
    din("gdn_conv_w", [2, 128, 12, 5]); din("gdn_a_log", [2, 8]); din("gdn_dt_bias", [2, 8]); din("gdn_norm_w", [2, 128])
    C.debug = debug
    C.dbg = {}
    if "s5tab" in debug:
        for nm_, shp_ in (("Er", [64, 32, NPOW]), ("Ei", [64, 32, NPOW]), ("bbr", [64, 32, 16]), ("bbi", [64, 32, 16])):
            C.dbg[nm_] = nc.dram_tensor("dbg_" + nm_, shp_, F32, kind="ExternalOutput")
    if "ob" in debug:
        C.dbg["ob"] = nc.dram_tensor("dbg_ob", [T, 512], F32, kind="ExternalOutput")
    if "oa" in debug:
        C.dbg["oa"] = nc.dram_tensor("dbg_oa", [4, T, 128], F32, kind="ExternalOutput")
    C.scr = {}
    for name, (shape, dt) in SCR.items():
        kind = "ExternalOutput" if name in debug else "Internal"
        C.scr[name] = Buf(nc.dram_tensor(name, shape, dt, kind=kind), name)
    C.out = Buf(nc.dram_tensor("outT", [D, 4096], F32, kind="ExternalOutput"), "outT")
    C.xin = Buf(C.din["xT"], "xT")
    with C.gstack:
        make_consts(nc, C)
        for l in range(nlayers):
            phase_adaln(nc, C, l)
            if 'mod' in debug:
                dump_sb(nc, C, C.mod[l], 'dbg_mod%d' % l, [128, 48, 2])
            if stop == "adaln":
                break
            xsrc = C.xin if l == 0 else C.scr["XB"]
            if 'proj' not in skip:
                phase_proj(nc, C, l, xsrc)
            if stop == "proj":
                break
            if 'mixA' not in skip:
                phase_mixA(nc, C, l, heads=C.heads)
            if stop == "mixA":
                break
            if 'mixB' not in skip:
                phase_mixB(nc, C, l)
            if stop == "mixB":
                break
            if 's5' not in skip:
                phase_s5(nc, C, l)
            if stop == "s5":
                break
            last = (l == nlayers - 1) and nlayers == 2
            if 'merge' not in skip:
                phase_merge(nc, C, l, xsrc, C.scr["XA"], tiles=(TTILES[1:] if last else None))
            if stop == "merge":
                break
            if 'ffn' not in skip:
                phase_ffn(nc, C, l, C.scr["XA"], C.scr["XB"], last)
            if stop == "ffn":
                break
    _SHARED[id(nc)]["stack"].close()
    return nc


def host_inputs(inp, b):
    f = np.float32
    m = {}
    xcat = np.concatenate([inp["ctx"][b], inp["x"][b]], axis=0)
    m["xT"] = np.ascontiguousarray(xcat.T)
    cv = np.stack([inp["c"][b].reshape(8, 128).T, inp["c_ctx"].reshape(8, 128).T], axis=-1)
    m["cvec"] = np.ascontiguousarray(cv.astype(f))
    m["ada_w"] = inp["ada_w"]
    m["ada_b"] = np.ascontiguousarray(inp["ada_b"].reshape(2, 48, 128).transpose(0, 2, 1))
    m["norm1_w"] = np.ascontiguousarray(inp["norm1_w"].reshape(2, 8, 128).transpose(0, 2, 1))
    m["norm2_w"] = np.ascontiguousarray(inp["norm2_w"].reshape(2, 8, 128).transpose(0, 2, 1))
    m["w_in"] = inp["w_in"]
    m["hgrn_lb"] = np.ascontiguousarray(inp["hgrn_lb_logits"].reshape(2, 8, 128).transpose(0, 2, 1))
    m["hgrn_norm_w"] = inp["hgrn_norm_w"]
    m["gdn_conv_w"] = np.ascontiguousarray(inp["gdn_conv_w"].reshape(2, 5, 12, 128).transpose(0, 3, 2, 1))
    m["gdn_a_log"] = np.ascontiguousarray(inp["gdn_a_log"].reshape(2, 8))
    m["gdn_dt_bias"] = np.ascontiguousarray(inp["gdn_dt_bias"].reshape(2, 8))
    m["gdn_norm_w"] = inp["gdn_norm_w"]
    for nm_ in ("w_branch_a", "w_branch_b", "w_branch_c", "s5_w_glu", "w_out", "w_ff1", "w_ff2"):
        m[nm_] = inp[nm_]
    m["final_norm_w"] = np.ascontiguousarray(inp["final_norm_w"].reshape(8, 128).T)
    m["s5_lam_re"] = np.ascontiguousarray(inp["s5_lam_re"].transpose(0, 3, 1, 2))
    m["s5_lam_im"] = np.ascontiguousarray(inp["s5_lam_im"].transpose(0, 3, 1, 2))
    m["s5_log_dt"] = inp["s5_log_dt"]
    m["s5_b_re"] = np.ascontiguousarray(inp["s5_b_re"].transpose(0, 2, 1, 3))
    m["s5_b_im"] = np.ascontiguousarray(inp["s5_b_im"].transpose(0, 2, 1, 3))
    m["s5_c_re"] = np.ascontiguousarray(inp["s5_c_re"].transpose(0, 3, 1, 2))
    m["s5_c_im"] = np.ascontiguousarray(inp["s5_c_im"].transpose(0, 3, 1, 2))
    dt_ = inp["s5_d"].reshape(2, 32, 16).transpose(0, 2, 1)
    m["s5_dtab"] = np.ascontiguousarray(np.tile(dt_, (1, 8, 1)))
    return m


_NC = None


def kernel(**inputs):
    global _NC
    inp = {k: np.asarray(v) for k, v in inputs.items()}
    if _NC is None:
        _NC = build()
    in_maps = [host_inputs(inp, b) for b in range(8)]
    res = run_bass_kernel_spmd(_NC, in_maps, core_ids=list(range(8)))
    out = np.stack([np.ascontiguousarray(res.results[b]["outT"].T) for b in range(8)], axis=0)
    return out.astype(np.float32)


CH_FWD = list(range(68))
CH_BWD = [3, 2, 1, 0] + list(range(67, 3, -1))


def make_masks(P, C):
    nc = P.nc
    st = C.gstack
    def gsb(name, shape, dt=F32):
        return Buf(st.enter_context(nc.sbuf_tensor(name, list(shape), dt)), name)
    one8 = gsb("one8", [64, 8, 64])
    P.I("vector", "memset", writes=[one8], ap=one8[:], constant=1.0)
    C.maskf = {}
    C.maski = {}
    for nm, cm, st_, op in (("U", -1, 1, ALU.is_ge), ("L", 1, -1, ALU.is_ge), ("Us", -1, 1, ALU.is_gt), ("Ls", 1, -1, ALU.is_gt)):
        mf = gsb("maskf" + nm, [64, 8, 64])
        P.I("gpsimd", "affine_select", reads=[one8], writes=[mf], out=mf[:], in_=one8[:], pattern=[[0, 8], [st_, 64]],
            compare_op=op, fill=0.0, base=0, channel_multiplier=cm)
        mi = gsb("maski" + nm, [64, 8, 64], I32)
        P.I("vector", "tensor_copy", reads=[mf], writes=[mi], out=mi[:], in_=mf[:])
        C.maskf[nm] = mf
        C.maski[nm] = mi
    C.rmask = gsb("rmask", [128, 512])
    P.I("vector", "memset", writes=[C.rmask], ap=C.rmask[:], constant=1.0)
    P.I("vector", "memset", reads=[C.rmask], writes=[C.rmask], ap=AP(C.rmask, 0, [[512, 128], [64, 8]]), constant=0.0)


def phase_mixA(nc, C, l, heads=range(4)):
    heads = list(heads)
    P = Prog(nc)
    lbl = P.sb("lbl", [128, 2, 8])
    lb = P.sb("lb", [128, 8])
    oml = P.sb("oml", [128, 8])
    if l == 0:
        P.I("vector", "memset", writes=[lb], ap=lb[:], constant=0.0)
        P.I("vector", "memset", writes=[oml], ap=oml[:], constant=1.0)
    else:
        P.D("sync", lbl[:], C.din["hgrn_lb"].ap().rearrange("l p e -> p l e"), writes=[lbl])
        P.I("vector", "tensor_tensor", reads=[lbl], writes=[lb], out=lb[:], in0=lbl[:, 1, :], in1=lbl[:, 0, :], op=ALU.subtract)
        P.I("scalar", "activation", reads=[lb], writes=[lb], out=lb[:], in_=lb[:], func=AF.Sigmoid)
        P.I("vector", "tensor_scalar", reads=[lb], writes=[oml], out=oml[:], in0=lb[:], scalar1=-1.0, scalar2=1.0, op0=ALU.mult, op1=ALU.add)
    nwb = P.sb("nwb", [64, 128])
    P.D("sync", nwb[:], C.din["hgrn_norm_w"][l].partition_broadcast(64), writes=[nwb])
    v_tms = [P.sb("v_tm%d" % i, [64, 68, 128], BF16) for i in range(1)]
    o_acc = P.sb("o_acc", [64, 68, 128])
    ngT = P.sb("ngT", [128, T], BF16)
    sets = []
    for i in range(2):
        sets.append(dict(attT=P.sb("attT%d" % i, [64, 68, 64], BF16), kd_tm=P.sb("kd_tm%d" % i, [64, 68, 128], BF16),
                         qg=P.sb("qg%d" % i, [128, T], BF16), dec=P.sb("dec%d" % i, [128, 68]), last_dir=[None]))
        P.I("gpsimd", "memset", writes=[sets[i]["attT"]], ap=sets[i]["attT"][:], constant=0.0)
    S = P.sb("S", [128, 128])
    Sb = P.sb("Sb", [128, 128], BF16)
    qpre = P.sb("qpre", [128, 512], BF16)
    W = {n: P.sb(n, [128, 512]) for n in ("q32", "F", "G", "Kt", "CUM", "Dd", "E", "E2")}
    qe = P.sb("qe", [128, 512], BF16)
    ke = P.sb("ke", [128, 512], BF16)
    kdT = P.sb("kdT", [128, 512], BF16)
    qeA = P.sb("qeA", [128, 512], BF16)
    keA = P.sb("keA", [128, 512], BF16)
    refA = P.sb("refA", [128, 16])
    sm = {n: P.sb(n, [128, 8]) for n in ("tot", "lastc", "refc")}
    pa = [P.ps("pa%d" % i, [128, 512]) for i in range(2)]
    pt = [P.ps("pt%d" % i, [128, 512]) for i in range(2)]
    po = [P.ps("po%d" % i, [128, 512]) for i in range(2)]
    pd = [P.ps("pd%d" % i, [128, 512]) for i in range(2)]
    for pz in pa:
        P.I("vector", "memset", writes=[pz], ap=pz[:], constant=0.0)
    gate = P.sb("gate", [64, 8, 128], BF16)
    sg = P.sb("sg", [64, 8, 128])
    sq = P.sb("sqo", [64, 8, 128])
    on = P.sb("on", [64, 8, 128])
    onb = P.sb("onb", [64, 8, 128], BF16)
    ssq = P.sb("ssq", [64, 8])
    agv = C.scr["AG"].t.ap().rearrange("(c p) d -> p c d", p=64)
    aiv = C.scr["AI"].t.ap().rearrange("(c p) d -> p c d", p=64)
    cnt = [0]

    def prep(h, dr, st_):
        attT, kd_tm, qg, dec = st_["attT"], st_["kd_tm"], st_["qg"], st_["dec"]
        dh = dr * 4 + h
        if st_["last_dir"][0] is not None and st_["last_dir"][0] != dr:
            P.I("gpsimd", "memset", reads=[attT], writes=[attT], ap=attT[:], constant=0.0)
        st_["last_dir"][0] = dr
        for (n0, n, w) in TTILES:
            nch = n // 64
            c0 = n0 // 64
            q32, Fb, G, Kt, CUM, Dd, E, E2 = (W[k] for k in ("q32", "F", "G", "Kt", "CUM", "Dd", "E", "E2"))
            P.D("sync", qpre[:, :n], C.scr["AQ"][h * 128:(h + 1) * 128, n0:n0 + n], reads=[C.scr["AQ"]], writes=[qpre])
            r0 = dr * 512 + h * 128
            P.D("sync", Fb[:, :n], C.scr["AF"][r0:r0 + 128, n0:n0 + n], reads=[C.scr["AF"]], writes=[Fb])
            P.I("scalar", "activation", reads=[qpre], writes=[q32], out=q32[:, :n], in_=qpre[:, :n], func=AF.Silu)
            P.I("scalar", "activation", reads=[Fb], writes=[Fb], out=Fb[:, :n], in_=Fb[:, :n], func=AF.Sigmoid)
            P.I("vector", "tensor_scalar", reads=[Fb, oml, lb], writes=[Fb], out=Fb[:, :n], in0=Fb[:, :n], scalar1=oml[:, dh:dh + 1],
                scalar2=lb[:, dh:dh + 1], op0=ALU.mult, op1=ALU.add)
            yield
            P.I("scalar", "activation", reads=[Fb], writes=[G], out=G[:, :n], in_=Fb[:, :n], func=AF.Ln)
            P.I("gpsimd", "tensor_scalar", reads=[Fb], writes=[Kt], out=Kt[:, :n], in0=Fb[:, :n], scalar1=-1.0, scalar2=1.0, op0=ALU.mult, op1=ALU.add)
            P.I("vector", "tensor_tensor_scan", reads=[G, C.rmask], writes=[CUM], out=CUM[:, :n], data0=C.rmask[:, :n], data1=G[:, :n],
                initial=0.0, op0=ALU.mult, op1=ALU.add)

            def cview(buf, off):
                return AP(buf, off, [[512, 128], [64, nch]])

            def bview(buf):
                return AP(buf, 0, [[8, 128], [1, nch], [0, 64]])

            def v3(buf):
                return AP(buf, 0, [[512, 128], [64, nch], [1, 64]])
            if dr == 1:
                P.I("vector", "tensor_copy", reads=[CUM], writes=[sm["tot"]], out=sm["tot"][:, :nch], in_=cview(CUM, 63))
                P.I("gpsimd", "tensor_tensor", reads=[G, CUM], writes=[G], out=G[:, :n], in0=G[:, :n], in1=CUM[:, :n], op=ALU.subtract)
                P.I("vector", "tensor_tensor", reads=[G, sm["tot"]], writes=[CUM], out=v3(CUM), in0=v3(G), in1=bview(sm["tot"]), op=ALU.add)
            yield
            P.I("vector", "tensor_copy", reads=[CUM], writes=[sm["lastc"]], out=sm["lastc"][:, :nch], in_=cview(CUM, 63 if dr == 0 else 0))
            P.I("vector", "tensor_copy", reads=[CUM], writes=[sm["refc"]], out=sm["refc"][:, :nch], in_=cview(CUM, 32))
            P.I("scalar", "activation", reads=[sm["lastc"]], writes=[(dec, c0)], out=dec[:, c0:c0 + nch], in_=sm["lastc"][:, :nch], func=AF.Exp)
            P.I("vector", "tensor_tensor", reads=[CUM, sm["refc"]], writes=[Dd], out=v3(Dd), in0=v3(CUM), in1=bview(sm["refc"]), op=ALU.subtract)
            P.I("vector", "tensor_scalar", reads=[Dd], writes=[Dd], out=Dd[:, :n], in0=Dd[:, :n], scalar1=80.0, scalar2=-80.0, op0=ALU.min, op1=ALU.max)
            yield
            P.I("scalar", "activation", reads=[Dd], writes=[E], out=E[:, :n], in_=Dd[:, :n], func=AF.Exp)
            P.I("gpsimd", "tensor_tensor", reads=[q32, E], writes=[qe], out=qe[:, :n], in0=q32[:, :n], in1=E[:, :n], op=ALU.mult)
            P.I("scalar", "activation", reads=[Dd], writes=[E2], out=E2[:, :n], in_=Dd[:, :n], func=AF.Exp, scale=-1.0)
            P.I("vector", "tensor_tensor", reads=[Kt, E2], writes=[ke], out=ke[:, :n], in0=Kt[:, :n], in1=E2[:, :n], op=ALU.mult)
            yield
            nb2 = 2 * nch
            P.I("vector", "tensor_copy", reads=[CUM], writes=[refA], out=refA[:, :nb2], in_=AP(CUM, 16, [[512, 128], [32, nb2]]))
            v3a = lambda buf: AP(buf, 0, [[512, 128], [32, nb2], [1, 32]])
            P.I("vector", "tensor_tensor", reads=[CUM, refA], writes=[Dd], out=v3a(Dd), in0=v3a(CUM), in1=AP(refA, 0, [[16, 128], [1, nb2], [0, 32]]), op=ALU.subtract)
            P.I("vector", "tensor_scalar", reads=[Dd], writes=[Dd], out=Dd[:, :n], in0=Dd[:, :n], scalar1=40.0, scalar2=-40.0, op0=ALU.min, op1=ALU.max)
            P.I("scalar", "activation", reads=[Dd], writes=[E], out=E[:, :n], in_=Dd[:, :n], func=AF.Exp)
            P.I("gpsimd", "tensor_tensor", reads=[q32, E], writes=[qeA], out=qeA[:, :n], in0=q32[:, :n], in1=E[:, :n], op=ALU.mult)
            yield
            P.I("scalar", "activation", reads=[Dd], writes=[E2], out=E2[:, :n], in_=Dd[:, :n], func=AF.Exp, scale=-1.0)
            P.I("vector", "tensor_tensor", reads=[Kt, E2], writes=[keA], out=keA[:, :n], in0=Kt[:, :n], in1=E2[:, :n], op=ALU.mult)
            P.I("scalar", "activation", reads=[CUM], writes=[E], out=E[:, :n], in_=CUM[:, :n], func=AF.Exp)
            P.I("gpsimd", "tensor_tensor", reads=[q32, E], writes=[(qg, n0)], out=qg[:, n0:n0 + n], in0=q32[:, :n], in1=E[:, :n], op=ALU.mult)
            yield
            P.I("vector", "tensor_tensor", reads=[CUM, sm["lastc"]], writes=[Dd], out=v3(Dd), in0=bview(sm["lastc"]), in1=v3(CUM), op=ALU.subtract)
            P.I("scalar", "activation", reads=[Dd], writes=[E2], out=E2[:, :n], in_=Dd[:, :n], func=AF.Exp)
            P.I("gpsimd", "tensor_tensor", reads=[Kt, E2], writes=[kdT], out=kdT[:, :n], in0=Kt[:, :n], in1=E2[:, :n], op=ALU.mult)
            ppa = pa[cnt[0] % 2]
            ppt = pt[cnt[0] % 2]
            cnt[0] += 1
            for j in range(nch):
                b_ = j * 64
                P.mm(ppa[0:32, b_:b_ + 32], keA[:, b_:b_ + 32], qeA[:, b_:b_ + 32], True, True, reads=[keA, qeA], writes=[ppa])
                P.mm(ppa[32:64, b_ + 32:b_ + 64], keA[:, b_ + 32:b_ + 64], qeA[:, b_ + 32:b_ + 64], True, True, reads=[keA, qeA], writes=[ppa])
                if dr == 0:
                    P.mm(ppa[0:32, b_ + 32:b_ + 64], ke[:, b_:b_ + 32], qe[:, b_ + 32:b_ + 64], True, True, reads=[ke, qe], writes=[ppa])
                else:
                    P.mm(ppa[32:64, b_:b_ + 32], ke[:, b_ + 32:b_ + 64], qe[:, b_:b_ + 32], True, True, reads=[ke, qe], writes=[ppa])
            yield
            mk = C.maski["U" if dr == 0 else "L"]
            P.I("vector", "copy_predicated", reads=[ppa, mk, (attT, n0)], writes=[(attT, n0)], out=attT[:, c0:c0 + nch, :],
                mask=mk[:, :nch, :], data=ppa[0:64, 0:nch * 64].rearrange("p (a b) -> p a b", b=64))
            ptv = ppt[0:64, :].bitcast(BF16).rearrange("p (a b) -> p a b", b=128)
            for j in range(nch):
                P.tr(ptv[:, j, :], kdT[:, j * 64:(j + 1) * 64], C.identb[:], reads=[kdT, C.identb], writes=[ppt])
            P.I("scalar", "activation", reads=[ppt], writes=[(kd_tm, n0)], out=kd_tm[:, c0:c0 + nch, :], in_=ptv[:, :nch, :], func=AF.Copy)
            yield

    def chain(h, dr, st_, v_tm):
        attT, kd_tm, qg, dec = st_["attT"], st_["kd_tm"], st_["qg"], st_["dec"]
        if dr == 0:
            P.D("sync", v_tm[:], aiv[:, :, h * 128:(h + 1) * 128], reads=[C.scr["AI"]], writes=[v_tm])
        P.I("vector", "memset", reads=[S], writes=[S], ap=S[:], constant=0.0)
        P.I("gpsimd", "memset", reads=[Sb], writes=[Sb], ap=Sb[:], constant=0.0)
        for ci, c in enumerate(CH_FWD if dr == 0 else CH_BWD):
            key = tkey(c * 64)
            ppo = po[ci % 2]
            ppd = pd[ci % 2]
            P.mm(ppo[0:64, 0:128], attT[:, c, :], v_tm[:, c, :], True, False, reads=[(attT, key), v_tm], writes=[ppo])
            P.mm(ppo[0:64, 0:128], qg[:, c * 64:(c + 1) * 64], Sb[:], False, True, reads=[(qg, key), Sb], writes=[ppo])
            if dr == 0:
                P.I("scalar", "activation", reads=[ppo], writes=[(o_acc, c)], out=o_acc[:, c, :], in_=ppo[0:64, 0:128], func=AF.Copy)
            else:
                P.I("vector", "tensor_tensor", reads=[ppo, (o_acc, c)], writes=[(o_acc, c)], out=o_acc[:, c, :], in0=ppo[0:64, 0:128],
                    in1=o_acc[:, c, :], op=ALU.add)
            P.mm(ppd[:, 0:128], kd_tm[:, c, :], v_tm[:, c, :], True, True, reads=[(kd_tm, key), v_tm], writes=[ppd])
            P.I("vector", "scalar_tensor_tensor", reads=[S, ppd, (dec, tkey(c * 64) // 64)], writes=[S], out=S[:], in0=S[:], scalar=dec[:, c:c + 1],
                in1=ppd[:, 0:128], op0=ALU.mult, op1=ALU.add)
            P.I("gpsimd", "tensor_copy", reads=[S], writes=[Sb], out=Sb[:], in_=S[:])
            yield
        if dr == 0:
            return
        for (n0, n, w) in TTILES:
            nch = n // 64
            c0 = n0 // 64
            ov = o_acc[:, c0:c0 + nch, :]
            okeys = [(o_acc, c) for c in range(c0, c0 + nch)]
            P.D("sync", gate[:, :nch, :], agv[:, c0:c0 + nch, h * 128:(h + 1) * 128], reads=[C.scr["AG"]], writes=[gate])
            P.I("scalar", "activation", reads=[gate], writes=[sg], out=sg[:, :nch, :], in_=gate[:, :nch, :], func=AF.Silu)
            P.I("gpsimd", "tensor_tensor", reads=okeys, writes=[sq], out=sq[:, :nch, :], in0=ov, in1=ov, op=ALU.mult)
            P.I("vector", "tensor_reduce", reads=[sq], writes=[ssq], out=ssq[:, :nch], in_=sq[:, :nch, :], axis=AX.X, op=ALU.add)
            P.I("scalar", "activation", reads=[ssq], writes=[ssq], out=ssq[:, :nch], in_=ssq[:, :nch], func=AF.Sqrt, scale=1.0 / 128, bias=EPS)
            P.I("vector", "reciprocal", reads=[ssq], writes=[ssq], out=ssq[:, :nch], in_=ssq[:, :nch])
            yield
            P.I("vector", "tensor_tensor", reads=okeys + [ssq], writes=[on], out=on[:, :nch, :], in0=ov, in1=AP(ssq, 0, [[8, 64], [1, nch], [0, 128]]), op=ALU.mult)
            P.I("gpsimd", "tensor_tensor", reads=[on, nwb], writes=[on], out=on[:, :nch, :], in0=on[:, :nch, :], in1=AP(nwb, 0, [[128, 64], [0, nch], [1, 128]]), op=ALU.mult)
            P.I("vector", "tensor_tensor", reads=[on, sg], writes=[onb], out=onb[:, :nch, :], in0=on[:, :nch, :], in1=sg[:, :nch, :], op=ALU.mult)
            ppt = pt[cnt[0] % 2]
            cnt[0] += 1
            ptv = ppt[:, 0:256].bitcast(BF16).rearrange("p (a b) -> p a b", b=64)
            for j in range(nch):
                P.tr(ptv[:, j, :], onb[:, j, :], C.identb[0:64, 0:64], reads=[onb, C.identb], writes=[ppt])
            P.I("scalar", "activation", reads=[ppt], writes=[(ngT, n0)], out=ngT[:, n0:n0 + n], in_=ppt[:, 0:256].bitcast(BF16)[:, 0:n], func=AF.Copy)
            yield
        P.D("sync", C.scr["NGA"][h * 128:(h + 1) * 128, :], ngT[:], reads=[ngT], writes=[(C.scr["NGA"], h)])
        if "oa" in C.debug:
            P.D("sync", C.dbg["oa"].ap().rearrange("h (c p) d -> h p c d", p=64)[h], o_acc[:], reads=[o_acc])
        yield

    units = [(h, dr) for h in heads for dr in (0, 1)]

    def drive(gens):
        alive = list(gens)
        while alive:
            for g_ in list(alive):
                try:
                    next(g_)
                except StopIteration:
                    alive.remove(g_)

    drive([prep(units[0][0], units[0][1], sets[0])])
    for i, (h, dr) in enumerate(units):
        gens = [chain(h, dr, sets[i % 2], v_tms[0])]
        if i + 1 < len(units):
            gens.append(prep(units[i + 1][0], units[i + 1][1], sets[(i + 1) % 2]))
        drive(gens)
    P.emit()


def phase_mixB(nc, C, l):
    with ExitStack() as ost:
        _phase_mixB(nc, C, l, ost)
    _mixB_post(nc, C, l)


def _phase_mixB(nc, C, l, ost):
    def psb(name, shape, dt=F32):
        _UID[0] += 1
        return Buf(ost.enter_context(nc.sbuf_tensor("%s_%d" % (name, _UID[0]), list(shape), dt)), name)
    qnT = [psb("qnT%d" % h, [128, T], BF16) for h in range(4)]
    knT = [psb("knT%d" % h, [128, T], BF16) for h in range(4)]
    vT = [psb("vT%d" % h, [128, T], BF16) for h in range(4)]
    bet = psb("bet", [64, 68, 8])
    nbet = psb("nbet", [64, 68, 8])
    gg = psb("gg", [64, 68, 8])
    nwb = psb("nwbB", [64, 128])
    P = Prog(nc)
    I = P.I
    banks = [P.ps("bk%d" % i, [128, 512]) for i in range(8)]
    b0, b1, b2, b3, b4, b5, b6, b7 = banks
    cw = P.sb("cw", [128, 12, 5])
    P.D("sync", cw[:], C.din["gdn_conv_w"][l], writes=[cw])
    xpads = [P.sb("xpad%d" % i, [128, T + 8], BF16) for i in range(1)]
    for xp in xpads:
        I("gpsimd", "memset", writes=[xp], ap=xp[:], constant=0.0)
    diag = P.sb("diag", [128, 5, 128], BF16)
    xs = P.sb("xs", [128, 512])
    sqb = P.sb("sqb", [128, 512], BF16)
    rs = P.sb("rs", [128, 512])
    bq = C.scr["BQKV"]
    for ch in range(12):
        xp = xpads[0]
        P.D("sync", xp[:, 2:258], bq[ch * 128:(ch + 1) * 128, 0:256], reads=[bq], writes=[(xp, 0)])
        P.D("sync", xp[:, 262:4358], bq[ch * 128:(ch + 1) * 128, 256:T], reads=[bq], writes=[(xp, 1)])
        for j in range(5):
            I("vector", "tensor_scalar", reads=[C.identb, cw], writes=[(diag, j)], out=diag[:, j, :], in0=C.identb[:], scalar1=cw[:, ch, j:j + 1],
              scalar2=None, op0=ALU.mult)
        kind, h = ch // 4, ch % 4
        for ti, (n0, n, w) in enumerate(TTILES):
            po_ = n0 + 2 if n0 == 0 else n0 + 6
            pc = banks[ti % 2]
            for j in range(5):
                P.mm(pc[:, :n], diag[:, j, :], xp[:, po_ + j - 2:po_ + j - 2 + n], j == 0, j == 4, reads=[diag, xp], writes=[pc])
            if kind == 2:
                I("scalar", "activation", reads=[pc], writes=[(vT[h], n0)], out=vT[h][:, n0:n0 + n], in_=pc[:, :n], func=AF.Silu)
                continue
            I("scalar", "activation", reads=[pc], writes=[xs], out=xs[:, :n], in_=pc[:, :n], func=AF.Silu)
            I("gpsimd", "tensor_tensor", reads=[xs], writes=[sqb], out=sqb[:, :n], in0=xs[:, :n], in1=xs[:, :n], op=ALU.mult)
            pq = banks[2 + ti % 2]
            P.mm(pq[:, :n], C.onesb[:], sqb[:, :n], True, True, reads=[sqb, C.onesb], writes=[pq])
            I("scalar", "activation", reads=[pq], writes=[rs], out=rs[:, :n], in_=pq[:, :n], func=AF.Sqrt, bias=EPS)
            I("vector", "reciprocal", reads=[rs], writes=[rs], out=rs[:, :n], in_=rs[:, :n])
            dst = qnT[h] if kind == 0 else knT[h]
            I("vector", "scalar_tensor_tensor", reads=[xs, rs], writes=[(dst, n0)], out=dst[:, n0:n0 + n], in0=xs[:, :n],
              scalar=(128 ** -0.5 if kind == 0 else 1.0), in1=rs[:, :n], op0=ALU.mult, op1=ALU.mult)
    bba = P.sb("bba", [64, 68, 16])
    P.D("sync", bba[:], C.scr["BBA"].t.ap().rearrange("(c p) e -> p c e", p=64), reads=[C.scr["BBA"]], writes=[bba])
    alb = P.sb("alb", [64, 8])
    dtb = P.sb("dtb", [64, 8])
    P.D("sync", alb[:], C.din["gdn_a_log"][l].partition_broadcast(64), writes=[alb])
    P.D("sync", dtb[:], C.din["gdn_dt_bias"][l].partition_broadcast(64), writes=[dtb])
    P.D("sync", nwb[:], C.din["gdn_norm_w"][l].partition_broadcast(64), writes=[nwb])
    xa = P.sb("xa", [64, 68, 8])
    t1 = P.sb("t1", [64, 68, 8])
    I("scalar", "activation", reads=[bba], writes=[bet], out=bet[:], in_=bba[:, :, 0:8], func=AF.Sigmoid)
    I("vector", "tensor_scalar", reads=[bet], writes=[nbet], out=nbet[:], in0=bet[:], scalar1=-1.0, scalar2=None, op0=ALU.mult)
    I("vector", "tensor_tensor", reads=[bba, dtb], writes=[xa], out=xa[:], in0=bba[:, :, 8:16], in1=AP(dtb, 0, [[8, 64], [0, 68], [1, 8]]), op=ALU.add)
    I("vector", "tensor_scalar", reads=[xa], writes=[t1], out=t1[:], in0=xa[:], scalar1=-1.0, scalar2=None, op0=ALU.mult)
    I("vector", "tensor_tensor", reads=[xa, t1], writes=[t1], out=t1[:], in0=xa[:], in1=t1[:], op=ALU.max)
    I("scalar", "activation", reads=[t1], writes=[t1], out=t1[:], in_=t1[:], func=AF.Exp, scale=-1.0)
    I("scalar", "activation", reads=[t1], writes=[t1], out=t1[:], in_=t1[:], func=AF.Ln, bias=1.0)
    I("vector", "tensor_scalar", reads=[xa], writes=[xa], out=xa[:], in0=xa[:], scalar1=0.0, scalar2=None, op0=ALU.max)
    I("vector", "tensor_tensor", reads=[xa, t1], writes=[xa], out=xa[:], in0=xa[:], in1=t1[:], op=ALU.add)
    I("scalar", "activation", reads=[alb], writes=[alb], out=alb[:], in_=alb[:], func=AF.Exp)
    I("vector", "scalar_tensor_tensor", reads=[xa, alb], writes=[gg], out=gg[:], in0=xa[:], scalar=-1.0, in1=AP(alb, 0, [[8, 64], [0, 68], [1, 8]]),
      op0=ALU.mult, op1=ALU.mult)
    P.emit()
    P = Prog(nc)
    I = P.I
    banks = [P.ps("bk%d" % i, [128, 512]) for i in range(8)]
    for bk in banks:
        I("vector", "memset", writes=[bk], ap=bk[:], constant=0.0)
    ident4 = AP(C.ident, 0, [[128, 64], [0, 4], [1, 64]])

    def v464(bank):
        return bank[0:64, 0:256].rearrange("p (h s) -> p h s", s=64)

    def v4128(bank):
        return bank[0:64, 0:512].rearrange("p (h s) -> p h s", s=128)

    def stream(dr, k0, k1, k2, k3):
        sfx = "_%d" % dr
        sb = lambda name, shape, dt=F32: P.sb(name + sfx, shape, dt)
        kv_tm = sb("kv_tm", [64, 8, 128], BF16)
        ct = sb("ct", [128, 8])
        ecum = sb("ecum", [64, 4]); edl = sb("edl", [64, 4]); etot = sb("etot", [128, 4]); be = sb("be", [64, 4])
        G1 = sb("G1", [64, 4, 64]); G2 = sb("G2", [64, 4, 64])
        m1 = sb("m1", [64, 4, 64]); m2 = sb("m2", [64, 4, 64])
        attT = sb("attTb", [64, 4, 64], BF16)
        tmpM = sb("tmpM", [64, 4, 64])
        X = [sb("X%d" % i, [64, 4, 64]) for i in range(2)]
        Y = [sb("Y%d" % i, [64, 4, 64]) for i in range(2)]
        R = [sb("R%d" % i, [64, 4, 64]) for i in range(2)]
        IM2 = sb("IM2", [64, 4, 64])
        TTb = sb("TTb", [64, 4, 64], BF16)
        vb = sb("vb", [64, 4, 128], BF16); kbe = sb("kbe", [64, 4, 128], BF16); kd = sb("kd", [64, 4, 128], BF16)
        u_sb = sb("u_sb", [64, 4, 128]); wTb = sb("wTb", [128, 4, 64])
        v_new = sb("v_new", [64, 4, 128], BF16)
        t2 = sb("t2", [64, 4, 128]); o_sb = [sb("o_sb%d" % i, [64, 4, 128]) for i in range(2)]
        S = sb("Sg", [128, 4, 128]); St = sb("St", [128, 4, 128]); Sb = sb("Sbg", [128, 4, 128], BF16)
        ODST = C.scr["OF"] if dr == 0 else C.scr["OBW"]
        I("vector", "memset", writes=[S], ap=S[:], constant=0.0)
        I("gpsimd", "memset", writes=[Sb], ap=Sb[:], constant=0.0)
        mU = C.maskf["U" if dr == 0 else "L"]
        mLs = C.maskf["Ls" if dr == 0 else "Us"]
        triM = mU[:, 0, :]
        tri4 = AP(mU, 0, [[512, 64], [0, 4], [1, 64]])
        for ci, c in enumerate(CH_FWD if dr == 0 else CH_BWD):
            key = tkey(c * 64)
            cs = slice(c * 64, (c + 1) * 64)
            goff = c * 8 + dr * 4
            g4 = AP(gg, goff, [[544, 64], [1, 4]])
            g4b = AP(gg, goff, [[544, 64], [1, 4], [0, 64]])
            bet4b = AP(bet, goff, [[544, 64], [1, 4], [0, 128]])
            nbet4b = AP(nbet, goff, [[544, 64], [1, 4], [0, 64]])
            bet4 = AP(bet, goff, [[544, 64], [1, 4]])
            k0v = k0[0:64, :].bitcast(BF16).rearrange("p (a b) -> p a b", b=128)
            for h in range(4):
                P.tr(k0v[:, h, :], knT[h][:, cs], C.identb[:], reads=[(knT[h], key), C.identb], writes=[k0])
                P.tr(k0v[:, 4 + h, :], vT[h][:, cs], C.identb[:], reads=[(vT[h], key), C.identb], writes=[k0])
            k1v = k1[0:64, :].rearrange("p (h two s) -> p h two s", h=4, two=2)
            for h in range(4):
                P.mm(k1[0:64, (2 * h) * 64:(2 * h + 1) * 64], knT[h][:, cs], knT[h][:, cs], True, True, reads=[(knT[h], key)], writes=[k1])
                P.mm(k1[0:64, (2 * h + 1) * 64:(2 * h + 2) * 64], knT[h][:, cs], qnT[h][:, cs], True, True, reads=[(knT[h], key), (qnT[h], key)], writes=[k1])
            P.mm(k2[0:64, 256:260], triM, g4, True, True, reads=[mU, gg], writes=[k2])
            P.mm(k2[:, 260:264], C.ones[0:64, :], g4, True, True, reads=[C.ones, gg], writes=[k2])
            yield
            I("scalar", "activation", reads=[k0], writes=[kv_tm], out=kv_tm[:], in_=k0v, func=AF.Copy)
            I("vector", "tensor_copy", reads=[k2], writes=[ct], out=ct[:], in_=k2[:, 256:264])
            I("scalar", "activation", reads=[ct], writes=[ecum], out=ecum[:], in_=ct[0:64, 0:4], func=AF.Exp)
            I("vector", "tensor_tensor", reads=[ct], writes=[edl], out=edl[:], in0=ct[0:64, 4:8], in1=ct[0:64, 0:4], op=ALU.subtract)
            I("scalar", "activation", reads=[edl], writes=[edl], out=edl[:], in_=edl[:], func=AF.Exp)
            I("scalar", "activation", reads=[ct], writes=[etot], out=etot[:], in_=ct[:, 4:8], func=AF.Exp)
            I("vector", "tensor_tensor", reads=[bet, ecum], writes=[be], out=be[:], in0=bet4, in1=ecum[:], op=ALU.mult)
            yield
            I("gpsimd", "tensor_copy", reads=[gg], writes=[G1], out=G1[:], in_=g4b)
            I("vector", "scalar_tensor_tensor", reads=[mU, gg], writes=[G2], out=G2[:], in0=tri4, scalar=-1.0, in1=g4b, op0=ALU.mult, op1=ALU.mult)
            for h in range(4):
                P.mm(k2[0:64, h * 64:(h + 1) * 64], G1[:, h, :], triM, True, False, reads=[G1, mU], writes=[k2])
                P.mm(k2[0:64, h * 64:(h + 1) * 64], G2[:, h, :], C.ones[0:64, 0:64], False, True, reads=[G2, C.ones], writes=[k2])
            yield
            Dv = v464(k2)
            I("vector", "tensor_scalar", reads=[k2], writes=[m1], out=m1[:], in0=Dv, scalar1=0.0, scalar2=None, op0=ALU.min)
            I("vector", "tensor_scalar", reads=[k2], writes=[m2], out=m2[:], in0=Dv, scalar1=0.0, scalar2=-1.0, op0=ALU.max, op1=ALU.mult)
            I("scalar", "activation", reads=[m1], writes=[m1], out=m1[:], in_=m1[:], func=AF.Exp)
            I("scalar", "activation", reads=[m2], writes=[m2], out=m2[:], in_=m2[:], func=AF.Exp)
            I("gpsimd", "tensor_tensor", reads=[m1, mU], writes=[m1], out=m1[:], in0=m1[:], in1=mU[:, 0:4, :], op=ALU.mult)
            I("gpsimd", "tensor_tensor", reads=[m2, mLs], writes=[m2], out=m2[:], in0=m2[:], in1=mLs[:, 0:4, :], op=ALU.mult)
            yield
            I("vector", "tensor_tensor", reads=[k1, m1], writes=[attT], out=attT[:], in0=k1v[:, :, 1, :], in1=m1[:], op=ALU.mult)
            I("vector", "tensor_tensor", reads=[k1, nbet], writes=[tmpM], out=tmpM[:], in0=k1v[:, :, 0, :], in1=nbet4b, op=ALU.mult)
            I("gpsimd", "tensor_tensor", reads=[tmpM, m2], writes=[X[0]], out=X[0][:], in0=tmpM[:], in1=m2[:], op=ALU.mult)
            k0n = k0[0:64, 0:256].rearrange("p (a b) -> p a b", b=64)
            for h in range(4):
                P.tr(k0n[:, h, :], X[0][:, h, :], C.ident[0:64, 0:64], reads=[X[0], C.ident], writes=[k0])
            yield
            I("scalar", "activation", reads=[k0], writes=[Y[0]], out=Y[0][:], in_=k0n, func=AF.Copy)
            I("vector", "tensor_tensor", reads=[k0, C.ident], writes=[R[0]], out=R[0][:], in0=k0n, in1=ident4, op=ALU.add)
            yield
            cur = 0
            for step in range(5):
                last = step == 4
                nxt = 1 - cur
                if not last:
                    for h in range(4):
                        P.mm(k2[0:64, h * 64:(h + 1) * 64], X[cur][:, h, :], Y[cur][:, h, :], True, True, reads=[X[cur], Y[cur]], writes=[k2])
                for h in range(4):
                    P.mm(k3[0:64, h * 64:(h + 1) * 64], Y[cur][:, h, :], X[cur][:, h, :], True, True, reads=[X[cur], Y[cur]], writes=[k3])
                if not last:
                    I("scalar", "activation", reads=[k2], writes=[Y[nxt]], out=Y[nxt][:], in_=v464(k2), func=AF.Copy)
                I("vector", "tensor_tensor", reads=[k3, C.ident], writes=[IM2], out=IM2[:], in0=v464(k3), in1=ident4, op=ALU.add)
                if not last:
                    I("scalar", "activation", reads=[k3], writes=[X[nxt]], out=X[nxt][:], in_=v464(k3), func=AF.Copy)
                yield
                for h in range(4):
                    P.mm(k2[0:64, h * 64:(h + 1) * 64], IM2[:, h, :], R[cur][:, h, :], True, True, reads=[IM2, R[cur]], writes=[k2])
                yield
                if not last:
                    I("vector", "tensor_copy", reads=[k2], writes=[R[nxt]], out=R[nxt][:], in_=v464(k2))
                else:
                    I("vector", "tensor_copy", reads=[k2], writes=[TTb], out=TTb[:], in_=v464(k2))
                cur = nxt
                yield
            TT_ = TTb
            I("gpsimd", "tensor_tensor", reads=[kv_tm, bet], writes=[vb], out=vb[:], in0=kv_tm[:, 4:8, :], in1=bet4b, op=ALU.mult)
            I("vector", "tensor_tensor", reads=[kv_tm, be], writes=[kbe], out=kbe[:], in0=kv_tm[:, 0:4, :], in1=AP(be, 0, [[4, 64], [1, 4], [0, 128]]), op=ALU.mult)
            I("gpsimd", "tensor_tensor", reads=[kv_tm, edl], writes=[kd], out=kd[:], in0=kv_tm[:, 0:4, :], in1=AP(edl, 0, [[4, 64], [1, 4], [0, 128]]), op=ALU.mult)
            for h in range(4):
                P.mm(k1[0:64, h * 128:(h + 1) * 128], TT_[:, h, :], vb[:, h, :], True, True, reads=[TT_, vb], writes=[k1])
            for h in range(4):
                P.mm(k0[:, h * 64:(h + 1) * 64], kbe[:, h, :], TT_[:, h, :], True, True, reads=[kbe, TT_], writes=[k0])
            yield
            I("scalar", "activation", reads=[k1], writes=[u_sb], out=u_sb[:], in_=v4128(k1), func=AF.Copy)
            I("vector", "tensor_copy", reads=[k0], writes=[wTb], out=wTb[:], in_=k0[:, 0:256].rearrange("p (h s) -> p h s", s=64))
            yield
            for h in range(4):
                P.mm(k1[0:64, h * 128:(h + 1) * 128], wTb[:, h, :], S[:, h, :], True, True, reads=[wTb, S], writes=[k1])
            for h in range(4):
                P.mm(k3[0:64, h * 128:(h + 1) * 128], qnT[h][:, cs], Sb[:, h, :], True, True, reads=[(qnT[h], key), Sb], writes=[k3])
            yield
            I("vector", "tensor_tensor", reads=[u_sb, k1], writes=[v_new], out=v_new[:], in0=u_sb[:], in1=v4128(k1), op=ALU.subtract)
            I("vector", "tensor_tensor", reads=[k3, ecum], writes=[t2], out=t2[:], in0=v4128(k3), in1=AP(ecum, 0, [[4, 64], [1, 4], [0, 128]]), op=ALU.mult)
            for h in range(4):
                P.mm(k3[0:64, h * 128:(h + 1) * 128], attT[:, h, :], v_new[:, h, :], True, True, reads=[attT, v_new], writes=[k3])
            for h in range(4):
                P.mm(k1[:, h * 128:(h + 1) * 128], kd[:, h, :], v_new[:, h, :], True, True, reads=[kd, v_new], writes=[k1])
            yield
            osb = o_sb[ci % 2]
            I("vector", "tensor_tensor", reads=[k3, t2], writes=[osb], out=osb[:], in0=v4128(k3), in1=t2[:], op=ALU.add)
            I("gpsimd", "tensor_tensor", reads=[S, etot], writes=[St], out=St[:], in0=S[:], in1=AP(etot, 0, [[4, 128], [1, 4], [0, 128]]), op=ALU.mult)
            I("vector", "tensor_tensor", reads=[St, k1], writes=[S], out=S[:], in0=St[:], in1=k1[:, :].rearrange("p (h s) -> p h s", s=128), op=ALU.add)
            I("scalar", "activation", reads=[S], writes=[Sb], out=Sb[:], in_=S[:], func=AF.Copy)
            P.D("sync", ODST[cs, :], osb[:].rearrange("p h d -> p (h d)"), reads=[osb], writes=[(ODST, c)])
            yield

    gens = [stream(0, *banks[0:4]), stream(1, *banks[4:8])]
    alive = list(gens)
    while alive:
        for g_ in list(alive):
            try:
                next(g_)
            except StopIteration:
                alive.remove(g_)
    P.emit()


def _mixB_post(nc, C, l):
    P = Prog(nc)
    I = P.I
    sb = P.sb
    nwb = sb("nwbB2", [64, 128])
    P.D("sync", nwb[:], C.din["gdn_norm_w"][l].partition_broadcast(64), writes=[nwb])
    pt = [P.ps("ptB%d" % i, [128, 512]) for i in range(2)]
    of = [sb("ofB%d" % i, [64, 8, 512]) for i in range(2)]
    ob = [sb("obB%d" % i, [64, 8, 512]) for i in range(2)]
    gate = [sb("gateB%d" % i, [64, 8, 512], BF16) for i in range(2)]
    sq = sb("sqB", [64, 8, 512]); ssq = sb("ssqB", [64, 32]); sg = sb("sgB", [64, 8, 512])
    onb = sb("onbB", [64, 8, 512], BF16)
    ngT = sb("ngTB", [128, 4, 512], BF16)
    tmv = lambda b_: b_.t.ap().rearrange("(c p) d -> p c d", p=64)
    ngb_v = C.scr["NGB"].t.ap().rearrange("(h p) t -> p h t", p=128)
    for ti, (n0, n, w) in enumerate(TTILES):
        nch = n // 64
        c0 = n0 // 64
        f_, b_, g_ = of[ti % 2], ob[ti % 2], gate[ti % 2]
        P.D("sync", f_[:, :nch, :], tmv(C.scr["OF"])[:, c0:c0 + nch, :], reads=[C.scr["OF"]], writes=[f_])
        P.D("sync", b_[:, :nch, :], tmv(C.scr["OBW"])[:, c0:c0 + nch, :], reads=[C.scr["OBW"]], writes=[b_])
        P.D("sync", g_[:, :nch, :], tmv(C.scr["BG"])[:, c0:c0 + nch, :], reads=[C.scr["BG"]], writes=[g_])
        I("gpsimd", "tensor_tensor", reads=[f_, b_], writes=[f_], out=f_[:, :nch, :], in0=f_[:, :nch, :], in1=b_[:, :nch, :], op=ALU.add)
        if "ob" in C.debug:
            P.D("sync", C.dbg["ob"].ap().rearrange("(c p) d -> p c d", p=64)[:, c0:c0 + nch, :], f_[:, :nch, :], reads=[f_])
        I("scalar", "activation", reads=[g_], writes=[sg], out=sg[:, :nch, :], in_=g_[:, :nch, :], func=AF.Silu)
        I("gpsimd", "tensor_tensor", reads=[f_], writes=[sq], out=sq[:, :nch, :], in0=f_[:, :nch, :], in1=f_[:, :nch, :], op=ALU.mult)
        I("vector", "tensor_reduce", reads=[sq], writes=[ssq], out=ssq[:, :nch * 4], in_=sq[:, :nch, :].rearrange("p c (h d) -> p (c h) d", d=128), axis=AX.X, op=ALU.add)
        I("scalar", "activation", reads=[ssq], writes=[ssq], out=ssq[:, :nch * 4], in_=ssq[:, :nch * 4], func=AF.Sqrt, scale=1.0 / 128, bias=EPS)
        I("vector", "reciprocal", reads=[ssq], writes=[ssq], out=ssq[:, :nch * 4], in_=ssq[:, :nch * 4])
        I("vector", "tensor_tensor", reads=[f_, ssq], writes=[sq], out=sq[:, :nch, :].rearrange("p c (h d) -> p (c h) d", d=128),
          in0=f_[:, :nch, :].rearrange("p c (h d) -> p (c h) d", d=128), in1=AP(ssq, 0, [[32, 64], [1, nch * 4], [0, 128]]), op=ALU.mult)
        I("gpsimd", "tensor_tensor", reads=[sq, nwb], writes=[sq], out=sq[:, :nch, :].rearrange("p c (h d) -> p (c h) d", d=128),
          in0=sq[:, :nch, :].rearrange("p c (h d) -> p (c h) d", d=128), in1=AP(nwb, 0, [[128, 64], [0, nch * 4], [1, 128]]), op=ALU.mult)
        I("vector", "tensor_tensor", reads=[sq, sg], writes=[onb], out=onb[:, :nch, :], in0=sq[:, :nch, :], in1=sg[:, :nch, :], op=ALU.mult)
        for h in range(4):
            ppt = pt[h % 2]
            ptv = ppt[:, 0:256].bitcast(BF16).rearrange("p (a b) -> p a b", b=64)
            for j in range(nch):
                P.tr(ptv[:, j, :], onb[:, j, h * 128:(h + 1) * 128], C.identb[0:64, 0:64], reads=[onb, C.identb], writes=[ppt])
            I("scalar", "activation", reads=[ppt], writes=[(ngT, h)], out=ngT[:, h, :n], in_=ppt[:, 0:256].bitcast(BF16)[:, 0:n], func=AF.Copy)
        P.D("sync", ngb_v[:, :, n0:n0 + n], ngT[:, :, :n], reads=[ngT], writes=[(C.scr["NGB"], n0)])
    P.emit()


TWO_PI = 6.283185307179586
PI = 3.141592653589793
NPOW = 129


def make_s5_masks(P, C):
    nc = P.nc
    st = C.gstack
    def gsb(name, shape, dt=F32):
        return Buf(st.enter_context(nc.sbuf_tensor(name, list(shape), dt)), name)
    A = gsb("s5mA", [8, 128]); Bge = gsb("s5mB1", [8, 128]); Ble = gsb("s5mB2", [8, 128]); on8 = gsb("s5on", [8, 128])
    C.maskZ = [gsb("maskZf", [128, 128]), gsb("maskZb", [128, 128])]
    pm = P.ps("pmask", [128, 512])
    P.I("vector", "memset", writes=[on8], ap=on8[:], constant=1.0)
    P.I("gpsimd", "affine_select", reads=[on8], writes=[A], out=A[:], in_=on8[:], pattern=[[1, 128]], compare_op=ALU.is_ge, fill=0.0, base=0, channel_multiplier=-16)
    P.I("gpsimd", "affine_select", reads=[A], writes=[A], out=A[:], in_=A[:], pattern=[[-1, 128]], compare_op=ALU.is_ge, fill=0.0, base=15, channel_multiplier=16)
    P.I("gpsimd", "affine_select", reads=[on8], writes=[Bge], out=Bge[:], in_=on8[:], pattern=[[1, 8], [0, 16]], compare_op=ALU.is_ge, fill=0.0, base=0, channel_multiplier=-1)
    P.I("gpsimd", "affine_select", reads=[on8], writes=[Ble], out=Ble[:], in_=on8[:], pattern=[[-1, 8], [0, 16]], compare_op=ALU.is_ge, fill=0.0, base=0, channel_multiplier=1)
    P.mm(pm[:, 0:128], A[:], Bge[:], True, True, reads=[A, Bge], writes=[pm])
    P.mm(pm[:, 128:256], A[:], Ble[:], True, True, reads=[A, Ble], writes=[pm])
    P.I("vector", "tensor_copy", reads=[pm], writes=[C.maskZ[0]], out=C.maskZ[0][:], in_=pm[:, 0:128])
    P.I("vector", "tensor_copy", reads=[pm], writes=[C.maskZ[1]], out=C.maskZ[1][:], in_=pm[:, 128:256])


def phase_s5(nc, C, l):
    CU = C.scr["CU"]
    YC = C.scr["YC"]
    cu_g = CU.t.ap().rearrange("(g c) n -> c g n", c=16)
    yc_g = YC.t.ap().rearrange("(g c) n -> c g n", c=16)
    P = Prog(nc)
    S5A = C.scr["S5A"]
    for q in range(4):
        tmpc = P.sb("tmpc%d" % q, [128, 256], BF16)
        ctr = P.sb("ctr%d" % q, [128, 8, 8, 4], BF16)
        P.D("sync", tmpc[:], CU[q * 128:(q + 1) * 128, 0:256], reads=[CU], writes=[tmpc])
        P.I("vector", "tensor_copy", reads=[tmpc], writes=[ctr], out=ctr[:], in_=tmpc[:].rearrange("p (sc blk t) -> p t blk sc", sc=4, blk=8))
        P.D("sync", S5A[q * 128:(q + 1) * 128, :], ctr[:].rearrange("p t b s -> p (t b s)"), reads=[ctr], writes=[(S5A, q)])
    P.emit()
    for hf in range(2):
        s5_half(nc, C, l, hf, cu_g, yc_g)
    P = Prog(nc)
    S5B = C.scr["S5B"]
    for q in range(4):
        tmpc = P.sb("tmpd%d" % q, [128, 256], BF16)
        ctr = P.sb("ctd%d" % q, [128, 8, 8, 4], BF16)
        P.D("sync", ctr[:].rearrange("p t b s -> p (t b s)"), S5B[q * 128:(q + 1) * 128, :], reads=[S5B], writes=[ctr])
        P.I("vector", "tensor_copy", reads=[ctr], writes=[tmpc], out=tmpc[:].rearrange("p (sc blk t) -> p t blk sc", sc=4, blk=8), in_=ctr[:])
        P.D("sync", YC[q * 128:(q + 1) * 128, 0:256], tmpc[:], reads=[tmpc], writes=[(YC, q)])
    P.emit()


def s5_half(nc, C, l, hf, cu_g, yc_g):
    G0 = hf * 16
    hst = ExitStack()
    cntr = [0]

    def pst(name, shape, dt=F32):
        _UID[0] += 1
        return Buf(hst.enter_context(nc.sbuf_tensor("s5p_%s_%d" % (name, _UID[0]), list(shape), dt)), name)

    with hst:
        Er = pst("Er", [64, 32, NPOW]); Ei = pst("Ei", [64, 32, NPOW])
        bbr = pst("bbr", [64, 32, 16]); bbi = pst("bbi", [64, 32, 16])
        cre = pst("cre", [64, 16, 16]); cim = pst("cim", [64, 16, 16]); ncim = pst("ncim", [64, 16, 16]); ncre = pst("ncre", [64, 16, 16])
        dtab = pst("dtab", [128, 16])
        U2 = pst("U2", [128, 16, 8, 68], BF16)
        Sall = pst("Sall", [64, 2, 2, 16, 68])
        XPb = pst("XPb", [64, 2, 2, 16, 68], BF16)
        P = Prog(nc)
        I = P.I
        sb = P.sb
        lre = sb("lre", [64, 2, 16]); lim = sb("lim", [64, 2, 16]); ldt = sb("ldt", [64, 2, 16])
        P.D("sync", lre[:], C.din["s5_lam_re"][l][:, :, G0:G0 + 16], writes=[lre])
        P.D("sync", lim[:], C.din["s5_lam_im"][l][:, :, G0:G0 + 16], writes=[lim])
        P.D("sync", ldt[:], C.din["s5_log_dt"][l].partition_broadcast(64)[:, :, G0:G0 + 16], writes=[ldt])
        br = sb("br", [64, 16, 16]); bi = sb("bi", [64, 16, 16])
        P.D("sync", br[:], C.din["s5_b_re"][l][:, G0:G0 + 16, :], writes=[br])
        P.D("sync", bi[:], C.din["s5_b_im"][l][:, G0:G0 + 16, :], writes=[bi])
        P.D("sync", cre[:], C.din["s5_c_re"][l][:, G0:G0 + 16, :], writes=[cre])
        P.D("sync", cim[:], C.din["s5_c_im"][l][:, G0:G0 + 16, :], writes=[cim])
        I("vector", "tensor_scalar", reads=[cim], writes=[ncim], out=ncim[:], in0=cim[:], scalar1=-1.0, scalar2=None, op0=ALU.mult)
        I("vector", "tensor_scalar", reads=[cre], writes=[ncre], out=ncre[:], in0=cre[:], scalar1=-1.0, scalar2=None, op0=ALU.mult)
        P.D("sync", dtab[:], C.din["s5_dtab"][l][:, G0:G0 + 16], writes=[dtab])
        U2c = sb("U2c", [128, 16, 32], BF16)
        s5a = C.scr["S5A"].t.ap().rearrange("(g c) (t x) -> t c g x", c=16, t=8)
        for t in range(8):
            src = cu_g[:, G0:G0 + 16, 256:T].rearrange("c g (blk t col) -> c g blk t col", blk=8, t=8)
            P.dma("sync", [lambda e, t=t, b_=b_, src=src: e.dma_start(out=U2[16 * t:16 * t + 16, :, b_, 4:68], in_=src[:, :, b_, t, :]) for b_ in range(8)],
                  reads=[C.scr["CU"]], writes=[(U2, t)])
            P.D("sync", U2c[16 * t:16 * t + 16, :, :], s5a[t][:, G0:G0 + 16, :], reads=[C.scr["S5A"]], writes=[(U2c, t)])
        I("vector", "tensor_copy", reads=[U2c, U2], writes=[U2], out=U2[:, :, :, 0:4], in_=U2c[:].rearrange("p g (b s) -> p g b s", s=4))
        dtt = sb("dtt", [64, 32]); xx = sb("xx", [64, 32]); th = sb("th", [64, 32]); lr = sb("lr", [64, 32])
        lre2 = lre[:].rearrange("p d g -> p (d g)"); lim2 = lim[:].rearrange("p d g -> p (d g)"); ldt2 = ldt[:].rearrange("p d g -> p (d g)")
        I("scalar", "activation", reads=[ldt], writes=[dtt], out=dtt[:], in_=ldt2, func=AF.Exp)
        I("vector", "tensor_scalar", reads=[lre], writes=[lr], out=lr[:], in0=lre2, scalar1=-1e-4, scalar2=None, op0=ALU.min)
        I("vector", "tensor_tensor", reads=[lr, dtt], writes=[xx], out=xx[:], in0=lr[:], in1=dtt[:], op=ALU.mult)
        I("vector", "tensor_tensor", reads=[lim, dtt], writes=[th], out=th[:], in0=lim2, in1=dtt[:], op=ALU.mult)
        jt = sb("jt", [64, NPOW])
        I("gpsimd", "iota", writes=[(jt, 0)], out=jt[:, 0:65], pattern=[[1, 65]], base=0, channel_multiplier=0, allow_small_or_imprecise_dtypes=True)
        I("gpsimd", "iota", writes=[(jt, 1)], out=jt[:, 65:129], pattern=[[-1, 64]], base=0, channel_multiplier=0, allow_small_or_imprecise_dtypes=True)
        ph = sb("ph", [64, 32, NPOW]); qf = sb("qf", [64, 32, NPOW]); qi = sb("qi", [64, 32, NPOW], I32); mk = sb("mk", [64, 32, NPOW])
        jb_ = AP(jt, 0, [[NPOW, 64], [0, 32], [1, NPOW]])
        thb = AP(th, 0, [[32, 64], [1, 32], [0, NPOW]])
        xb = AP(xx, 0, [[32, 64], [1, 32], [0, NPOW]])
        I("vector", "tensor_tensor", reads=[th, jt], writes=[ph], out=ph[:], in0=thb, in1=jb_, op=ALU.mult)
        I("vector", "tensor_scalar", reads=[ph], writes=[qf], out=qf[:], in0=ph[:], scalar1=1.0 / TWO_PI, scalar2=None, op0=ALU.mult)
        I("vector", "tensor_copy", reads=[qf], writes=[qi], out=qi[:], in_=qf[:])
        I("vector", "tensor_copy", reads=[qi], writes=[qf], out=qf[:], in_=qi[:])
        I("vector", "scalar_tensor_tensor", reads=[qf, ph], writes=[ph], out=ph[:], in0=qf[:], scalar=-TWO_PI, in1=ph[:], op0=ALU.mult, op1=ALU.add)
        for (cmp_, sgn) in ((ALU.is_gt, -1.0), (ALU.is_lt, 1.0)):
            I("vector", "tensor_scalar", reads=[ph], writes=[mk], out=mk[:], in0=ph[:], scalar1=(PI if sgn < 0 else -PI), scalar2=None, op0=cmp_)
            I("vector", "scalar_tensor_tensor", reads=[mk, ph], writes=[ph], out=ph[:], in0=mk[:], scalar=sgn * TWO_PI, in1=ph[:], op0=ALU.mult, op1=ALU.add)
        I("scalar", "activation", reads=[ph], writes=[Ei], out=Ei[:], in_=ph[:], func=AF.Sin)
        I("vector", "tensor_scalar", reads=[ph], writes=[ph], out=ph[:], in0=ph[:], scalar1=PI / 2, scalar2=None, op0=ALU.add)
        I("vector", "tensor_scalar", reads=[ph], writes=[mk], out=mk[:], in0=ph[:], scalar1=PI, scalar2=None, op0=ALU.is_gt)
        I("vector", "scalar_tensor_tensor", reads=[mk, ph], writes=[ph], out=ph[:], in0=mk[:], scalar=-TWO_PI, in1=ph[:], op0=ALU.mult, op1=ALU.add)
        I("scalar", "activation", reads=[ph], writes=[Er], out=Er[:], in_=ph[:], func=AF.Sin)
        I("vector", "tensor_tensor", reads=[xx, jt], writes=[qf], out=qf[:], in0=xb, in1=jb_, op=ALU.mult)
        I("scalar", "activation", reads=[qf], writes=[qf], out=qf[:], in_=qf[:], func=AF.Exp)
        I("vector", "tensor_tensor", reads=[Er, qf], writes=[Er], out=Er[:], in0=Er[:], in1=qf[:], op=ALU.mult)
        I("gpsimd", "tensor_tensor", reads=[Ei, qf], writes=[Ei], out=Ei[:], in0=Ei[:], in1=qf[:], op=ALU.mult)

        def col(tab, j):
            return AP(tab, j, [[32 * NPOW, 64], [NPOW, 32]])
        den = sb("den", [64, 32]); t0 = sb("t0", [64, 32]); t1 = sb("t1b", [64, 32]); crr = sb("crr", [64, 32]); cii = sb("cii", [64, 32]); am1 = sb("am1", [64, 32])
        I("vector", "tensor_tensor", reads=[lr], writes=[den], out=den[:], in0=lr[:], in1=lr[:], op=ALU.mult)
        I("vector", "tensor_tensor", reads=[lim], writes=[t0], out=t0[:], in0=lim2, in1=lim2, op=ALU.mult)
        I("vector", "tensor_tensor", reads=[den, t0], writes=[den], out=den[:], in0=den[:], in1=t0[:], op=ALU.add)
        I("vector", "reciprocal", reads=[den], writes=[den], out=den[:], in_=den[:])
        I("vector", "tensor_scalar", reads=[Er], writes=[am1], out=am1[:], in0=col(Er, 1), scalar1=-1.0, scalar2=None, op0=ALU.add)
        I("vector", "tensor_tensor", reads=[am1, lr], writes=[t0], out=t0[:], in0=am1[:], in1=lr[:], op=ALU.mult)
        I("vector", "tensor_tensor", reads=[Ei, lim], writes=[t1], out=t1[:], in0=col(Ei, 1), in1=lim2, op=ALU.mult)
        I("vector", "tensor_tensor", reads=[t0, t1], writes=[crr], out=crr[:], in0=t0[:], in1=t1[:], op=ALU.add)
        I("vector", "tensor_tensor", reads=[crr, den], writes=[crr], out=crr[:], in0=crr[:], in1=den[:], op=ALU.mult)
        I("vector", "tensor_tensor", reads=[Ei, lr], writes=[t0], out=t0[:], in0=col(Ei, 1), in1=lr[:], op=ALU.mult)
        I("vector", "tensor_tensor", reads=[am1, lim], writes=[t1], out=t1[:], in0=am1[:], in1=lim2, op=ALU.mult)
        I("vector", "tensor_tensor", reads=[t0, t1], writes=[cii], out=cii[:], in0=t0[:], in1=t1[:], op=ALU.subtract)
        I("vector", "tensor_tensor", reads=[cii, den], writes=[cii], out=cii[:], in0=cii[:], in1=den[:], op=ALU.mult)
        tb = sb("tb", [64, 32, 16])
        crb4 = AP(crr, 0, [[32, 64], [16, 2], [1, 16], [0, 16]]); cib4 = AP(cii, 0, [[32, 64], [16, 2], [1, 16], [0, 16]])
        br4 = AP(br, 0, [[256, 64], [0, 2], [16, 16], [1, 16]]); bi4 = AP(bi, 0, [[256, 64], [0, 2], [16, 16], [1, 16]])
        o4 = lambda t_: t_[:].rearrange("p (d g) c -> p d g c", d=2)
        I("vector", "tensor_tensor", reads=[crr, br], writes=[bbr], out=o4(bbr), in0=crb4, in1=br4, op=ALU.mult)
        I("vector", "tensor_tensor", reads=[cii, bi], writes=[tb], out=o4(tb), in0=cib4, in1=bi4, op=ALU.mult)
        I("vector", "tensor_tensor", reads=[bbr, tb], writes=[bbr], out=bbr[:], in0=bbr[:], in1=tb[:], op=ALU.subtract)
        I("vector", "tensor_tensor", reads=[crr, bi], writes=[bbi], out=o4(bbi), in0=crb4, in1=bi4, op=ALU.mult)
        I("vector", "tensor_tensor", reads=[cii, br], writes=[tb], out=o4(tb), in0=cib4, in1=br4, op=ALU.mult)
        I("vector", "tensor_tensor", reads=[bbi, tb], writes=[bbi], out=bbi[:], in0=bbi[:], in1=tb[:], op=ALU.add)
        if "s5tab" in C.debug and hf == 0:
            P.D("sync", C.dbg["Er"].ap(), Er[:], reads=[Er]); P.D("sync", C.dbg["Ei"].ap(), Ei[:], reads=[Ei])
            P.D("sync", C.dbg["bbr"].ap(), bbr[:], reads=[bbr]); P.D("sync", C.dbg["bbi"].ap(), bbi[:], reads=[bbi])
        P.emit()
        if getattr(C, 's5stop', None) == 'tab':
            return
        s5_main(nc, C, l, hf, locals())


def s5_main(nc, C, l, hf, L):
    Er, Ei, bbr, bbi, cre, cim, ncim, ncre, dtab, U2, Sall, XPb = (L[k] for k in
        ("Er", "Ei", "bbr", "bbi", "cre", "cim", "ncim", "ncre", "dtab", "U2", "Sall", "XPb"))
    G0 = hf * 16
    P = Prog(nc)
    I = P.I
    sb = P.sb
    banks = [P.ps("s5b%d" % i, [128, 512]) for i in range(8)]
    PS = 32 * NPOW

    def Ev(tab, dg, c0, n):
        return AP(tab, dg * NPOW + c0, [[PS, 64], [1, n], [0, 16]])

    def Bv(tab, dg, n):
        return AP(tab, dg * 16, [[512, 64], [0, n], [1, 16]])

    def Cv(tab, gl, n):
        return AP(tab, gl * 16, [[256, 64], [0, n], [1, 16]])

    tA = [sb("tA%d" % i, [64, 65, 16]) for i in range(2)]
    tB = [sb("tB%d" % i, [64, 65, 16]) for i in range(2)]
    cnt = [0]

    def cprod(outre, outim, n, er, ei, xr, xi, xrn=None, sub_im=False):
        k = cnt[0] % 2
        cnt[0] += 1
        a, b = tA[k], tB[k]
        e1, e2 = ("vector", "gpsimd") if k == 0 else ("gpsimd", "vector")
        I(e1, "tensor_tensor", reads=[Er, bbr, cre], writes=[a], out=a[:, :n, :], in0=er, in1=xr, op=ALU.mult)
        I(e2, "tensor_tensor", reads=[Ei, bbi, cim], writes=[b], out=b[:, :n, :], in0=ei, in1=xi, op=ALU.mult)
        I(e1, "tensor_tensor", reads=[a, b], writes=[outre], out=outre[:, :n, :], in0=a[:, :n, :], in1=b[:, :n, :], op=ALU.subtract)
        if not sub_im:
            I(e1, "tensor_tensor", reads=[Er, bbi], writes=[a], out=a[:, :n, :], in0=er, in1=xi, op=ALU.mult)
            I(e2, "tensor_tensor", reads=[Ei, bbr], writes=[b], out=b[:, :n, :], in0=ei, in1=xr, op=ALU.mult)
            I(e2, "tensor_tensor", reads=[a, b], writes=[outim], out=outim[:, :n, :], in0=a[:, :n, :], in1=b[:, :n, :], op=ALU.add)
        else:
            I(e1, "tensor_tensor", reads=[Er, ncim], writes=[a], out=a[:, :n, :], in0=er, in1=xrn, op=ALU.mult)
            I(e2, "tensor_tensor", reads=[Ei, cre], writes=[b], out=b[:, :n, :], in0=ei, in1=xr, op=ALU.mult)
            I(e2, "tensor_tensor", reads=[a, b], writes=[outim], out=outim[:, :n, :], in0=a[:, :n, :], in1=b[:, :n, :], op=ALU.subtract)

    Wre = [sb("Wre%d" % i, [64, 64, 16], BF16) for i in range(2)]
    Wim = [sb("Wim%d" % i, [64, 64, 16], BF16) for i in range(2)]
    PTs = [sb("PTs%d" % i, [128, 8, 128], BF16) for i in range(2)]
    it = 0
    for d in range(2):
        for gl in range(16):
            dg = d * 16 + gl
            wr, wi, pts = Wre[it % 2], Wim[it % 2], PTs[it % 2]
            c0 = 65 if d == 0 else 0
            cprod(wr, wi, 64, Ev(Er, dg, c0, 64), Ev(Ei, dg, c0, 64), Bv(bbr, dg, 64), Bv(bbi, dg, 64))
            pt = banks[it % 2]
            ptv = pt[:, :].bitcast(BF16).rearrange("p (a b) -> p a b", b=128)
            for blk in range(8):
                P.tr(ptv[:, blk, 0:64], wr[:, blk * 8:(blk + 1) * 8, :].rearrange("p m c -> p (m c)"), C.identb[0:64, 0:64], reads=[wr, C.identb], writes=[pt])
                P.tr(ptv[:, blk, 64:128], wi[:, blk * 8:(blk + 1) * 8, :].rearrange("p m c -> p (m c)"), C.identb[0:64, 0:64], reads=[wi, C.identb], writes=[pt])
            I("scalar", "activation", reads=[pt], writes=[pts], out=pts[:], in_=ptv, func=AF.Copy)
            ps = banks[2 + it % 2]
            for blk in range(8):
                P.mm(ps[0:64, 0:68], pts[:, blk, 0:64], U2[:, gl, blk, :], blk == 0, blk == 7, reads=[pts, U2], writes=[ps])
            for blk in range(8):
                P.mm(ps[0:64, 68:136], pts[:, blk, 64:128], U2[:, gl, blk, :], blk == 0, blk == 7, reads=[pts, U2], writes=[ps])
            I("scalar", "activation", reads=[ps], writes=[(Sall, d)], out=Sall[:, :, d, gl, :], in_=ps[0:64, 0:136].rearrange("p (a b) -> p a b", b=68), func=AF.Copy)
            it += 1
    if getattr(C, 's5stop', None) == 'st1':
        P.emit()
        return
    s1 = sb("s1", [64, 16, 68]); s2 = sb("s2", [64, 16, 68]); s3 = sb("s3", [64, 16, 68]); s4 = sb("s4", [64, 16, 68])
    a63r = AP(Er, 63, [[PS, 64], [NPOW, 16], [0, 68]]); a63i = AP(Ei, 63, [[PS, 64], [NPOW, 16], [0, 68]])
    Sr = Sall[:, 0, 0, :, :]; Si = Sall[:, 1, 0, :, :]
    I("vector", "tensor_tensor", reads=[(Sall, 0), Er], writes=[s1], out=s1[:], in0=Sr, in1=a63r, op=ALU.mult)
    I("gpsimd", "tensor_tensor", reads=[(Sall, 0), Ei], writes=[s2], out=s2[:], in0=Si, in1=a63i, op=ALU.mult)
    I("vector", "tensor_tensor", reads=[(Sall, 0), Er], writes=[s3], out=s3[:], in0=Si, in1=a63r, op=ALU.mult)
    I("gpsimd", "tensor_tensor", reads=[(Sall, 0), Ei], writes=[s4], out=s4[:], in0=Sr, in1=a63i, op=ALU.mult)
    I("vector", "tensor_tensor", reads=[s1, s2], writes=[(Sall, 0)], out=Sr, in0=s1[:], in1=s2[:], op=ALU.subtract)
    I("gpsimd", "tensor_tensor", reads=[s3, s4, (Sall, 0)], writes=[(Sall, 0)], out=Si, in0=s3[:], in1=s4[:], op=ALU.add)
    SS = 2 * 2 * 16 * 68
    for d in range(2):
        eng = "vector" if d == 0 else "gpsimd"
        AA = sb("AA%d" % d, [64, 2, 16]); AC = sb("AC%d" % d, [64, 2, 16])
        a64r = AP(Er, d * 16 * NPOW + 64, [[PS, 64], [NPOW, 16]]); a64i = AP(Ei, d * 16 * NPOW + 64, [[PS, 64], [NPOW, 16]])
        I(eng, "tensor_copy", reads=[Er], writes=[AA], out=AA[:, 0, :], in_=a64r)
        I(eng, "tensor_copy", reads=[Er, AA], writes=[AA], out=AA[:, 1, :], in_=a64r)
        I(eng, "tensor_copy", reads=[Ei], writes=[AC], out=AC[:, 0, :], in_=a64i)
        I(eng, "tensor_scalar", reads=[Ei, AC], writes=[AC], out=AC[:, 1, :], in0=a64i, scalar1=-1.0, scalar2=None, op0=ALU.mult)
        X2 = [sb("X2_%d_%d" % (d, i), [64, 2, 16]) for i in range(2)]
        p1 = sb("p1_%d" % d, [64, 2, 16]); p2 = sb("p2_%d" % d, [64, 2, 16]); dec = sb("dec_%d" % d, [64, 2, 16])
        I(eng, "memset", writes=[X2[0]], ap=X2[0][:], constant=0.0)
        for i, sc in enumerate(CH_FWD if d == 0 else CH_BWD):
            xc, xn = X2[i % 2], X2[(i + 1) % 2]
            sv = AP(Sall, d * 16 * 68 + sc, [[SS, 64], [2 * 16 * 68, 2], [68, 16]])
            xv = AP(XPb, d * 16 * 68 + sc, [[SS, 64], [2 * 16 * 68, 2], [68, 16]])
            if d == 0:
                I(eng, "tensor_copy", reads=[xc], writes=[(XPb, d)], out=xv, in_=xc[:])
            I(eng, "tensor_tensor", reads=[xc, AA], writes=[p1], out=p1[:], in0=xc[:], in1=AA[:], op=ALU.mult)
            I(eng, "tensor_tensor", reads=[xc, AC], writes=[p2], out=p2[:], in0=xc[:], in1=AC[:], op=ALU.mult)
            I(eng, "tensor_tensor", reads=[p1, p2], writes=[dec], out=dec[:, 0, :], in0=p1[:, 0, :], in1=p2[:, 1, :], op=ALU.add)
            I(eng, "tensor_tensor", reads=[p1, p2, dec], writes=[dec], out=dec[:, 1, :], in0=p1[:, 1, :], in1=p2[:, 0, :], op=ALU.add)
            if d == 1:
                I(eng, "tensor_copy", reads=[dec], writes=[(XPb, d)], out=xv, in_=dec[:])
            I(eng, "tensor_tensor", reads=[dec, (Sall, d)], writes=[xn], out=xn[:], in0=dec[:], in1=sv, op=ALU.add)
    if getattr(C, 's5stop', None) == 'scan':
        P.emit()
        return
    Wf = [sb("Wfr", [64, 8, 16], BF16), sb("Wfi", [64, 8, 16], BF16)]
    Rf = [sb("Rft", [64, 65, 16], BF16), sb("Rfb", [64, 65, 16], BF16)]
    Wb = [sb("Wbr", [64, 64, 16], BF16), sb("Wbi", [64, 64, 16], BF16)]
    Rb = [sb("Rbt", [64, 64, 16], BF16), sb("Rbb", [64, 64, 16], BF16)]
    Zf = sb("Zf", [128, 64, 16], BF16)
    Zb = sb("Zb", [128, 8, 128], BF16)
    Ysb = sb("Ysb", [128, 16, 8, 68], BF16)
    flat = lambda ap_: ap_.rearrange("p m c -> p (m c)")
    for gl in range(16):
        df, db = gl, 16 + gl
        cprod(Wf[0], Wf[1], 8, Ev(Er, df, 65, 8), Ev(Ei, df, 65, 8), Bv(bbr, df, 8), Bv(bbi, df, 8))
        cprod(Rf[0], Rf[1], 65, Ev(Er, df, 0, 65), Ev(Ei, df, 0, 65), Cv(cre, gl, 65), Cv(cim, gl, 65), xrn=Cv(ncim, gl, 65), sub_im=True)
        cprod(Wb[0], Wb[1], 64, Ev(Er, db, 0, 64), Ev(Ei, db, 0, 64), Bv(bbr, db, 64), Bv(bbi, db, 64))
        cprod(Rb[0], Rb[1], 64, Ev(Er, db, 65, 64), Ev(Ei, db, 65, 64), Cv(cre, gl, 64), Cv(cim, gl, 64), xrn=Cv(ncim, gl, 64), sub_im=True)
        z0, z1, z2, z3 = banks[0], banks[1], banks[2], banks[3]
        for hh, zb in enumerate((z0, z1)):
            P.mm(zb[:, :], flat(Wf[0][:, :, :]), flat(Rf[0][:, hh * 32:(hh + 1) * 32, :]), True, False, reads=[Wf[0], Rf[0]], writes=[zb])
            P.mm(zb[:, :], flat(Wf[1][:, :, :]), flat(Rf[1][:, hh * 32:(hh + 1) * 32, :]), False, True, reads=[Wf[1], Rf[1]], writes=[zb])
        I("vector", "tensor_tensor", reads=[z0, C.maskZ[0]], writes=[(Zf, 0)], out=flat(Zf[:, 0:8, :]), in0=z0[:, 0:128], in1=C.maskZ[0][:], op=ALU.mult)
        I("scalar", "activation", reads=[z0], writes=[(Zf, 1)], out=flat(Zf[:, 8:32, :]), in_=z0[:, 128:512], func=AF.Copy)
        I("scalar", "activation", reads=[z1], writes=[(Zf, 2)], out=flat(Zf[:, 32:64, :]), in_=z1[:, :], func=AF.Copy)
        for dl in range(8):
            zb = z2 if dl < 4 else z3
            o_ = zb[:, (dl % 4) * 128:(dl % 4 + 1) * 128]
            P.mm(o_, flat(Wb[0][:, dl * 8:(dl + 1) * 8, :]), flat(Rb[0][:, 0:8, :]), True, False, reads=[Wb[0], Rb[0]], writes=[zb])
            P.mm(o_, flat(Wb[1][:, dl * 8:(dl + 1) * 8, :]), flat(Rb[1][:, 0:8, :]), False, True, reads=[Wb[1], Rb[1]], writes=[zb])
        I("vector", "tensor_tensor", reads=[z2, C.maskZ[1]], writes=[(Zb, 0)], out=Zb[:, 0, :], in0=z2[:, 0:128], in1=C.maskZ[1][:], op=ALU.mult)
        I("scalar", "activation", reads=[z2], writes=[(Zb, 1)], out=Zb[:, 1:4, :], in_=z2[:, 128:512].rearrange("p (a b) -> p a b", b=128), func=AF.Copy)
        I("scalar", "activation", reads=[z3], writes=[(Zb, 2)], out=Zb[:, 4:8, :], in_=z3[:, :].rearrange("p (a b) -> p a b", b=128), func=AF.Copy)
        if getattr(C, 's5stop', None) in ('z', 'z3'):
            continue
        for hb in range(2):
            yb = banks[4 + (2 * gl + hb) % 4]
            for i in range(4):
                ib = 4 * hb + i
                o_ = yb[:, i * 68:(i + 1) * 68]
                ops_ = []
                for jb in range(0, ib + 1):
                    ops_.append((flat(Zf[:, (ib - jb) * 8:(ib - jb + 1) * 8, :]), U2[:, gl, jb, :], [Zf, U2]))
                for jb in range(ib, 8):
                    ops_.append((Zb[:, jb - ib, :], U2[:, gl, jb, :], [Zb, U2]))
                ops_.append((flat(Rf[0][:, 8 * ib + 1:8 * ib + 9, :]), XPb[:, 0, 0, gl, :], [Rf[0], XPb]))
                ops_.append((flat(Rf[1][:, 8 * ib + 1:8 * ib + 9, :]), XPb[:, 1, 0, gl, :], [Rf[1], XPb]))
                ops_.append((flat(Rb[0][:, 8 * ib:8 * ib + 8, :]), XPb[:, 0, 1, gl, :], [Rb[0], XPb]))
                ops_.append((flat(Rb[1][:, 8 * ib:8 * ib + 8, :]), XPb[:, 1, 1, gl, :], [Rb[1], XPb]))
                for k, (lh, rh, rd) in enumerate(ops_):
                    P.mm(o_, lh, rh, k == 0, k == len(ops_) - 1, reads=rd, writes=[yb])
            I("vector", "scalar_tensor_tensor", reads=[U2, dtab, yb], writes=[(Ysb, gl)], out=Ysb[:, gl, 4 * hb:4 * hb + 4, :], in0=U2[:, gl, 4 * hb:4 * hb + 4, :],
              scalar=dtab[:, gl:gl + 1], in1=yb[:, 0:272].rearrange("p (a b) -> p a b", b=68), op0=ALU.mult, op1=ALU.add)
    if getattr(C, 's5stop', None) in ('st3', 'z3'):
        P.emit()
        return
    Yc2 = sb("Yc2", [128, 16, 32], BF16)
    I("vector", "tensor_copy", reads=[Ysb], writes=[Yc2], out=Yc2[:].rearrange("p g (b s) -> p g b s", s=4), in_=Ysb[:, :, :, 0:4])
    yc_g = C.scr["YC"].t.ap().rearrange("(g c) n -> c g n", c=16)
    s5b = C.scr["S5B"].t.ap().rearrange("(g c) (t x) -> t c g x", c=16, t=8)
    for t in range(8):
        dst = yc_g[:, G0:G0 + 16, 256:T].rearrange("c g (blk t col) -> c g blk t col", blk=8, t=8)
        P.dma("sync", [lambda e, t=t, b_=b_, dst=dst: e.dma_start(out=dst[:, :, b_, t, :], in_=Ysb[16 * t:16 * t + 16, :, b_, 4:68]) for b_ in range(8)],
              reads=[Ysb], writes=[(C.scr["YC"], (hf, t))])
        P.D("sync", s5b[t][:, G0:G0 + 16, :], Yc2[16 * t:16 * t + 16, :, :], reads=[Yc2], writes=[(C.scr["S5B"], (hf, t))])
    P.emit()


def phase_merge(nc, C, l, xsrc, xdst, tiles=None):
    P = Prog(nc)
    I = P.I
    sb = P.sb
    wts = {}
    for nm, src, kc, ncol in (("ba", "w_branch_a", 4, 1024), ("bb", "w_branch_b", 4, 1024), ("bc", "w_branch_c", 4, 1024),
                              ("glu", "s5_w_glu", 4, 512), ("wo", "w_out", 8, 1024)):
        wts[nm] = sb("w_" + nm, [128, kc, ncol], BF16)
        wv = C.din[src][l].rearrange("(k p) c -> p k c", p=128)
        for k in range(kc):
            P.D("gpsimd", wts[nm][:, k, :], wv[:, k, :], writes=[(wts[nm], k)])
    nga = sb("nga", [128, 4, 512], BF16); ngb = sb("ngb", [128, 4, 512], BF16); ycb = sb("ycb", [128, 4, 512], BF16)
    mg = sb("mg", [128, 24, 512], BF16)
    xT = sb("xTm", [128, 8, 512])
    yc = sb("yc32", [128, 4, 512]); x2 = sb("x2", [128, 4, 512]); xh = sb("xh", [128, 4, 512])
    tt = x2
    zb = sb("zb", [128, 4, 512], BF16); zz = sb("zz", [128, 4, 512], BF16)
    zf = yc
    sig = sb("sigg", [128, 512])
    gts = [sb("gt%d" % i, [128, 3, 512]) for i in range(2)]
    m1 = sb("mm1", [128, 512]); m2 = sb("mm2", [128, 512]); m3 = sb("mm3", [128, 512])
    mrg = sb("mrg", [128, 8, 512], BF16)
    xn = sb("xn", [128, 8, 512])
    pg = P.ps("pg", [128, 512]); po = P.ps("po", [128, 512])
    pabc = [[P.ps("pabc%d_%d" % (i, j), [128, 512]) for j in range(3)] for i in range(2)]
    view = lambda b_: b_.t.ap().rearrange("(k p) t -> p k t", p=128)
    xv = view(xsrc); xo = view(xdst)
    for (n0, n, w) in (tiles or TTILES):
        ts_ = slice(n0, n0 + n)
        P.D("sync", nga[:, :, :n], view(C.scr["NGA"])[:, :, ts_], reads=[C.scr["NGA"]], writes=[nga])
        P.D("sync", ngb[:, :, :n], view(C.scr["NGB"])[:, :, ts_], reads=[C.scr["NGB"]], writes=[ngb])
        P.D("sync", ycb[:, :, :n], view(C.scr["YC"])[:, :, ts_], reads=[C.scr["YC"]], writes=[ycb])
        P.D("sync", mg[:, :, :n], view(C.scr["MG"])[:, :, ts_], reads=[C.scr["MG"]], writes=[mg])
        P.D("sync", xT[:, :, :n], xv[:, :, ts_], reads=[(xsrc, n0)], writes=[xT])
        I("vector", "tensor_copy", reads=[ycb], writes=[yc], out=yc[:, :, :n], in_=ycb[:, :, :n])
        I("gpsimd", "tensor_tensor", reads=[yc], writes=[x2], out=x2[:, :, :n], in0=yc[:, :, :n], in1=yc[:, :, :n], op=ALU.mult)
        I("vector", "tensor_scalar", reads=[x2], writes=[x2], out=x2[:, :, :n], in0=x2[:, :, :n], scalar1=0.044715, scalar2=1.0, op0=ALU.mult, op1=ALU.add)
        I("gpsimd", "tensor_tensor", reads=[x2, yc], writes=[x2], out=x2[:, :, :n], in0=x2[:, :, :n], in1=yc[:, :, :n], op=ALU.mult)
        I("scalar", "activation", reads=[x2], writes=[tt], out=tt[:, :, :n], in_=x2[:, :, :n], func=AF.Tanh, scale=0.7978845608028654)
        I("scalar", "mul", reads=[yc], writes=[xh], out=xh[:, :, :n], in_=yc[:, :, :n], mul=0.5)
        I("vector", "scalar_tensor_tensor", reads=[tt, xh], writes=[zf], out=zf[:, :, :n], in0=tt[:, :, :n], scalar=1.0, in1=xh[:, :, :n], op0=ALU.add, op1=ALU.mult)
        I("gpsimd", "tensor_copy", reads=[zf], writes=[zb], out=zb[:, :, :n], in_=zf[:, :, :n])
        for m in range(4):
            for k in range(4):
                P.mm(pg[:, :n], wts["glu"][:, k, m * 128:(m + 1) * 128], zb[:, k, :n], k == 0, k == 3, reads=[wts["glu"], zb], writes=[pg])
            I("scalar", "activation", reads=[pg], writes=[sig], out=sig[:, :n], in_=pg[:, :n], func=AF.Sigmoid)
            I("vector", "tensor_tensor", reads=[zf, sig], writes=[(zz, m)], out=zz[:, m, :n], in0=zf[:, m, :n], in1=sig[:, :n], op=ALU.mult)
        for oc in range(8):
            pa, pb, pc = pabc[oc % 2]
            gt = gts[oc % 2]
            for (pp, wn, src) in ((pa, "ba", nga), (pb, "bb", ngb), (pc, "bc", zz)):
                for k in range(4):
                    P.mm(pp[:, :n], wts[wn][:, k, oc * 128:(oc + 1) * 128], src[:, k, :n], k == 0, k == 3, reads=[wts[wn], src], writes=[pp])
            mgv = AP(mg, oc * 512, [[24 * 512, 128], [8 * 512, 3], [1, n]])
            I("scalar", "activation", reads=[mg], writes=[gt], out=gt[:, :, :n], in_=mgv, func=AF.Sigmoid)
            I("vector", "tensor_tensor", reads=[pa, gt], writes=[m1], out=m1[:, :n], in0=pa[:, :n], in1=gt[:, 0, :n], op=ALU.mult)
            I("vector", "tensor_tensor", reads=[pb, gt], writes=[m2], out=m2[:, :n], in0=pb[:, :n], in1=gt[:, 1, :n], op=ALU.mult)
            I("vector", "tensor_tensor", reads=[pc, gt], writes=[m3], out=m3[:, :n], in0=pc[:, :n], in1=gt[:, 2, :n], op=ALU.mult)
            I("gpsimd", "tensor_tensor", reads=[m1, m2], writes=[m1], out=m1[:, :n], in0=m1[:, :n], in1=m2[:, :n], op=ALU.add)
            I("gpsimd", "tensor_tensor", reads=[m1, m3], writes=[(mrg, oc)], out=mrg[:, oc, :n], in0=m1[:, :n], in1=m3[:, :n], op=ALU.add)
        for oc in range(8):
            for k in range(8):
                P.mm(po[:, :n], wts["wo"][:, k, oc * 128:(oc + 1) * 128], mrg[:, k, :n], k == 0, k == 7, reads=[wts["wo"], mrg], writes=[po])
            I("vector", "scalar_tensor_tensor", reads=[po, xT, C.mod[l]], writes=[(xn, oc)], out=xn[:, oc, :n], in0=po[:, :n],
              scalar=C.mod[l][:, 16 + oc, w:w + 1], in1=xT[:, oc, :n], op0=ALU.mult, op1=ALU.add)
        P.D("sync", xo[:, :, ts_], xn[:, :, :n], reads=[xn], writes=[(xdst, n0)])
    P.emit()


FTILES = [(0, 256, 1)] + [(256 + 1024 * i, 1024, 0) for i in range(4)]


def phase_ffn(nc, C, l, xsrc, xdst, last):
    if last:
        tiles = [(256 + 1024 * i, 1024) for i in range(4)]
    else:
        tiles = [(0, 1280), (1280, 1024), (2304, 1024), (3328, 1024)]
    for (t0, tn) in tiles:
        sub = []
        i = t0
        while i < t0 + tn:
            if i < 256:
                sub.append((i, 256 - i, 1)); i = 256
            else:
                sz = min(512, t0 + tn - i)
                sub.append((i, sz, 0)); i += sz
        with ExitStack() as ost:
            _UID[0] += 1
            hT = Buf(ost.enter_context(nc.sbuf_tensor("hT2_%d" % _UID[0], [128, 8, tn], BF16)), "hT2")
            P = Prog(nc)
            compute_hT(P, C, xsrc, hT, C.g2[l], C.mod[l], 24, tiles=sub, hoff=t0)
            P.emit()
            ffn_tile(nc, C, l, xsrc, xdst, last, hT, t0, tn, [(a - t0, b, w_) for (a, b, w_) in sub])


def ffn_tile(nc, C, l, xsrc, xdst, last, hT, t0, tn, subs):
    P = Prog(nc)
    I = P.I
    sb = P.sb
    mid = sb("mid", [128, 32, tn], BF16)
    w1 = [sb("w1_%d" % i, [128, 8, 512], BF16) for i in range(2)]
    w2 = [sb("w2_%d" % i, [128, 32, 128], BF16) for i in range(2)]
    rl = [sb("rl%d" % i, [128, 512]) for i in range(2)]
    xt = [sb("xtf%d" % i, [128, tn]) for i in range(2)]
    pss = [P.ps("pf%d" % i, [128, 512]) for i in range(4)]
    w1v = C.din["w_ff1"][l].rearrange("(k p) c -> p k c", p=128)
    w2v = C.din["w_ff2"][l].rearrange("(k p) c -> p k c", p=128)
    xv = xsrc.t.ap().rearrange("(k p) t -> p k t", p=128)
    pi = 0
    for g in range(8):
        wb = w1[g % 2]
        P.D("gpsimd", wb[:], w1v[:, :, g * 512:(g + 1) * 512], writes=[wb])
        for m in range(4):
            mc = g * 4 + m
            for (s0, sn, w) in subs:
                pp = pss[pi % 4]; r_ = rl[pi % 2]; pi += 1
                for k in range(8):
                    P.mm(pp[:, :sn], wb[:, k, m * 128:(m + 1) * 128], hT[:, k, s0:s0 + sn], k == 0, k == 7, reads=[wb, hT], writes=[pp])
                I("scalar", "activation", reads=[pp], writes=[r_], out=r_[:, :sn], in_=pp[:, :sn], func=AF.Relu)
                I("gpsimd" if pi % 2 else "vector", "tensor_tensor", reads=[r_], writes=[(mid, (mc, s0))], out=mid[:, mc, s0:s0 + sn], in0=r_[:, :sn], in1=r_[:, :sn], op=ALU.mult)
    if last:
        xn = sb("xnf", [128, 8, tn])
    else:
        xns = [sb("xns%d" % i, [128, tn]) for i in range(2)]
    for oc in range(8):
        wb = w2[oc % 2]
        P.D("gpsimd", wb[:], w2v[:, :, oc * 128:(oc + 1) * 128], writes=[wb])
        x_ = xt[oc % 2]
        P.D("sync", x_[:, :tn], xv[:, oc, t0:t0 + tn], reads=[(xsrc, t0)], writes=[x_])
        for (s0, sn, w) in subs:
            pp = pss[pi % 4]; pi += 1
            for k in range(32):
                P.mm(pp[:, :sn], wb[:, k, :], mid[:, k, s0:s0 + sn], k == 0, k == 31, reads=[wb, mid], writes=[pp])
            if last:
                I("vector", "scalar_tensor_tensor", reads=[pp, x_, C.mod[l]], writes=[(xn, (oc, s0))], out=xn[:, oc, s0:s0 + sn], in0=pp[:, :sn],
                  scalar=C.mod[l][:, 40 + oc, w:w + 1], in1=x_[:, s0:s0 + sn], op0=ALU.mult, op1=ALU.add)
            else:
                xo_ = xns[oc % 2]
                I("vector", "scalar_tensor_tensor", reads=[pp, x_, C.mod[l]], writes=[(xo_, s0)], out=xo_[:, s0:s0 + sn], in0=pp[:, :sn],
                  scalar=C.mod[l][:, 40 + oc, w:w + 1], in1=x_[:, s0:s0 + sn], op0=ALU.mult, op1=ALU.add)
        if not last:
            P.D("sync", xdst[oc * 128:(oc + 1) * 128, t0:t0 + tn], xns[oc % 2][:, :tn], reads=[xns[oc % 2]], writes=[(xdst, (oc, t0))])
    if last:
        fw = sb("fw", [128, 8])
        P.D("sync", fw[:], C.din["final_norm_w"][:], writes=[fw])
        sq = sb("sqf", [128, 512])
        rs = sb("rsf", [128, 512])
        ov = C.out.t.ap().rearrange("(k p) t -> p k t", p=128)
        for (s0, sn, w) in subs:
            pp = pss[pi % 4]; pi += 1
            for k in range(8):
                I("gpsimd" if k % 2 else "vector", "tensor_tensor", reads=[xn], writes=[sq], out=sq[:, :sn], in0=xn[:, k, s0:s0 + sn], in1=xn[:, k, s0:s0 + sn], op=ALU.mult)
                P.mm(pp[:, :sn], C.ones[:], sq[:, :sn], k == 0, k == 7, reads=[sq, C.ones], writes=[pp])
            I("scalar", "activation", reads=[pp], writes=[rs], out=rs[:, :sn], in_=pp[:, :sn], func=AF.Sqrt, scale=1.0 / D, bias=EPS)
            I("vector", "reciprocal", reads=[rs], writes=[rs], out=rs[:, :sn], in_=rs[:, :sn])
            for k in range(8):
                I("vector", "scalar_tensor_tensor", reads=[xn, fw, rs], writes=[xn], out=xn[:, k, s0:s0 + sn], in0=xn[:, k, s0:s0 + sn], scalar=fw[:, k:k + 1],
                  in1=rs[:, :sn], op0=ALU.mult, op1=ALU.mult)
        P.D("sync", ov[:, :, t0 - 256:t0 - 256 + tn], xn[:, :, :tn], reads=[xn], writes=[(C.out, t0)])
    P.emit()
```

```python
import numpy as np
from contextlib import ExitStack
import concourse.bass as bass
import concourse.mybir as mybir
from concourse.bass_utils import run_bass_kernel_spmd

F32 = mybir.dt.float32
BF16 = mybir.dt.bfloat16
I32 = mybir.dt.int32
U8 = mybir.dt.uint8
AF = mybir.ActivationFunctionType
ALU = mybir.AluOpType
AX = mybir.AxisListType

ENGS = ("tensor", "vector", "scalar", "gpsimd", "sync")

T = 4352
NCTX = 256
NCH = 68
D = 1024
DIN = 8208
EPS = 1e-6


class Buf:
    def __init__(self, t, name):
        self.t = t
        self.name = name
        self.tr = {}

    def __getitem__(self, idx):
        return self.t[idx]


def _norm(x):
    if isinstance(x, Buf):
        return (x, "*")
    return x


_UID = [0]


_SHARED = {}


def _shared(nc, n_dma_sems=16):
    k = id(nc)
    if k not in _SHARED:
        st = ExitStack()
        sh = dict(stack=st, esem={}, ecount={e: 0 for e in ENGS}, dsems={}, dval={}, dnext={}, waited={e: {} for e in ENGS})
        for e in ENGS:
            sh["esem"][e] = st.enter_context(nc.semaphore("es_" + e))
        for q in ("sync", "gpsimd"):
            sh["dsems"][q] = [st.enter_context(nc.semaphore("ds_%s%d" % (q, i))) for i in range(n_dma_sems)]
            sh["dval"][q] = [0] * n_dma_sems
            sh["dnext"][q] = 0
        _SHARED[k] = sh
    return _SHARED[k]


class Prog:
    def __init__(self, nc, n_dma_sems=16):
        self.nc = nc
        self.stack = ExitStack()
        sh = _shared(nc, n_dma_sems)
        self.sh = sh
        self.ops = {e: [] for e in ENGS}
        self.esem = sh["esem"]
        self.ecount = sh["ecount"]
        self.dsems = sh["dsems"]
        self.dval = sh["dval"]
        self.dnext = sh["dnext"]
        self.waited = sh["waited"]
        self.nops = 0
        self._nm = 0

    def sb(self, name, shape, dt=F32):
        _UID[0] += 1
        return Buf(self.stack.enter_context(self.nc.sbuf_tensor("%s_%d" % (name, _UID[0]), list(shape), dt)), name)

    def ps(self, name, shape, dt=F32):
        _UID[0] += 1
        b = Buf(self.stack.enter_context(self.nc.psum_tensor("%s_%d" % (name, _UID[0]), list(shape), dt)), name)
        b.excl = True
        return b

    def _deps(self, reads, writes):
        deps = {}

        def add(tok):
            if tok is None:
                return
            s, v = tok
            k = id(s)
            if k not in deps or deps[k][1] < v:
                deps[k] = (s, v)

        for b, key in map(_norm, reads):
            keys = list(b.tr.keys()) if key == "*" else [key, "*"]
            for k in keys:
                tr = b.tr.get(k)
                if tr:
                    add(tr[0])
        for b, key in map(_norm, writes):
            keys = list(b.tr.keys()) if key == "*" else [key, "*"]
            for k in keys:
                tr = b.tr.get(k)
                if tr:
                    add(tr[0])
                    for tok in tr[1].values():
                        add(tok)
        return deps

    def _update(self, reads, writes, tok):
        s, v = tok
        for b, key in map(_norm, reads):
            tr = b.tr.setdefault(key, [None, {}])
            tr[1][id(s)] = tok
        for b, key in map(_norm, writes):
            if key == "*":
                b.tr = {"*": [tok, {}]}
            else:
                b.tr[key] = [tok, {}]

    def _waits(self, eng, deps, skip_own=False):
        w = []
        wd = self.waited[eng]
        for k, (s, v) in deps.items():
            if skip_own and s is self.esem[eng]:
                continue
            if wd.get(k, 0) >= v:
                continue
            wd[k] = v
            w.append((s, v))
        return w

    @staticmethod
    def _excl(reads, writes):
        r2, w2 = [], []
        for x in reads:
            b = x if isinstance(x, Buf) else x[0]
            (w2 if getattr(b, "excl", False) else r2).append(b if getattr(b, "excl", False) else x)
        for x in writes:
            b = x if isinstance(x, Buf) else x[0]
            w2.append(b if getattr(b, "excl", False) else x)
        return r2, w2

    def op(self, eng, fn, reads=(), writes=()):
        reads, writes = self._excl(reads, writes)
        deps = self._deps(reads, writes)
        waits = self._waits(eng, deps, skip_own=(eng == "tensor"))
        self.ecount[eng] += 1
        tok = (self.esem[eng], self.ecount[eng])
        self.ops[eng].append((waits, [fn], tok[0], 1))
        self._update(reads, writes, tok)
        self.nops += 1
        return tok

    def I(self, eng, method, reads=(), writes=(), **kw):
        return self.op(eng, lambda e: getattr(e, method)(**kw), reads, writes)

    def mm(self, out, lhsT, rhs, start, stop, reads, writes):
        return self.op("tensor", lambda e: e.matmul(out, lhsT=lhsT, rhs=rhs, start=start, stop=stop), reads, writes)

    def tr(self, out, in_, ident, reads, writes):
        return self.op("tensor", lambda e: e.transpose(out=out, in_=in_, identity=ident), reads, writes)

    def dma(self, q, fns, reads=(), writes=()):
        if not isinstance(fns, (list, tuple)):
            fns = [fns]
        deps = self._deps(reads, writes)
        i = self.dnext[q]
        self.dnext[q] = (i + 1) % len(self.dsems[q])
        s = self.dsems[q][i]
        prev = self.dval[q][i]
        if prev > 0:
            k = id(s)
            if k not in deps or deps[k][1] < prev:
                deps[k] = (s, prev)
        waits = self._waits(q, deps)
        val = prev + 16 * len(fns)
        self.dval[q][i] = val
        tok = (s, val)
        self.ops[q].append((waits, list(fns), s, 16))
        self._update(reads, writes, tok)
        self.nops += 1
        return tok

    def D(self, q, out, in_, reads=(), writes=(), **kw):
        return self.dma(q, lambda e: e.dma_start(out=out, in_=in_, **kw), reads, writes)

    def emit(self):
        nc = self.nc
        for q in self.dsems:
            fin = []
            for s, v in zip(self.dsems[q], self.dval[q]):
                if v > 0 and self.waited[q].get(id(s), 0) < v:
                    fin.append((s, v))
            if fin:
                self.ops[q].append((fin, [], None, 0))
        ops = self.ops
        with nc.Block() as block:
            def mk(ename):
                def body(e):
                    for waits, fns, sem, inc in ops[ename]:
                        for s, v in waits:
                            e.wait_ge(s, v)
                        for fn in fns:
                            fn(e).then_inc(sem, inc)
                return body
            for ename in ENGS:
                if ops[ename]:
                    getattr(block, ename)(mk(ename))
        self.stack.close()


def AP(t, offset, dims):
    tt = t.t if isinstance(t, Buf) else t
    return bass.AP(tt, offset, [list(d) for d in dims])


TTILES = [(0, 256, 1)] + [(256 + 512 * i, 512, 0) for i in range(8)]

FM_ROUTES = [
    (0, 512, "AQ", 0), (1024, 1024, "AF", 0), (2560, 1536, "BQKV", 0), (4624, 512, "CU", 0), (5136, 3072, "MG", 0),
]
TM_ROUTES = [
    (512, 512, "AI"), (2048, 512, "AG"), (4096, 512, "BG"), (4608, 16, "BBA"),
]
SCR = {
    "AQ": ([512, T], BF16), "AF": ([1024, T], F32), "BQKV": ([1536, T], BF16), "CU": ([512, T], BF16),
    "MG": ([3072, T], BF16), "AI": ([T, 512], BF16), "AG": ([T, 512], BF16), "BG": ([T, 512], BF16),
    "BBA": ([T, 16], F32),
    "XA": ([D, T], F32), "XB": ([D, T], F32),
    "NGA": ([512, T], BF16), "NGB": ([512, T], BF16), "YC": ([512, T], BF16),
    "OF": ([T, 512], F32),
    "OBW": ([T, 512], F32), "S5A": ([512, 256], BF16), "S5B": ([512, 256], BF16),
}


class Ctx:
    pass


def tkey(tok):
    return 0 if tok < 256 else 256 + ((tok - 256) // 512) * 512


def dump_sb(nc, C, buf, name, shape):
    P = Prog(nc)
    d = nc.dram_tensor(name, list(shape), F32, kind="ExternalOutput")
    P.D("sync", d.ap(), buf[:], reads=[buf])
    P.emit()


def make_consts(nc, C):
    P = Prog(nc)
    st = C.gstack
    def gsb(name, shape, dt=F32):
        return Buf(st.enter_context(nc.sbuf_tensor(name, list(shape), dt)), name)
    C.ident = gsb("ident", [128, 128], F32)
    C.identb = gsb("identb", [128, 128], BF16)
    C.ones = gsb("ones", [128, 128], F32)
    C.onesb = gsb("onesb", [128, 128], BF16)
    P.I("gpsimd", "memset", writes=[C.ident], ap=C.ident[:], constant=0.0)
    P.I("gpsimd", "affine_select", reads=[C.ident], writes=[C.ident], out=C.ident[:], in_=C.ident[:],
        pattern=[[-1, 128]], compare_op=ALU.not_equal, fill=1.0, base=0, channel_multiplier=1)
    P.I("vector", "tensor_copy", reads=[C.ident], writes=[C.identb], out=C.identb[:], in_=C.ident[:])
    P.I("vector", "memset", writes=[C.ones], ap=C.ones[:], constant=1.0)
    P.I("vector", "memset", writes=[C.onesb], ap=C.onesb[:], constant=1.0)
    make_masks(P, C)
    make_s5_masks(P, C)
    C.mod = [gsb("mod%d" % l, [128, 48, 2], F32) for l in range(2)]
    C.g1 = [gsb("g1_%d" % l, [128, 8, 2], F32) for l in range(2)]
    C.g2 = [gsb("g2_%d" % l, [128, 8, 2], F32) for l in range(2)]
    P.emit()


def phase_adaln(nc, C, l):
    P = Prog(nc)
    cv = P.sb("cv", [128, 8, 2])
    sc = P.sb("sc", [128, 8, 2])
    ab = P.sb("ab", [128, 48])
    nw = P.sb("nw", [128, 16])
    pm = P.ps("pm", [128, 48, 2])
    P.D("sync", cv[:], C.din["cvec"][:], reads=[], writes=[cv])
    P.D("sync", ab[:], C.din["ada_b"][l], writes=[ab])
    P.D("sync", nw[:, 0:8], C.din["norm1_w"][l], writes=[(nw, 0)])
    P.D("sync", nw[:, 8:16], C.din["norm2_w"][l], writes=[(nw, 1)])
    P.I("scalar", "activation", reads=[cv], writes=[sc], out=sc[:], in_=cv[:], func=AF.Silu)
    wbufs = [P.sb("adw%d" % i, [128, 8, 512]) for i in range(2)]
    aw = C.din["ada_w"][l].rearrange("(k p) c -> p k c", p=128)
    for og in range(12):
        wb = wbufs[og % 2]
        P.dma("sync", [lambda e, k=k, wb=wb, og=og: e.dma_start(out=wb[:, k, :], in_=aw[:, k, og * 512:(og + 1) * 512]) for k in range(8)],
              writes=[wb])
        for m in range(4):
            j = og * 4 + m
            for k in range(8):
                P.mm(pm[:, j, :], wb[:, k, m * 128:(m + 1) * 128], sc[:, k, :], k == 0, k == 7, reads=[wb, sc], writes=[(pm, j)])
    mod = C.mod[l]
    abb = AP(ab, 0, [[48, 128], [1, 48], [0, 2]])
    P.I("vector", "tensor_tensor", reads=[pm, ab], writes=[mod], out=mod[:], in0=pm[:], in1=abb, op=ALU.add)
    for (g, soff, noff) in ((C.g1[l], 8, 0), (C.g2[l], 32, 8)):
        nwb = AP(nw, noff, [[16, 128], [1, 8], [0, 2]])
        P.I("vector", "scalar_tensor_tensor", reads=[mod, nw], writes=[g], out=g[:], in0=mod[:, soff:soff + 8, :], scalar=1.0,
            in1=nwb, op0=ALU.add, op1=ALU.mult)
    P.emit()


def phase_proj(nc, C, l, xsrc):
    P = Prog(nc)
    hT = P.sb("hT", [128, 8, T], BF16)
    compute_hT(P, C, xsrc, hT, C.g1[l], C.mod[l], 0)
    win = C.din["w_in"][l].rearrange("(k p) c -> p k c", p=128)
    wbufs = [P.sb("wb%d" % i, [128, 8, 512], BF16) for i in range(2)]
    pss = [P.ps("pp%d" % i, [128, 512]) for i in range(4)]
    stF = [P.sb("stF%d" % i, [128, T], F32) for i in range(2)]
    wi = 0
    pi = 0
    si = 0
    ei = 0
    for (c0, ncols, sname, row0) in FM_ROUTES:
        dst = C.scr[sname]
        dt = SCR[sname][1]
        for g0 in range(0, ncols, 512):
            wb = wbufs[wi % 2]
            wi += 1
            P.D("gpsimd", wb[:], win[:, :, c0 + g0:c0 + g0 + 512], writes=[wb])
            for m in range(4):
                stb = stF[si % 2]
                si += 1
                stv = stb[:] if dt == F32 else stb[:].bitcast(BF16)[:, 0:T]
                for (n0, nsz, w) in TTILES:
                    pp = pss[pi % 4]
                    pi += 1
                    for k in range(8):
                        P.mm(pp[:, :nsz], wb[:, k, m * 128:(m + 1) * 128], hT[:, k, n0:n0 + nsz], k == 0, k == 7,
                             reads=[wb, (hT, n0)], writes=[pp])
                    if ei % 2 == 0:
                        P.I("scalar", "activation", reads=[pp], writes=[(stb, n0)], out=stv[:, n0:n0 + nsz], in_=pp[:, :nsz], func=AF.Copy)
                    else:
                        P.I("vector", "tensor_copy", reads=[pp], writes=[(stb, n0)], out=stv[:, n0:n0 + nsz], in_=pp[:, :nsz])
                    ei += 1
                r0 = row0 + g0 + m * 128
                P.D("sync", dst[r0:r0 + 128, :], stv, reads=[stb], writes=[(dst, r0)])
    stT = [P.sb("stT%d" % i, [128, 512], F32) for i in range(3)]
    for (c0, ncols, sname) in TM_ROUTES:
        dst = C.scr[sname]
        dt = SCR[sname][1]
        wb = wbufs[wi % 2]
        wi += 1
        P.D("gpsimd", wb[:, :, :ncols], win[:, :, c0:c0 + ncols], writes=[wb])
        for tb in range(T // 128):
            pp = pss[pi % 4]
            pi += 1
            for k in range(8):
                P.mm(pp[:, :ncols], hT[:, k, tb * 128:(tb + 1) * 128], wb[:, k, :ncols], k == 0, k == 7,
                     reads=[wb, (hT, tkey(tb * 128))], writes=[pp])
            stb = stT[si % 3]
            si += 1
            stv = stb[:] if dt == F32 else stb[:].bitcast(BF16)[:, 0:512]
            if ei % 2 == 0:
                P.I("scalar", "activation", reads=[pp], writes=[stb], out=stv[:, :ncols], in_=pp[:, :ncols], func=AF.Copy)
            else:
                P.I("vector", "tensor_copy", reads=[pp], writes=[stb], out=stv[:, :ncols], in_=pp[:, :ncols])
            ei += 1
            P.D("sync", dst[tb * 128:(tb + 1) * 128, :], stv[:, :ncols], reads=[stb], writes=[(dst, tb)])
    P.emit()


def compute_hT(P, C, xsrc, hT, g, mod, shoff, tiles=None, hoff=0):
    nc = P.nc
    xv = xsrc.t.ap().rearrange("(k p) t -> p k t", p=128)
    xts = [P.sb("xt%d" % i, [128, 8, 512]) for i in range(2)]
    sq = P.sb("sq", [128, 8, 512])
    rs = P.sb("rs", [128, 512])
    pss = P.ps("pss", [128, 512])
    for ti, (n0, nsz, w) in enumerate(tiles or TTILES):
        xt = xts[ti % 2]
        P.D("sync", xt[:, :, :nsz], xv[:, :, n0:n0 + nsz], reads=[(xsrc, n0)], writes=[xt])
        P.I("scalar", "activation", reads=[xt], writes=[sq], out=sq[:, :, :nsz], in_=xt[:, :, :nsz], func=AF.Square)
        for k in range(8):
            P.mm(pss[:, :nsz], C.ones[:], sq[:, k, :nsz], k == 0, k == 7, reads=[sq, C.ones], writes=[pss])
        P.I("scalar", "activation", reads=[pss], writes=[rs], out=rs[:, :nsz], in_=pss[:, :nsz], func=AF.Sqrt,
            scale=1.0 / D, bias=EPS)
        P.I("vector", "reciprocal", reads=[rs], writes=[rs], out=rs[:, :nsz], in_=rs[:, :nsz])
        rsb = AP(rs, 0, [[512, 128], [0, 8], [1, nsz]])
        P.I("vector", "tensor_tensor", reads=[xt, rs], writes=[sq], out=sq[:, :, :nsz], in0=xt[:, :, :nsz], in1=rsb, op=ALU.mult)
        for k in range(8):
            eng = "gpsimd" if k % 2 == 0 else "vector"
            P.I(eng, "tensor_scalar", reads=[sq, g, mod], writes=[(hT, n0)], out=hT[:, k, n0 - hoff:n0 - hoff + nsz], in0=sq[:, k, :nsz],
                scalar1=g[:, k, w:w + 1], scalar2=mod[:, shoff + k, w:w + 1], op0=ALU.mult, op1=ALU.add)


def build(nlayers=2, stop=None, debug=(), skip=(), heads=range(4), s5stop=None):
    nc = bass.Bass("TRN2", target_bir_lowering=False)
    C = Ctx()
    C.gstack = ExitStack()
    C.din = {}
    C.heads = heads
    C.s5stop = s5stop
    C.skip = skip

    def din(name, shape, dt=F32):
        C.din[name] = nc.dram_tensor(name, list(shape), dt, kind="ExternalInput")

    din("xT", [D, T]); din("cvec", [128, 8, 2]); din("ada_w", [2, D, 6 * D]); din("ada_b", [2, 128, 48])
    din("norm1_w", [2, 128, 8]); din("norm2_w", [2, 128, 8]); din("w_in", [2, D, DIN])
    din("hgrn_lb", [2, 128, 8]); din("hgrn_norm_w", [2, 128])
    din("s5_lam_re", [2, 64, 2, 32]); din("s5_lam_im", [2, 64, 2, 32]); din("s5_log_dt", [2, 2, 32])
    din("s5_b_re", [2, 64, 32, 16]); din("s5_b_im", [2, 64, 32, 16]); din("s5_c_re", [2, 64, 32, 16]); din("s5_c_im", [2, 64, 32, 16]); din("s5_dtab", [2, 128, 32])
    for nm_, shp_ in (("w_branch_a", [2, 512, D]), ("w_branch_b", [2, 512, D]), ("w_branch_c", [2, 512, D]), ("s5_w_glu", [2, 512, 512]), ("w_out", [2, D, D]), ("w_ff1", [2, D, 4 * D]), ("w_ff2", [2, 4 * D, D]), ("final_norm_w", [128, 8])):
        din(nm_, shp_)
    din("gdn_conv_w", [2, 128, 12, 5]); din("gdn_a_log", [2, 8]); din("gdn_dt_bias", [2, 8]); din("gdn_norm_w", [2, 128])
    C.debug = debug
    C.dbg = {}
    if "s5tab" in debug:
        for nm_, shp_ in (("Er", [64, 32, NPOW]), ("Ei", [64, 32, NPOW]), ("bbr", [64, 32, 16]), ("bbi", [64, 32, 16])):
            C.dbg[nm_] = nc.dram_tensor("dbg_" + nm_, shp_, F32, kind="ExternalOutput")
    if "ob" in debug:
        C.dbg["ob"] = nc.dram_tensor("dbg_ob", [T, 512], F32, kind="ExternalOutput")
    if "oa" in debug:
        C.dbg["oa"] = nc.dram_tensor("dbg_oa", [4, T, 128], F32, kind="ExternalOutput")
    C.scr = {}
    for name, (shape, dt) in SCR.items():
        kind = "ExternalOutput" if name in debug else "Internal"
        C.scr[name] = Buf(nc.dram_tensor(name, shape, dt, kind=kind), name)
    C.out = Buf(nc.dram_tensor("outT", [D, 4096], F32, kind="ExternalOutput"), "outT")
    C.xin = Buf(C.din["xT"], "xT")
    with C.gstack:
        make_consts(nc, C)
        for l in range(nlayers):
            phase_adaln(nc, C, l)
            if 'mod' in debug:
                dump_sb(nc, C, C.mod[l], 'dbg_mod%d' % l, [128, 48, 2])
            if stop == "adaln":
                break
            xsrc = C.xin if l == 0 else C.scr["XB"]
            if 'proj' not in skip:
                phase_proj(nc, C, l, xsrc)
            if stop == "proj":
                break
            if 'mixA' not in skip:
                phase_mixA(nc, C, l, heads=C.heads)
            if stop == "mixA":
                break
            if 'mixB' not in skip:
                phase_mixB(nc, C, l)
            if stop == "mixB":
                break
            if 's5' not in skip:
                phase_s5(nc, C, l)
            if stop == "s5":
                break
            last = (l == nlayers - 1) and nlayers == 2
            if 'merge' not in skip:
                phase_merge(nc, C, l, xsrc, C.scr["XA"], tiles=(TTILES[1:] if last else None))
            if stop == "merge":
                break
            if 'ffn' not in skip:
                phase_ffn(nc, C, l, C.scr["XA"], C.scr["XB"], last)
            if stop == "ffn":
                break
    _SHARED[id(nc)]["stack"].close()
    return nc


def host_inputs(inp, b):
    f = np.float32
    m = {}
    xcat = np.concatenate([inp["ctx"][b], inp["x"][b]], axis=0)
    m["xT"] = np.ascontiguousarray(xcat.T)
    cv = np.stack([inp["c"][b].reshape(8, 128).T, inp["c_ctx"].reshape(8, 128).T], axis=-1)
    m["cvec"] = np.ascontiguousarray(cv.astype(f))
    m["ada_w"] = inp["ada_w"]
    m["ada_b"] = np.ascontiguousarray(inp["ada_b"].reshape(2, 48, 128).transpose(0, 2, 1))
    m["norm1_w"] = np.ascontiguousarray(inp["norm1_w"].reshape(2, 8, 128).transpose(0, 2, 1))
    m["norm2_w"] = np.ascontiguousarray(inp["norm2_w"].reshape(2, 8, 128).transpose(0, 2, 1))
    m["w_in"] = inp["w_in"]
    m["hgrn_lb"] = np.ascontiguousarray(inp["hgrn_lb_logits"].reshape(2, 8, 128).transpose(0, 2, 1))
    m["hgrn_norm_w"] = inp["hgrn_norm_w"]
    m["gdn_conv_w"] = np.ascontiguousarray(inp["gdn_conv_w"].reshape(2, 5, 12, 128).transpose(0, 3, 2, 1))
    m["gdn_a_log"] = np.ascontiguousarray(inp["gdn_a_log"].reshape(2, 8))
    m["gdn_dt_bias"] = np.ascontiguousarray(inp["gdn_dt_bias"].reshape(2, 8))
    m["gdn_norm_w"] = inp["gdn_norm_w"]
    for nm_ in ("w_branch_a", "w_branch_b", "w_branch_c", "s5_w_glu", "w_out", "w_ff1", "w_ff2"):
        m[nm_] = inp[nm_]
    m["final_norm_w"] = np.ascontiguousarray(inp["final_norm_w"].reshape(8, 128).T)
    m["s5_lam_re"] = np.ascontiguousarray(inp["s5_lam_re"].transpose(0, 3, 1, 2))
    m["s5_lam_im"] = np.ascontiguousarray(inp["s5_lam_im"].transpose(0, 3, 1, 2))
    m["s5_log_dt"] = inp["s5_log_dt"]
    m["s5_b_re"] = np.ascontiguousarray(inp["s5_b_re"].transpose(0, 2, 1, 3))
    m["s5_b_im"] = np.ascontiguousarray(inp["s5_b_im"].transpose(0, 2, 1, 3))
    m["s5_c_re"] = np.ascontiguousarray(inp["s5_c_re"].transpose(0, 3, 1, 2))
    m["s5_c_im"] = np.ascontiguousarray(inp["s5_c_im"].transpose(0, 3, 1, 2))
    dt_ = inp["s5_d"].reshape(2, 32, 16).transpose(0, 2, 1)
    m["s5_dtab"] = np.ascontiguousarray(np.tile(dt_, (1, 8, 1)))
    return m


_NC = None


def kernel(**inputs):
    global _NC
    inp = {k: np.asarray(v) for k, v in inputs.items()}
    if _NC is None:
        _NC = build()
    in_maps = [host_inputs(inp, b) for b in range(8)]
    res = run_bass_kernel_spmd(_NC, in_maps, core_ids=list(range(8)))
    out = np.stack([np.ascontiguousarray(res.results[b]["outT"].T) for b in range(8)], axis=0)
    return out.astype(np.float32)


CH_FWD = list(range(68))
CH_BWD = [3, 2, 1, 0] + list(range(67, 3, -1))


def make_masks(P, C):
    nc = P.nc
    st = C.gstack
    def gsb(name, shape, dt=F32):
        return Buf(st.enter_context(nc.sbuf_tensor(name, list(shape), dt)), name)
    one8 = gsb("one8", [64, 8, 64])
    P.I("vector", "memset", writes=[one8], ap=one8[:], constant=1.0)
    C.maskf = {}
    C.maski = {}
    for nm, cm, st_, op in (("U", -1, 1, ALU.is_ge), ("L", 1, -1, ALU.is_ge), ("Us", -1, 1, ALU.is_gt), ("Ls", 1, -1, ALU.is_gt)):
        mf = gsb("maskf" + nm, [64, 8, 64])
        P.I("gpsimd", "affine_select", reads=[one8], writes=[mf], out=mf[:], in_=one8[:], pattern=[[0, 8], [st_, 64]],
            compare_op=op, fill=0.0, base=0, channel_multiplier=cm)
        mi = gsb("maski" + nm, [64, 8, 64], I32)
        P.I("vector", "tensor_copy", reads=[mf], writes=[mi], out=mi[:], in_=mf[:])
        C.maskf[nm] = mf
        C.maski[nm] = mi
    C.rmask = gsb("rmask", [128, 512])
    P.I("vector", "memset", writes=[C.rmask], ap=C.rmask[:], constant=1.0)
    P.I("vector", "memset", reads=[C.rmask], writes=[C.rmask], ap=AP(C.rmask, 0, [[512, 128], [64, 8]]), constant=0.0)


def phase_mixA(nc, C, l, heads=range(4)):
    heads = list(heads)
    P = Prog(nc)
    lbl = P.sb("lbl", [128, 2, 8])
    lb = P.sb("lb", [128, 8])
    oml = P.sb("oml", [128, 8])
    if l == 0:
        P.I("vector", "memset", writes=[lb], ap=lb[:], constant=0.0)
        P.I("vector", "memset", writes=[oml], ap=oml[:], constant=1.0)
    else:
        P.D("sync", lbl[:], C.din["hgrn_lb"].ap().rearrange("l p e -> p l e"), writes=[lbl])
        P.I("vector", "tensor_tensor", reads=[lbl], writes=[lb], out=lb[:], in0=lbl[:, 1, :], in1=lbl[:, 0, :], op=ALU.subtract)
        P.I("scalar", "activation", reads=[lb], writes=[lb], out=lb[:], in_=lb[:], func=AF.Sigmoid)
        P.I("vector", "tensor_scalar", reads=[lb], writes=[oml], out=oml[:], in0=lb[:], scalar1=-1.0, scalar2=1.0, op0=ALU.mult, op1=ALU.add)
    nwb = P.sb("nwb", [64, 128])
    P.D("sync", nwb[:], C.din["hgrn_norm_w"][l].partition_broadcast(64), writes=[nwb])
    v_tms = [P.sb("v_tm%d" % i, [64, 68, 128], BF16) for i in range(1)]
    o_acc = P.sb("o_acc", [64, 68, 128])
    ngT = P.sb("ngT", [128, T], BF16)
    sets = []
    for i in range(2):
        sets.append(dict(attT=P.sb("attT%d" % i, [64, 68, 64], BF16), kd_tm=P.sb("kd_tm%d" % i, [64, 68, 128], BF16),
                         qg=P.sb("qg%d" % i, [128, T], BF16), dec=P.sb("dec%d" % i, [128, 68]), last_dir=[None]))
        P.I("gpsimd", "memset", writes=[sets[i]["attT"]], ap=sets[i]["attT"][:], constant=0.0)
    S = P.sb("S", [128, 128])
    Sb = P.sb("Sb", [128, 128], BF16)
    qpre = P.sb("qpre", [128, 512], BF16)
    W = {n: P.sb(n, [128, 512]) for n in ("q32", "F", "G", "Kt", "CUM", "Dd", "E", "E2")}
    qe = P.sb("qe", [128, 512], BF16)
    ke = P.sb("ke", [128, 512], BF16)
    kdT = P.sb("kdT", [128, 512], BF16)
    qeA = P.sb("qeA", [128, 512], BF16)
    keA = P.sb("keA", [128, 512], BF16)
    refA = P.sb("refA", [128, 16])
    sm = {n: P.sb(n, [128, 8]) for n in ("tot", "lastc", "refc")}
    pa = [P.ps("pa%d" % i, [128, 512]) for i in range(2)]
    pt = [P.ps("pt%d" % i, [128, 512]) for i in range(2)]
    po = [P.ps("po%d" % i, [128, 512]) for i in range(2)]
    pd = [P.ps("pd%d" % i, [128, 512]) for i in range(2)]
    for pz in pa:
        P.I("vector", "memset", writes=[pz], ap=pz[:], constant=0.0)
    gate = P.sb("gate", [64, 8, 128], BF16)
    sg = P.sb("sg", [64, 8, 128])
    sq = P.sb("sqo", [64, 8, 128])
    on = P.sb("on", [64, 8, 128])
    onb = P.sb("onb", [64, 8, 128], BF16)
    ssq = P.sb("ssq", [64, 8])
    agv = C.scr["AG"].t.ap().rearrange("(c p) d -> p c d", p=64)
    aiv = C.scr["AI"].t.ap().rearrange("(c p) d -> p c d", p=64)
    cnt = [0]

    def prep(h, dr, st_):
        attT, kd_tm, qg, dec = st_["attT"], st_["kd_tm"], st_["qg"], st_["dec"]
        dh = dr * 4 + h
        if st_["last_dir"][0] is not None and st_["last_dir"][0] != dr:
            P.I("gpsimd", "memset", reads=[attT], writes=[attT], ap=attT[:], constant=0.0)
        st_["last_dir"][0] = dr
        for (n0, n, w) in TTILES:
            nch = n // 64
            c0 = n0 // 64
            q32, Fb, G, Kt, CUM, Dd, E, E2 = (W[k] for k in ("q32", "F", "G", "Kt", "CUM", "Dd", "E", "E2"))
            P.D("sync", qpre[:, :n], C.scr["AQ"][h * 128:(h + 1) * 128, n0:n0 + n], reads=[C.scr["AQ"]], writes=[qpre])
            r0 = dr * 512 + h * 128
            P.D("sync", Fb[:, :n], C.scr["AF"][r0:r0 + 128, n0:n0 + n], reads=[C.scr["AF"]], writes=[Fb])
            P.I("scalar", "activation", reads=[qpre], writes=[q32], out=q32[:, :n], in_=qpre[:, :n], func=AF.Silu)
            P.I("scalar", "activation", reads=[Fb], writes=[Fb], out=Fb[:, :n], in_=Fb[:, :n], func=AF.Sigmoid)
            P.I("vector", "tensor_scalar", reads=[Fb, oml, lb], writes=[Fb], out=Fb[:, :n], in0=Fb[:, :n], scalar1=oml[:, dh:dh + 1],
                scalar2=lb[:, dh:dh + 1], op0=ALU.mult, op1=ALU.add)
            yield
            P.I("scalar", "activation", reads=[Fb], writes=[G], out=G[:, :n], in_=Fb[:, :n], func=AF.Ln)
            P.I("gpsimd", "tensor_scalar", reads=[Fb], writes=[Kt], out=Kt[:, :n], in0=Fb[:, :n], scalar1=-1.0, scalar2=1.0, op0=ALU.mult, op1=ALU.add)
            P.I("vector", "tensor_tensor_scan", reads=[G, C.rmask], writes=[CUM], out=CUM[:, :n], data0=C.rmask[:, :n], data1=G[:, :n],
                initial=0.0, op0=ALU.mult, op1=ALU.add)

            def cview(buf, off):
                return AP(buf, off, [[512, 128], [64, nch]])

            def bview(buf):
                return AP(buf, 0, [[8, 128], [1, nch], [0, 64]])

            def v3(buf):
                return AP(buf, 0, [[512, 128], [64, nch], [1, 64]])
            if dr == 1:
                P.I("vector", "tensor_copy", reads=[CUM], writes=[sm["tot"]], out=sm["tot"][:, :nch], in_=cview(CUM, 63))
                P.I("gpsimd", "tensor_tensor", reads=[G, CUM], writes=[G], out=G[:, :n], in0=G[:, :n], in1=CUM[:, :n], op=ALU.subtract)
                P.I("vector", "tensor_tensor", reads=[G, sm["tot"]], writes=[CUM], out=v3(CUM), in0=v3(G), in1=bview(sm["tot"]), op=ALU.add)
            yield
            P.I("vector", "tensor_copy", reads=[CUM], writes=[sm["lastc"]], out=sm["lastc"][:, :nch], in_=cview(CUM, 63 if dr == 0 else 0))
            P.I("vector", "tensor_copy", reads=[CUM], writes=[sm["refc"]], out=sm["refc"][:, :nch], in_=cview(CUM, 32))
            P.I("scalar", "activation", reads=[sm["lastc"]], writes=[(dec, c0)], out=dec[:, c0:c0 + nch], in_=sm["lastc"][:, :nch], func=AF.Exp)
            P.I("vector", "tensor_tensor", reads=[CUM, sm["refc"]], writes=[Dd], out=v3(Dd), in0=v3(CUM), in1=bview(sm["refc"]), op=ALU.subtract)
            P.I("vector", "tensor_scalar", reads=[Dd], writes=[Dd], out=Dd[:, :n], in0=Dd[:, :n], scalar1=80.0, scalar2=-80.0, op0=ALU.min, op1=ALU.max)
            yield
            P.I("scalar", "activation", reads=[Dd], writes=[E], out=E[:, :n], in_=Dd[:, :n], func=AF.Exp)
            P.I("gpsimd", "tensor_tensor", reads=[q32, E], writes=[qe], out=qe[:, :n], in0=q32[:, :n], in1=E[:, :n], op=ALU.mult)
            P.I("scalar", "activation", reads=[Dd], writes=[E2], out=E2[:, :n], in_=Dd[:, :n], func=AF.Exp, scale=-1.0)
            P.I("vector", "tensor_tensor", reads=[Kt, E2], writes=[ke], out=ke[:, :n], in0=Kt[:, :n], in1=E2[:, :n], op=ALU.mult)
            yield
            nb2 = 2 * nch
            P.I("vector", "tensor_copy", reads=[CUM], writes=[refA], out=refA[:, :nb2], in_=AP(CUM, 16, [[512, 128], [32, nb2]]))
            v3a = lambda buf: AP(buf, 0, [[512, 128], [32, nb2], [1, 32]])
            P.I("vector", "tensor_tensor", reads=[CUM, refA], writes=[Dd], out=v3a(Dd), in0=v3a(CUM), in1=AP(refA, 0, [[16, 128], [1, nb2], [0, 32]]), op=ALU.subtract)
            P.I("vector", "tensor_scalar", reads=[Dd], writes=[Dd], out=Dd[:, :n], in0=Dd[:, :n], scalar1=40.0, scalar2=-40.0, op0=ALU.min, op1=ALU.max)
            P.I("scalar", "activation", reads=[Dd], writes=[E], out=E[:, :n], in_=Dd[:, :n], func=AF.Exp)
            P.I("gpsimd", "tensor_tensor", reads=[q32, E], writes=[qeA], out=qeA[:, :n], in0=q32[:, :n], in1=E[:, :n], op=ALU.mult)
            yield
            P.I("scalar", "activation", reads=[Dd], writes=[E2], out=E2[:, :n], in_=Dd[:, :n], func=AF.Exp, scale=-1.0)
            P.I("vector", "tensor_tensor", reads=[Kt, E2], writes=[keA], out=keA[:, :n], in0=Kt[:, :n], in1=E2[:, :n], op=ALU.mult)
            P.I("scalar", "activation", reads=[CUM], writes=[E], out=E[:, :n], in_=CUM[:, :n], func=AF.Exp)
            P.I("gpsimd", "tensor_tensor", reads=[q32, E], writes=[(qg, n0)], out=qg[:, n0:n0 + n], in0=q32[:, :n], in1=E[:, :n], op=ALU.mult)
            yield
            P.I("vector", "tensor_tensor", reads=[CUM, sm["lastc"]], writes=[Dd], out=v3(Dd), in0=bview(sm["lastc"]), in1=v3(CUM), op=ALU.subtract)
            P.I("scalar", "activation", reads=[Dd], writes=[E2], out=E2[:, :n], in_=Dd[:, :n], func=AF.Exp)
            P.I("gpsimd", "tensor_tensor", reads=[Kt, E2], writes=[kdT], out=kdT[:, :n], in0=Kt[:, :n], in1=E2[:, :n], op=ALU.mult)
            ppa = pa[cnt[0] % 2]
            ppt = pt[cnt[0] % 2]
            cnt[0] += 1
            for j in range(nch):
                b_ = j * 64
                P.mm(ppa[0:32, b_:b_ + 32], keA[:, b_:b_ + 32], qeA[:, b_:b_ + 32], True, True, reads=[keA, qeA], writes=[ppa])
                P.mm(ppa[32:64, b_ + 32:b_ + 64], keA[:, b_ + 32:b_ + 64], qeA[:, b_ + 32:b_ + 64], True, True, reads=[keA, qeA], writes=[ppa])
                if dr == 0:
                    P.mm(ppa[0:32, b_ + 32:b_ + 64], ke[:, b_:b_ + 32], qe[:, b_ + 32:b_ + 64], True, True, reads=[ke, qe], writes=[ppa])
                else:
                    P.mm(ppa[32:64, b_:b_ + 32], ke[:, b_ + 32:b_ + 64], qe[:, b_:b_ + 32], True, True, reads=[ke, qe], writes=[ppa])
            yield
            mk = C.maski["U" if dr == 0 else "L"]
            P.I("vector", "copy_predicated", reads=[ppa, mk, (attT, n0)], writes=[(attT, n0)], out=attT[:, c0:c0 + nch, :],
                mask=mk[:, :nch, :], data=ppa[0:64, 0:nch * 64].rearrange("p (a b) -> p a b", b=64))
            ptv = ppt[0:64, :].bitcast(BF16).rearrange("p (a b) -> p a b", b=128)
            for j in range(nch):
                P.tr(ptv[:, j, :], kdT[:, j * 64:(j + 1) * 64], C.identb[:], reads=[kdT, C.identb], writes=[ppt])
            P.I("scalar", "activation", reads=[ppt], writes=[(kd_tm, n0)], out=kd_tm[:, c0:c0 + nch, :], in_=ptv[:, :nch, :], func=AF.Copy)
            yield

    def chain(h, dr, st_, v_tm):
        attT, kd_tm, qg, dec = st_["attT"], st_["kd_tm"], st_["qg"], st_["dec"]
        if dr == 0:
            P.D("sync", v_tm[:], aiv[:, :, h * 128:(h + 1) * 128], reads=[C.scr["AI"]], writes=[v_tm])
        P.I("vector", "memset", reads=[S], writes=[S], ap=S[:], constant=0.0)
        P.I("gpsimd", "memset", reads=[Sb], writes=[Sb], ap=Sb[:], constant=0.0)
        for ci, c in enumerate(CH_FWD if dr == 0 else CH_BWD):
            key = tkey(c * 64)
            ppo = po[ci % 2]
            ppd = pd[ci % 2]
            P.mm(ppo[0:64, 0:128], attT[:, c, :], v_tm[:, c, :], True, False, reads=[(attT, key), v_tm], writes=[ppo])
            P.mm(ppo[0:64, 0:128], qg[:, c * 64:(c + 1) * 64], Sb[:], False, True, reads=[(qg, key), Sb], writes=[ppo])
            if dr == 0:
                P.I("scalar", "activation", reads=[ppo], writes=[(o_acc, c)], out=o_acc[:, c, :], in_=ppo[0:64, 0:128], func=AF.Copy)
            else:
                P.I("vector", "tensor_tensor", reads=[ppo, (o_acc, c)], writes=[(o_acc, c)], out=o_acc[:, c, :], in0=ppo[0:64, 0:128],
                    in1=o_acc[:, c, :], op=ALU.add)
            P.mm(ppd[:, 0:128], kd_tm[:, c, :], v_tm[:, c, :], True, True, reads=[(kd_tm, key), v_tm], writes=[ppd])
            P.I("vector", "scalar_tensor_tensor", reads=[S, ppd, (dec, tkey(c * 64) // 64)], writes=[S], out=S[:], in0=S[:], scalar=dec[:, c:c + 1],
                in1=ppd[:, 0:128], op0=ALU.mult, op1=ALU.add)
            P.I("gpsimd", "tensor_copy", reads=[S], writes=[Sb], out=Sb[:], in_=S[:])
            yield
        if dr == 0:
            return
        for (n0, n, w) in TTILES:
            nch = n // 64
            c0 = n0 // 64
            ov = o_acc[:, c0:c0 + nch, :]
            okeys = [(o_acc, c) for c in range(c0, c0 + nch)]
            P.D("sync", gate[:, :nch, :], agv[:, c0:c0 + nch, h * 128:(h + 1) * 128], reads=[C.scr["AG"]], writes=[gate])
            P.I("scalar", "activation", reads=[gate], writes=[sg], out=sg[:, :nch, :], in_=gate[:, :nch, :], func=AF.Silu)
            P.I("gpsimd", "tensor_tensor", reads=okeys, writes=[sq], out=sq[:, :nch, :], in0=ov, in1=ov, op=ALU.mult)
            P.I("vector", "tensor_reduce", reads=[sq], writes=[ssq], out=ssq[:, :nch], in_=sq[:, :nch, :], axis=AX.X, op=ALU.add)
            P.I("scalar", "activation", reads=[ssq], writes=[ssq], out=ssq[:, :nch], in_=ssq[:, :nch], func=AF.Sqrt, scale=1.0 / 128, bias=EPS)
            P.I("vector", "reciprocal", reads=[ssq], writes=[ssq], out=ssq[:, :nch], in_=ssq[:, :nch])
            yield
            P.I("vector", "tensor_tensor", reads=okeys + [ssq], writes=[on], out=on[:, :nch, :], in0=ov, in1=AP(ssq, 0, [[8, 64], [1, nch], [0, 128]]), op=ALU.mult)
            P.I("gpsimd", "tensor_tensor", reads=[on, nwb], writes=[on], out=on[:, :nch, :], in0=on[:, :nch, :], in1=AP(nwb, 0, [[128, 64], [0, nch], [1, 128]]), op=ALU.mult)
            P.I("vector", "tensor_tensor", reads=[on, sg], writes=[onb], out=onb[:, :nch, :], in0=on[:, :nch, :], in1=sg[:, :nch, :], op=ALU.mult)
            ppt = pt[cnt[0] % 2]
            cnt[0] += 1
            ptv = ppt[:, 0:256].bitcast(BF16).rearrange("p (a b) -> p a b", b=64)
            for j in range(nch):
                P.tr(ptv[:, j, :], onb[:, j, :], C.identb[0:64, 0:64], reads=[onb, C.identb], writes=[ppt])
            P.I("scalar", "activation", reads=[ppt], writes=[(ngT, n0)], out=ngT[:, n0:n0 + n], in_=ppt[:, 0:256].bitcast(BF16)[:, 0:n], func=AF.Copy)
            yield
        P.D("sync", C.scr["NGA"][h * 128:(h + 1) * 128, :], ngT[:], reads=[ngT], writes=[(C.scr["NGA"], h)])
        if "oa" in C.debug:
            P.D("sync", C.dbg["oa"].ap().rearrange("h (c p) d -> h p c d", p=64)[h], o_acc[:], reads=[o_acc])
        yield

    units = [(h, dr) for h in heads for dr in (0, 1)]

    def drive(gens):
        alive = list(gens)
        while alive:
            for g_ in list(alive):
                try:
                    next(g_)
                except StopIteration:
                    alive.remove(g_)

    drive([prep(units[0][0], units[0][1], sets[0])])
    for i, (h, dr) in enumerate(units):
        gens = [chain(h, dr, sets[i % 2], v_tms[0])]
        if i + 1 < len(units):
            gens.append(prep(units[i + 1][0], units[i + 1][1], sets[(i + 1) % 2]))
        drive(gens)
    P.emit()


def phase_mixB(nc, C, l):
    with ExitStack() as ost:
        _phase_mixB(nc, C, l, ost)
    _mixB_post(nc, C, l)


def _phase_mixB(nc, C, l, ost):
    def psb(name, shape, dt=F32):
        _UID[0] += 1
        return Buf(ost.enter_context(nc.sbuf_tensor("%s_%d" % (name, _UID[0]), list(shape), dt)), name)
    qnT = [psb("qnT%d" % h, [128, T], BF16) for h in range(4)]
    knT = [psb("knT%d" % h, [128, T], BF16) for h in range(4)]
    vT = [psb("vT%d" % h, [128, T], BF16) for h in range(4)]
    bet = psb("bet", [64, 68, 8])
    nbet = psb("nbet", [64, 68, 8])
    gg = psb("gg", [64, 68, 8])
    nwb = psb("nwbB", [64, 128])
    P = Prog(nc)
    I = P.I
    banks = [P.ps("bk%d" % i, [128, 512]) for i in range(8)]
    b0, b1, b2, b3, b4, b5, b6, b7 = banks
    cw = P.sb("cw", [128, 12, 5])
    P.D("sync", cw[:], C.din["gdn_conv_w"][l], writes=[cw])
    xpads = [P.sb("xpad%d" % i, [128, T + 8], BF16) for i in range(1)]
    for xp in xpads:
        I("gpsimd", "memset", writes=[xp], ap=xp[:], constant=0.0)
    diag = P.sb("diag", [128, 5, 128], BF16)
    xs = P.sb("xs", [128, 512])
    sqb = P.sb("sqb", [128, 512], BF16)
    rs = P.sb("rs", [128, 512])
    bq = C.scr["BQKV"]
    for ch in range(12):
        xp = xpads[0]
        P.D("sync", xp[:, 2:258], bq[ch * 128:(ch + 1) * 128, 0:256], reads=[bq], writes=[(xp, 0)])
        P.D("sync", xp[:, 262:4358], bq[ch * 128:(ch + 1) * 128, 256:T], reads=[bq], writes=[(xp, 1)])
        for j in range(5):
            I("vector", "tensor_scalar", reads=[C.identb, cw], writes=[(diag, j)], out=diag[:, j, :], in0=C.identb[:], scalar1=cw[:, ch, j:j + 1],
              scalar2=None, op0=ALU.mult)
        kind, h = ch // 4, ch % 4
        for ti, (n0, n, w) in enumerate(TTILES):
            po_ = n0 + 2 if n0 == 0 else n0 + 6
            pc = banks[ti % 2]
            for j in range(5):
                P.mm(pc[:, :n], diag[:, j, :], xp[:, po_ + j - 2:po_ + j - 2 + n], j == 0, j == 4, reads=[diag, xp], writes=[pc])
            if kind == 2:
                I("scalar", "activation", reads=[pc], writes=[(vT[h], n0)], out=vT[h][:, n0:n0 + n], in_=pc[:, :n], func=AF.Silu)
                continue
            I("scalar", "activation", reads=[pc], writes=[xs], out=xs[:, :n], in_=pc[:, :n], func=AF.Silu)
            I("gpsimd", "tensor_tensor", reads=[xs], writes=[sqb], out=sqb[:, :n], in0=xs[:, :n], in1=xs[:, :n], op=ALU.mult)
            pq = banks[2 + ti % 2]
            P.mm(pq[:, :n], C.onesb[:], sqb[:, :n], True, True, reads=[sqb, C.onesb], writes=[pq])
            I("scalar", "activation", reads=[pq], writes=[rs], out=rs[:, :n], in_=pq[:, :n], func=AF.Sqrt, bias=EPS)
            I("vector", "reciprocal", reads=[rs], writes=[rs], out=rs[:, :n], in_=rs[:, :n])
            dst = qnT[h] if kind == 0 else knT[h]
            I("vector", "scalar_tensor_tensor", reads=[xs, rs], writes=[(dst, n0)], out=dst[:, n0:n0 + n], in0=xs[:, :n],
              scalar=(128 ** -0.5 if kind == 0 else 1.0), in1=rs[:, :n], op0=ALU.mult, op1=ALU.mult)
    bba = P.sb("bba", [64, 68, 16])
    P.D("sync", bba[:], C.scr["BBA"].t.ap().rearrange("(c p) e -> p c e", p=64), reads=[C.scr["BBA"]], writes=[bba])
    alb = P.sb("alb", [64, 8])
    dtb = P.sb("dtb", [64, 8])
    P.D("sync", alb[:], C.din["gdn_a_log"][l].partition_broadcast(64), writes=[alb])
    P.D("sync", dtb[:], C.din["gdn_dt_bias"][l].partition_broadcast(64), writes=[dtb])
    P.D("sync", nwb[:], C.din["gdn_norm_w"][l].partition_broadcast(64), writes=[nwb])
    xa = P.sb("xa", [64, 68, 8])
    t1 = P.sb("t1", [64, 68, 8])
    I("scalar", "activation", reads=[bba], writes=[bet], out=bet[:], in_=bba[:, :, 0:8], func=AF.Sigmoid)
    I("vector", "tensor_scalar", reads=[bet], writes=[nbet], out=nbet[:], in0=bet[:], scalar1=-1.0, scalar2=None, op0=ALU.mult)
    I("vector", "tensor_tensor", reads=[bba, dtb], writes=[xa], out=xa[:], in0=bba[:, :, 8:16], in1=AP(dtb, 0, [[8, 64], [0, 68], [1, 8]]), op=ALU.add)
    I("vector", "tensor_scalar", reads=[xa], writes=[t1], out=t1[:], in0=xa[:], scalar1=-1.0, scalar2=None, op0=ALU.mult)
    I("vector", "tensor_tensor", reads=[xa, t1], writes=[t1], out=t1[:], in0=xa[:], in1=t1[:], op=ALU.max)
    I("scalar", "activation", reads=[t1], writes=[t1], out=t1[:], in_=t1[:], func=AF.Exp, scale=-1.0)
    I("scalar", "activation", reads=[t1], writes=[t1], out=t1[:], in_=t1[:], func=AF.Ln, bias=1.0)
    I("vector", "tensor_scalar", reads=[xa], writes=[xa], out=xa[:], in0=xa[:], scalar1=0.0, scalar2=None, op0=ALU.max)
    I("vector", "tensor_tensor", reads=[xa, t1], writes=[xa], out=xa[:], in0=xa[:], in1=t1[:], op=ALU.add)
    I("scalar", "activation", reads=[alb], writes=[alb], out=alb[:], in_=alb[:], func=AF.Exp)
    I("vector", "scalar_tensor_tensor", reads=[xa, alb], writes=[gg], out=gg[:], in0=xa[:], scalar=-1.0, in1=AP(alb, 0, [[8, 64], [0, 68], [1, 8]]),
      op0=ALU.mult, op1=ALU.mult)
    P.emit()
    P = Prog(nc)
    I = P.I
    banks = [P.ps("bk%d" % i, [128, 512]) for i in range(8)]
    for bk in banks:
        I("vector", "memset", writes=[bk], ap=bk[:], constant=0.0)
    ident4 = AP(C.ident, 0, [[128, 64], [0, 4], [1, 64]])

    def v464(bank):
        return bank[0:64, 0:256].rearrange("p (h s) -> p h s", s=64)

    def v4128(bank):
        return bank[0:64, 0:512].rearrange("p (h s) -> p h s", s=128)

    def stream(dr, k0, k1, k2, k3):
        sfx = "_%d" % dr
        sb = lambda name, shape, dt=F32: P.sb(name + sfx, shape, dt)
        kv_tm = sb("kv_tm", [64, 8, 128], BF16)
        ct = sb("ct", [128, 8])
        ecum = sb("ecum", [64, 4]); edl = sb("edl", [64, 4]); etot = sb("etot", [128, 4]); be = sb("be", [64, 4])
        G1 = sb("G1", [64, 4, 64]); G2 = sb("G2", [64, 4, 64])
        m1 = sb("m1", [64, 4, 64]); m2 = sb("m2", [64, 4, 64])
        attT = sb("attTb", [64, 4, 64], BF16)
        tmpM = sb("tmpM", [64, 4, 64])
        X = [sb("X%d" % i, [64, 4, 64]) for i in range(2)]
        Y = [sb("Y%d" % i, [64, 4, 64]) for i in range(2)]
        R = [sb("R%d" % i, [64, 4, 64]) for i in range(2)]
        IM2 = sb("IM2", [64, 4, 64])
        TTb = sb("TTb", [64, 4, 64], BF16)
        vb = sb("vb", [64, 4, 128], BF16); kbe = sb("kbe", [64, 4, 128], BF16); kd = sb("kd", [64, 4, 128], BF16)
        u_sb = sb("u_sb", [64, 4, 128]); wTb = sb("wTb", [128, 4, 64])
        v_new = sb("v_new", [64, 4, 128], BF16)
        t2 = sb("t2", [64, 4, 128]); o_sb = [sb("o_sb%d" % i, [64, 4, 128]) for i in range(2)]
        S = sb("Sg", [128, 4, 128]); St = sb("St", [128, 4, 128]); Sb = sb("Sbg", [128, 4, 128], BF16)
        ODST = C.scr["OF"] if dr == 0 else C.scr["OBW"]
        I("vector", "memset", writes=[S], ap=S[:], constant=0.0)
        I("gpsimd", "memset", writes=[Sb], ap=Sb[:], constant=0.0)
        mU = C.maskf["U" if dr == 0 else "L"]
        mLs = C.maskf["Ls" if dr == 0 else "Us"]
        triM = mU[:, 0, :]
        tri4 = AP(mU, 0, [[512, 64], [0, 4], [1, 64]])
        for ci, c in enumerate(CH_FWD if dr == 0 else CH_BWD):
            key = tkey(c * 64)
            cs = slice(c * 64, (c + 1) * 64)
            goff = c * 8 + dr * 4
            g4 = AP(gg, goff, [[544, 64], [1, 4]])
            g4b = AP(gg, goff, [[544, 64], [1, 4], [0, 64]])
            bet4b = AP(bet, goff, [[544, 64], [1, 4], [0, 128]])
            nbet4b = AP(nbet, goff, [[544, 64], [1, 4], [0, 64]])
            bet4 = AP(bet, goff, [[544, 64], [1, 4]])
            k0v = k0[0:64, :].bitcast(BF16).rearrange("p (a b) -> p a b", b=128)
            for h in range(4):
                P.tr(k0v[:, h, :], knT[h][:, cs], C.identb[:], reads=[(knT[h], key), C.identb], writes=[k0])
                P.tr(k0v[:, 4 + h, :], vT[h][:, cs], C.identb[:], reads=[(vT[h], key), C.identb], writes=[k0])
            k1v = k1[0:64, :].rearrange("p (h two s) -> p h two s", h=4, two=2)
            for h in range(4):
                P.mm(k1[0:64, (2 * h) * 64:(2 * h + 1) * 64], knT[h][:, cs], knT[h][:, cs], True, True, reads=[(knT[h], key)], writes=[k1])
                P.mm(k1[0:64, (2 * h + 1) * 64:(2 * h + 2) * 64], knT[h][:, cs], qnT[h][:, cs], True, True, reads=[(knT[h], key), (qnT[h], key)], writes=[k1])
            P.mm(k2[0:64, 256:260], triM, g4, True, True, reads=[mU, gg], writes=[k2])
            P.mm(k2[:, 260:264], C.ones[0:64, :], g4, True, True, reads=[C.ones, gg], writes=[k2])
            yield
            I("scalar", "activation", reads=[k0], writes=[kv_tm], out=kv_tm[:], in_=k0v, func=AF.Copy)
            I("vector", "tensor_copy", reads=[k2], writes=[ct], out=ct[:], in_=k2[:, 256:264])
            I("scalar", "activation", reads=[ct], writes=[ecum], out=ecum[:], in_=ct[0:64, 0:4], func=AF.Exp)
            I("vector", "tensor_tensor", reads=[ct], writes=[edl], out=edl[:], in0=ct[0:64, 4:8], in1=ct[0:64, 0:4], op=ALU.subtract)
            I("scalar", "activation", reads=[edl], writes=[edl], out=edl[:], in_=edl[:], func=AF.Exp)
            I("scalar", "activation", reads=[ct], writes=[etot], out=etot[:], in_=ct[:, 4:8], func=AF.Exp)
            I("vector", "tensor_tensor", reads=[bet, ecum], writes=[be], out=be[:], in0=bet4, in1=ecum[:], op=ALU.mult)
            yield
            I("gpsimd", "tensor_copy", reads=[gg], writes=[G1], out=G1[:], in_=g4b)
            I("vector", "scalar_tensor_tensor", reads=[mU, gg], writes=[G2], out=G2[:], in0=tri4, scalar=-1.0, in1=g4b, op0=ALU.mult, op1=ALU.mult)
            for h in range(4):
                P.mm(k2[0:64, h * 64:(h + 1) * 64], G1[:, h, :], triM, True, False, reads=[G1, mU], writes=[k2])
                P.mm(k2[0:64, h * 64:(h + 1) * 64], G2[:, h, :], C.ones[0:64, 0:64], False, True, reads=[G2, C.ones], writes=[k2])
            yield
            Dv = v464(k2)
            I("vector", "tensor_scalar", reads=[k2], writes=[m1], out=m1[:], in0=Dv, scalar1=0.0, scalar2=None, op0=ALU.min)
            I("vector", "tensor_scalar", reads=[k2], writes=[m2], out=m2[:], in0=Dv, scalar1=0.0, scalar2=-1.0, op0=ALU.max, op1=ALU.mult)
            I("scalar", "activation", reads=[m1], writes=[m1], out=m1[:], in_=m1[:], func=AF.Exp)
            I("scalar", "activation", reads=[m2], writes=[m2], out=m2[:], in_=m2[:], func=AF.Exp)
            I("gpsimd", "tensor_tensor", reads=[m1, mU], writes=[m1], out=m1[:], in0=m1[:], in1=mU[:, 0:4, :], op=ALU.mult)
            I("gpsimd", "tensor_tensor", reads=[m2, mLs], writes=[m2], out=m2[:], in0=m2[:], in1=mLs[:, 0:4, :], op=ALU.mult)
            yield
            I("vector", "tensor_tensor", reads=[k1, m1], writes=[attT], out=attT[:], in0=k1v[:, :, 1, :], in1=m1[:], op=ALU.mult)
            I("vector", "tensor_tensor", reads=[k1, nbet], writes=[tmpM], out=tmpM[:], in0=k1v[:, :, 0, :], in1=nbet4b, op=ALU.mult)
            I("gpsimd", "tensor_tensor", reads=[tmpM, m2], writes=[X[0]], out=X[0][:], in0=tmpM[:], in1=m2[:], op=ALU.mult)
            k0n = k0[0:64, 0:256].rearrange("p (a b) -> p a b", b=64)
            for h in range(4):
                P.tr(k0n[:, h, :], X[0][:, h, :], C.ident[0:64, 0:64], reads=[X[0], C.ident], writes=[k0])
            yield
            I("scalar", "activation", reads=[k0], writes=[Y[0]], out=Y[0][:], in_=k0n, func=AF.Copy)
            I("vector", "tensor_tensor", reads=[k0, C.ident], writes=[R[0]], out=R[0][:], in0=k0n, in1=ident4, op=ALU.add)
            yield
            cur = 0
            for step in range(5):
                last = step == 4
                nxt = 1 - cur
                if not last:
                    for h in range(4):
                        P.mm(k2[0:64, h * 64:(h + 1) * 64], X[cur][:, h, :], Y[cur][:, h, :], True, True, reads=[X[cur], Y[cur]], writes=[k2])
                for h in range(4):
                    P.mm(k3[0:64, h * 64:(h + 1) * 64], Y[cur][:, h, :], X[cur][:, h, :], True, True, reads=[X[cur], Y[cur]], writes=[k3])
                if not last:
                    I("scalar", "activation", reads=[k2], writes=[Y[nxt]], out=Y[nxt][:], in_=v464(k2), func=AF.Copy)
                I("vector", "tensor_tensor", reads=[k3, C.ident], writes=[IM2], out=IM2[:], in0=v464(k3), in1=ident4, op=ALU.add)
                if not last:
                    I("scalar", "activation", reads=[k3], writes=[X[nxt]], out=X[nxt][:], in_=v464(k3), func=AF.Copy)
                yield
                for h in range(4):
                    P.mm(k2[0:64, h * 64:(h + 1) * 64], IM2[:, h, :], R[cur][:, h, :], True, True, reads=[IM2, R[cur]], writes=[k2])
                yield
                if not last:
                    I("vector", "tensor_copy", reads=[k2], writes=[R[nxt]], out=R[nxt][:], in_=v464(k2))
                else:
                    I("vector", "tensor_copy", reads=[k2], writes=[TTb], out=TTb[:], in_=v464(k2))
                cur = nxt
                yield
            TT_ = TTb
            I("gpsimd", "tensor_tensor", reads=[kv_tm, bet], writes=[vb], out=vb[:], in0=kv_tm[:, 4:8, :], in1=bet4b, op=ALU.mult)
            I("vector", "tensor_tensor", reads=[kv_tm, be], writes=[kbe], out=kbe[:], in0=kv_tm[:, 0:4, :], in1=AP(be, 0, [[4, 64], [1, 4], [0, 128]]), op=ALU.mult)
            I("gpsimd", "tensor_tensor", reads=[kv_tm, edl], writes=[kd], out=kd[:], in0=kv_tm[:, 0:4, :], in1=AP(edl, 0, [[4, 64], [1, 4], [0, 128]]), op=ALU.mult)
            for h in range(4):
                P.mm(k1[0:64, h * 128:(h + 1) * 128], TT_[:, h, :], vb[:, h, :], True, True, reads=[TT_, vb], writes=[k1])
            for h in range(4):
                P.mm(k0[:, h * 64:(h + 1) * 64], kbe[:, h, :], TT_[:, h, :], True, True, reads=[kbe, TT_], writes=[k0])
            yield
            I("scalar", "activation", reads=[k1], writes=[u_sb], out=u_sb[:], in_=v4128(k1), func=AF.Copy)
            I("vector", "tensor_copy", reads=[k0], writes=[wTb], out=wTb[:], in_=k0[:, 0:256].rearrange("p (h s) -> p h s", s=64))
            yield
            for h in range(4):
                P.mm(k1[0:64, h * 128:(h + 1) * 128], wTb[:, h, :], S[:, h, :], True, True, reads=[wTb, S], writes=[k1])
            for h in range(4):
                P.mm(k3[0:64, h * 128:(h + 1) * 128], qnT[h][:, cs], Sb[:, h, :], True, True, reads=[(qnT[h], key), Sb], writes=[k3])
            yield
            I("vector", "tensor_tensor", reads=[u_sb, k1], writes=[v_new], out=v_new[:], in0=u_sb[:], in1=v4128(k1), op=ALU.subtract)
            I("vector", "tensor_tensor", reads=[k3, ecum], writes=[t2], out=t2[:], in0=v4128(k3), in1=AP(ecum, 0, [[4, 64], [1, 4], [0, 128]]), op=ALU.mult)
            for h in range(4):
                P.mm(k3[0:64, h * 128:(h + 1) * 128], attT[:, h, :], v_new[:, h, :], True, True, reads=[attT, v_new], writes=[k3])
            for h in range(4):
                P.mm(k1[:, h * 128:(h + 1) * 128], kd[:, h, :], v_new[:, h, :], True, True, reads=[kd, v_new], writes=[k1])
            yield
            osb = o_sb[ci % 2]
            I("vector", "tensor_tensor", reads=[k3, t2], writes=[osb], out=osb[:], in0=v4128(k3), in1=t2[:], op=ALU.add)
            I("gpsimd", "tensor_tensor", reads=[S, etot], writes=[St], out=St[:], in0=S[:], in1=AP(etot, 0, [[4, 128], [1, 4], [0, 128]]), op=ALU.mult)
            I("vector", "tensor_tensor", reads=[St, k1], writes=[S], out=S[:], in0=St[:], in1=k1[:, :].rearrange("p (h s) -> p h s", s=128), op=ALU.add)
            I("scalar", "activation", reads=[S], writes=[Sb], out=Sb[:], in_=S[:], func=AF.Copy)
            P.D("sync", ODST[cs, :], osb[:].rearrange("p h d -> p (h d)"), reads=[osb], writes=[(ODST, c)])
            yield

    gens = [stream(0, *banks[0:4]), stream(1, *banks[4:8])]
    alive = list(gens)
    while alive:
        for g_ in list(alive):
            try:
                next(g_)
            except StopIteration:
                alive.remove(g_)
    P.emit()


def _mixB_post(nc, C, l):
    P = Prog(nc)
    I = P.I
    sb = P.sb
    nwb = sb("nwbB2", [64, 128])
    P.D("sync", nwb[:], C.din["gdn_norm_w"][l].partition_broadcast(64), writes=[nwb])
    pt = [P.ps("ptB%d" % i, [128, 512]) for i in range(2)]
    of = [sb("ofB%d" % i, [64, 8, 512]) for i in range(2)]
    ob = [sb("obB%d" % i, [64, 8, 512]) for i in range(2)]
    gate = [sb("gateB%d" % i, [64, 8, 512], BF16) for i in range(2)]
    sq = sb("sqB", [64, 8, 512]); ssq = sb("ssqB", [64, 32]); sg = sb("sgB", [64, 8, 512])
    onb = sb("onbB", [64, 8, 512], BF16)
    ngT = sb("ngTB", [128, 4, 512], BF16)
    tmv = lambda b_: b_.t.ap().rearrange("(c p) d -> p c d", p=64)
    ngb_v = C.scr["NGB"].t.ap().rearrange("(h p) t -> p h t", p=128)
    for ti, (n0, n, w) in enumerate(TTILES):
        nch = n // 64
        c0 = n0 // 64
        f_, b_, g_ = of[ti % 2], ob[ti % 2], gate[ti % 2]
        P.D("sync", f_[:, :nch, :], tmv(C.scr["OF"])[:, c0:c0 + nch, :], reads=[C.scr["OF"]], writes=[f_])
        P.D("sync", b_[:, :nch, :], tmv(C.scr["OBW"])[:, c0:c0 + nch, :], reads=[C.scr["OBW"]], writes=[b_])
        P.D("sync", g_[:, :nch, :], tmv(C.scr["BG"])[:, c0:c0 + nch, :], reads=[C.scr["BG"]], writes=[g_])
        I("gpsimd", "tensor_tensor", reads=[f_, b_], writes=[f_], out=f_[:, :nch, :], in0=f_[:, :nch, :], in1=b_[:, :nch, :], op=ALU.add)
        if "ob" in C.debug:
            P.D("sync", C.dbg["ob"].ap().rearrange("(c p) d -> p c d", p=64)[:, c0:c0 + nch, :], f_[:, :nch, :], reads=[f_])
        I("scalar", "activation", reads=[g_], writes=[sg], out=sg[:, :nch, :], in_=g_[:, :nch, :], func=AF.Silu)
        I("gpsimd", "tensor_tensor", reads=[f_], writes=[sq], out=sq[:, :nch, :], in0=f_[:, :nch, :], in1=f_[:, :nch, :], op=ALU.mult)
        I("vector", "tensor_reduce", reads=[sq], writes=[ssq], out=ssq[:, :nch * 4], in_=sq[:, :nch, :].rearrange("p c (h d) -> p (c h) d", d=128), axis=AX.X, op=ALU.add)
        I("scalar", "activation", reads=[ssq], writes=[ssq], out=ssq[:, :nch * 4], in_=ssq[:, :nch * 4], func=AF.Sqrt, scale=1.0 / 128, bias=EPS)
        I("vector", "reciprocal", reads=[ssq], writes=[ssq], out=ssq[:, :nch * 4], in_=ssq[:, :nch * 4])
        I("vector", "tensor_tensor", reads=[f_, ssq], writes=[sq], out=sq[:, :nch, :].rearrange("p c (h d) -> p (c h) d", d=128),
          in0=f_[:, :nch, :].rearrange("p c (h d) -> p (c h) d", d=128), in1=AP(ssq, 0, [[32, 64], [1, nch * 4], [0, 128]]), op=ALU.mult)
        I("gpsimd", "tensor_tensor", reads=[sq, nwb], writes=[sq], out=sq[:, :nch, :].rearrange("p c (h d) -> p (c h) d", d=128),
          in0=sq[:, :nch, :].rearrange("p c (h d) -> p (c h) d", d=128), in1=AP(nwb, 0, [[128, 64], [0, nch * 4], [1, 128]]), op=ALU.mult)
        I("vector", "tensor_tensor", reads=[sq, sg], writes=[onb], out=onb[:, :nch, :], in0=sq[:, :nch, :], in1=sg[:, :nch, :], op=ALU.mult)
        for h in range(4):
            ppt = pt[h % 2]
            ptv = ppt[:, 0:256].bitcast(BF16).rearrange("p (a b) -> p a b", b=64)
            for j in range(nch):
                P.tr(ptv[:, j, :], onb[:, j, h * 128:(h + 1) * 128], C.identb[0:64, 0:64], reads=[onb, C.identb], writes=[ppt])
            I("scalar", "activation", reads=[ppt], writes=[(ngT, h)], out=ngT[:, h, :n], in_=ppt[:, 0:256].bitcast(BF16)[:, 0:n], func=AF.Copy)
        P.D("sync", ngb_v[:, :, n0:n0 + n], ngT[:, :, :n], reads=[ngT], writes=[(C.scr["NGB"], n0)])
    P.emit()


TWO_PI = 6.283185307179586
PI = 3.141592653589793
NPOW = 129


def make_s5_masks(P, C):
    nc = P.nc
    st = C.gstack
    def gsb(name, shape, dt=F32):
        return Buf(st.enter_context(nc.sbuf_tensor(name, list(shape), dt)), name)
    A = gsb("s5mA", [8, 128]); Bge = gsb("s5mB1", [8, 128]); Ble = gsb("s5mB2", [8, 128]); on8 = gsb("s5on", [8, 128])
    C.maskZ = [gsb("maskZf", [128, 128]), gsb("maskZb", [128, 128])]
    pm = P.ps("pmask", [128, 512])
    P.I("vector", "memset", writes=[on8], ap=on8[:], constant=1.0)
    P.I("gpsimd", "affine_select", reads=[on8], writes=[A], out=A[:], in_=on8[:], pattern=[[1, 128]], compare_op=ALU.is_ge, fill=0.0, base=0, channel_multiplier=-16)
    P.I("gpsimd", "affine_select", reads=[A], writes=[A], out=A[:], in_=A[:], pattern=[[-1, 128]], compare_op=ALU.is_ge, fill=0.0, base=15, channel_multiplier=16)
    P.I("gpsimd", "affine_select", reads=[on8], writes=[Bge], out=Bge[:], in_=on8[:], pattern=[[1, 8], [0, 16]], compare_op=ALU.is_ge, fill=0.0, base=0, channel_multiplier=-1)
    P.I("gpsimd", "affine_select", reads=[on8], writes=[Ble], out=Ble[:], in_=on8[:], pattern=[[-1, 8], [0, 16]], compare_op=ALU.is_ge, fill=0.0, base=0, channel_multiplier=1)
    P.mm(pm[:, 0:128], A[:], Bge[:], True, True, reads=[A, Bge], writes=[pm])
    P.mm(pm[:, 128:256], A[:], Ble[:], True, True, reads=[A, Ble], writes=[pm])
    P.I("vector", "tensor_copy", reads=[pm], writes=[C.maskZ[0]], out=C.maskZ[0][:], in_=pm[:, 0:128])
    P.I("vector", "tensor_copy", reads=[pm], writes=[C.maskZ[1]], out=C.maskZ[1][:], in_=pm[:, 128:256])


def phase_s5(nc, C, l):
    CU = C.scr["CU"]
    YC = C.scr["YC"]
    cu_g = CU.t.ap().rearrange("(g c) n -> c g n", c=16)
    yc_g = YC.t.ap().rearrange("(g c) n -> c g n", c=16)
    P = Prog(nc)
    S5A = C.scr["S5A"]
    for q in range(4):
        tmpc = P.sb("tmpc%d" % q, [128, 256], BF16)
        ctr = P.sb("ctr%d" % q, [128, 8, 8, 4], BF16)
        P.D("sync", tmpc[:], CU[q * 128:(q + 1) * 128, 0:256], reads=[CU], writes=[tmpc])
        P.I("vector", "tensor_copy", reads=[tmpc], writes=[ctr], out=ctr[:], in_=tmpc[:].rearrange("p (sc blk t) -> p t blk sc", sc=4, blk=8))
        P.D("sync", S5A[q * 128:(q + 1) * 128, :], ctr[:].rearrange("p t b s -> p (t b s)"), reads=[ctr], writes=[(S5A, q)])
    P.emit()
    for hf in range(2):
        s5_half(nc, C, l, hf, cu_g, yc_g)
    P = Prog(nc)
    S5B = C.scr["S5B"]
    for q in range(4):
        tmpc = P.sb("tmpd%d" % q, [128, 256], BF16)
        ctr = P.sb("ctd%d" % q, [128, 8, 8, 4], BF16)
        P.D("sync", ctr[:].rearrange("p t b s -> p (t b s)"), S5B[q * 128:(q + 1) * 128, :], reads=[S5B], writes=[ctr])
        P.I("vector", "tensor_copy", reads=[ctr], writes=[tmpc], out=tmpc[:].rearrange("p (sc blk t) -> p t blk sc", sc=4, blk=8), in_=ctr[:])
        P.D("sync", YC[q * 128:(q + 1) * 128, 0:256], tmpc[:], reads=[tmpc], writes=[(YC, q)])
    P.emit()


def s5_half(nc, C, l, hf, cu_g, yc_g):
    G0 = hf * 16
    hst = ExitStack()
    cntr = [0]

    def pst(name, shape, dt=F32):
        _UID[0] += 1
        return Buf(hst.enter_context(nc.sbuf_tensor("s5p_%s_%d" % (name, _UID[0]), list(shape), dt)), name)

    with hst:
        Er = pst("Er", [64, 32, NPOW]); Ei = pst("Ei", [64, 32, NPOW])
        bbr = pst("bbr", [64, 32, 16]); bbi = pst("bbi", [64, 32, 16])
        cre = pst("cre", [64, 16, 16]); cim = pst("cim", [64, 16, 16]); ncim = pst("ncim", [64, 16, 16]); ncre = pst("ncre", [64, 16, 16])
        dtab = pst("dtab", [128, 16])
        U2 = pst("U2", [128, 16, 8, 68], BF16)
        Sall = pst("Sall", [64, 2, 2, 16, 68])
        XPb = pst("XPb", [64, 2, 2, 16, 68], BF16)
        P = Prog(nc)
        I = P.I
        sb = P.sb
        lre = sb("lre", [64, 2, 16]); lim = sb("lim", [64, 2, 16]); ldt = sb("ldt", [64, 2, 16])
        P.D("sync", lre[:], C.din["s5_lam_re"][l][:, :, G0:G0 + 16], writes=[lre])
        P.D("sync", lim[:], C.din["s5_lam_im"][l][:, :, G0:G0 + 16], writes=[lim])
        P.D("sync", ldt[:], C.din["s5_log_dt"][l].partition_broadcast(64)[:, :, G0:G0 + 16], writes=[ldt])
        br = sb("br", [64, 16, 16]); bi = sb("bi", [64, 16, 16])
        P.D("sync", br[:], C.din["s5_b_re"][l][:, G0:G0 + 16, :], writes=[br])
        P.D("sync", bi[:], C.din["s5_b_im"][l][:, G0:G0 + 16, :], writes=[bi])
        P.D("sync", cre[:], C.din["s5_c_re"][l][:, G0:G0 + 16, :], writes=[cre])
        P.D("sync", cim[:], C.din["s5_c_im"][l][:, G0:G0 + 16, :], writes=[cim])
        I("vector", "tensor_scalar", reads=[cim], writes=[ncim], out=ncim[:], in0=cim[:], scalar1=-1.0, scalar2=None, op0=ALU.mult)
        I("vector", "tensor_scalar", reads=[cre], writes=[ncre], out=ncre[:], in0=cre[:], scalar1=-1.0, scalar2=None, op0=ALU.mult)
        P.D("sync", dtab[:], C.din["s5_dtab"][l][:, G0:G0 + 16], writes=[dtab])
        U2c = sb("U2c", [128, 16, 32], BF16)
        s5a = C.scr["S5A"].t.ap().rearrange("(g c) (t x) -> t c g x", c=16, t=8)
        for t in range(8):
            src = cu_g[:, G0:G0 + 16, 256:T].rearrange("c g (blk t col) -> c g blk t col", blk=8, t=8)
            P.dma("sync", [lambda e, t=t, b_=b_, src=src: e.dma_start(out=U2[16 * t:16 * t + 16, :, b_, 4:68], in_=src[:, :, b_, t, :]) for b_ in range(8)],
                  reads=[C.scr["CU"]], writes=[(U2, t)])
            P.D("sync", U2c[16 * t:16 * t + 16, :, :], s5a[t][:, G0:G0 + 16, :], reads=[C.scr["S5A"]], writes=[(U2c, t)])
        I("vector", "tensor_copy", reads=[U2c, U2], writes=[U2], out=U2[:, :, :, 0:4], in_=U2c[:].rearrange("p g (b s) -> p g b s", s=4))
        dtt = sb("dtt", [64, 32]); xx = sb("xx", [64, 32]); th = sb("th", [64, 32]); lr = sb("lr", [64, 32])
        lre2 = lre[:].rearrange("p d g -> p (d g)"); lim2 = lim[:].rearrange("p d g -> p (d g)"); ldt2 = ldt[:].rearrange("p d g -> p (d g)")
        I("scalar", "activation", reads=[ldt], writes=[dtt], out=dtt[:], in_=ldt2, func=AF.Exp)
        I("vector", "tensor_scalar", reads=[lre], writes=[lr], out=lr[:], in0=lre2, scalar1=-1e-4, scalar2=None, op0=ALU.min)
        I("vector", "tensor_tensor", reads=[lr, dtt], writes=[xx], out=xx[:], in0=lr[:], in1=dtt[:], op=ALU.mult)
        I("vector", "tensor_tensor", reads=[lim, dtt], writes=[th], out=th[:], in0=lim2, in1=dtt[:], op=ALU.mult)
        jt = sb("jt", [64, NPOW])
        I("gpsimd", "iota", writes=[(jt, 0)], out=jt[:, 0:65], pattern=[[1, 65]], base=0, channel_multiplier=0, allow_small_or_imprecise_dtypes=True)
        I("gpsimd", "iota", writes=[(jt, 1)], out=jt[:, 65:129], pattern=[[-1, 64]], base=0, channel_multiplier=0, allow_small_or_imprecise_dtypes=True)
        ph = sb("ph", [64, 32, NPOW]); qf = sb("qf", [64, 32, NPOW]); qi = sb("qi", [64, 32, NPOW], I32); mk = sb("mk", [64, 32, NPOW])
        jb_ = AP(jt, 0, [[NPOW, 64], [0, 32], [1, NPOW]])
        thb = AP(th, 0, [[32, 64], [1, 32], [0, NPOW]])
        xb = AP(xx, 0, [[32, 64], [1, 32], [0, NPOW]])
        I("vector", "tensor_tensor", reads=[th, jt], writes=[ph], out=ph[:], in0=thb, in1=jb_, op=ALU.mult)
        I("vector", "tensor_scalar", reads=[ph], writes=[qf], out=qf[:], in0=ph[:], scalar1=1.0 / TWO_PI, scalar2=None, op0=ALU.mult)
        I("vector", "tensor_copy", reads=[qf], writes=[qi], out=qi[:], in_=qf[:])
        I("vector", "tensor_copy", reads=[qi], writes=[qf], out=qf[:], in_=qi[:])
        I("vector", "scalar_tensor_tensor", reads=[qf, ph], writes=[ph], out=ph[:], in0=qf[:], scalar=-TWO_PI, in1=ph[:], op0=ALU.mult, op1=ALU.add)
        for (cmp_, sgn) in ((ALU.is_gt, -1.0), (ALU.is_lt, 1.0)):
            I("vector", "tensor_scalar", reads=[ph], writes=[mk], out=mk[:], in0=ph[:], scalar1=(PI if sgn < 0 else -PI), scalar2=None, op0=cmp_)
            I("vector", "scalar_tensor_tensor", reads=[mk, ph], writes=[ph], out=ph[:], in0=mk[:], scalar=sgn * TWO_PI, in1=ph[:], op0=ALU.mult, op1=ALU.add)
        I("scalar", "activation", reads=[ph], writes=[Ei], out=Ei[:], in_=ph[:], func=AF.Sin)
        I("vector", "tensor_scalar", reads=[ph], writes=[ph], out=ph[:], in0=ph[:], scalar1=PI / 2, scalar2=None, op0=ALU.add)
        I("vector", "tensor_scalar", reads=[ph], writes=[mk], out=mk[:], in0=ph[:], scalar1=PI, scalar2=None, op0=ALU.is_gt)
        I("vector", "scalar_tensor_tensor", reads=[mk, ph], writes=[ph], out=ph[:], in0=mk[:], scalar=-TWO_PI, in1=ph[:], op0=ALU.mult, op1=ALU.add)
        I("scalar", "activation", reads=[ph], writes=[Er], out=Er[:], in_=ph[:], func=AF.Sin)
        I("vector", "tensor_tensor", reads=[xx, jt], writes=[qf], out=qf[:], in0=xb, in1=jb_, op=ALU.mult)
        I("scalar", "activation", reads=[qf], writes=[qf], out=qf[:], in_=qf[:], func=AF.Exp)
        I("vector", "tensor_tensor", reads=[Er, qf], writes=[Er], out=Er[:], in0=Er[:], in1=qf[:], op=ALU.mult)
        I("gpsimd", "tensor_tensor", reads=[Ei, qf], writes=[Ei], out=Ei[:], in0=Ei[:], in1=qf[:], op=ALU.mult)

        def col(tab, j):
            return AP(tab, j, [[32 * NPOW, 64], [NPOW, 32]])
        den = sb("den", [64, 32]); t0 = sb("t0", [64, 32]); t1 = sb("t1b", [64, 32]); crr = sb("crr", [64, 32]); cii = sb("cii", [64, 32]); am1 = sb("am1", [64, 32])
        I("vector", "tensor_tensor", reads=[lr], writes=[den], out=den[:], in0=lr[:], in1=lr[:], op=ALU.mult)
        I("vector", "tensor_tensor", reads=[lim], writes=[t0], out=t0[:], in0=lim2, in1=lim2, op=ALU.mult)
        I("vector", "tensor_tensor", reads=[den, t0], writes=[den], out=den[:], in0=den[:], in1=t0[:], op=ALU.add)
        I("vector", "reciprocal", reads=[den], writes=[den], out=den[:], in_=den[:])
        I("vector", "tensor_scalar", reads=[Er], writes=[am1], out=am1[:], in0=col(Er, 1), scalar1=-1.0, scalar2=None, op0=ALU.add)
        I("vector", "tensor_tensor", reads=[am1, lr], writes=[t0], out=t0[:], in0=am1[:], in1=lr[:], op=ALU.mult)
        I("vector", "tensor_tensor", reads=[Ei, lim], writes=[t1], out=t1[:], in0=col(Ei, 1), in1=lim2, op=ALU.mult)
        I("vector", "tensor_tensor", reads=[t0, t1], writes=[crr], out=crr[:], in0=t0[:], in1=t1[:], op=ALU.add)
        I("vector", "tensor_tensor", reads=[crr, den], writes=[crr], out=crr[:], in0=crr[:], in1=den[:], op=ALU.mult)
        I("vector", "tensor_tensor", reads=[Ei, lr], writes=[t0], out=t0[:], in0=col(Ei, 1), in1=lr[:], op=ALU.mult)
        I("vector", "tensor_tensor", reads=[am1, lim], writes=[t1], out=t1[:], in0=am1[:], in1=lim2, op=ALU.mult)
        I("vector", "tensor_tensor", reads=[t0, t1], writes=[cii], out=cii[:], in0=t0[:], in1=t1[:], op=ALU.subtract)
        I("vector", "tensor_tensor", reads=[cii, den], writes=[cii], out=cii[:], in0=cii[:], in1=den[:], op=ALU.mult)
        tb = sb("tb", [64, 32, 16])
        crb4 = AP(crr, 0, [[32, 64], [16, 2], [1, 16], [0, 16]]); cib4 = AP(cii, 0, [[32, 64], [16, 2], [1, 16], [0, 16]])
        br4 = AP(br, 0, [[256, 64], [0, 2], [16, 16], [1, 16]]); bi4 = AP(bi, 0, [[256, 64], [0, 2], [16, 16], [1, 16]])
        o4 = lambda t_: t_[:].rearrange("p (d g) c -> p d g c", d=2)
        I("vector", "tensor_tensor", reads=[crr, br], writes=[bbr], out=o4(bbr), in0=crb4, in1=br4, op=ALU.mult)
        I("vector", "tensor_tensor", reads=[cii, bi], writes=[tb], out=o4(tb), in0=cib4, in1=bi4, op=ALU.mult)
        I("vector", "tensor_tensor", reads=[bbr, tb], writes=[bbr], out=bbr[:], in0=bbr[:], in1=tb[:], op=ALU.subtract)
        I("vector", "tensor_tensor", reads=[crr, bi], writes=[bbi], out=o4(bbi), in0=crb4, in1=bi4, op=ALU.mult)
        I("vector", "tensor_tensor", reads=[cii, br], writes=[tb], out=o4(tb), in0=cib4, in1=br4, op=ALU.mult)
        I("vector", "tensor_tensor", reads=[bbi, tb], writes=[bbi], out=bbi[:], in0=bbi[:], in1=tb[:], op=ALU.add)
        if "s5tab" in C.debug and hf == 0:
            P.D("sync", C.dbg["Er"].ap(), Er[:], reads=[Er]); P.D("sync", C.dbg["Ei"].ap(), Ei[:], reads=[Ei])
            P.D("sync", C.dbg["bbr"].ap(), bbr[:], reads=[bbr]); P.D("sync", C.dbg["bbi"].ap(), bbi[:], reads=[bbi])
        P.emit()
        if getattr(C, 's5stop', None) == 'tab':
            return
        s5_main(nc, C, l, hf, locals())


def s5_main(nc, C, l, hf, L):
    Er, Ei, bbr, bbi, cre, cim, ncim, ncre, dtab, U2, Sall, XPb = (L[k] for k in
        ("Er", "Ei", "bbr", "bbi", "cre", "cim", "ncim", "ncre", "dtab", "U2", "Sall", "XPb"))
    G0 = hf * 16
    P = Prog(nc)
    I = P.I
    sb = P.sb
    banks = [P.ps("s5b%d" % i, [128, 512]) for i in range(8)]
    PS = 32 * NPOW

    def Ev(tab, dg, c0, n):
        return AP(tab, dg * NPOW + c0, [[PS, 64], [1, n], [0, 16]])

    def Bv(tab, dg, n):
        return AP(tab, dg * 16, [[512, 64], [0, n], [1, 16]])

    def Cv(tab, gl, n):
        return AP(tab, gl * 16, [[256, 64], [0, n], [1, 16]])

    tA = [sb("tA%d" % i, [64, 65, 16]) for i in range(2)]
    tB = [sb("tB%d" % i, [64, 65, 16]) for i in range(2)]
    cnt = [0]

    def cprod(outre, outim, n, er, ei, xr, xi, xrn=None, sub_im=False):
        k = cnt[0] % 2
        cnt[0] += 1
        a, b = tA[k], tB[k]
        e1, e2 = ("vector", "gpsimd") if k == 0 else ("gpsimd", "vector")
        I(e1, "tensor_tensor", reads=[Er, bbr, cre], writes=[a], out=a[:, :n, :], in0=er, in1=xr, op=ALU.mult)
        I(e2, "tensor_tensor", reads=[Ei, bbi, cim], writes=[b], out=b[:, :n, :], in0=ei, in1=xi, op=ALU.mult)
        I(e1, "tensor_tensor", reads=[a, b], writes=[outre], out=outre[:, :n, :], in0=a[:, :n, :], in1=b[:, :n, :], op=ALU.subtract)
        if not sub_im:
            I(e1, "tensor_tensor", reads=[Er, bbi], writes=[a], out=a[:, :n, :], in0=er, in1=xi, op=ALU.mult)
            I(e2, "tensor_tensor", reads=[Ei, bbr], writes=[b], out=b[:, :n, :], in0=ei, in1=xr, op=ALU.mult)
            I(e2, "tensor_tensor", reads=[a, b], writes=[outim], out=outim[:, :n, :], in0=a[:, :n, :], in1=b[:, :n, :], op=ALU.add)
        else:
            I(e1, "tensor_tensor", reads=[Er, ncim], writes=[a], out=a[:, :n, :], in0=er, in1=xrn, op=ALU.mult)
            I(e2, "tensor_tensor", reads=[Ei, cre], writes=[b], out=b[:, :n, :], in0=ei, in1=xr, op=ALU.mult)
            I(e2, "tensor_tensor", reads=[a, b], writes=[outim], out=outim[:, :n, :], in0=a[:, :n, :], in1=b[:, :n, :], op=ALU.subtract)

    Wre = [sb("Wre%d" % i, [64, 64, 16], BF16) for i in range(2)]
    Wim = [sb("Wim%d" % i, [64, 64, 16], BF16) for i in range(2)]
    PTs = [sb("PTs%d" % i, [128, 8, 128], BF16) for i in range(2)]
    it = 0
    for d in range(2):
        for gl in range(16):
            dg = d * 16 + gl
            wr, wi, pts = Wre[it % 2], Wim[it % 2], PTs[it % 2]
            c0 = 65 if d == 0 else 0
            cprod(wr, wi, 64, Ev(Er, dg, c0, 64), Ev(Ei, dg, c0, 64), Bv(bbr, dg, 64), Bv(bbi, dg, 64))
            pt = banks[it % 2]
            ptv = pt[:, :].bitcast(BF16).rearrange("p (a b) -> p a b", b=128)
            for blk in range(8):
                P.tr(ptv[:, blk, 0:64], wr[:, blk * 8:(blk + 1) * 8, :].rearrange("p m c -> p (m c)"), C.identb[0:64, 0:64], reads=[wr, C.identb], writes=[pt])
                P.tr(ptv[:, blk, 64:128], wi[:, blk * 8:(blk + 1) * 8, :].rearrange("p m c -> p (m c)"), C.identb[0:64, 0:64], reads=[wi, C.identb], writes=[pt])
            I("scalar", "activation", reads=[pt], writes=[pts], out=pts[:], in_=ptv, func=AF.Copy)
            ps = banks[2 + it % 2]
            for blk in range(8):
                P.mm(ps[0:64, 0:68], pts[:, blk, 0:64], U2[:, gl, blk, :], blk == 0, blk == 7, reads=[pts, U2], writes=[ps])
            for blk in range(8):
                P.mm(ps[0:64, 68:136], pts[:, blk, 64:128], U2[:, gl, blk, :], blk == 0, blk == 7, reads=[pts, U2], writes=[ps])
            I("scalar", "activation", reads=[ps], writes=[(Sall, d)], out=Sall[:, :, d, gl, :], in_=ps[0:64, 0:136].rearrange("p (a b) -> p a b", b=68), func=AF.Copy)
            it += 1
    if getattr(C, 's5stop', None) == 'st1':
        P.emit()
        return
    s1 = sb("s1", [64, 16, 68]); s2 = sb("s2", [64, 16, 68]); s3 = sb("s3", [64, 16, 68]); s4 = sb("s4", [64, 16, 68])
    a63r = AP(Er, 63, [[PS, 64], [NPOW, 16], [0, 68]]); a63i = AP(Ei, 63, [[PS, 64], [NPOW, 16], [0, 68]])
    Sr = Sall[:, 0, 0, :, :]; Si = Sall[:, 1, 0, :, :]
    I("vector", "tensor_tensor", reads=[(Sall, 0), Er], writes=[s1], out=s1[:], in0=Sr, in1=a63r, op=ALU.mult)
    I("gpsimd", "tensor_tensor", reads=[(Sall, 0), Ei], writes=[s2], out=s2[:], in0=Si, in1=a63i, op=ALU.mult)
    I("vector", "tensor_tensor", reads=[(Sall, 0), Er], writes=[s3], out=s3[:], in0=Si, in1=a63r, op=ALU.mult)
    I("gpsimd", "tensor_tensor", reads=[(Sall, 0), Ei], writes=[s4], out=s4[:], in0=Sr, in1=a63i, op=ALU.mult)
    I("vector", "tensor_tensor", reads=[s1, s2], writes=[(Sall, 0)], out=Sr, in0=s1[:], in1=s2[:], op=ALU.subtract)
    I("gpsimd", "tensor_tensor", reads=[s3, s4, (Sall, 0)], writes=[(Sall, 0)], out=Si, in0=s3[:], in1=s4[:], op=ALU.add)
    SS = 2 * 2 * 16 * 68
    for d in range(2):
        eng = "vector" if d == 0 else "gpsimd"
        AA = sb("AA%d" % d, [64, 2, 16]); AC = sb("AC%d" % d, [64, 2, 16])
        a64r = AP(Er, d * 16 * NPOW + 64, [[PS, 64], [NPOW, 16]]); a64i = AP(Ei, d * 16 * NPOW + 64, [[PS, 64], [NPOW, 16]])
        I(eng, "tensor_copy", reads=[Er], writes=[AA], out=AA[:, 0, :], in_=a64r)
        I(eng, "tensor_copy", reads=[Er, AA], writes=[AA], out=AA[:, 1, :], in_=a64r)
        I(eng, "tensor_copy", reads=[Ei], writes=[AC], out=AC[:, 0, :], in_=a64i)
        I(eng, "tensor_scalar", reads=[Ei, AC], writes=[AC], out=AC[:, 1, :], in0=a64i, scalar1=-1.0, scalar2=None, op0=ALU.mult)
        X2 = [sb("X2_%d_%d" % (d, i), [64, 2, 16]) for i in range(2)]
        p1 = sb("p1_%d" % d, [64, 2, 16]); p2 = sb("p2_%d" % d, [64, 2, 16]); dec = sb("dec_%d" % d, [64, 2, 16])
        I(eng, "memset", writes=[X2[0]], ap=X2[0][:], constant=0.0)
        for i, sc in enumerate(CH_FWD if d == 0 else CH_BWD):
            xc, xn = X2[i % 2], X2[(i + 1) % 2]
            sv = AP(Sall, d * 16 * 68 + sc, [[SS, 64], [2 * 16 * 68, 2], [68, 16]])
            xv = AP(XPb, d * 16 * 68 + sc, [[SS, 64], [2 * 16 * 68, 2], [68, 16]])
            if d == 0:
                I(eng, "tensor_copy", reads=[xc], writes=[(XPb, d)], out=xv, in_=xc[:])
            I(eng, "tensor_tensor", reads=[xc, AA], writes=[p1], out=p1[:], in0=xc[:], in1=AA[:], op=ALU.mult)
            I(eng, "tensor_tensor", reads=[xc, AC], writes=[p2], out=p2[:], in0=xc[:], in1=AC[:], op=ALU.mult)
            I(eng, "tensor_tensor", reads=[p1, p2], writes=[dec], out=dec[:, 0, :], in0=p1[:, 0, :], in1=p2[:, 1, :], op=ALU.add)
            I(eng, "tensor_tensor", reads=[p1, p2, dec], writes=[dec], out=dec[:, 1, :], in0=p1[:, 1, :], in1=p2[:, 0, :], op=ALU.add)
            if d == 1:
                I(eng, "tensor_copy", reads=[dec], writes=[(XPb, d)], out=xv, in_=dec[:])
            I(eng, "tensor_tensor", reads=[dec, (Sall, d)], writes=[xn], out=xn[:], in0=dec[:], in1=sv, op=ALU.add)
    if getattr(C, 's5stop', None) == 'scan':
        P.emit()
        return
    Wf = [sb("Wfr", [64, 8, 16], BF16), sb("Wfi", [64, 8, 16], BF16)]
    Rf = [sb("Rft", [64, 65, 16], BF16), sb("Rfb", [64, 65, 16], BF16)]
    Wb = [sb("Wbr", [64, 64, 16], BF16), sb("Wbi", [64, 64, 16], BF16)]
    Rb = [sb("Rbt", [64, 64, 16], BF16), sb("Rbb", [64, 64, 16], BF16)]
    Zf = sb("Zf", [128, 64, 16], BF16)
    Zb = sb("Zb", [128, 8, 128], BF16)
    Ysb = sb("Ysb", [128, 16, 8, 68], BF16)
    flat = lambda ap_: ap_.rearrange("p m c -> p (m c)")
    for gl in range(16):
        df, db = gl, 16 + gl
        cprod(Wf[0], Wf[1], 8, Ev(Er, df, 65, 8), Ev(Ei, df, 65, 8), Bv(bbr, df, 8), Bv(bbi, df, 8))
        cprod(Rf[0], Rf[1], 65, Ev(Er, df, 0, 65), Ev(Ei, df, 0, 65), Cv(cre, gl, 65), Cv(cim, gl, 65), xrn=Cv(ncim, gl, 65), sub_im=True)
        cprod(Wb[0], Wb[1], 64, Ev(Er, db, 0, 64), Ev(Ei, db, 0, 64), Bv(bbr, db, 64), Bv(bbi, db, 64))
        cprod(Rb[0], Rb[1], 64, Ev(Er, db, 65, 64), Ev(Ei, db, 65, 64), Cv(cre, gl, 64), Cv(cim, gl, 64), xrn=Cv(ncim, gl, 64), sub_im=True)
        z0, z1, z2, z3 = banks[0], banks[1], banks[2], banks[3]
        for hh, zb in enumerate((z0, z1)):
            P.mm(zb[:, :], flat(Wf[0][:, :, :]), flat(Rf[0][:, hh * 32:(hh + 1) * 32, :]), True, False, reads=[Wf[0], Rf[0]], writes=[zb])
            P.mm(zb[:, :], flat(Wf[1][:, :, :]), flat(Rf[1][:, hh * 32:(hh + 1) * 32, :]), False, True, reads=[Wf[1], Rf[1]], writes=[zb])
        I("vector", "tensor_tensor", reads=[z0, C.maskZ[0]], writes=[(Zf, 0)], out=flat(Zf[:, 0:8, :]), in0=z0[:, 0:128], in1=C.maskZ[0][:], op=ALU.mult)
        I("scalar", "activation", reads=[z0], writes=[(Zf, 1)], out=flat(Zf[:, 8:32, :]), in_=z0[:, 128:512], func=AF.Copy)
        I("scalar", "activation", reads=[z1], writes=[(Zf, 2)], out=flat(Zf[:, 32:64, :]), in_=z1[:, :], func=AF.Copy)
        for dl in range(8):
            zb = z2 if dl < 4 else z3
            o_ = zb[:, (dl % 4) * 128:(dl % 4 + 1) * 128]
            P.mm(o_, flat(Wb[0][:, dl * 8:(dl + 1) * 8, :]), flat(Rb[0][:, 0:8, :]), True, False, reads=[Wb[0], Rb[0]], writes=[zb])
            P.mm(o_, flat(Wb[1][:, dl * 8:(dl + 1) * 8, :]), flat(Rb[1][:, 0:8, :]), False, True, reads=[Wb[1], Rb[1]], writes=[zb])
        I("vector", "tensor_tensor", reads=[z2, C.maskZ[1]], writes=[(Zb, 0)], out=Zb[:, 0, :], in0=z2[:, 0:128], in1=C.maskZ[1][:], op=ALU.mult)
        I("scalar", "activation", reads=[z2], writes=[(Zb, 1)], out=Zb[:, 1:4, :], in_=z2[:, 128:512].rearrange("p (a b) -> p a b", b=128), func=AF.Copy)
        I("scalar", "activation", reads=[z3], writes=[(Zb, 2)], out=Zb[:, 4:8, :], in_=z3[:, :].rearrange("p (a b) -> p a b", b=128), func=AF.Copy)
        if getattr(C, 's5stop', None) in ('z', 'z3'):
            continue
        for hb in range(2):
            yb = banks[4 + (2 * gl + hb) % 4]
            for i in range(4):
                ib = 4 * hb + i
                o_ = yb[:, i * 68:(i + 1) * 68]
                ops_ = []
                for jb in range(0, ib + 1):
                    ops_.append((flat(Zf[:, (ib - jb) * 8:(ib - jb + 1) * 8, :]), U2[:, gl, jb, :], [Zf, U2]))
                for jb in range(ib, 8):
                    ops_.append((Zb[:, jb - ib, :], U2[:, gl, jb, :], [Zb, U2]))
                ops_.append((flat(Rf[0][:, 8 * ib + 1:8 * ib + 9, :]), XPb[:, 0, 0, gl, :], [Rf[0], XPb]))
                ops_.append((flat(Rf[1][:, 8 * ib + 1:8 * ib + 9, :]), XPb[:, 1, 0, gl, :], [Rf[1], XPb]))
                ops_.append((flat(Rb[0][:, 8 * ib:8 * ib + 8, :]), XPb[:, 0, 1, gl, :], [Rb[0], XPb]))
                ops_.append((flat(Rb[1][:, 8 * ib:8 * ib + 8, :]), XPb[:, 1, 1, gl, :], [Rb[1], XPb]))
                for k, (lh, rh, rd) in enumerate(ops_):
                    P.mm(o_, lh, rh, k == 0, k == len(ops_) - 1, reads=rd, writes=[yb])
            I("vector", "scalar_tensor_tensor", reads=[U2, dtab, yb], writes=[(Ysb, gl)], out=Ysb[:, gl, 4 * hb:4 * hb + 4, :], in0=U2[:, gl, 4 * hb:4 * hb + 4, :],
              scalar=dtab[:, gl:gl + 1], in1=yb[:, 0:272].rearrange("p (a b) -> p a b", b=68), op0=ALU.mult, op1=ALU.add)
    if getattr(C, 's5stop', None) in ('st3', 'z3'):
        P.emit()
        return
    Yc2 = sb("Yc2", [128, 16, 32], BF16)
    I("vector", "tensor_copy", reads=[Ysb], writes=[Yc2], out=Yc2[:].rearrange("p g (b s) -> p g b s", s=4), in_=Ysb[:, :, :, 0:4])
    yc_g = C.scr["YC"].t.ap().rearrange("(g c) n -> c g n", c=16)
    s5b = C.scr["S5B"].t.ap().rearrange("(g c) (t x) -> t c g x", c=16, t=8)
    for t in range(8):
        dst = yc_g[:, G0:G0 + 16, 256:T].rearrange("c g (blk t col) -> c g blk t col", blk=8, t=8)
        P.dma("sync", [lambda e, t=t, b_=b_, dst=dst: e.dma_start(out=dst[:, :, b_, t, :], in_=Ysb[16 * t:16 * t + 16, :, b_, 4:68]) for b_ in range(8)],
              reads=[Ysb], writes=[(C.scr["YC"], (hf, t))])
        P.D("sync", s5b[t][:, G0:G0 + 16, :], Yc2[16 * t:16 * t + 16, :, :], reads=[Yc2], writes=[(C.scr["S5B"], (hf, t))])
    P.emit()


def phase_merge(nc, C, l, xsrc, xdst, tiles=None):
    P = Prog(nc)
    I = P.I
    sb = P.sb
    wts = {}
    for nm, src, kc, ncol in (("ba", "w_branch_a", 4, 1024), ("bb", "w_branch_b", 4, 1024), ("bc", "w_branch_c", 4, 1024),
                              ("glu", "s5_w_glu", 4, 512), ("wo", "w_out", 8, 1024)):
        wts[nm] = sb("w_" + nm, [128, kc, ncol], BF16)
        wv = C.din[src][l].rearrange("(k p) c -> p k c", p=128)
        for k in range(kc):
            P.D("gpsimd", wts[nm][:, k, :], wv[:, k, :], writes=[(wts[nm], k)])
    nga = sb("nga", [128, 4, 512], BF16); ngb = sb("ngb", [128, 4, 512], BF16); ycb = sb("ycb", [128, 4, 512], BF16)
    mg = sb("mg", [128, 24, 512], BF16)
    xT = sb("xTm", [128, 8, 512])
    yc = sb("yc32", [128, 4, 512]); x2 = sb("x2", [128, 4, 512]); xh = sb("xh", [128, 4, 512])
    tt = x2
    zb = sb("zb", [128, 4, 512], BF16); zz = sb("zz", [128, 4, 512], BF16)
    zf = yc
    sig = sb("sigg", [128, 512])
    gts = [sb("gt%d" % i, [128, 3, 512]) for i in range(2)]
    m1 = sb("mm1", [128, 512]); m2 = sb("mm2", [128, 512]); m3 = sb("mm3", [128, 512])
    mrg = sb("mrg", [128, 8, 512], BF16)
    xn = sb("xn", [128, 8, 512])
    pg = P.ps("pg", [128, 512]); po = P.ps("po", [128, 512])
    pabc = [[P.ps("pabc%d_%d" % (i, j), [128, 512]) for j in range(3)] for i in range(2)]
    view = lambda b_: b_.t.ap().rearrange("(k p) t -> p k t", p=128)
    xv = view(xsrc); xo = view(xdst)
    for (n0, n, w) in (tiles or TTILES):
        ts_ = slice(n0, n0 + n)
        P.D("sync", nga[:, :, :n], view(C.scr["NGA"])[:, :, ts_], reads=[C.scr["NGA"]], writes=[nga])
        P.D("sync", ngb[:, :, :n], view(C.scr["NGB"])[:, :, ts_], reads=[C.scr["NGB"]], writes=[ngb])
        P.D("sync", ycb[:, :, :n], view(C.scr["YC"])[:, :, ts_], reads=[C.scr["YC"]], writes=[ycb])
        P.D("sync", mg[:, :, :n], view(C.scr["MG"])[:, :, ts_], reads=[C.scr["MG"]], writes=[mg])
        P.D("sync", xT[:, :, :n], xv[:, :, ts_], reads=[(xsrc, n0)], writes=[xT])
        I("vector", "tensor_copy", reads=[ycb], writes=[yc], out=yc[:, :, :n], in_=ycb[:, :, :n])
        I("gpsimd", "tensor_tensor", reads=[yc], writes=[x2], out=x2[:, :, :n], in0=yc[:, :, :n], in1=yc[:, :, :n], op=ALU.mult)
        I("vector", "tensor_scalar", reads=[x2], writes=[x2], out=x2[:, :, :n], in0=x2[:, :, :n], scalar1=0.044715, scalar2=1.0, op0=ALU.mult, op1=ALU.add)
        I("gpsimd", "tensor_tensor", reads=[x2, yc], writes=[x2], out=x2[:, :, :n], in0=x2[:, :, :n], in1=yc[:, :, :n], op=ALU.mult)
        I("scalar", "activation", reads=[x2], writes=[tt], out=tt[:, :, :n], in_=x2[:, :, :n], func=AF.Tanh, scale=0.7978845608028654)
        I("scalar", "mul", reads=[yc], writes=[xh], out=xh[:, :, :n], in_=yc[:, :, :n], mul=0.5)
        I("vector", "scalar_tensor_tensor", reads=[tt, xh], writes=[zf], out=zf[:, :, :n], in0=tt[:, :, :n], scalar=1.0, in1=xh[:, :, :n], op0=ALU.add, op1=ALU.mult)
        I("gpsimd", "tensor_copy", reads=[zf], writes=[zb], out=zb[:, :, :n], in_=zf[:, :, :n])
        for m in range(4):
            for k in range(4):
                P.mm(pg[:, :n], wts["glu"][:, k, m * 128:(m + 1) * 128], zb[:, k, :n], k == 0, k == 3, reads=[wts["glu"], zb], writes=[pg])
            I("scalar", "activation", reads=[pg], writes=[sig], out=sig[:, :n], in_=pg[:, :n], func=AF.Sigmoid)
            I("vector", "tensor_tensor", reads=[zf, sig], writes=[(zz, m)], out=zz[:, m, :n], in0=zf[:, m, :n], in1=sig[:, :n], op=ALU.mult)
        for oc in range(8):
            pa, pb, pc = pabc[oc % 2]
            gt = gts[oc % 2]
            for (pp, wn, src) in ((pa, "ba", nga), (pb, "bb", ngb), (pc, "bc", zz)):
                for k in range(4):
                    P.mm(pp[:, :n], wts[wn][:, k, oc * 128:(oc + 1) * 128], src[:, k, :n], k == 0, k == 3, reads=[wts[wn], src], writes=[pp])
            mgv = AP(mg, oc * 512, [[24 * 512, 128], [8 * 512, 3], [1, n]])
            I("scalar", "activation", reads=[mg], writes=[gt], out=gt[:, :, :n], in_=mgv, func=AF.Sigmoid)
            I("vector", "tensor_tensor", reads=[pa, gt], writes=[m1], out=m1[:, :n], in0=pa[:, :n], in1=gt[:, 0, :n], op=ALU.mult)
            I("vector", "tensor_tensor", reads=[pb, gt], writes=[m2], out=m2[:, :n], in0=pb[:, :n], in1=gt[:, 1, :n], op=ALU.mult)
            I("vector", "tensor_tensor", reads=[pc, gt], writes=[m3], out=m3[:, :n], in0=pc[:, :n], in1=gt[:, 2, :n], op=ALU.mult)
            I("gpsimd", "tensor_tensor", reads=[m1, m2], writes=[m1], out=m1[:, :n], in0=m1[:, :n], in1=m2[:, :n], op=ALU.add)
            I("gpsimd", "tensor_tensor", reads=[m1, m3], writes=[(mrg, oc)], out=mrg[:, oc, :n], in0=m1[:, :n], in1=m3[:, :n], op=ALU.add)
        for oc in range(8):
            for k in range(8):
                P.mm(po[:, :n], wts["wo"][:, k, oc * 128:(oc + 1) * 128], mrg[:, k, :n], k == 0, k == 7, reads=[wts["wo"], mrg], writes=[po])
            I("vector", "scalar_tensor_tensor", reads=[po, xT, C.mod[l]], writes=[(xn, oc)], out=xn[:, oc, :n], in0=po[:, :n],
              scalar=C.mod[l][:, 16 + oc, w:w + 1], in1=xT[:, oc, :n], op0=ALU.mult, op1=ALU.add)
        P.D("sync", xo[:, :, ts_], xn[:, :, :n], reads=[xn], writes=[(xdst, n0)])
    P.emit()


FTILES = [(0, 256, 1)] + [(256 + 1024 * i, 1024, 0) for i in range(4)]


def phase_ffn(nc, C, l, xsrc, xdst, last):
    if last:
        tiles = [(256 + 1024 * i, 1024) for i in range(4)]
    else:
        tiles = [(0, 1536), (1536, 1536), (3072, 1280)]
    for (t0, tn) in tiles:
        sub = []
        i = t0
        while i < t0 + tn:
            if i < 256:
                sub.append((i, 256 - i, 1)); i = 256
            else:
                sz = min(512, t0 + tn - i)
                sub.append((i, sz, 0)); i += sz
        with ExitStack() as ost:
            _UID[0] += 1
            hT = Buf(ost.enter_context(nc.sbuf_tensor("hT2_%d" % _UID[0], [128, 8, tn], BF16)), "hT2")
            P = Prog(nc)
            compute_hT(P, C, xsrc, hT, C.g2[l], C.mod[l], 24, tiles=sub, hoff=t0)
            P.emit()
            ffn_tile(nc, C, l, xsrc, xdst, last, hT, t0, tn, [(a - t0, b, w_) for (a, b, w_) in sub])


def ffn_tile(nc, C, l, xsrc, xdst, last, hT, t0, tn, subs):
    P = Prog(nc)
    I = P.I
    sb = P.sb
    mid = sb("mid", [128, 32, tn], BF16)
    w1 = [sb("w1_%d" % i, [128, 8, 512], BF16) for i in range(2)]
    w2 = [sb("w2_%d" % i, [128, 32, 128], BF16) for i in range(2)]
    rl = [sb("rl%d" % i, [128, 512]) for i in range(2)]
    xt = [sb("xtf%d" % i, [128, tn]) for i in range(2)]
    pss = [P.ps("pf%d" % i, [128, 512]) for i in range(4)]
    w1v = C.din["w_ff1"][l].rearrange("(k p) c -> p k c", p=128)
    w2v = C.din["w_ff2"][l].rearrange("(k p) c -> p k c", p=128)
    xv = xsrc.t.ap().rearrange("(k p) t -> p k t", p=128)
    pi = 0
    for g in range(8):
        wb = w1[g % 2]
        P.D("gpsimd", wb[:], w1v[:, :, g * 512:(g + 1) * 512], writes=[wb])
        for m in range(4):
            mc = g * 4 + m
            for (s0, sn, w) in subs:
                pp = pss[pi % 4]; r_ = rl[pi % 2]; pi += 1
                for k in range(8):
                    P.mm(pp[:, :sn], wb[:, k, m * 128:(m + 1) * 128], hT[:, k, s0:s0 + sn], k == 0, k == 7, reads=[wb, hT], writes=[pp])
                I("scalar", "activation", reads=[pp], writes=[r_], out=r_[:, :sn], in_=pp[:, :sn], func=AF.Relu)
                I("gpsimd" if pi % 2 else "vector", "tensor_tensor", reads=[r_], writes=[(mid, (mc, s0))], out=mid[:, mc, s0:s0 + sn], in0=r_[:, :sn], in1=r_[:, :sn], op=ALU.mult)
    if last:
        xn = sb("xnf", [128, 8, tn])
    else:
        xns = [sb("xns%d" % i, [128, tn]) for i in range(2)]
    for oc in range(8):
        wb = w2[oc % 2]
        P.D("gpsimd", wb[:], w2v[:, :, oc * 128:(oc + 1) * 128], writes=[wb])
        x_ = xt[oc % 2]
        P.D("sync", x_[:, :tn], xv[:, oc, t0:t0 + tn], reads=[(xsrc, t0)], writes=[x_])
        for (s0, sn, w) in subs:
            pp = pss[pi % 4]; pi += 1
            for k in range(32):
                P.mm(pp[:, :sn], wb[:, k, :], mid[:, k, s0:s0 + sn], k == 0, k == 31, reads=[wb, mid], writes=[pp])
            if last:
                I("vector", "scalar_tensor_tensor", reads=[pp, x_, C.mod[l]], writes=[(xn, (oc, s0))], out=xn[:, oc, s0:s0 + sn], in0=pp[:, :sn],
                  scalar=C.mod[l][:, 40 + oc, w:w + 1], in1=x_[:, s0:s0 + sn], op0=ALU.mult, op1=ALU.add)
            else:
                xo_ = xns[oc % 2]
                I("vector", "scalar_tensor_tensor", reads=[pp, x_, C.mod[l]], writes=[(xo_, s0)], out=xo_[:, s0:s0 + sn], in0=pp[:, :sn],
                  scalar=C.mod[l][:, 40 + oc, w:w + 1], in1=x_[:, s0:s0 + sn], op0=ALU.mult, op1=ALU.add)
        if not last:
            P.D("sync", xdst[oc * 128:(oc + 1) * 128, t0:t0 + tn], xns[oc % 2][:, :tn], reads=[xns[oc % 2]], writes=[(xdst, (oc, t0))])
    if last:
        fw = sb("fw", [128, 8])
        P.D("sync", fw[:], C.din["final_norm_w"][:], writes=[fw])
        sq = sb("sqf", [128, 512])
        rs = sb("rsf", [128, 512])
        ov = C.out.t.ap().rearrange("(k p) t -> p k t", p=128)
        for (s0, sn, w) in subs:
            pp = pss[pi % 4]; pi += 1
            for k in range(8):
                I("gpsimd" if k % 2 else "vector", "tensor_tensor", reads=[xn], writes=[sq], out=sq[:, :sn], in0=xn[:, k, s0:s0 + sn], in1=xn[:, k, s0:s0 + sn], op=ALU.mult)
                P.mm(pp[:, :sn], C.ones[:], sq[:, :sn], k == 0, k == 7, reads=[sq, C.ones], writes=[pp])
            I("scalar", "activation", reads=[pp], writes=[rs], out=rs[:, :sn], in_=pp[:, :sn], func=AF.Sqrt, scale=1.0 / D, bias=EPS)
            I("vector", "reciprocal", reads=[rs], writes=[rs], out=rs[:, :sn], in_=rs[:, :sn])
            for k in range(8):
                I("vector", "scalar_tensor_tensor", reads=[xn, fw, rs], writes=[xn], out=xn[:, k, s0:s0 + sn], in0=xn[:, k, s0:s0 + sn], scalar=fw[:, k:k + 1],
                  in1=rs[:, :sn], op0=ALU.mult, op1=ALU.mult)
        P.D("sync", ov[:, :, t0 - 256:t0 - 256 + tn], xn[:, :, :tn], reads=[xn], writes=[(C.out, t0)])
    P.emit()
```

```python
import numpy as np
from contextlib import ExitStack
import concourse.bass as bass
import concourse.mybir as mybir
from concourse.bass_utils import run_bass_kernel_spmd

F32 = mybir.dt.float32
BF16 = mybir.dt.bfloat16
I32 = mybir.dt.int32
U8 = mybir.dt.uint8
AF = mybir.ActivationFunctionType
ALU = mybir.AluOpType
AX = mybir.AxisListType

ENGS = ("tensor", "vector", "scalar", "gpsimd", "sync")

T = 4352
NCTX = 256
NCH = 68
D = 1024
DIN = 8208
EPS = 1e-6


class Buf:
    def __init__(self, t, name):
        self.t = t
        self.name = name
        self.tr = {}

    def __getitem__(self, idx):
        return self.t[idx]


def _norm(x):
    if isinstance(x, Buf):
        return (x, "*")
    return x


_UID = [0]


_SHARED = {}


def _shared(nc, n_dma_sems=16):
    k = id(nc)
    if k not in _SHARED:
        st = ExitStack()
        sh = dict(stack=st, esem={}, ecount={e: 0 for e in ENGS}, dsems={}, dval={}, dnext={}, waited={e: {} for e in ENGS})
        for e in ENGS:
            sh["esem"][e] = st.enter_context(nc.semaphore("es_" + e))
        for q in ("sync", "gpsimd"):
            sh["dsems"][q] = [st.enter_context(nc.semaphore("ds_%s%d" % (q, i))) for i in range(n_dma_sems)]
            sh["dval"][q] = [0] * n_dma_sems
            sh["dnext"][q] = 0
        _SHARED[k] = sh
    return _SHARED[k]


class Prog:
    def __init__(self, nc, n_dma_sems=16):
        self.nc = nc
        self.stack = ExitStack()
        sh = _shared(nc, n_dma_sems)
        self.sh = sh
        self.ops = {e: [] for e in ENGS}
        self.esem = sh["esem"]
        self.ecount = sh["ecount"]
        self.dsems = sh["dsems"]
        self.dval = sh["dval"]
        self.dnext = sh["dnext"]
        self.waited = sh["waited"]
        self.nops = 0
        self._nm = 0

    def sb(self, name, shape, dt=F32):
        _UID[0] += 1
        return Buf(self.stack.enter_context(self.nc.sbuf_tensor("%s_%d" % (name, _UID[0]), list(shape), dt)), name)

    def ps(self, name, shape, dt=F32):
        _UID[0] += 1
        b = Buf(self.stack.enter_context(self.nc.psum_tensor("%s_%d" % (name, _UID[0]), list(shape), dt)), name)
        b.excl = True
        return b

    def _deps(self, reads, writes):
        deps = {}

        def add(tok):
            if tok is None:
                return
            s, v = tok
            k = id(s)
            if k not in deps or deps[k][1] < v:
                deps[k] = (s, v)

        for b, key in map(_norm, reads):
            keys = list(b.tr.keys()) if key == "*" else [key, "*"]
            for k in keys:
                tr = b.tr.get(k)
                if tr:
                    add(tr[0])
        for b, key in map(_norm, writes):
            keys = list(b.tr.keys()) if key == "*" else [key, "*"]
            for k in keys:
                tr = b.tr.get(k)
                if tr:
                    add(tr[0])
                    for tok in tr[1].values():
                        add(tok)
        return deps

    def _update(self, reads, writes, tok):
        s, v = tok
        for b, key in map(_norm, reads):
            tr = b.tr.setdefault(key, [None, {}])
            tr[1][id(s)] = tok
        for b, key in map(_norm, writes):
            if key == "*":
                b.tr = {"*": [tok, {}]}
            else:
                b.tr[key] = [tok, {}]

    def _waits(self, eng, deps, skip_own=False):
        w = []
        wd = self.waited[eng]
        for k, (s, v) in deps.items():
            if skip_own and s is self.esem[eng]:
                continue
            if wd.get(k, 0) >= v:
                continue
            wd[k] = v
            w.append((s, v))
        return w

    @staticmethod
    def _excl(reads, writes):
        r2, w2 = [], []
        for x in reads:
            b = x if isinstance(x, Buf) else x[0]
            (w2 if getattr(b, "excl", False) else r2).append(b if getattr(b, "excl", False) else x)
        for x in writes:
            b = x if isinstance(x, Buf) else x[0]
            w2.append(b if getattr(b, "excl", False) else x)
        return r2, w2

    def op(self, eng, fn, reads=(), writes=()):
        reads, writes = self._excl(reads, writes)
        deps = self._deps(reads, writes)
        waits = self._waits(eng, deps, skip_own=(eng == "tensor"))
        self.ecount[eng] += 1
        tok = (self.esem[eng], self.ecount[eng])
        self.ops[eng].append((waits, [fn], tok[0], 1))
        self._update(reads, writes, tok)
        self.nops += 1
        return tok

    def I(self, eng, method, reads=(), writes=(), **kw):
        return self.op(eng, lambda e: getattr(e, method)(**kw), reads, writes)

    def mm(self, out, lhsT, rhs, start, stop, reads, writes):
        return self.op("tensor", lambda e: e.matmul(out, lhsT=lhsT, rhs=rhs, start=start, stop=stop), reads, writes)

    def tr(self, out, in_, ident, reads, writes):
        return self.op("tensor", lambda e: e.transpose(out=out, in_=in_, identity=ident), reads, writes)

    def dma(self, q, fns, reads=(), writes=()):
        if not isinstance(fns, (list, tuple)):
            fns = [fns]
        deps = self._deps(reads, writes)
        i = self.dnext[q]
        self.dnext[q] = (i + 1) % len(self.dsems[q])
        s = self.dsems[q][i]
        prev = self.dval[q][i]
        if prev > 0:
            k = id(s)
            if k not in deps or deps[k][1] < prev:
                deps[k] = (s, prev)
        waits = self._waits(q, deps)
        val = prev + 16 * len(fns)
        self.dval[q][i] = val
        tok = (s, val)
        self.ops[q].append((waits, list(fns), s, 16))
        self._update(reads, writes, tok)
        self.nops += 1
        return tok

    def D(self, q, out, in_, reads=(), writes=(), **kw):
        return self.dma(q, lambda e: e.dma_start(out=out, in_=in_, **kw), reads, writes)

    def emit(self):
        nc = self.nc
        for q in self.dsems:
            fin = []
            for s, v in zip(self.dsems[q], self.dval[q]):
                if v > 0 and self.waited[q].get(id(s), 0) < v:
                    fin.append((s, v))
            if fin:
                self.ops[q].append((fin, [], None, 0))
        ops = self.ops
        with nc.Block() as block:
            def mk(ename):
                def body(e):
                    for waits, fns, sem, inc in ops[ename]:
                        for s, v in waits:
                            e.wait_ge(s, v)
                        for fn in fns:
                            fn(e).then_inc(sem, inc)
                return body
            for ename in ENGS:
                if ops[ename]:
                    getattr(block, ename)(mk(ename))
        self.stack.close()


def AP(t, offset, dims):
    tt = t.t if isinstance(t, Buf) else t
    return bass.AP(tt, offset, [list(d) for d in dims])


TTILES = [(0, 256, 1)] + [(256 + 512 * i, 512, 0) for i in range(8)]

FM_ROUTES = [
    (0, 512, "AQ", 0), (1024, 1024, "AF", 0), (2560, 1536, "BQKV", 0), (4624, 512, "CU", 0), (5136, 3072, "MG", 0),
]
TM_ROUTES = [
    (512, 512, "AI"), (2048, 512, "AG"), (4096, 512, "BG"), (4608, 16, "BBA"),
]
SCR = {
    "AQ": ([512, T], BF16), "AF": ([1024, T], F32), "BQKV": ([1536, T], BF16), "CU": ([512, T], BF16),
    "MG": ([3072, T], BF16), "AI": ([T, 512], BF16), "AG": ([T, 512], BF16), "BG": ([T, 512], BF16),
    "BBA": ([T, 16], F32),
    "XA": ([D, T], F32), "XB": ([D, T], F32),
    "NGA": ([512, T], BF16), "NGB": ([512, T], BF16), "YC": ([512, T], BF16),
    "OF": ([T, 512], F32),
    "OBW": ([T, 512], F32), "S5A": ([512, 256], BF16), "S5B": ([512, 256], BF16),
}


class Ctx:
    pass


def tkey(tok):
    return 0 if tok < 256 else 256 + ((tok - 256) // 512) * 512


def dump_sb(nc, C, buf, name, shape):
    P = Prog(nc)
    d = nc.dram_tensor(name, list(shape), F32, kind="ExternalOutput")
    P.D("sync", d.ap(), buf[:], reads=[buf])
    P.emit()


def make_consts(nc, C):
    P = Prog(nc)
    st = C.gstack
    def gsb(name, shape, dt=F32):
        return Buf(st.enter_context(nc.sbuf_tensor(name, list(shape), dt)), name)
    C.ident = gsb("ident", [128, 128], F32)
    C.identb = gsb("identb", [128, 128], BF16)
    C.ones = gsb("ones", [128, 128], F32)
    C.onesb = gsb("onesb", [128, 128], BF16)
    P.I("gpsimd", "memset", writes=[C.ident], ap=C.ident[:], constant=0.0)
    P.I("gpsimd", "affine_select", reads=[C.ident], writes=[C.ident], out=C.ident[:], in_=C.ident[:],
        pattern=[[-1, 128]], compare_op=ALU.not_equal, fill=1.0, base=0, channel_multiplier=1)
    P.I("vector", "tensor_copy", reads=[C.ident], writes=[C.identb], out=C.identb[:], in_=C.ident[:])
    P.I("vector", "memset", writes=[C.ones], ap=C.ones[:], constant=1.0)
    P.I("vector", "memset", writes=[C.onesb], ap=C.onesb[:], constant=1.0)
    make_masks(P, C)
    make_s5_masks(P, C)
    C.mod = [gsb("mod%d" % l, [128, 48, 2], F32) for l in range(2)]
    C.g1 = [gsb("g1_%d" % l, [128, 8, 2], F32) for l in range(2)]
    C.g2 = [gsb("g2_%d" % l, [128, 8, 2], F32) for l in range(2)]
    P.emit()


def phase_adaln(nc, C, l):
    P = Prog(nc)
    cv = P.sb("cv", [128, 8, 2])
    sc = P.sb("sc", [128, 8, 2])
    ab = P.sb("ab", [128, 48])
    nw = P.sb("nw", [128, 16])
    pm = P.ps("pm", [128, 48, 2])
    P.D("sync", cv[:], C.din["cvec"][:], reads=[], writes=[cv])
    P.D("sync", ab[:], C.din["ada_b"][l], writes=[ab])
    P.D("sync", nw[:, 0:8], C.din["norm1_w"][l], writes=[(nw, 0)])
    P.D("sync", nw[:, 8:16], C.din["norm2_w"][l], writes=[(nw, 1)])
    P.I("scalar", "activation", reads=[cv], writes=[sc], out=sc[:], in_=cv[:], func=AF.Silu)
    wbufs = [P.sb("adw%d" % i, [128, 8, 512]) for i in range(2)]
    aw = C.din["ada_w"][l].rearrange("(k p) c -> p k c", p=128)
    for og in range(12):
        wb = wbufs[og % 2]
        P.dma("sync", [lambda e, k=k, wb=wb, og=og: e.dma_start(out=wb[:, k, :], in_=aw[:, k, og * 512:(og + 1) * 512]) for k in range(8)],
              writes=[wb])
        for m in range(4):
            j = og * 4 + m
            for k in range(8):
                P.mm(pm[:, j, :], wb[:, k, m * 128:(m + 1) * 128], sc[:, k, :], k == 0, k == 7, reads=[wb, sc], writes=[(pm, j)])
    mod = C.mod[l]
    abb = AP(ab, 0, [[48, 128], [1, 48], [0, 2]])
    P.I("vector", "tensor_tensor", reads=[pm, ab], writes=[mod], out=mod[:], in0=pm[:], in1=abb, op=ALU.add)
    for (g, soff, noff) in ((C.g1[l], 8, 0), (C.g2[l], 32, 8)):
        nwb = AP(nw, noff, [[16, 128], [1, 8], [0, 2]])
        P.I("vector", "scalar_tensor_tensor", reads=[mod, nw], writes=[g], out=g[:], in0=mod[:, soff:soff + 8, :], scalar=1.0,
            in1=nwb, op0=ALU.add, op1=ALU.mult)
    P.emit()


def phase_proj(nc, C, l, xsrc):
    P = Prog(nc)
    hT = P.sb("hT", [128, 8, T], BF16)
    compute_hT(P, C, xsrc, hT, C.g1[l], C.mod[l], 0)
    win = C.din["w_in"][l].rearrange("(k p) c -> p k c", p=128)
    wbufs = [P.sb("wb%d" % i, [128, 8, 512], BF16) for i in range(2)]
    pss = [P.ps("pp%d" % i, [128, 512]) for i in range(4)]
    stF = [P.sb("stF%d" % i, [128, T], F32) for i in range(2)]
    wi = 0
    pi = 0
    si = 0
    ei = 0
    for (c0, ncols, sname, row0) in FM_ROUTES:
        dst = C.scr[sname]
        dt = SCR[sname][1]
        for g0 in range(0, ncols, 512):
            wb = wbufs[wi % 2]
            wi += 1
            P.D("gpsimd", wb[:], win[:, :, c0 + g0:c0 + g0 + 512], writes=[wb])
            for m in range(4):
                stb = stF[si % 2]
                si += 1
                stv = stb[:] if dt == F32 else stb[:].bitcast(BF16)[:, 0:T]
                for (n0, nsz, w) in TTILES:
                    pp = pss[pi % 4]
                    pi += 1
                    for k in range(8):
                        P.mm(pp[:, :nsz], wb[:, k, m * 128:(m + 1) * 128], hT[:, k, n0:n0 + nsz], k == 0, k == 7,
                             reads=[wb, (hT, n0)], writes=[pp])
                    if ei % 2 == 0:
                        P.I("scalar", "activation", reads=[pp], writes=[(stb, n0)], out=stv[:, n0:n0 + nsz], in_=pp[:, :nsz], func=AF.Copy)
                    else:
                        P.I("vector", "tensor_copy", reads=[pp], writes=[(stb, n0)], out=stv[:, n0:n0 + nsz], in_=pp[:, :nsz])
                    ei += 1
                r0 = row0 + g0 + m * 128
                P.D("sync", dst[r0:r0 + 128, :], stv, reads=[stb], writes=[(dst, r0)])
    stT = [P.sb("stT%d" % i, [128, 512], F32) for i in range(3)]
    for (c0, ncols, sname) in TM_ROUTES:
        dst = C.scr[sname]
        dt = SCR[sname][1]
        wb = wbufs[wi % 2]
        wi += 1
        P.D("gpsimd", wb[:, :, :ncols], win[:, :, c0:c0 + ncols], writes=[wb])
        for tb in range(T // 128):
            pp = pss[pi % 4]
            pi += 1
            for k in range(8):
                P.mm(pp[:, :ncols], hT[:, k, tb * 128:(tb + 1) * 128], wb[:, k, :ncols], k == 0, k == 7,
                     reads=[wb, (hT, tkey(tb * 128))], writes=[pp])
            stb = stT[si % 3]
            si += 1
            stv = stb[:] if dt == F32 else stb[:].bitcast(BF16)[:, 0:512]
            if ei % 2 == 0:
                P.I("scalar", "activation", reads=[pp], writes=[stb], out=stv[:, :ncols], in_=pp[:, :ncols], func=AF.Copy)
            else:
                P.I("vector", "tensor_copy", reads=[pp], writes=[stb], out=stv[:, :ncols], in_=pp[:, :ncols])
            ei += 1
            P.D("sync", dst[tb * 128:(tb + 1) * 128, :], stv[:, :ncols], reads=[stb], writes=[(dst, tb)])
    P.emit()


def compute_hT(P, C, xsrc, hT, g, mod, shoff, tiles=None, hoff=0):
    nc = P.nc
    xv = xsrc.t.ap().rearrange("(k p) t -> p k t", p=128)
    xts = [P.sb("xt%d" % i, [128, 8, 512]) for i in range(2)]
    sq = P.sb("sq", [128, 8, 512])
    rs = P.sb("rs", [128, 512])
    pss = P.ps("pss", [128, 512])
    for ti, (n0, nsz, w) in enumerate(tiles or TTILES):
        xt = xts[ti % 2]
        P.D("sync", xt[:, :, :nsz], xv[:, :, n0:n0 + nsz], reads=[(xsrc, n0)], writes=[xt])
        P.I("scalar", "activation", reads=[xt], writes=[sq], out=sq[:, :, :nsz], in_=xt[:, :, :nsz], func=AF.Square)
        for k in range(8):
            P.mm(pss[:, :nsz], C.ones[:], sq[:, k, :nsz], k == 0, k == 7, reads=[sq, C.ones], writes=[pss])
        P.I("scalar", "activation", reads=[pss], writes=[rs], out=rs[:, :nsz], in_=pss[:, :nsz], func=AF.Sqrt,
            scale=1.0 / D, bias=EPS)
        P.I("vector", "reciprocal", reads=[rs], writes=[rs], out=rs[:, :nsz], in_=rs[:, :nsz])
        rsb = AP(rs, 0, [[512, 128], [0, 8], [1, nsz]])
        P.I("vector", "tensor_tensor", reads=[xt, rs], writes=[sq], out=sq[:, :, :nsz], in0=xt[:, :, :nsz], in1=rsb, op=ALU.mult)
        for k in range(8):
            eng = "gpsimd" if k % 2 == 0 else "vector"
            P.I(eng, "tensor_scalar", reads=[sq, g, mod], writes=[(hT, n0)], out=hT[:, k, n0 - hoff:n0 - hoff + nsz], in0=sq[:, k, :nsz],
                scalar1=g[:, k, w:w + 1], scalar2=mod[:, shoff + k, w:w + 1], op0=ALU.mult, op1=ALU.add)


def build(nlayers=2, stop=None, debug=(), skip=(), heads=range(4), s5stop=None):
    nc = bass.Bass("TRN2", target_bir_lowering=False)
    C = Ctx()
    C.gstack = ExitStack()
    C.din = {}
    C.heads = heads
    C.s5stop = s5stop
    C.skip = skip

    def din(name, shape, dt=F32):
        C.din[name] = nc.dram_tensor(name, list(shape), dt, kind="ExternalInput")

    din("xT", [D, T]); din("cvec", [128, 8, 2]); din("ada_w", [2, D, 6 * D]); din("ada_b", [2, 128, 48])
    din("norm1_w", [2, 128, 8]); din("norm2_w", [2, 128, 8]); din("w_in", [2, D, DIN])
    din("hgrn_lb", [2, 128, 8]); din("hgrn_norm_w", [2, 128])
    din("s5_lam_re", [2, 64, 2, 32]); din("s5_lam_im", [2, 64, 2, 32]); din("s5_log_dt", [2, 2, 32])
    din("s5_b_re", [2, 64, 32, 16]); din("s5_b_im", [2, 64, 32, 16]); din("s5_c_re", [2, 64, 32, 16]); din("s5_c_im", [2, 64, 32, 16]); din("s5_dtab", [2, 128, 32])
    for nm_, shp_ in (("w_branch_a", [2, 512, D]), ("w_branch_b", [2, 512, D]), ("w_branch_c", [2, 512, D]), ("s5_w_glu", [2, 512, 512]), ("w_out", [2, D, D]), ("w_ff1", [2, D, 4 * D]), ("w_ff2", [2, 4 * D, D]), ("final_norm_w", [128, 8])):
        din(nm_, shp_)
    din("gdn_conv_w", [2, 128, 12, 5]); din("gdn_a_log", [2, 8]); din("gdn_dt_bias", [2, 8]); din("gdn_norm_w", [2, 128])
    C.debug = debug
    C.dbg = {}
    if "s5tab" in debug:
        for nm_, shp_ in (("Er", [64, 32, NPOW]), ("Ei", [64, 32, NPOW]), ("bbr", [64, 32, 16]), ("bbi", [64, 32, 16])):
            C.dbg[nm_] = nc.dram_tensor("dbg_" + nm_, shp_, F32, kind="ExternalOutput")
    if "ob" in debug:
        C.dbg["ob"] = nc.dram_tensor("dbg_ob", [T, 512], F32, kind="ExternalOutput")
    if "oa" in debug:
        C.dbg["oa"] = nc.dram_tensor("dbg_oa", [4, T, 128], F32, kind="ExternalOutput")
    C.scr = {}
    for name, (shape, dt) in SCR.items():
        kind = "ExternalOutput" if name in debug else "Internal"
        C.scr[name] = Buf(nc.dram_tensor(name, shape, dt, kind=kind), name)
    C.out = Buf(nc.dram_tensor("outT", [D, 4096], F32, kind="ExternalOutput"), "outT")
    C.xin = Buf(C.din["xT"], "xT")
    with C.gstack:
        make_consts(nc, C)
        for l in range(nlayers):
            phase_adaln(nc, C, l)
            if 'mod' in debug:
                dump_sb(nc, C, C.mod[l], 'dbg_mod%d' % l, [128, 48, 2])
            if stop == "adaln":
                break
            xsrc = C.xin if l == 0 else C.scr["XB"]
            if 'proj' not in skip:
                phase_proj(nc, C, l, xsrc)
            if stop == "proj":
                break
            if 'mixA' not in skip:
                phase_mixA(nc, C, l, heads=C.heads)
            if stop == "mixA":
                break
            if 'mixB' not in skip:
                phase_mixB(nc, C, l)
            if stop == "mixB":
                break
            if 's5' not in skip:
                phase_s5(nc, C, l)
            if stop == "s5":
                break
            last = (l == nlayers - 1) and nlayers == 2
            if 'merge' not in skip:
                phase_merge(nc, C, l, xsrc, C.scr["XA"], tiles=(TTILES[1:] if last else None))
            if stop == "merge":
                break
            if 'ffn' not in skip:
                phase_ffn(nc, C, l, C.scr["XA"], C.scr["XB"], last)
            if stop == "ffn":
                break
    _SHARED[id(nc)]["stack"].close()
    return nc


def host_inputs(inp, b):
    f = np.float32
    m = {}
    xcat = np.concatenate([inp["ctx"][b], inp["x"][b]], axis=0)
    m["xT"] = np.ascontiguousarray(xcat.T)
    cv = np.stack([inp["c"][b].reshape(8, 128).T, inp["c_ctx"].reshape(8, 128).T], axis=-1)
    m["cvec"] = np.ascontiguousarray(cv.astype(f))
    m["ada_w"] = inp["ada_w"]
    m["ada_b"] = np.ascontiguousarray(inp["ada_b"].reshape(2, 48, 128).transpose(0, 2, 1))
    m["norm1_w"] = np.ascontiguousarray(inp["norm1_w"].reshape(2, 8, 128).transpose(0, 2, 1))
    m["norm2_w"] = np.ascontiguousarray(inp["norm2_w"].reshape(2, 8, 128).transpose(0, 2, 1))
    m["w_in"] = inp["w_in"]
    m["hgrn_lb"] = np.ascontiguousarray(inp["hgrn_lb_logits"].reshape(2, 8, 128).transpose(0, 2, 1))
    m["hgrn_norm_w"] = inp["hgrn_norm_w"]
    m["gdn_conv_w"] = np.ascontiguousarray(inp["gdn_conv_w"].reshape(2, 5, 12, 128).transpose(0, 3, 2, 1))
    m["gdn_a_log"] = np.ascontiguousarray(inp["gdn_a_log"].reshape(2, 8))
    m["gdn_dt_bias"] = np.ascontiguousarray(inp["gdn_dt_bias"].reshape(2, 8))
    m["gdn_norm_w"] = inp["gdn_norm_w"]
    for nm_ in ("w_branch_a", "w_branch_b", "w_branch_c", "s5_w_glu", "w_out", "w_ff1", "w_ff2"):
        m[nm_] = inp[nm_]
    m["final_norm_w"] = np.ascontiguousarray(inp["final_norm_w"].reshape(8, 128).T)
    m["s5_lam_re"] = np.ascontiguousarray(inp["s5_lam_re"].transpose(0, 3, 1, 2))
    m["s5_lam_im"] = np.ascontiguousarray(inp["s5_lam_im"].transpose(0, 3, 1, 2))
    m["s5_log_dt"] = inp["s5_log_dt"]
    m["s5_b_re"] = np.ascontiguousarray(inp["s5_b_re"].transpose(0, 2, 1, 3))
    m["s5_b_im"] = np.ascontiguousarray(inp["s5_b_im"].transpose(0, 2, 1, 3))
    m["s5_c_re"] = np.ascontiguousarray(inp["s5_c_re"].transpose(0, 3, 1, 2))
    m["s5_c_im"] = np.ascontiguousarray(inp["s5_c_im"].transpose(0, 3, 1, 2))
    dt_ = inp["s5_d"].reshape(2, 32, 16).transpose(0, 2, 1)
    m["s5_dtab"] = np.ascontiguousarray(np.tile(dt_, (1, 8, 1)))
    return m


_NC = None


def kernel(**inputs):
    global _NC
    inp = {k: np.asarray(v) for k, v in inputs.items()}
    if _NC is None:
        _NC = build()
    in_maps = [host_inputs(inp, b) for b in range(8)]
    res = run_bass_kernel_spmd(_NC, in_maps, core_ids=list(range(8)))
    out = np.stack([np.ascontiguousarray(res.results[b]["outT"].T) for b in range(8)], axis=0)
    return out.astype(np.float32)


CH_FWD = list(range(68))
CH_BWD = [3, 2, 1, 0] + list(range(67, 3, -1))


def make_masks(P, C):
    nc = P.nc
    st = C.gstack
    def gsb(name, shape, dt=F32):
        return Buf(st.enter_context(nc.sbuf_tensor(name, list(shape), dt)), name)
    one8 = gsb("one8", [64, 8, 64])
    P.I("vector", "memset", writes=[one8], ap=one8[:], constant=1.0)
    C.maskf = {}
    C.maski = {}
    for nm, cm, st_, op in (("U", -1, 1, ALU.is_ge), ("L", 1, -1, ALU.is_ge), ("Us", -1, 1, ALU.is_gt), ("Ls", 1, -1, ALU.is_gt)):
        mf = gsb("maskf" + nm, [64, 8, 64])
        P.I("gpsimd", "affine_select", reads=[one8], writes=[mf], out=mf[:], in_=one8[:], pattern=[[0, 8], [st_, 64]],
            compare_op=op, fill=0.0, base=0, channel_multiplier=cm)
        mi = gsb("maski" + nm, [64, 8, 64], I32)
        P.I("vector", "tensor_copy", reads=[mf], writes=[mi], out=mi[:], in_=mf[:])
        C.maskf[nm] = mf
        C.maski[nm] = mi
    C.rmask = gsb("rmask", [128, 512])
    P.I("vector", "memset", writes=[C.rmask], ap=C.rmask[:], constant=1.0)
    P.I("vector", "memset", reads=[C.rmask], writes=[C.rmask], ap=AP(C.rmask, 0, [[512, 128], [64, 8]]), constant=0.0)


def phase_mixA(nc, C, l, heads=range(4)):
    heads = list(heads)
    P = Prog(nc)
    lbl = P.sb("lbl", [128, 2, 8])
    lb = P.sb("lb", [128, 8])
    oml = P.sb("oml", [128, 8])
    if l == 0:
        P.I("vector", "memset", writes=[lb], ap=lb[:], constant=0.0)
        P.I("vector", "memset", writes=[oml], ap=oml[:], constant=1.0)
    else:
        P.D("sync", lbl[:], C.din["hgrn_lb"].ap().rearrange("l p e -> p l e"), writes=[lbl])
        P.I("vector", "tensor_tensor", reads=[lbl], writes=[lb], out=lb[:], in0=lbl[:, 1, :], in1=lbl[:, 0, :], op=ALU.subtract)
        P.I("scalar", "activation", reads=[lb], writes=[lb], out=lb[:], in_=lb[:], func=AF.Sigmoid)
        P.I("vector", "tensor_scalar", reads=[lb], writes=[oml], out=oml[:], in0=lb[:], scalar1=-1.0, scalar2=1.0, op0=ALU.mult, op1=ALU.add)
    nwb = P.sb("nwb", [64, 128])
    P.D("sync", nwb[:], C.din["hgrn_norm_w"][l].partition_broadcast(64), writes=[nwb])
    v_tms = [P.sb("v_tm%d" % i, [64, 68, 128], BF16) for i in range(1)]
    o_acc = P.sb("o_acc", [64, 68, 128])
    ngT = P.sb("ngT", [128, T], BF16)
    sets = []
    for i in range(2):
        sets.append(dict(attT=P.sb("attT%d" % i, [64, 68, 64], BF16), kd_tm=P.sb("kd_tm%d" % i, [64, 68, 128], BF16),
                         qg=P.sb("qg%d" % i, [128, T], BF16), dec=P.sb("dec%d" % i, [128, 68]), last_dir=[None]))
        P.I("gpsimd", "memset", writes=[sets[i]["attT"]], ap=sets[i]["attT"][:], constant=0.0)
    S = P.sb("S", [128, 128])
    Sb = P.sb("Sb", [128, 128], BF16)
    qpre = P.sb("qpre", [128, 512], BF16)
    W = {n: P.sb(n, [128, 512]) for n in ("q32", "F", "G", "Kt", "CUM", "Dd", "E", "E2")}
    qe = P.sb("qe", [128, 512], BF16)
    ke = P.sb("ke", [128, 512], BF16)
    kdT = P.sb("kdT", [128, 512], BF16)
    qeA = P.sb("qeA", [128, 512], BF16)
    keA = P.sb("keA", [128, 512], BF16)
    refA = P.sb("refA", [128, 16])
    sm = {n: P.sb(n, [128, 8]) for n in ("tot", "lastc", "refc")}
    pa = [P.ps("pa%d" % i, [128, 512]) for i in range(2)]
    pt = [P.ps("pt%d" % i, [128, 512]) for i in range(2)]
    po = [P.ps("po%d" % i, [128, 512]) for i in range(2)]
    pd = [P.ps("pd%d" % i, [128, 512]) for i in range(2)]
    for pz in pa:
        P.I("vector", "memset", writes=[pz], ap=pz[:], constant=0.0)
    gate = P.sb("gate", [64, 8, 128], BF16)
    sg = P.sb("sg", [64, 8, 128])
    sq = P.sb("sqo", [64, 8, 128])
    on = P.sb("on", [64, 8, 128])
    onb = P.sb("onb", [64, 8, 128], BF16)
    ssq = P.sb("ssq", [64, 8])
    agv = C.scr["AG"].t.ap().rearrange("(c p) d -> p c d", p=64)
    aiv = C.scr["AI"].t.ap().rearrange("(c p) d -> p c d", p=64)
    cnt = [0]

    def prep(h, dr, st_):
        attT, kd_tm, qg, dec = st_["attT"], st_["kd_tm"], st_["qg"], st_["dec"]
        dh = dr * 4 + h
        if st_["last_dir"][0] is not None and st_["last_dir"][0] != dr:
            P.I("gpsimd", "memset", reads=[attT], writes=[attT], ap=attT[:], constant=0.0)
        st_["last_dir"][0] = dr
        for (n0, n, w) in TTILES:
            nch = n // 64
            c0 = n0 // 64
            q32, Fb, G, Kt, CUM, Dd, E, E2 = (W[k] for k in ("q32", "F", "G", "Kt", "CUM", "Dd", "E", "E2"))
            P.D("sync", qpre[:, :n], C.scr["AQ"][h * 128:(h + 1) * 128, n0:n0 + n], reads=[C.scr["AQ"]], writes=[qpre])
            r0 = dr * 512 + h * 128
            P.D("sync", Fb[:, :n], C.scr["AF"][r0:r0 + 128, n0:n0 + n], reads=[C.scr["AF"]], writes=[Fb])
            P.I("scalar", "activation", reads=[qpre], writes=[q32], out=q32[:, :n], in_=qpre[:, :n], func=AF.Silu)
            P.I("scalar", "activation", reads=[Fb], writes=[Fb], out=Fb[:, :n], in_=Fb[:, :n], func=AF.Sigmoid)
            P.I("vector", "tensor_scalar", reads=[Fb, oml, lb], writes=[Fb], out=Fb[:, :n], in0=Fb[:, :n], scalar1=oml[:, dh:dh + 1],
                scalar2=lb[:, dh:dh + 1], op0=ALU.mult, op1=ALU.add)
            yield
            P.I("scalar", "activation", reads=[Fb], writes=[G], out=G[:, :n], in_=Fb[:, :n], func=AF.Ln)
            P.I("gpsimd", "tensor_scalar", reads=[Fb], writes=[Kt], out=Kt[:, :n], in0=Fb[:, :n], scalar1=-1.0, scalar2=1.0, op0=ALU.mult, op1=ALU.add)
            P.I("vector", "tensor_tensor_scan", reads=[G, C.rmask], writes=[CUM], out=CUM[:, :n], data0=C.rmask[:, :n], data1=G[:, :n],
                initial=0.0, op0=ALU.mult, op1=ALU.add)

            def cview(buf, off):
                return AP(buf, off, [[512, 128], [64, nch]])

            def bview(buf):
                return AP(buf, 0, [[8, 128], [1, nch], [0, 64]])

            def v3(buf):
                return AP(buf, 0, [[512, 128], [64, nch], [1, 64]])
            if dr == 1:
                P.I("vector", "tensor_copy", reads=[CUM], writes=[sm["tot"]], out=sm["tot"][:, :nch], in_=cview(CUM, 63))
                P.I("gpsimd", "tensor_tensor", reads=[G, CUM], writes=[G], out=G[:, :n], in0=G[:, :n], in1=CUM[:, :n], op=ALU.subtract)
                P.I("vector", "tensor_tensor", reads=[G, sm["tot"]], writes=[CUM], out=v3(CUM), in0=v3(G), in1=bview(sm["tot"]), op=ALU.add)
            yield
            P.I("vector", "tensor_copy", reads=[CUM], writes=[sm["lastc"]], out=sm["lastc"][:, :nch], in_=cview(CUM, 63 if dr == 0 else 0))
            P.I("vector", "tensor_copy", reads=[CUM], writes=[sm["refc"]], out=sm["refc"][:, :nch], in_=cview(CUM, 32))
            P.I("scalar", "activation", reads=[sm["lastc"]], writes=[(dec, c0)], out=dec[:, c0:c0 + nch], in_=sm["lastc"][:, :nch], func=AF.Exp)
            P.I("vector", "tensor_tensor", reads=[CUM, sm["refc"]], writes=[Dd], out=v3(Dd), in0=v3(CUM), in1=bview(sm["refc"]), op=ALU.subtract)
            P.I("vector", "tensor_scalar", reads=[Dd], writes=[Dd], out=Dd[:, :n], in0=Dd[:, :n], scalar1=80.0, scalar2=-80.0, op0=ALU.min, op1=ALU.max)
            yield
            P.I("scalar", "activation", reads=[Dd], writes=[E], out=E[:, :n], in_=Dd[:, :n], func=AF.Exp)
            P.I("gpsimd", "tensor_tensor", reads=[q32, E], writes=[qe], out=qe[:, :n], in0=q32[:, :n], in1=E[:, :n], op=ALU.mult)
            P.I("scalar", "activation", reads=[Dd], writes=[E2], out=E2[:, :n], in_=Dd[:, :n], func=AF.Exp, scale=-1.0)
            P.I("vector", "tensor_tensor", reads=[Kt, E2], writes=[ke], out=ke[:, :n], in0=Kt[:, :n], in1=E2[:, :n], op=ALU.mult)
            yield
            nb2 = 2 * nch
            P.I("vector", "tensor_copy", reads=[CUM], writes=[refA], out=refA[:, :nb2], in_=AP(CUM, 16, [[512, 128], [32, nb2]]))
            v3a = lambda buf: AP(buf, 0, [[512, 128], [32, nb2], [1, 32]])
            P.I("vector", "tensor_tensor", reads=[CUM, refA], writes=[Dd], out=v3a(Dd), in0=v3a(CUM), in1=AP(refA, 0, [[16, 128], [1, nb2], [0, 32]]), op=ALU.subtract)
            P.I("vector", "tensor_scalar", reads=[Dd], writes=[Dd], out=Dd[:, :n], in0=Dd[:, :n], scalar1=40.0, scalar2=-40.0, op0=ALU.min, op1=ALU.max)
            P.I("scalar", "activation", reads=[Dd], writes=[E], out=E[:, :n], in_=Dd[:, :n], func=AF.Exp)
            P.I("gpsimd", "tensor_tensor", reads=[q32, E], writes=[qeA], out=qeA[:, :n], in0=q32[:, :n], in1=E[:, :n], op=ALU.mult)
            yield
            P.I("scalar", "activation", reads=[Dd], writes=[E2], out=E2[:, :n], in_=Dd[:, :n], func=AF.Exp, scale=-1.0)
            P.I("vector", "tensor_tensor", reads=[Kt, E2], writes=[keA], out=keA[:, :n], in0=Kt[:, :n], in1=E2[:, :n], op=ALU.mult)
            P.I("scalar", "activation", reads=[CUM], writes=[E], out=E[:, :n], in_=CUM[:, :n], func=AF.Exp)
            P.I("gpsimd", "tensor_tensor", reads=[q32, E], writes=[(qg, n0)], out=qg[:, n0:n0 + n], in0=q32[:, :n], in1=E[:, :n], op=ALU.mult)
            yield
            P.I("vector", "tensor_tensor", reads=[CUM, sm["lastc"]], writes=[Dd], out=v3(Dd), in0=bview(sm["lastc"]), in1=v3(CUM), op=ALU.subtract)
            P.I("scalar", "activation", reads=[Dd], writes=[E2], out=E2[:, :n], in_=Dd[:, :n], func=AF.Exp)
            P.I("gpsimd", "tensor_tensor", reads=[Kt, E2], writes=[kdT], out=kdT[:, :n], in0=Kt[:, :n], in1=E2[:, :n], op=ALU.mult)
            ppa = pa[cnt[0] % 2]
            ppt = pt[cnt[0] % 2]
            cnt[0] += 1
            for j in range(nch):
                b_ = j * 64
                P.mm(ppa[0:32, b_:b_ + 32], keA[:, b_:b_ + 32], qeA[:, b_:b_ + 32], True, True, reads=[keA, qeA], writes=[ppa])
                P.mm(ppa[32:64, b_ + 32:b_ + 64], keA[:, b_ + 32:b_ + 64], qeA[:, b_ + 32:b_ + 64], True, True, reads=[keA, qeA], writes=[ppa])
                if dr == 0:
                    P.mm(ppa[0:32, b_ + 32:b_ + 64], ke[:, b_:b_ + 32], qe[:, b_ + 32:b_ + 64], True, True, reads=[ke, qe], writes=[ppa])
                else:
                    P.mm(ppa[32:64, b_:b_ + 32], ke[:, b_ + 32:b_ + 64], qe[:, b_:b_ + 32], True, True, reads=[ke, qe], writes=[ppa])
            yield
            mk = C.maski["U" if dr == 0 else "L"]
            P.I("vector", "copy_predicated", reads=[ppa, mk, (attT, n0)], writes=[(attT, n0)], out=attT[:, c0:c0 + nch, :],
                mask=mk[:, :nch, :], data=ppa[0:64, 0:nch * 64].rearrange("p (a b) -> p a b", b=64))
            ptv = ppt[0:64, :].bitcast(BF16).rearrange("p (a b) -> p a b", b=128)
            for j in range(nch):
                P.tr(ptv[:, j, :], kdT[:, j * 64:(j + 1) * 64], C.identb[:], reads=[kdT, C.identb], writes=[ppt])
            P.I("scalar", "activation", reads=[ppt], writes=[(kd_tm, n0)], out=kd_tm[:, c0:c0 + nch, :], in_=ptv[:, :nch, :], func=AF.Copy)
            yield

    def chain(h, dr, st_, v_tm):
        attT, kd_tm, qg, dec = st_["attT"], st_["kd_tm"], st_["qg"], st_["dec"]
        if dr == 0:
            P.D("sync", v_tm[:], aiv[:, :, h * 128:(h + 1) * 128], reads=[C.scr["AI"]], writes=[v_tm])
        P.I("vector", "memset", reads=[S], writes=[S], ap=S[:], constant=0.0)
        P.I("gpsimd", "memset", reads=[Sb], writes=[Sb], ap=Sb[:], constant=0.0)
        for ci, c in enumerate(CH_FWD if dr == 0 else CH_BWD):
            key = tkey(c * 64)
            ppo = po[ci % 2]
            ppd = pd[ci % 2]
            P.mm(ppo[0:64, 0:128], attT[:, c, :], v_tm[:, c, :], True, False, reads=[(attT, key), v_tm], writes=[ppo])
            P.mm(ppo[0:64, 0:128], qg[:, c * 64:(c + 1) * 64], Sb[:], False, True, reads=[(qg, key), Sb], writes=[ppo])
            if dr == 0:
                P.I("scalar", "activation", reads=[ppo], writes=[(o_acc, c)], out=o_acc[:, c, :], in_=ppo[0:64, 0:128], func=AF.Copy)
            else:
                P.I("vector", "tensor_tensor", reads=[ppo, (o_acc, c)], writes=[(o_acc, c)], out=o_acc[:, c, :], in0=ppo[0:64, 0:128],
                    in1=o_acc[:, c, :], op=ALU.add)
            P.mm(ppd[:, 0:128], kd_tm[:, c, :], v_tm[:, c, :], True, True, reads=[(kd_tm, key), v_tm], writes=[ppd])
            P.I("vector", "scalar_tensor_tensor", reads=[S, ppd, (dec, tkey(c * 64) // 64)], writes=[S], out=S[:], in0=S[:], scalar=dec[:, c:c + 1],
                in1=ppd[:, 0:128], op0=ALU.mult, op1=ALU.add)
            P.I("gpsimd", "tensor_copy", reads=[S], writes=[Sb], out=Sb[:], in_=S[:])
            yield
        if dr == 0:
            return
        for (n0, n, w) in TTILES:
            nch = n // 64
            c0 = n0 // 64
            ov = o_acc[:, c0:c0 + nch, :]
            okeys = [(o_acc, c) for c in range(c0, c0 + nch)]
            P.D("sync", gate[:, :nch, :], agv[:, c0:c0 + nch, h * 128:(h + 1) * 128], reads=[C.scr["AG"]], writes=[gate])
            P.I("scalar", "activation", reads=[gate], writes=[sg], out=sg[:, :nch, :], in_=gate[:, :nch, :], func=AF.Silu)
            P.I("gpsimd", "tensor_tensor", reads=okeys, writes=[sq], out=sq[:, :nch, :], in0=ov, in1=ov, op=ALU.mult)
            P.I("vector", "tensor_reduce", reads=[sq], writes=[ssq], out=ssq[:, :nch], in_=sq[:, :nch, :], axis=AX.X, op=ALU.add)
            P.I("scalar", "activation", reads=[ssq], writes=[ssq], out=ssq[:, :nch], in_=ssq[:, :nch], func=AF.Sqrt, scale=1.0 / 128, bias=EPS)
            P.I("vector", "reciprocal", reads=[ssq], writes=[ssq], out=ssq[:, :nch], in_=ssq[:, :nch])
            yield
            P.I("vector", "tensor_tensor", reads=okeys + [ssq], writes=[on], out=on[:, :nch, :], in0=ov, in1=AP(ssq, 0, [[8, 64], [1, nch], [0, 128]]), op=ALU.mult)
            P.I("gpsimd", "tensor_tensor", reads=[on, nwb], writes=[on], out=on[:, :nch, :], in0=on[:, :nch, :], in1=AP(nwb, 0, [[128, 64], [0, nch], [1, 128]]), op=ALU.mult)
            P.I("vector", "tensor_tensor", reads=[on, sg], writes=[onb], out=onb[:, :nch, :], in0=on[:, :nch, :], in1=sg[:, :nch, :], op=ALU.mult)
            ppt = pt[cnt[0] % 2]
            cnt[0] += 1
            ptv = ppt[:, 0:256].bitcast(BF16).rearrange("p (a b) -> p a b", b=64)
            for j in range(nch):
                P.tr(ptv[:, j, :], onb[:, j, :], C.identb[0:64, 0:64], reads=[onb, C.identb], writes=[ppt])
            P.I("scalar", "activation", reads=[ppt], writes=[(ngT, n0)], out=ngT[:, n0:n0 + n], in_=ppt[:, 0:256].bitcast(BF16)[:, 0:n], func=AF.Copy)
            yield
        P.D("sync", C.scr["NGA"][h * 128:(h + 1) * 128, :], ngT[:], reads=[ngT], writes=[(C.scr["NGA"], h)])
        if "oa" in C.debug:
            P.D("sync", C.dbg["oa"].ap().rearrange("h (c p) d -> h p c d", p=64)[h], o_acc[:], reads=[o_acc])
        yield

    units = [(h, dr) for h in heads for dr in (0, 1)]

    def drive(gens):
        alive = list(gens)
        while alive:
            for g_ in list(alive):
                try:
                    next(g_)
                except StopIteration:
                    alive.remove(g_)

    drive([prep(units[0][0], units[0][1], sets[0])])
    for i, (h, dr) in enumerate(units):
        gens = [chain(h, dr, sets[i % 2], v_tms[0])]
        if i + 1 < len(units):
            gens.append(prep(units[i + 1][0], units[i + 1][1], sets[(i + 1) % 2]))
        drive(gens)
    P.emit()


def phase_mixB(nc, C, l):
    with ExitStack() as ost:
        _phase_mixB(nc, C, l, ost)
    _mixB_post(nc, C, l)


def _phase_mixB(nc, C, l, ost):
    def psb(name, shape, dt=F32):
        _UID[0] += 1
        return Buf(ost.enter_context(nc.sbuf_tensor("%s_%d" % (name, _UID[0]), list(shape), dt)), name)
    qnT = [psb("qnT%d" % h, [128, T], BF16) for h in range(4)]
    knT = [psb("knT%d" % h, [128, T], BF16) for h in range(4)]
    vT = [psb("vT%d" % h, [128, T], BF16) for h in range(4)]
    bet = psb("bet", [64, 68, 8])
    nbet = psb("nbet", [64, 68, 8])
    gg = psb("gg", [64, 68, 8])
    nwb = psb("nwbB", [64, 128])
    P = Prog(nc)
    I = P.I
    banks = [P.ps("bk%d" % i, [128, 512]) for i in range(8)]
    b0, b1, b2, b3, b4, b5, b6, b7 = banks
    cw = P.sb("cw", [128, 12, 5])
    P.D("sync", cw[:], C.din["gdn_conv_w"][l], writes=[cw])
    xpads = [P.sb("xpad%d" % i, [128, T + 8], BF16) for i in range(1)]
    for xp in xpads:
        I("gpsimd", "memset", writes=[xp], ap=xp[:], constant=0.0)
    diag = P.sb("diag", [128, 5, 128], BF16)
    xs = P.sb("xs", [128, 512])
    sqb = P.sb("sqb", [128, 512], BF16)
    rs = P.sb("rs", [128, 512])
    bq = C.scr["BQKV"]
    for ch in range(12):
        xp = xpads[0]
        P.D("sync", xp[:, 2:258], bq[ch * 128:(ch + 1) * 128, 0:256], reads=[bq], writes=[(xp, 0)])
        P.D("sync", xp[:, 262:4358], bq[ch * 128:(ch + 1) * 128, 256:T], reads=[bq], writes=[(xp, 1)])
        for j in range(5):
            I("vector", "tensor_scalar", reads=[C.identb, cw], writes=[(diag, j)], out=diag[:, j, :], in0=C.identb[:], scalar1=cw[:, ch, j:j + 1],
              scalar2=None, op0=ALU.mult)
        kind, h = ch // 4, ch % 4
        for ti, (n0, n, w) in enumerate(TTILES):
            po_ = n0 + 2 if n0 == 0 else n0 + 6
            pc = banks[ti % 2]
            for j in range(5):
                P.mm(pc[:, :n], diag[:, j, :], xp[:, po_ + j - 2:po_ + j - 2 + n], j == 0, j == 4, reads=[diag, xp], writes=[pc])
            if kind == 2:
                I("scalar", "activation", reads=[pc], writes=[(vT[h], n0)], out=vT[h][:, n0:n0 + n], in_=pc[:, :n], func=AF.Silu)
                continue
            I("scalar", "activation", reads=[pc], writes=[xs], out=xs[:, :n], in_=pc[:, :n], func=AF.Silu)
            I("gpsimd", "tensor_tensor", reads=[xs], writes=[sqb], out=sqb[:, :n], in0=xs[:, :n], in1=xs[:, :n], op=ALU.mult)
            pq = banks[2 + ti % 2]
            P.mm(pq[:, :n], C.onesb[:], sqb[:, :n], True, True, reads=[sqb, C.onesb], writes=[pq])
            I("scalar", "activation", reads=[pq], writes=[rs], out=rs[:, :n], in_=pq[:, :n], func=AF.Sqrt, bias=EPS)
            I("vector", "reciprocal", reads=[rs], writes=[rs], out=rs[:, :n], in_=rs[:, :n])
            dst = qnT[h] if kind == 0 else knT[h]
            I("vector", "scalar_tensor_tensor", reads=[xs, rs], writes=[(dst, n0)], out=dst[:, n0:n0 + n], in0=xs[:, :n],
              scalar=(128 ** -0.5 if kind == 0 else 1.0), in1=rs[:, :n], op0=ALU.mult, op1=ALU.mult)
    bba = P.sb("bba", [64, 68, 16])
    P.D("sync", bba[:], C.scr["BBA"].t.ap().rearrange("(c p) e -> p c e", p=64), reads=[C.scr["BBA"]], writes=[bba])
    alb = P.sb("alb", [64, 8])
    dtb = P.sb("dtb", [64, 8])
    P.D("sync", alb[:], C.din["gdn_a_log"][l].partition_broadcast(64), writes=[alb])
    P.D("sync", dtb[:], C.din["gdn_dt_bias"][l].partition_broadcast(64), writes=[dtb])
    P.D("sync", nwb[:], C.din["gdn_norm_w"][l].partition_broadcast(64), writes=[nwb])
    xa = P.sb("xa", [64, 68, 8])
    t1 = P.sb("t1", [64, 68, 8])
    I("scalar", "activation", reads=[bba], writes=[bet], out=bet[:], in_=bba[:, :, 0:8], func=AF.Sigmoid)
    I("vector", "tensor_scalar", reads=[bet], writes=[nbet], out=nbet[:], in0=bet[:], scalar1=-1.0, scalar2=None, op0=ALU.mult)
    I("vector", "tensor_tensor", reads=[bba, dtb], writes=[xa], out=xa[:], in0=bba[:, :, 8:16], in1=AP(dtb, 0, [[8, 64], [0, 68], [1, 8]]), op=ALU.add)
    I("vector", "tensor_scalar", reads=[xa], writes=[t1], out=t1[:], in0=xa[:], scalar1=-1.0, scalar2=None, op0=ALU.mult)
    I("vector", "tensor_tensor", reads=[xa, t1], writes=[t1], out=t1[:], in0=xa[:], in1=t1[:], op=ALU.max)
    I("scalar", "activation", reads=[t1], writes=[t1], out=t1[:], in_=t1[:], func=AF.Exp, scale=-1.0)
    I("scalar", "activation", reads=[t1], writes=[t1], out=t1[:], in_=t1[:], func=AF.Ln, bias=1.0)
    I("vector", "tensor_scalar", reads=[xa], writes=[xa], out=xa[:], in0=xa[:], scalar1=0.0, scalar2=None, op0=ALU.max)
    I("vector", "tensor_tensor", reads=[xa, t1], writes=[xa], out=xa[:], in0=xa[:], in1=t1[:], op=ALU.add)
    I("scalar", "activation", reads=[alb], writes=[alb], out=alb[:], in_=alb[:], func=AF.Exp)
    I("vector", "scalar_tensor_tensor", reads=[xa, alb], writes=[gg], out=gg[:], in0=xa[:], scalar=-1.0, in1=AP(alb, 0, [[8, 64], [0, 68], [1, 8]]),
      op0=ALU.mult, op1=ALU.mult)
    P.emit()
    P = Prog(nc)
    I = P.I
    banks = [P.ps("bk%d" % i, [128, 512]) for i in range(8)]
    for bk in banks:
        I("vector", "memset", writes=[bk], ap=bk[:], constant=0.0)
    ident4 = AP(C.ident, 0, [[128, 64], [0, 4], [1, 64]])

    def v464(bank):
        return bank[0:64, 0:256].rearrange("p (h s) -> p h s", s=64)

    def v4128(bank):
        return bank[0:64, 0:512].rearrange("p (h s) -> p h s", s=128)

    def stream(dr, k0, k1, k2, k3):
        sfx = "_%d" % dr
        sb = lambda name, shape, dt=F32: P.sb(name + sfx, shape, dt)
        kv_tm = sb("kv_tm", [64, 8, 128], BF16)
        ct = sb("ct", [128, 8])
        ecum = sb("ecum", [64, 4]); edl = sb("edl", [64, 4]); etot = sb("etot", [128, 4]); be = sb("be", [64, 4])
        G1 = sb("G1", [64, 4, 64]); G2 = sb("G2", [64, 4, 64])
        m1 = sb("m1", [64, 4, 64]); m2 = sb("m2", [64, 4, 64])
        attT = sb("attTb", [64, 4, 64], BF16)
        tmpM = sb("tmpM", [64, 4, 64])
        X = [sb("X%d" % i, [64, 4, 64]) for i in range(2)]
        Y = [sb("Y%d" % i, [64, 4, 64]) for i in range(2)]
        R = [sb("R%d" % i, [64, 4, 64]) for i in range(2)]
        IM2 = sb("IM2", [64, 4, 64])
        TTb = sb("TTb", [64, 4, 64], BF16)
        vb = sb("vb", [64, 4, 128], BF16); kbe = sb("kbe", [64, 4, 128], BF16); kd = sb("kd", [64, 4, 128], BF16)
        u_sb = sb("u_sb", [64, 4, 128]); wTb = sb("wTb", [128, 4, 64])
        v_new = sb("v_new", [64, 4, 128], BF16)
        t2 = sb("t2", [64, 4, 128]); o_sb = [sb("o_sb%d" % i, [64, 4, 128]) for i in range(2)]
        S = sb("Sg", [128, 4, 128]); St = sb("St", [128, 4, 128]); Sb = sb("Sbg", [128, 4, 128], BF16)
        ODST = C.scr["OF"] if dr == 0 else C.scr["OBW"]
        I("vector", "memset", writes=[S], ap=S[:], constant=0.0)
        I("gpsimd", "memset", writes=[Sb], ap=Sb[:], constant=0.0)
        mU = C.maskf["U" if dr == 0 else "L"]
        mLs = C.maskf["Ls" if dr == 0 else "Us"]
        triM = mU[:, 0, :]
        tri4 = AP(mU, 0, [[512, 64], [0, 4], [1, 64]])
        for ci, c in enumerate(CH_FWD if dr == 0 else CH_BWD):
            key = tkey(c * 64)
            cs = slice(c * 64, (c + 1) * 64)
            goff = c * 8 + dr * 4
            g4 = AP(gg, goff, [[544, 64], [1, 4]])
            g4b = AP(gg, goff, [[544, 64], [1, 4], [0, 64]])
            bet4b = AP(bet, goff, [[544, 64], [1, 4], [0, 128]])
            nbet4b = AP(nbet, goff, [[544, 64], [1, 4], [0, 64]])
            bet4 = AP(bet, goff, [[544, 64], [1, 4]])
            k0v = k0[0:64, :].bitcast(BF16).rearrange("p (a b) -> p a b", b=128)
            for h in range(4):
                P.tr(k0v[:, h, :], knT[h][:, cs], C.identb[:], reads=[(knT[h], key), C.identb], writes=[k0])
                P.tr(k0v[:, 4 + h, :], vT[h][:, cs], C.identb[:], reads=[(vT[h], key), C.identb], writes=[k0])
            k1v = k1[0:64, :].rearrange("p (h two s) -> p h two s", h=4, two=2)
            for h in range(4):
                P.mm(k1[0:64, (2 * h) * 64:(2 * h + 1) * 64], knT[h][:, cs], knT[h][:, cs], True, True, reads=[(knT[h], key)], writes=[k1])
                P.mm(k1[0:64, (2 * h + 1) * 64:(2 * h + 2) * 64], knT[h][:, cs], qnT[h][:, cs], True, True, reads=[(knT[h], key), (qnT[h], key)], writes=[k1])
            P.mm(k2[0:64, 256:260], triM, g4, True, True, reads=[mU, gg], writes=[k2])
            P.mm(k2[:, 260:264], C.ones[0:64, :], g4, True, True, reads=[C.ones, gg], writes=[k2])
            yield
            I("scalar", "activation", reads=[k0], writes=[kv_tm], out=kv_tm[:], in_=k0v, func=AF.Copy)
            I("vector", "tensor_copy", reads=[k2], writes=[ct], out=ct[:], in_=k2[:, 256:264])
            I("scalar", "activation", reads=[ct], writes=[ecum], out=ecum[:], in_=ct[0:64, 0:4], func=AF.Exp)
            I("vector", "tensor_tensor", reads=[ct], writes=[edl], out=edl[:], in0=ct[0:64, 4:8], in1=ct[0:64, 0:4], op=ALU.subtract)
            I("scalar", "activation", reads=[edl], writes=[edl], out=edl[:], in_=edl[:], func=AF.Exp)
            I("scalar", "activation", reads=[ct], writes=[etot], out=etot[:], in_=ct[:, 4:8], func=AF.Exp)
            I("vector", "tensor_tensor", reads=[bet, ecum], writes=[be], out=be[:], in0=bet4, in1=ecum[:], op=ALU.mult)
            yield
            I("gpsimd", "tensor_copy", reads=[gg], writes=[G1], out=G1[:], in_=g4b)
            I("vector", "scalar_tensor_tensor", reads=[mU, gg], writes=[G2], out=G2[:], in0=tri4, scalar=-1.0, in1=g4b, op0=ALU.mult, op1=ALU.mult)
            for h in range(4):
                P.mm(k2[0:64, h * 64:(h + 1) * 64], G1[:, h, :], triM, True, False, reads=[G1, mU], writes=[k2])
                P.mm(k2[0:64, h * 64:(h + 1) * 64], G2[:, h, :], C.ones[0:64, 0:64], False, True, reads=[G2, C.ones], writes=[k2])
            yield
            Dv = v464(k2)
            I("vector", "tensor_scalar", reads=[k2], writes=[m1], out=m1[:], in0=Dv, scalar1=0.0, scalar2=None, op0=ALU.min)
            I("vector", "tensor_scalar", reads=[k2], writes=[m2], out=m2[:], in0=Dv, scalar1=0.0, scalar2=-1.0, op0=ALU.max, op1=ALU.mult)
            I("scalar", "activation", reads=[m1], writes=[m1], out=m1[:], in_=m1[:], func=AF.Exp)
            I("scalar", "activation", reads=[m2], writes=[m2], out=m2[:], in_=m2[:], func=AF.Exp)
            I("gpsimd", "tensor_tensor", reads=[m1, mU], writes=[m1], out=m1[:], in0=m1[:], in1=mU[:, 0:4, :], op=ALU.mult)
            I("gpsimd", "tensor_tensor", reads=[m2, mLs], writes=[m2], out=m2[:], in0=m2[:], in1=mLs[:, 0:4, :], op=ALU.mult)
            yield
            I("vector", "tensor_tensor", reads=[k1, m1], writes=[attT], out=attT[:], in0=k1v[:, :, 1, :], in1=m1[:], op=ALU.mult)
            I("vector", "tensor_tensor", reads=[k1, nbet], writes=[tmpM], out=tmpM[:], in0=k1v[:, :, 0, :], in1=nbet4b, op=ALU.mult)
            I("gpsimd", "tensor_tensor", reads=[tmpM, m2], writes=[X[0]], out=X[0][:], in0=tmpM[:], in1=m2[:], op=ALU.mult)
            k0n = k0[0:64, 0:256].rearrange("p (a b) -> p a b", b=64)
            for h in range(4):
                P.tr(k0n[:, h, :], X[0][:, h, :], C.ident[0:64, 0:64], reads=[X[0], C.ident], writes=[k0])
            yield
            I("scalar", "activation", reads=[k0], writes=[Y[0]], out=Y[0][:], in_=k0n, func=AF.Copy)
            I("vector", "tensor_tensor", reads=[k0, C.ident], writes=[R[0]], out=R[0][:], in0=k0n, in1=ident4, op=ALU.add)
            yield
            cur = 0
            for step in range(5):
                last = step == 4
                nxt = 1 - cur
                if not last:
                    for h in range(4):
                        P.mm(k2[0:64, h * 64:(h + 1) * 64], X[cur][:, h, :], Y[cur][:, h, :], True, True, reads=[X[cur], Y[cur]], writes=[k2])
                for h in range(4):
                    P.mm(k3[0:64, h * 64:(h + 1) * 64], Y[cur][:, h, :], X[cur][:, h, :], True, True, reads=[X[cur], Y[cur]], writes=[k3])
                if not last:
                    I("scalar", "activation", reads=[k2], writes=[Y[nxt]], out=Y[nxt][:], in_=v464(k2), func=AF.Copy)
                I("vector", "tensor_tensor", reads=[k3, C.ident], writes=[IM2], out=IM2[:], in0=v464(k3), in1=ident4, op=ALU.add)
                if not last:
                    I("scalar", "activation", reads=[k3], writes=[X[nxt]], out=X[nxt][:], in_=v464(k3), func=AF.Copy)
                yield
                for h in range(4):
                    P.mm(k2[0:64, h * 64:(h + 1) * 64], IM2[:, h, :], R[cur][:, h, :], True, True, reads=[IM2, R[cur]], writes=[k2])
                yield
                if not last:
                    I("vector", "tensor_copy", reads=[k2], writes=[R[nxt]], out=R[nxt][:], in_=v464(k2))
                else:
                    I("vector", "tensor_copy", reads=[k2], writes=[TTb], out=TTb[:], in_=v464(k2))
                cur = nxt
                yield
            TT_ = TTb
            I("gpsimd", "tensor_tensor", reads=[kv_tm, bet], writes=[vb], out=vb[:], in0=kv_tm[:, 4:8, :], in1=bet4b, op=ALU.mult)
            I("vector", "tensor_tensor", reads=[kv_tm, be], writes=[kbe], out=kbe[:], in0=kv_tm[:, 0:4, :], in1=AP(be, 0, [[4, 64], [1, 4], [0, 128]]), op=ALU.mult)
            I("gpsimd", "tensor_tensor", reads=[kv_tm, edl], writes=[kd], out=kd[:], in0=kv_tm[:, 0:4, :], in1=AP(edl, 0, [[4, 64], [1, 4], [0, 128]]), op=ALU.mult)
            for h in range(4):
                P.mm(k1[0:64, h * 128:(h + 1) * 128], TT_[:, h, :], vb[:, h, :], True, True, reads=[TT_, vb], writes=[k1])
            for h in range(4):
                P.mm(k0[:, h * 64:(h + 1) * 64], kbe[:, h, :], TT_[:, h, :], True, True, reads=[kbe, TT_], writes=[k0])
            yield
            I("scalar", "activation", reads=[k1], writes=[u_sb], out=u_sb[:], in_=v4128(k1), func=AF.Copy)
            I("vector", "tensor_copy", reads=[k0], writes=[wTb], out=wTb[:], in_=k0[:, 0:256].rearrange("p (h s) -> p h s", s=64))
            yield
            for h in range(4):
                P.mm(k1[0:64, h * 128:(h + 1) * 128], wTb[:, h, :], S[:, h, :], True, True, reads=[wTb, S], writes=[k1])
            for h in range(4):
                P.mm(k3[0:64, h * 128:(h + 1) * 128], qnT[h][:, cs], Sb[:, h, :], True, True, reads=[(qnT[h], key), Sb], writes=[k3])
            yield
            I("vector", "tensor_tensor", reads=[u_sb, k1], writes=[v_new], out=v_new[:], in0=u_sb[:], in1=v4128(k1), op=ALU.subtract)
            I("vector", "tensor_tensor", reads=[k3, ecum], writes=[t2], out=t2[:], in0=v4128(k3), in1=AP(ecum, 0, [[4, 64], [1, 4], [0, 128]]), op=ALU.mult)
            for h in range(4):
                P.mm(k3[0:64, h * 128:(h + 1) * 128], attT[:, h, :], v_new[:, h, :], True, True, reads=[attT, v_new], writes=[k3])
            for h in range(4):
                P.mm(k1[:, h * 128:(h + 1) * 128], kd[:, h, :], v_new[:, h, :], True, True, reads=[kd, v_new], writes=[k1])
            yield
            osb = o_sb[ci % 2]
            I("vector", "tensor_tensor", reads=[k3, t2], writes=[osb], out=osb[:], in0=v4128(k3), in1=t2[:], op=ALU.add)
            I("gpsimd", "tensor_tensor", reads=[S, etot], writes=[St], out=St[:], in0=S[:], in1=AP(etot, 0, [[4, 128], [1, 4], [0, 128]]), op=ALU.mult)
            I("vector", "tensor_tensor", reads=[St, k1], writes=[S], out=S[:], in0=St[:], in1=k1[:, :].rearrange("p (h s) -> p h s", s=128), op=ALU.add)
            I("scalar", "activation", reads=[S], writes=[Sb], out=Sb[:], in_=S[:], func=AF.Copy)
            P.D("sync", ODST[cs, :], osb[:].rearrange("p h d -> p (h d)"), reads=[osb], writes=[(ODST, c)])
            yield

    gens = [stream(0, *banks[0:4]), stream(1, *banks[4:8])]
    alive = list(gens)
    while alive:
        for g_ in list(alive):
            try:
                next(g_)
            except StopIteration:
                alive.remove(g_)
    P.emit()


def _mixB_post(nc, C, l):
    P = Prog(nc)
    I = P.I
    sb = P.sb
    nwb = sb("nwbB2", [64, 128])
    P.D("sync", nwb[:], C.din["gdn_norm_w"][l].partition_broadcast(64), writes=[nwb])
    pt = [P.ps("ptB%d" % i, [128, 512]) for i in range(2)]
    of = [sb("ofB%d" % i, [64, 8, 512]) for i in range(2)]
    ob = [sb("obB%d" % i, [64, 8, 512]) for i in range(2)]
    gate = [sb("gateB%d" % i, [64, 8, 512], BF16) for i in range(2)]
    sq = sb("sqB", [64, 8, 512]); ssq = sb("ssqB", [64, 32]); sg = sb("sgB", [64, 8, 512])
    onb = sb("onbB", [64, 8, 512], BF16)
    ngT = sb("ngTB", [128, 4, 512], BF16)
    tmv = lambda b_: b_.t.ap().rearrange("(c p) d -> p c d", p=64)
    ngb_v = C.scr["NGB"].t.ap().rearrange("(h p) t -> p h t", p=128)
    for ti, (n0, n, w) in enumerate(TTILES):
        nch = n // 64
        c0 = n0 // 64
        f_, b_, g_ = of[ti % 2], ob[ti % 2], gate[ti % 2]
        P.D("sync", f_[:, :nch, :], tmv(C.scr["OF"])[:, c0:c0 + nch, :], reads=[C.scr["OF"]], writes=[f_])
        P.D("sync", b_[:, :nch, :], tmv(C.scr["OBW"])[:, c0:c0 + nch, :], reads=[C.scr["OBW"]], writes=[b_])
        P.D("sync", g_[:, :nch, :], tmv(C.scr["BG"])[:, c0:c0 + nch, :], reads=[C.scr["BG"]], writes=[g_])
        I("gpsimd", "tensor_tensor", reads=[f_, b_], writes=[f_], out=f_[:, :nch, :], in0=f_[:, :nch, :], in1=b_[:, :nch, :], op=ALU.add)
        if "ob" in C.debug:
            P.D("sync", C.dbg["ob"].ap().rearrange("(c p) d -> p c d", p=64)[:, c0:c0 + nch, :], f_[:, :nch, :], reads=[f_])
        I("scalar", "activation", reads=[g_], writes=[sg], out=sg[:, :nch, :], in_=g_[:, :nch, :], func=AF.Silu)
        I("gpsimd", "tensor_tensor", reads=[f_], writes=[sq], out=sq[:, :nch, :], in0=f_[:, :nch, :], in1=f_[:, :nch, :], op=ALU.mult)
        I("vector", "tensor_reduce", reads=[sq], writes=[ssq], out=ssq[:, :nch * 4], in_=sq[:, :nch, :].rearrange("p c (h d) -> p (c h) d", d=128), axis=AX.X, op=ALU.add)
        I("scalar", "activation", reads=[ssq], writes=[ssq], out=ssq[:, :nch * 4], in_=ssq[:, :nch * 4], func=AF.Sqrt, scale=1.0 / 128, bias=EPS)
        I("vector", "reciprocal", reads=[ssq], writes=[ssq], out=ssq[:, :nch * 4], in_=ssq[:, :nch * 4])
        I("vector", "tensor_tensor", reads=[f_, ssq], writes=[sq], out=sq[:, :nch, :].rearrange("p c (h d) -> p (c h) d", d=128),
          in0=f_[:, :nch, :].rearrange("p c (h d) -> p (c h) d", d=128), in1=AP(ssq, 0, [[32, 64], [1, nch * 4], [0, 128]]), op=ALU.mult)
        I("gpsimd", "tensor_tensor", reads=[sq, nwb], writes=[sq], out=sq[:, :nch, :].rearrange("p c (h d) -> p (c h) d", d=128),
          in0=sq[:, :nch, :].rearrange("p c (h d) -> p (c h) d", d=128), in1=AP(nwb, 0, [[128, 64], [0, nch * 4], [1, 128]]), op=ALU.mult)
        I("vector", "tensor_tensor", reads=[sq, sg], writes=[onb], out=onb[:, :nch, :], in0=sq[:, :nch, :], in1=sg[:, :nch, :], op=ALU.mult)
        for h in range(4):
            ppt = pt[h % 2]
            ptv = ppt[:, 0:256].bitcast(BF16).rearrange("p (a b) -> p a b", b=64)
            for j in range(nch):
                P.tr(ptv[:, j, :], onb[:, j, h * 128:(h + 1) * 128], C.identb[0:64, 0:64], reads=[onb, C.identb], writes=[ppt])
            I("scalar", "activation", reads=[ppt], writes=[(ngT, h)], out=ngT[:, h, :n], in_=ppt[:, 0:256].bitcast(BF16)[:, 0:n], func=AF.Copy)
        P.D("sync", ngb_v[:, :, n0:n0 + n], ngT[:, :, :n], reads=[ngT], writes=[(C.scr["NGB"], n0)])
    P.emit()


TWO_PI = 6.283185307179586
PI = 3.141592653589793
NPOW = 129


def make_s5_masks(P, C):
    nc = P.nc
    st = C.gstack
    def gsb(name, shape, dt=F32):
        return Buf(st.enter_context(nc.sbuf_tensor(name, list(shape), dt)), name)
    A = gsb("s5mA", [8, 128]); Bge = gsb("s5mB1", [8, 128]); Ble = gsb("s5mB2", [8, 128]); on8 = gsb("s5on", [8, 128])
    C.maskZ = [gsb("maskZf", [128, 128]), gsb("maskZb", [128, 128])]
    pm = P.ps("pmask", [128, 512])
    P.I("vector", "memset", writes=[on8], ap=on8[:], constant=1.0)
    P.I("gpsimd", "affine_select", reads=[on8], writes=[A], out=A[:], in_=on8[:], pattern=[[1, 128]], compare_op=ALU.is_ge, fill=0.0, base=0, channel_multiplier=-16)
    P.I("gpsimd", "affine_select", reads=[A], writes=[A], out=A[:], in_=A[:], pattern=[[-1, 128]], compare_op=ALU.is_ge, fill=0.0, base=15, channel_multiplier=16)
    P.I("gpsimd", "affine_select", reads=[on8], writes=[Bge], out=Bge[:], in_=on8[:], pattern=[[1, 8], [0, 16]], compare_op=ALU.is_ge, fill=0.0, base=0, channel_multiplier=-1)
    P.I("gpsimd", "affine_select", reads=[on8], writes=[Ble], out=Ble[:], in_=on8[:], pattern=[[-1, 8], [0, 16]], compare_op=ALU.is_ge, fill=0.0, base=0, channel_multiplier=1)
    P.mm(pm[:, 0:128], A[:], Bge[:], True, True, reads=[A, Bge], writes=[pm])
    P.mm(pm[:, 128:256], A[:], Ble[:], True, True, reads=[A, Ble], writes=[pm])
    P.I("vector", "tensor_copy", reads=[pm], writes=[C.maskZ[0]], out=C.maskZ[0][:], in_=pm[:, 0:128])
    P.I("vector", "tensor_copy", reads=[pm], writes=[C.maskZ[1]], out=C.maskZ[1][:], in_=pm[:, 128:256])


def phase_s5(nc, C, l):
    CU = C.scr["CU"]
    YC = C.scr["YC"]
    cu_g = CU.t.ap().rearrange("(g c) n -> c g n", c=16)
    yc_g = YC.t.ap().rearrange("(g c) n -> c g n", c=16)
    P = Prog(nc)
    S5A = C.scr["S5A"]
    for q in range(4):
        tmpc = P.sb("tmpc%d" % q, [128, 256], BF16)
        ctr = P.sb("ctr%d" % q, [128, 8, 8, 4], BF16)
        P.D("sync", tmpc[:], CU[q * 128:(q + 1) * 128, 0:256], reads=[CU], writes=[tmpc])
        P.I("vector", "tensor_copy", reads=[tmpc], writes=[ctr], out=ctr[:], in_=tmpc[:].rearrange("p (sc blk t) -> p t blk sc", sc=4, blk=8))
        P.D("sync", S5A[q * 128:(q + 1) * 128, :], ctr[:].rearrange("p t b s -> p (t b s)"), reads=[ctr], writes=[(S5A, q)])
    P.emit()
    for hf in range(2):
        s5_half(nc, C, l, hf, cu_g, yc_g)
    P = Prog(nc)
    S5B = C.scr["S5B"]
    for q in range(4):
        tmpc = P.sb("tmpd%d" % q, [128, 256], BF16)
        ctr = P.sb("ctd%d" % q, [128, 8, 8, 4], BF16)
        P.D("sync", ctr[:].rearrange("p t b s -> p (t b s)"), S5B[q * 128:(q + 1) * 128, :], reads=[S5B], writes=[ctr])
        P.I("vector", "tensor_copy", reads=[ctr], writes=[tmpc], out=tmpc[:].rearrange("p (sc blk t) -> p t blk sc", sc=4, blk=8), in_=ctr[:])
        P.D("sync", YC[q * 128:(q + 1) * 128, 0:256], tmpc[:], reads=[tmpc], writes=[(YC, q)])
    P.emit()


def s5_half(nc, C, l, hf, cu_g, yc_g):
    G0 = hf * 16
    hst = ExitStack()
    cntr = [0]

    def pst(name, shape, dt=F32):
        _UID[0] += 1
        return Buf(hst.enter_context(nc.sbuf_tensor("s5p_%s_%d" % (name, _UID[0]), list(shape), dt)), name)

    with hst:
        Er = pst("Er", [64, 32, NPOW]); Ei = pst("Ei", [64, 32, NPOW])
        bbr = pst("bbr", [64, 32, 16]); bbi = pst("bbi", [64, 32, 16])
        cre = pst("cre", [64, 16, 16]); cim = pst("cim", [64, 16, 16]); ncim = pst("ncim", [64, 16, 16]); ncre = pst("ncre", [64, 16, 16])
        dtab = pst("dtab", [128, 16])
        U2 = pst("U2", [128, 16, 8, 68], BF16)
        Sall = pst("Sall", [64, 2, 2, 16, 68])
        XPb = pst("XPb", [64, 2, 2, 16, 68], BF16)
        P = Prog(nc)
        I = P.I
        sb = P.sb
        lre = sb("lre", [64, 2, 16]); lim = sb("lim", [64, 2, 16]); ldt = sb("ldt", [64, 2, 16])
        P.D("sync", lre[:], C.din["s5_lam_re"][l][:, :, G0:G0 + 16], writes=[lre])
        P.D("sync", lim[:], C.din["s5_lam_im"][l][:, :, G0:G0 + 16], writes=[lim])
        P.D("sync", ldt[:], C.din["s5_log_dt"][l].partition_broadcast(64)[:, :, G0:G0 + 16], writes=[ldt])
        br = sb("br", [64, 16, 16]); bi = sb("bi", [64, 16, 16])
        P.D("sync", br[:], C.din["s5_b_re"][l][:, G0:G0 + 16, :], writes=[br])
        P.D("sync", bi[:], C.din["s5_b_im"][l][:, G0:G0 + 16, :], writes=[bi])
        P.D("sync", cre[:], C.din["s5_c_re"][l][:, G0:G0 + 16, :], writes=[cre])
        P.D("sync", cim[:], C.din["s5_c_im"][l][:, G0:G0 + 16, :], writes=[cim])
        I("vector", "tensor_scalar", reads=[cim], writes=[ncim], out=ncim[:], in0=cim[:], scalar1=-1.0, scalar2=None, op0=ALU.mult)
        I("vector", "tensor_scalar", reads=[cre], writes=[ncre], out=ncre[:], in0=cre[:], scalar1=-1.0, scalar2=None, op0=ALU.mult)
        P.D("sync", dtab[:], C.din["s5_dtab"][l][:, G0:G0 + 16], writes=[dtab])
        U2c = sb("U2c", [128, 16, 32], BF16)
        s5a = C.scr["S5A"].t.ap().rearrange("(g c) (t x) -> t c g x", c=16, t=8)
        for t in range(8):
            src = cu_g[:, G0:G0 + 16, 256:T].rearrange("c g (blk t col) -> c g blk t col", blk=8, t=8)
            P.dma("sync", [lambda e, t=t, b_=b_, src=src: e.dma_start(out=U2[16 * t:16 * t + 16, :, b_, 4:68], in_=src[:, :, b_, t, :]) for b_ in range(8)],
                  reads=[C.scr["CU"]], writes=[(U2, t)])
            P.D("sync", U2c[16 * t:16 * t + 16, :, :], s5a[t][:, G0:G0 + 16, :], reads=[C.scr["S5A"]], writes=[(U2c, t)])
        I("vector", "tensor_copy", reads=[U2c, U2], writes=[U2], out=U2[:, :, :, 0:4], in_=U2c[:].rearrange("p g (b s) -> p g b s", s=4))
        dtt = sb("dtt", [64, 32]); xx = sb("xx", [64, 32]); th = sb("th", [64, 32]); lr = sb("lr", [64, 32])
        lre2 = lre[:].rearrange("p d g -> p (d g)"); lim2 = lim[:].rearrange("p d g -> p (d g)"); ldt2 = ldt[:].rearrange("p d g -> p (d g)")
        I("scalar", "activation", reads=[ldt], writes=[dtt], out=dtt[:], in_=ldt2, func=AF.Exp)
        I("vector", "tensor_scalar", reads=[lre], writes=[lr], out=lr[:], in0=lre2, scalar1=-1e-4, scalar2=None, op0=ALU.min)
        I("vector", "tensor_tensor", reads=[lr, dtt], writes=[xx], out=xx[:], in0=lr[:], in1=dtt[:], op=ALU.mult)
        I("vector", "tensor_tensor", reads=[lim, dtt], writes=[th], out=th[:], in0=lim2, in1=dtt[:], op=ALU.mult)
        jt = sb("jt", [64, NPOW])
        I("gpsimd", "iota", writes=[(jt, 0)], out=jt[:, 0:65], pattern=[[1, 65]], base=0, channel_multiplier=0, allow_small_or_imprecise_dtypes=True)
        I("gpsimd", "iota", writes=[(jt, 1)], out=jt[:, 65:129], pattern=[[-1, 64]], base=0, channel_multiplier=0, allow_small_or_imprecise_dtypes=True)
        ph = sb("ph", [64, 32, NPOW]); qf = sb("qf", [64, 32, NPOW]); qi = sb("qi", [64, 32, NPOW], I32); mk = sb("mk", [64, 32, NPOW])
        jb_ = AP(jt, 0, [[NPOW, 64], [0, 32], [1, NPOW]])
        thb = AP(th, 0, [[32, 64], [1, 32], [0, NPOW]])
        xb = AP(xx, 0, [[32, 64], [1, 32], [0, NPOW]])
        I("vector", "tensor_tensor", reads=[th, jt], writes=[ph], out=ph[:], in0=thb, in1=jb_, op=ALU.mult)
        I("vector", "tensor_scalar", reads=[ph], writes=[qf], out=qf[:], in0=ph[:], scalar1=1.0 / TWO_PI, scalar2=None, op0=ALU.mult)
        I("vector", "tensor_copy", reads=[qf], writes=[qi], out=qi[:], in_=qf[:])
        I("vector", "tensor_copy", reads=[qi], writes=[qf], out=qf[:], in_=qi[:])
        I("vector", "scalar_tensor_tensor", reads=[qf, ph], writes=[ph], out=ph[:], in0=qf[:], scalar=-TWO_PI, in1=ph[:], op0=ALU.mult, op1=ALU.add)
        for (cmp_, sgn) in ((ALU.is_gt, -1.0), (ALU.is_lt, 1.0)):
            I("vector", "tensor_scalar", reads=[ph], writes=[mk], out=mk[:], in0=ph[:], scalar1=(PI if sgn < 0 else -PI), scalar2=None, op0=cmp_)
            I("vector", "scalar_tensor_tensor", reads=[mk, ph], writes=[ph], out=ph[:], in0=mk[:], scalar=sgn * TWO_PI, in1=ph[:], op0=ALU.mult, op1=ALU.add)
        I("scalar", "activation", reads=[ph], writes=[Ei], out=Ei[:], in_=ph[:], func=AF.Sin)
        I("vector", "tensor_scalar", reads=[ph], writes=[ph], out=ph[:], in0=ph[:], scalar1=PI / 2, scalar2=None, op0=ALU.add)
        I("vector", "tensor_scalar", reads=[ph], writes=[mk], out=mk[:], in0=ph[:], scalar1=PI, scalar2=None, op0=ALU.is_gt)
        I("vector", "scalar_tensor_tensor", reads=[mk, ph], writes=[ph], out=ph[:], in0=mk[:], scalar=-TWO_PI, in1=ph[:], op0=ALU.mult, op1=ALU.add)
        I("scalar", "activation", reads=[ph], writes=[Er], out=Er[:], in_=ph[:], func=AF.Sin)
        I("vector", "tensor_tensor", reads=[xx, jt], writes=[qf], out=qf[:], in0=xb, in1=jb_, op=ALU.mult)
        I("scalar", "activation", reads=[qf], writes=[qf], out=qf[:], in_=qf[:], func=AF.Exp)
        I("vector", "tensor_tensor", reads=[Er, qf], writes=[Er], out=Er[:], in0=Er[:], in1=qf[:], op=ALU.mult)
        I("gpsimd", "tensor_tensor", reads=[Ei, qf], writes=[Ei], out=Ei[:], in0=Ei[:], in1=qf[:], op=ALU.mult)

        def col(tab, j):
            return AP(tab, j, [[32 * NPOW, 64], [NPOW, 32]])
        den = sb("den", [64, 32]); t0 = sb("t0", [64, 32]); t1 = sb("t1b", [64, 32]); crr = sb("crr", [64, 32]); cii = sb("cii", [64, 32]); am1 = sb("am1", [64, 32])
        I("vector", "tensor_tensor", reads=[lr], writes=[den], out=den[:], in0=lr[:], in1=lr[:], op=ALU.mult)
        I("vector", "tensor_tensor", reads=[lim], writes=[t0], out=t0[:], in0=lim2, in1=lim2, op=ALU.mult)
        I("vector", "tensor_tensor", reads=[den, t0], writes=[den], out=den[:], in0=den[:], in1=t0[:], op=ALU.add)
        I("vector", "reciprocal", reads=[den], writes=[den], out=den[:], in_=den[:])
        I("vector", "tensor_scalar", reads=[Er], writes=[am1], out=am1[:], in0=col(Er, 1), scalar1=-1.0, scalar2=None, op0=ALU.add)
        I("vector", "tensor_tensor", reads=[am1, lr], writes=[t0], out=t0[:], in0=am1[:], in1=lr[:], op=ALU.mult)
        I("vector", "tensor_tensor", reads=[Ei, lim], writes=[t1], out=t1[:], in0=col(Ei, 1), in1=lim2, op=ALU.mult)
        I("vector", "tensor_tensor", reads=[t0, t1], writes=[crr], out=crr[:], in0=t0[:], in1=t1[:], op=ALU.add)
        I("vector", "tensor_tensor", reads=[crr, den], writes=[crr], out=crr[:], in0=crr[:], in1=den[:], op=ALU.mult)
        I("vector", "tensor_tensor", reads=[Ei, lr], writes=[t0], out=t0[:], in0=col(Ei, 1), in1=lr[:], op=ALU.mult)
        I("vector", "tensor_tensor", reads=[am1, lim], writes=[t1], out=t1[:], in0=am1[:], in1=lim2, op=ALU.mult)
        I("vector", "tensor_tensor", reads=[t0, t1], writes=[cii], out=cii[:], in0=t0[:], in1=t1[:], op=ALU.subtract)
        I("vector", "tensor_tensor", reads=[cii, den], writes=[cii], out=cii[:], in0=cii[:], in1=den[:], op=ALU.mult)
        tb = sb("tb", [64, 32, 16])
        crb4 = AP(crr, 0, [[32, 64], [16, 2], [1, 16], [0, 16]]); cib4 = AP(cii, 0, [[32, 64], [16, 2], [1, 16], [0, 16]])
        br4 = AP(br, 0, [[256, 64], [0, 2], [16, 16], [1, 16]]); bi4 = AP(bi, 0, [[256, 64], [0, 2], [16, 16], [1, 16]])
        o4 = lambda t_: t_[:].rearrange("p (d g) c -> p d g c", d=2)
        I("vector", "tensor_tensor", reads=[crr, br], writes=[bbr], out=o4(bbr), in0=crb4, in1=br4, op=ALU.mult)
        I("vector", "tensor_tensor", reads=[cii, bi], writes=[tb], out=o4(tb), in0=cib4, in1=bi4, op=ALU.mult)
        I("vector", "tensor_tensor", reads=[bbr, tb], writes=[bbr], out=bbr[:], in0=bbr[:], in1=tb[:], op=ALU.subtract)
        I("vector", "tensor_tensor", reads=[crr, bi], writes=[bbi], out=o4(bbi), in0=crb4, in1=bi4, op=ALU.mult)
        I("vector", "tensor_tensor", reads=[cii, br], writes=[tb], out=o4(tb), in0=cib4, in1=br4, op=ALU.mult)
        I("vector", "tensor_tensor", reads=[bbi, tb], writes=[bbi], out=bbi[:], in0=bbi[:], in1=tb[:], op=ALU.add)
        if "s5tab" in C.debug and hf == 0:
            P.D("sync", C.dbg["Er"].ap(), Er[:], reads=[Er]); P.D("sync", C.dbg["Ei"].ap(), Ei[:], reads=[Ei])
            P.D("sync", C.dbg["bbr"].ap(), bbr[:], reads=[bbr]); P.D("sync", C.dbg["bbi"].ap(), bbi[:], reads=[bbi])
        P.emit()
        if getattr(C, 's5stop', None) == 'tab':
            return
        s5_main(nc, C, l, hf, locals())


def s5_main(nc, C, l, hf, L):
    Er, Ei, bbr, bbi, cre, cim, ncim, ncre, dtab, U2, Sall, XPb = (L[k] for k in
        ("Er", "Ei", "bbr", "bbi", "cre", "cim", "ncim", "ncre", "dtab", "U2", "Sall", "XPb"))
    G0 = hf * 16
    P = Prog(nc)
    I = P.I
    sb = P.sb
    banks = [P.ps("s5b%d" % i, [128, 512]) for i in range(8)]
    PS = 32 * NPOW

    def Ev(tab, dg, c0, n):
        return AP(tab, dg * NPOW + c0, [[PS, 64], [1, n], [0, 16]])

    def Bv(tab, dg, n):
        return AP(tab, dg * 16, [[512, 64], [0, n], [1, 16]])

    def Cv(tab, gl, n):
        return AP(tab, gl * 16, [[256, 64], [0, n], [1, 16]])

    tA = [sb("tA%d" % i, [64, 65, 16]) for i in range(2)]
    tB = [sb("tB%d" % i, [64, 65, 16]) for i in range(2)]
    cnt = [0]

    def cprod(outre, outim, n, er, ei, xr, xi, xrn=None, sub_im=False):
        k = cnt[0] % 2
        cnt[0] += 1
        a, b = tA[k], tB[k]
        e1, e2 = ("vector", "gpsimd") if k == 0 else ("gpsimd", "vector")
        I(e1, "tensor_tensor", reads=[Er, bbr, cre], writes=[a], out=a[:, :n, :], in0=er, in1=xr, op=ALU.mult)
        I(e2, "tensor_tensor", reads=[Ei, bbi, cim], writes=[b], out=b[:, :n, :], in0=ei, in1=xi, op=ALU.mult)
        I(e1, "tensor_tensor", reads=[a, b], writes=[outre], out=outre[:, :n, :], in0=a[:, :n, :], in1=b[:, :n, :], op=ALU.subtract)
        if not sub_im:
            I("vector", "tensor_tensor", reads=[Er, bbi], writes=[a], out=a[:, :n, :], in0=er, in1=xi, op=ALU.mult)
            I(e2, "tensor_tensor", reads=[Ei, bbr], writes=[b], out=b[:, :n, :], in0=ei, in1=xr, op=ALU.mult)
            I(e2, "tensor_tensor", reads=[a, b], writes=[outim], out=outim[:, :n, :], in0=a[:, :n, :], in1=b[:, :n, :], op=ALU.add)
        else:
            I("vector", "tensor_tensor", reads=[Er, ncim], writes=[a], out=a[:, :n, :], in0=er, in1=xrn, op=ALU.mult)
            I(e2, "tensor_tensor", reads=[Ei, cre], writes=[b], out=b[:, :n, :], in0=ei, in1=xr, op=ALU.mult)
            I(e2, "tensor_tensor", reads=[a, b], writes=[outim], out=outim[:, :n, :], in0=a[:, :n, :], in1=b[:, :n, :], op=ALU.subtract)

    Wre = [sb("Wre%d" % i, [64, 64, 16], BF16) for i in range(2)]
    Wim = [sb("Wim%d" % i, [64, 64, 16], BF16) for i in range(2)]
    PTs = [sb("PTs%d" % i, [128, 8, 128], BF16) for i in range(2)]
    it = 0
    for d in range(2):
        for gl in range(16):
            dg = d * 16 + gl
            wr, wi, pts = Wre[it % 2], Wim[it % 2], PTs[it % 2]
            c0 = 65 if d == 0 else 0
            cprod(wr, wi, 64, Ev(Er, dg, c0, 64), Ev(Ei, dg, c0, 64), Bv(bbr, dg, 64), Bv(bbi, dg, 64))
            pt = banks[it % 2]
            ptv = pt[:, :].bitcast(BF16).rearrange("p (a b) -> p a b", b=128)
            for blk in range(8):
                P.tr(ptv[:, blk, 0:64], wr[:, blk * 8:(blk + 1) * 8, :].rearrange("p m c -> p (m c)"), C.identb[0:64, 0:64], reads=[wr, C.identb], writes=[pt])
                P.tr(ptv[:, blk, 64:128], wi[:, blk * 8:(blk + 1) * 8, :].rearrange("p m c -> p (m c)"), C.identb[0:64, 0:64], reads=[wi, C.identb], writes=[pt])
            I("scalar", "activation", reads=[pt], writes=[pts], out=pts[:], in_=ptv, func=AF.Copy)
            ps = banks[2 + it % 2]
            for blk in range(8):
                P.mm(ps[0:64, 0:68], pts[:, blk, 0:64], U2[:, gl, blk, :], blk == 0, blk == 7, reads=[pts, U2], writes=[ps])
            for blk in range(8):
                P.mm(ps[0:64, 68:136], pts[:, blk, 64:128], U2[:, gl, blk, :], blk == 0, blk == 7, reads=[pts, U2], writes=[ps])
            I("scalar", "activation", reads=[ps], writes=[(Sall, d)], out=Sall[:, :, d, gl, :], in_=ps[0:64, 0:136].rearrange("p (a b) -> p a b", b=68), func=AF.Copy)
            it += 1
    if getattr(C, 's5stop', None) == 'st1':
        P.emit()
        return
    s1 = sb("s1", [64, 16, 68]); s2 = sb("s2", [64, 16, 68]); s3 = sb("s3", [64, 16, 68]); s4 = sb("s4", [64, 16, 68])
    a63r = AP(Er, 63, [[PS, 64], [NPOW, 16], [0, 68]]); a63i = AP(Ei, 63, [[PS, 64], [NPOW, 16], [0, 68]])
    Sr = Sall[:, 0, 0, :, :]; Si = Sall[:, 1, 0, :, :]
    I("vector", "tensor_tensor", reads=[(Sall, 0), Er], writes=[s1], out=s1[:], in0=Sr, in1=a63r, op=ALU.mult)
    I("gpsimd", "tensor_tensor", reads=[(Sall, 0), Ei], writes=[s2], out=s2[:], in0=Si, in1=a63i, op=ALU.mult)
    I("vector", "tensor_tensor", reads=[(Sall, 0), Er], writes=[s3], out=s3[:], in0=Si, in1=a63r, op=ALU.mult)
    I("gpsimd", "tensor_tensor", reads=[(Sall, 0), Ei], writes=[s4], out=s4[:], in0=Sr, in1=a63i, op=ALU.mult)
    I("vector", "tensor_tensor", reads=[s1, s2], writes=[(Sall, 0)], out=Sr, in0=s1[:], in1=s2[:], op=ALU.subtract)
    I("gpsimd", "tensor_tensor", reads=[s3, s4, (Sall, 0)], writes=[(Sall, 0)], out=Si, in0=s3[:], in1=s4[:], op=ALU.add)
    SS = 2 * 2 * 16 * 68
    for d in range(2):
        eng = "vector" if d == 0 else "gpsimd"
        AA = sb("AA%d" % d, [64, 2, 16]); AC = sb("AC%d" % d, [64, 2, 16])
        a64r = AP(Er, d * 16 * NPOW + 64, [[PS, 64], [NPOW, 16]]); a64i = AP(Ei, d * 16 * NPOW + 64, [[PS, 64], [NPOW, 16]])
        I(eng, "tensor_copy", reads=[Er], writes=[AA], out=AA[:, 0, :], in_=a64r)
        I(eng, "tensor_copy", reads=[Er, AA], writes=[AA], out=AA[:, 1, :], in_=a64r)
        I(eng, "tensor_copy", reads=[Ei], writes=[AC], out=AC[:, 0, :], in_=a64i)
        I(eng, "tensor_scalar", reads=[Ei, AC], writes=[AC], out=AC[:, 1, :], in0=a64i, scalar1=-1.0, scalar2=None, op0=ALU.mult)
        X2 = [sb("X2_%d_%d" % (d, i), [64, 2, 16]) for i in range(2)]
        p1 = sb("p1_%d" % d, [64, 2, 16]); p2 = sb("p2_%d" % d, [64, 2, 16]); dec = sb("dec_%d" % d, [64, 2, 16])
        I(eng, "memset", writes=[X2[0]], ap=X2[0][:], constant=0.0)
        for i, sc in enumerate(CH_FWD if d == 0 else CH_BWD):
            xc, xn = X2[i % 2], X2[(i + 1) % 2]
            sv = AP(Sall, d * 16 * 68 + sc, [[SS, 64], [2 * 16 * 68, 2], [68, 16]])
            xv = AP(XPb, d * 16 * 68 + sc, [[SS, 64], [2 * 16 * 68, 2], [68, 16]])
            if d == 0:
                I(eng, "tensor_copy", reads=[xc], writes=[(XPb, d)], out=xv, in_=xc[:])
            I(eng, "tensor_tensor", reads=[xc, AA], writes=[p1], out=p1[:], in0=xc[:], in1=AA[:], op=ALU.mult)
            I(eng, "tensor_tensor", reads=[xc, AC], writes=[p2], out=p2[:], in0=xc[:], in1=AC[:], op=ALU.mult)
            I(eng, "tensor_tensor", reads=[p1, p2], writes=[dec], out=dec[:, 0, :], in0=p1[:, 0, :], in1=p2[:, 1, :], op=ALU.add)
            I(eng, "tensor_tensor", reads=[p1, p2, dec], writes=[dec], out=dec[:, 1, :], in0=p1[:, 1, :], in1=p2[:, 0, :], op=ALU.add)
            if d == 1:
                I(eng, "tensor_copy", reads=[dec], writes=[(XPb, d)], out=xv, in_=dec[:])
            I(eng, "tensor_tensor", reads=[dec, (Sall, d)], writes=[xn], out=xn[:], in0=dec[:], in1=sv, op=ALU.add)
    if getattr(C, 's5stop', None) == 'scan':
        P.emit()
        return
    Wf = [sb("Wfr", [64, 8, 16], BF16), sb("Wfi", [64, 8, 16], BF16)]
    Rf = [sb("Rft", [64, 65, 16], BF16), sb("Rfb", [64, 65, 16], BF16)]
    Wb = [sb("Wbr", [64, 64, 16], BF16), sb("Wbi", [64, 64, 16], BF16)]
    Rb = [sb("Rbt", [64, 64, 16], BF16), sb("Rbb", [64, 64, 16], BF16)]
    Zf = sb("Zf", [128, 64, 16], BF16)
    Zb = sb("Zb", [128, 8, 128], BF16)
    Ysb = sb("Ysb", [128, 16, 8, 68], BF16)
    flat = lambda ap_: ap_.rearrange("p m c -> p (m c)")
    for gl in range(16):
        df, db = gl, 16 + gl
        cprod(Wf[0], Wf[1], 8, Ev(Er, df, 65, 8), Ev(Ei, df, 65, 8), Bv(bbr, df, 8), Bv(bbi, df, 8))
        cprod(Rf[0], Rf[1], 65, Ev(Er, df, 0, 65), Ev(Ei, df, 0, 65), Cv(cre, gl, 65), Cv(cim, gl, 65), xrn=Cv(ncim, gl, 65), sub_im=True)
        cprod(Wb[0], Wb[1], 64, Ev(Er, db, 0, 64), Ev(Ei, db, 0, 64), Bv(bbr, db, 64), Bv(bbi, db, 64))
        cprod(Rb[0], Rb[1], 64, Ev(Er, db, 65, 64), Ev(Ei, db, 65, 64), Cv(cre, gl, 64), Cv(cim, gl, 64), xrn=Cv(ncim, gl, 64), sub_im=True)
        z0, z1, z2, z3 = banks[0], banks[1], banks[2], banks[3]
        for hh, zb in enumerate((z0, z1)):
            P.mm(zb[:, :], flat(Wf[0][:, :, :]), flat(Rf[0][:, hh * 32:(hh + 1) * 32, :]), True, False, reads=[Wf[0], Rf[0]], writes=[zb])
            P.mm(zb[:, :], flat(Wf[1][:, :, :]), flat(Rf[1][:, hh * 32:(hh + 1) * 32, :]), False, True, reads=[Wf[1], Rf[1]], writes=[zb])
        I("vector", "tensor_tensor", reads=[z0, C.maskZ[0]], writes=[(Zf, 0)], out=flat(Zf[:, 0:8, :]), in0=z0[:, 0:128], in1=C.maskZ[0][:], op=ALU.mult)
        I("scalar", "activation", reads=[z0], writes=[(Zf, 1)], out=flat(Zf[:, 8:32, :]), in_=z0[:, 128:512], func=AF.Copy)
        I("scalar", "activation", reads=[z1], writes=[(Zf, 2)], out=flat(Zf[:, 32:64, :]), in_=z1[:, :], func=AF.Copy)
        for dl in range(8):
            zb = z2 if dl < 4 else z3
            o_ = zb[:, (dl % 4) * 128:(dl % 4 + 1) * 128]
            P.mm(o_, flat(Wb[0][:, dl * 8:(dl + 1) * 8, :]), flat(Rb[0][:, 0:8, :]), True, False, reads=[Wb[0], Rb[0]], writes=[zb])
            P.mm(o_, flat(Wb[1][:, dl * 8:(dl + 1) * 8, :]), flat(Rb[1][:, 0:8, :]), False, True, reads=[Wb[1], Rb[1]], writes=[zb])
        I("vector", "tensor_tensor", reads=[z2, C.maskZ[1]], writes=[(Zb, 0)], out=Zb[:, 0, :], in0=z2[:, 0:128], in1=C.maskZ[1][:], op=ALU.mult)
        I("scalar", "activation", reads=[z2], writes=[(Zb, 1)], out=Zb[:, 1:4, :], in_=z2[:, 128:512].rearrange("p (a b) -> p a b", b=128), func=AF.Copy)
        I("scalar", "activation", reads=[z3], writes=[(Zb, 2)], out=Zb[:, 4:8, :], in_=z3[:, :].rearrange("p (a b) -> p a b", b=128), func=AF.Copy)
        if getattr(C, 's5stop', None) in ('z', 'z3'):
            continue
        for hb in range(2):
            yb = banks[4 + (2 * gl + hb) % 4]
            for i in range(4):
                ib = 4 * hb + i
                o_ = yb[:, i * 68:(i + 1) * 68]
                ops_ = []
                for jb in range(0, ib + 1):
                    ops_.append((flat(Zf[:, (ib - jb) * 8:(ib - jb + 1) * 8, :]), U2[:, gl, jb, :], [Zf, U2]))
                for jb in range(ib, 8):
                    ops_.append((Zb[:, jb - ib, :], U2[:, gl, jb, :], [Zb, U2]))
                ops_.append((flat(Rf[0][:, 8 * ib + 1:8 * ib + 9, :]), XPb[:, 0, 0, gl, :], [Rf[0], XPb]))
                ops_.append((flat(Rf[1][:, 8 * ib + 1:8 * ib + 9, :]), XPb[:, 1, 0, gl, :], [Rf[1], XPb]))
                ops_.append((flat(Rb[0][:, 8 * ib:8 * ib + 8, :]), XPb[:, 0, 1, gl, :], [Rb[0], XPb]))
                ops_.append((flat(Rb[1][:, 8 * ib:8 * ib + 8, :]), XPb[:, 1, 1, gl, :], [Rb[1], XPb]))
                for k, (lh, rh, rd) in enumerate(ops_):
                    P.mm(o_, lh, rh, k == 0, k == len(ops_) - 1, reads=rd, writes=[yb])
            I("vector", "scalar_tensor_tensor", reads=[U2, dtab, yb], writes=[(Ysb, gl)], out=Ysb[:, gl, 4 * hb:4 * hb + 4, :], in0=U2[:, gl, 4 * hb:4 * hb + 4, :],
              scalar=dtab[:, gl:gl + 1], in1=yb[:, 0:272].rearrange("p (a b) -> p a b", b=68), op0=ALU.mult, op1=ALU.add)
    if getattr(C, 's5stop', None) in ('st3', 'z3'):
        P.emit()
        return
    Yc2 = sb("Yc2", [128, 16, 32], BF16)
    I("vector", "tensor_copy", reads=[Ysb], writes=[Yc2], out=Yc2[:].rearrange("p g (b s) -> p g b s", s=4), in_=Ysb[:, :, :, 0:4])
    yc_g = C.scr["YC"].t.ap().rearrange("(g c) n -> c g n", c=16)
    s5b = C.scr["S5B"].t.ap().rearrange("(g c) (t x) -> t c g x", c=16, t=8)
    for t in range(8):
        dst = yc_g[:, G0:G0 + 16, 256:T].rearrange("c g (blk t col) -> c g blk t col", blk=8, t=8)
        P.dma("sync", [lambda e, t=t, b_=b_, dst=dst: e.dma_start(out=dst[:, :, b_, t, :], in_=Ysb[16 * t:16 * t + 16, :, b_, 4:68]) for b_ in range(8)],
              reads=[Ysb], writes=[(C.scr["YC"], (hf, t))])
        P.D("sync", s5b[t][:, G0:G0 + 16, :], Yc2[16 * t:16 * t + 16, :, :], reads=[Yc2], writes=[(C.scr["S5B"], (hf, t))])
    P.emit()


def phase_merge(nc, C, l, xsrc, xdst, tiles=None):
    P = Prog(nc)
    I = P.I
    sb = P.sb
    wts = {}
    for nm, src, kc, ncol in (("ba", "w_branch_a", 4, 1024), ("bb", "w_branch_b", 4, 1024), ("bc", "w_branch_c", 4, 1024),
                              ("glu", "s5_w_glu", 4, 512), ("wo", "w_out", 8, 1024)):
        wts[nm] = sb("w_" + nm, [128, kc, ncol], BF16)
        wv = C.din[src][l].rearrange("(k p) c -> p k c", p=128)
        for k in range(kc):
            P.D("gpsimd", wts[nm][:, k, :], wv[:, k, :], writes=[(wts[nm], k)])
    nga = sb("nga", [128, 4, 512], BF16); ngb = sb("ngb", [128, 4, 512], BF16); ycb = sb("ycb", [128, 4, 512], BF16)
    mg = sb("mg", [128, 24, 512], BF16)
    xT = sb("xTm", [128, 8, 512])
    yc = sb("yc32", [128, 4, 512]); x2 = sb("x2", [128, 4, 512]); xh = sb("xh", [128, 4, 512])
    tt = x2
    zb = sb("zb", [128, 4, 512], BF16); zz = sb("zz", [128, 4, 512], BF16)
    zf = yc
    sig = sb("sigg", [128, 512])
    gts = [sb("gt%d" % i, [128, 3, 512]) for i in range(2)]
    m1 = sb("mm1", [128, 512]); m2 = sb("mm2", [128, 512]); m3 = sb("mm3", [128, 512])
    mrg = sb("mrg", [128, 8, 512], BF16)
    xn = sb("xn", [128, 8, 512])
    pg = P.ps("pg", [128, 512]); po = P.ps("po", [128, 512])
    pabc = [[P.ps("pabc%d_%d" % (i, j), [128, 512]) for j in range(3)] for i in range(2)]
    view = lambda b_: b_.t.ap().rearrange("(k p) t -> p k t", p=128)
    xv = view(xsrc); xo = view(xdst)
    for (n0, n, w) in (tiles or TTILES):
        ts_ = slice(n0, n0 + n)
        P.D("sync", nga[:, :, :n], view(C.scr["NGA"])[:, :, ts_], reads=[C.scr["NGA"]], writes=[nga])
        P.D("sync", ngb[:, :, :n], view(C.scr["NGB"])[:, :, ts_], reads=[C.scr["NGB"]], writes=[ngb])
        P.D("sync", ycb[:, :, :n], view(C.scr["YC"])[:, :, ts_], reads=[C.scr["YC"]], writes=[ycb])
        P.D("sync", mg[:, :, :n], view(C.scr["MG"])[:, :, ts_], reads=[C.scr["MG"]], writes=[mg])
        P.D("sync", xT[:, :, :n], xv[:, :, ts_], reads=[(xsrc, n0)], writes=[xT])
        I("vector", "tensor_copy", reads=[ycb], writes=[yc], out=yc[:, :, :n], in_=ycb[:, :, :n])
        I("gpsimd", "tensor_tensor", reads=[yc], writes=[x2], out=x2[:, :, :n], in0=yc[:, :, :n], in1=yc[:, :, :n], op=ALU.mult)
        I("vector", "tensor_scalar", reads=[x2], writes=[x2], out=x2[:, :, :n], in0=x2[:, :, :n], scalar1=0.044715, scalar2=1.0, op0=ALU.mult, op1=ALU.add)
        I("gpsimd", "tensor_tensor", reads=[x2, yc], writes=[x2], out=x2[:, :, :n], in0=x2[:, :, :n], in1=yc[:, :, :n], op=ALU.mult)
        I("scalar", "activation", reads=[x2], writes=[tt], out=tt[:, :, :n], in_=x2[:, :, :n], func=AF.Tanh, scale=0.7978845608028654)
        I("scalar", "mul", reads=[yc], writes=[xh], out=xh[:, :, :n], in_=yc[:, :, :n], mul=0.5)
        I("vector", "scalar_tensor_tensor", reads=[tt, xh], writes=[zf], out=zf[:, :, :n], in0=tt[:, :, :n], scalar=1.0, in1=xh[:, :, :n], op0=ALU.add, op1=ALU.mult)
        I("gpsimd", "tensor_copy", reads=[zf], writes=[zb], out=zb[:, :, :n], in_=zf[:, :, :n])
        for m in range(4):
            for k in range(4):
                P.mm(pg[:, :n], wts["glu"][:, k, m * 128:(m + 1) * 128], zb[:, k, :n], k == 0, k == 3, reads=[wts["glu"], zb], writes=[pg])
            I("scalar", "activation", reads=[pg], writes=[sig], out=sig[:, :n], in_=pg[:, :n], func=AF.Sigmoid)
            I("vector", "tensor_tensor", reads=[zf, sig], writes=[(zz, m)], out=zz[:, m, :n], in0=zf[:, m, :n], in1=sig[:, :n], op=ALU.mult)
        for oc in range(8):
            pa, pb, pc = pabc[oc % 2]
            gt = gts[oc % 2]
            for (pp, wn, src) in ((pa, "ba", nga), (pb, "bb", ngb), (pc, "bc", zz)):
                for k in range(4):
                    P.mm(pp[:, :n], wts[wn][:, k, oc * 128:(oc + 1) * 128], src[:, k, :n], k == 0, k == 3, reads=[wts[wn], src], writes=[pp])
            mgv = AP(mg, oc * 512, [[24 * 512, 128], [8 * 512, 3], [1, n]])
            I("scalar", "activation", reads=[mg], writes=[gt], out=gt[:, :, :n], in_=mgv, func=AF.Sigmoid)
            I("vector", "tensor_tensor", reads=[pa, gt], writes=[m1], out=m1[:, :n], in0=pa[:, :n], in1=gt[:, 0, :n], op=ALU.mult)
            I("vector", "tensor_tensor", reads=[pb, gt], writes=[m2], out=m2[:, :n], in0=pb[:, :n], in1=gt[:, 1, :n], op=ALU.mult)
            I("vector", "tensor_tensor", reads=[pc, gt], writes=[m3], out=m3[:, :n], in0=pc[:, :n], in1=gt[:, 2, :n], op=ALU.mult)
            I("gpsimd", "tensor_tensor", reads=[m1, m2], writes=[m1], out=m1[:, :n], in0=m1[:, :n], in1=m2[:, :n], op=ALU.add)
            I("gpsimd", "tensor_tensor", reads=[m1, m3], writes=[(mrg, oc)], out=mrg[:, oc, :n], in0=m1[:, :n], in1=m3[:, :n], op=ALU.add)
        for oc in range(8):
            for k in range(8):
                P.mm(po[:, :n], wts["wo"][:, k, oc * 128:(oc + 1) * 128], mrg[:, k, :n], k == 0, k == 7, reads=[wts["wo"], mrg], writes=[po])
            I("vector", "scalar_tensor_tensor", reads=[po, xT, C.mod[l]], writes=[(xn, oc)], out=xn[:, oc, :n], in0=po[:, :n],
              scalar=C.mod[l][:, 16 + oc, w:w + 1], in1=xT[:, oc, :n], op0=ALU.mult, op1=ALU.add)
        P.D("sync", xo[:, :, ts_], xn[:, :, :n], reads=[xn], writes=[(xdst, n0)])
    P.emit()


FTILES = [(0, 256, 1)] + [(256 + 1024 * i, 1024, 0) for i in range(4)]


def phase_ffn(nc, C, l, xsrc, xdst, last):
    if last:
        tiles = [(256 + 1024 * i, 1024) for i in range(4)]
    else:
        tiles = [(0, 1536), (1536, 1536), (3072, 1280)]
    for (t0, tn) in tiles:
        sub = []
        i = t0
        while i < t0 + tn:
            if i < 256:
                sub.append((i, 256 - i, 1)); i = 256
            else:
                sz = min(512, t0 + tn - i)
                sub.append((i, sz, 0)); i += sz
        with ExitStack() as ost:
            _UID[0] += 1
            hT = Buf(ost.enter_context(nc.sbuf_tensor("hT2_%d" % _UID[0], [128, 8, tn], BF16)), "hT2")
            P = Prog(nc)
            compute_hT(P, C, xsrc, hT, C.g2[l], C.mod[l], 24, tiles=sub, hoff=t0)
            P.emit()
            ffn_tile(nc, C, l, xsrc, xdst, last, hT, t0, tn, [(a - t0, b, w_) for (a, b, w_) in sub])


def ffn_tile(nc, C, l, xsrc, xdst, last, hT, t0, tn, subs):
    P = Prog(nc)
    I = P.I
    sb = P.sb
    mid = sb("mid", [128, 32, tn], BF16)
    w1 = [sb("w1_%d" % i, [128, 8, 512], BF16) for i in range(2)]
    w2 = [sb("w2_%d" % i, [128, 32, 128], BF16) for i in range(2)]
    rl = [sb("rl%d" % i, [128, 512]) for i in range(2)]
    xt = [sb("xtf%d" % i, [128, tn]) for i in range(2)]
    pss = [P.ps("pf%d" % i, [128, 512]) for i in range(4)]
    w1v = C.din["w_ff1"][l].rearrange("(k p) c -> p k c", p=128)
    w2v = C.din["w_ff2"][l].rearrange("(k p) c -> p k c", p=128)
    xv = xsrc.t.ap().rearrange("(k p) t -> p k t", p=128)
    pi = 0
    for g in range(8):
        wb = w1[g % 2]
        P.D("gpsimd", wb[:], w1v[:, :, g * 512:(g + 1) * 512], writes=[wb])
        for m in range(4):
            mc = g * 4 + m
            for (s0, sn, w) in subs:
                pp = pss[pi % 4]; r_ = rl[pi % 2]; pi += 1
                for k in range(8):
                    P.mm(pp[:, :sn], wb[:, k, m * 128:(m + 1) * 128], hT[:, k, s0:s0 + sn], k == 0, k == 7, reads=[wb, hT], writes=[pp])
                I("scalar", "activation", reads=[pp], writes=[r_], out=r_[:, :sn], in_=pp[:, :sn], func=AF.Relu)
                I("gpsimd" if pi % 2 else "vector", "tensor_tensor", reads=[r_], writes=[(mid, (mc, s0))], out=mid[:, mc, s0:s0 + sn], in0=r_[:, :sn], in1=r_[:, :sn], op=ALU.mult)
    if last:
        xn = sb("xnf", [128, 8, tn])
    else:
        xns = [sb("xns%d" % i, [128, tn]) for i in range(2)]
    for oc in range(8):
        wb = w2[oc % 2]
        P.D("gpsimd", wb[:], w2v[:, :, oc * 128:(oc + 1) * 128], writes=[wb])
        x_ = xt[oc % 2]
        P.D("sync", x_[:, :tn], xv[:, oc, t0:t0 + tn], reads=[(xsrc, t0)], writes=[x_])
        for (s0, sn, w) in subs:
            pp = pss[pi % 4]; pi += 1
            for k in range(32):
                P.mm(pp[:, :sn], wb[:, k, :], mid[:, k, s0:s0 + sn], k == 0, k == 31, reads=[wb, mid], writes=[pp])
            if last:
                I("vector", "scalar_tensor_tensor", reads=[pp, x_, C.mod[l]], writes=[(xn, (oc, s0))], out=xn[:, oc, s0:s0 + sn], in0=pp[:, :sn],
                  scalar=C.mod[l][:, 40 + oc, w:w + 1], in1=x_[:, s0:s0 + sn], op0=ALU.mult, op1=ALU.add)
            else:
                xo_ = xns[oc % 2]
                I("vector", "scalar_tensor_tensor", reads=[pp, x_, C.mod[l]], writes=[(xo_, s0)], out=xo_[:, s0:s0 + sn], in0=pp[:, :sn],
                  scalar=C.mod[l][:, 40 + oc, w:w + 1], in1=x_[:, s0:s0 + sn], op0=ALU.mult, op1=ALU.add)
        if not last:
            P.D("sync", xdst[oc * 128:(oc + 1) * 128, t0:t0 + tn], xns[oc % 2][:, :tn], reads=[xns[oc % 2]], writes=[(xdst, (oc, t0))])
    if last:
        fw = sb("fw", [128, 8])
        P.D("sync", fw[:], C.din["final_norm_w"][:], writes=[fw])
        sq = sb("sqf", [128, 512])
        rs = sb("rsf", [128, 512])
        ov = C.out.t.ap().rearrange("(k p) t -> p k t", p=128)
        for (s0, sn, w) in subs:
            pp = pss[pi % 4]; pi += 1
            for k in range(8):
                I("gpsimd" if k % 2 else "vector", "tensor_tensor", reads=[xn], writes=[sq], out=sq[:, :sn], in0=xn[:, k, s0:s0 + sn], in1=xn[:, k, s0:s0 + sn], op=ALU.mult)
                P.mm(pp[:, :sn], C.ones[:], sq[:, :sn], k == 0, k == 7, reads=[sq, C.ones], writes=[pp])
            I("scalar", "activation", reads=[pp], writes=[rs], out=rs[:, :sn], in_=pp[:, :sn], func=AF.Sqrt, scale=1.0 / D, bias=EPS)
            I("vector", "reciprocal", reads=[rs], writes=[rs], out=rs[:, :sn], in_=rs[:, :sn])
            for k in range(8):
                I("vector", "scalar_tensor_tensor", reads=[xn, fw, rs], writes=[xn], out=xn[:, k, s0:s0 + sn], in0=xn[:, k, s0:s0 + sn], scalar=fw[:, k:k + 1],
                  in1=rs[:, :sn], op0=ALU.mult, op1=ALU.mult)
        P.D("sync", ov[:, :, t0 - 256:t0 - 256 + tn], xn[:, :, :tn], reads=[xn], writes=[(C.out, t0)])
    P.emit()
```
